# Optimizing a Trainium2 kernel written in Bass

```python
import math
import jax, jax.numpy as jnp
from jax import lax
import numpy as np

D_MODEL = 1024
BATCH = 4
SEQ = 4096
DEPTH = 4
DEC_BATCH = 32
DEC_SEQ = 4
PAST_LEN = 8192
PAGE_SIZE = 128

N_MIXERS = 2
N_GLA_LAYERS = (DEPTH + N_MIXERS - 1) // N_MIXERS
N_DIL_LAYERS = DEPTH // N_MIXERS
GLA_HEADS = 4
GLA_KD = D_MODEL // 2
GLA_VD = D_MODEL
GLA_DK = GLA_KD // GLA_HEADS
GLA_DV = GLA_VD // GLA_HEADS
GLA_GATE_RANK = 16
GLA_GATE_NORM = 16.0
GLA_CHUNK = 32
GLA_IN = 2 * GLA_KD + 2 * GLA_VD + GLA_GATE_RANK
DIL_GROUPS = ((128, 1), (512, 4), (2048, 16))
N_GROUPS = len(DIL_GROUPS)
DIL_HEADS = 16
DIL_HD = D_MODEL // DIL_HEADS
DIL_WIDTH = DIL_HEADS * DIL_HD
DIL_IN = N_GROUPS * 3 * DIL_WIDTH
DIL_BLOCK = 128
DIL_SCALE = DIL_HD ** -0.5
NUM_BUCKETS = 32
MAX_DISTANCE = 2048
D_FF = 2816
CONV_WIDTH = 3
EPS = 1e-6
NEG = -1e30

kernel_name = "hybrid_gla_dilated_swa_convffn_step"


def _rmsnorm(x, g):
    xf = x.astype(jnp.float32)
    y = xf * lax.rsqrt(jnp.mean(xf * xf, axis=-1, keepdims=True) + EPS)
    return (y * g.astype(jnp.float32)).astype(x.dtype)


def _rel_bucket(dist):
    max_exact = NUM_BUCKETS // 2
    df = jnp.maximum(dist, 1).astype(jnp.float32)
    large = max_exact + (jnp.log(df / max_exact) / math.log(MAX_DISTANCE / max_exact)
                         * (NUM_BUCKETS - max_exact)).astype(jnp.int32)
    large = jnp.minimum(large, NUM_BUCKETS - 1)
    return jnp.where(dist < max_exact, dist, large)


def _group_bias(rel_bias, g, d, steps):
    tab = rel_bias[:, g * DIL_HEADS:(g + 1) * DIL_HEADS].astype(jnp.float32)
    return jnp.moveaxis(tab[_rel_bucket(steps * d)], -1, 0)


def _gla_scan(q, k, v, g, s0):
    B, T, H = q.shape[:3]
    C = GLA_CHUNK
    n = -(-T // C)
    pad = n * C - T

    def prep(a):
        a = jnp.pad(a, ((0, 0), (0, pad), (0, 0), (0, 0)))
        return a.reshape(B, n, C, H, a.shape[-1]).transpose(1, 0, 3, 2, 4)

    causal = jnp.tril(jnp.ones((C, C), dtype=bool))
    mid = C // 2

    def step(S, inp):
        qc, kc, vc, gc = inp
        b = jnp.cumsum(gc, axis=2)
        ref = b[:, :, mid:mid + 1]
        a = jnp.einsum('bhid,bhjd->bhij', qc * jnp.exp(b - ref), kc * jnp.exp(ref - b))
        a = jnp.where(causal, a, 0.0)
        o = (jnp.einsum('bhij,bhje->bhie', a, vc)
             + jnp.einsum('bhid,bhde->bhie', qc * jnp.exp(b), S))
        b_last = b[:, :, -1:]
        S = (jnp.exp(b_last[:, :, 0])[..., None] * S
             + jnp.einsum('bhjd,bhje->bhde', kc * jnp.exp(b_last - b), vc))
        return S, o

    sT, o = lax.scan(step, s0, (prep(q), prep(k), prep(v), prep(g)))
    o = o.transpose(1, 0, 3, 2, 4).reshape(B, n * C, H, v.shape[-1])[:, :T]
    return o, sT


def _gla_mixer(h, w_in, w_g2, b_g, g_norm, w_out, s0):
    B, T, _ = h.shape
    p = h @ w_in
    q, k, v, r, gz = jnp.split(
        p, [GLA_KD, 2 * GLA_KD, 2 * GLA_KD + GLA_VD, 2 * GLA_KD + 2 * GLA_VD], axis=-1)
    heads = lambda a, e: a.reshape(B, T, GLA_HEADS, e).astype(jnp.float32)
    glog = jax.nn.log_sigmoid((gz @ w_g2 + b_g).astype(jnp.float32)) / GLA_GATE_NORM
    o, sT = _gla_scan(heads(q, GLA_DK) * (GLA_DK ** -0.5), heads(k, GLA_DK), heads(v, GLA_DV),
                      glog.reshape(B, T, GLA_HEADS, GLA_DK), s0.astype(jnp.float32))
    o = _rmsnorm(o, g_norm) * jax.nn.silu(heads(r, GLA_DV))
    return o.reshape(B, T, GLA_VD).astype(h.dtype) @ w_out, sT


def _dil_qkv(h, w_in, q_gain, k_gain):
    B, T, _ = h.shape
    p = (h @ w_in).reshape(B, T, N_GROUPS, 3, DIL_HEADS, DIL_HD)
    return _rmsnorm(p[:, :, :, 0], q_gain), _rmsnorm(p[:, :, :, 1], k_gain), p[:, :, :, 2]


def _dil_group_prompt(q, k, v, bias, d, J):
    B, T, H, E = q.shape
    L = T // d
    nb = -(-L // DIL_BLOCK)
    Lp = nb * DIL_BLOCK

    def sub(a):
        a = a.reshape(B, L, d, H, E).transpose(0, 2, 1, 3, 4)
        a = jnp.pad(a, ((0, 0), (0, 0), (0, Lp - L), (0, 0), (0, 0)))
        return a.reshape(B, d, nb, DIL_BLOCK, H, E)

    def window(a):
        prev = jnp.pad(a, ((0, 0), (0, 0), (1, 0), (0, 0), (0, 0), (0, 0)))[:, :, :-1]
        return jnp.concatenate([prev, a], axis=3)

    qb = sub(q)
    kw, vw = window(sub(k)), window(sub(v))
    n_i = jnp.arange(nb)[:, None, None]
    q_i = jnp.arange(DIL_BLOCK)[None, :, None]
    k_i = jnp.arange(2 * DIL_BLOCK)[None, None, :]
    rel = q_i + DIL_BLOCK - k_i
    valid = (rel >= 0) & (rel <= J) & ((n_i - 1) * DIL_BLOCK + k_i >= 0)
    s = jnp.einsum('brnqhe,brnkhe->brnhqk', qb, kw,
                   preferred_element_type=jnp.float32) * DIL_SCALE + bias
    s = jnp.where(valid[None, None, :, None], s, NEG)
    m = jnp.max(s, axis=-1, keepdims=True)
    p = jnp.exp(s - m)
    den = jnp.sum(p, axis=-1)
    o = jnp.einsum('brnhqk,brnkhe->brnqhe', p, vw.astype(jnp.float32))
    o = o / jnp.swapaxes(den, -1, -2)[..., None]
    lse = m[..., 0] + jnp.log(den)
    o = o.reshape(B, d, Lp, H, E)[:, :, :L].transpose(0, 2, 1, 3, 4).reshape(B, T, H, E)
    lse = jnp.swapaxes(lse, -1, -2).reshape(B, d, Lp, H)[:, :, :L]
    lse = lse.transpose(0, 2, 1, 3).reshape(B, T, H)
    return o, lse


def _dil_group_sample(q, k_new, v_new, k_buf, v_buf, bias, d, J):
    S = q.shape[1]
    Wb = k_buf.shape[1]
    kx = jnp.concatenate([k_buf.astype(k_new.dtype), k_new], axis=1)
    vx = jnp.concatenate([v_buf.astype(v_new.dtype), v_new], axis=1)
    idx = Wb + jnp.arange(S)[:, None] - d * jnp.arange(J + 1)[None, :]
    valid = idx >= 0
    idxc = jnp.maximum(idx, 0)
    kg, vg = kx[:, idxc], vx[:, idxc]
    s = jnp.einsum('bshe,bsjhe->bhsj', q, kg,
                   preferred_element_type=jnp.float32) * DIL_SCALE + bias[:, None, :]
    s = jnp.where(valid, s, NEG)
    m = jnp.max(s, axis=-1, keepdims=True)
    p = jnp.exp(s - m)
    den = jnp.sum(p, axis=-1)
    o = jnp.einsum('bhsj,bsjhe->bshe', p, vg.astype(jnp.float32))
    o = o / jnp.swapaxes(den, 1, 2)[..., None]
    lse = jnp.swapaxes(m[..., 0] + jnp.log(den), 1, 2)
    return o, lse


def _combine_groups(outs, lses):
    w = jax.nn.softmax(jnp.stack(lses, axis=0), axis=0)
    return jnp.sum(w[..., None] * jnp.stack(outs, axis=0), axis=0)


def _dil_mixer_prompt(h, w_in, q_gain, k_gain, w_out, rel_bias):
    B, T, _ = h.shape
    q, k, v = _dil_qkv(h, w_in, q_gain, k_gain)
    rel = jnp.arange(DIL_BLOCK)[:, None] + DIL_BLOCK - jnp.arange(2 * DIL_BLOCK)[None, :]
    outs, lses, k_rows, v_rows = [], [], [], []
    for g, (W, d) in enumerate(DIL_GROUPS):
        J = W // d
        bias = _group_bias(rel_bias, g, d, jnp.clip(rel, 0, J))
        o, lse = _dil_group_prompt(q[:, :, g], k[:, :, g], v[:, :, g], bias, d, J)
        outs.append(o)
        lses.append(lse)
        keep = min(W, T)
        k_rows.append(k[:, T - keep:, g])
        v_rows.append(v[:, T - keep:, g])
    o = _combine_groups(outs, lses)
    return o.reshape(B, T, DIL_WIDTH).astype(h.dtype) @ w_out, k_rows, v_rows


def _dil_mixer_sample(h, w_in, q_gain, k_gain, w_out, rel_bias, k_bufs, v_bufs):
    B, T, _ = h.shape
    q, k, v = _dil_qkv(h, w_in, q_gain, k_gain)
    outs, lses, k_rows, v_rows = [], [], [], []
    for g, (W, d) in enumerate(DIL_GROUPS):
        J = W // d
        bias = _group_bias(rel_bias, g, d, jnp.arange(J + 1))
        o, lse = _dil_group_sample(q[:, :, g], k[:, :, g], v[:, :, g],
                                   k_bufs[g], v_bufs[g], bias, d, J)
        outs.append(o)
        lses.append(lse)
        k_rows.append(k[:, :, g])
        v_rows.append(v[:, :, g])
    o = _combine_groups(outs, lses)
    return o.reshape(B, T, DIL_WIDTH).astype(h.dtype) @ w_out, k_rows, v_rows


def _conv_ffn(h, w_up, conv_w, conv_b, w_down, buf):
    T = h.shape[1]
    u = h @ w_up
    ux = jnp.concatenate([buf.astype(u.dtype), u], axis=1)
    c = conv_b + sum(ux[:, i:i + T] * conv_w[i] for i in range(CONV_WIDTH))
    gate, val = jnp.split(c, 2, axis=-1)
    return (jax.nn.silu(gate) * val) @ w_down, ux[:, -(CONV_WIDTH - 1):]


def setup_inputs(seed: int = 0) -> dict:
    key = jax.random.key(seed)
    ks = jax.random.split(key, 32)
    f32 = jnp.float32
    nrm = lambda k, shape, scale=1.0: jax.random.normal(k, shape, f32) * scale
    wb = [min(w, PAST_LEN) for (w, _) in DIL_GROUPS]
    cshape = lambda i: (N_DIL_LAYERS, DEC_BATCH, wb[i], DIL_HEADS, DIL_HD)
    return {
        "x_prompt": nrm(ks[0], (BATCH, SEQ, D_MODEL)),
        "x_sample": nrm(ks[1], (DEC_BATCH, DEC_SEQ, D_MODEL)),
        "state_gla": nrm(ks[2], (N_GLA_LAYERS, DEC_BATCH, GLA_HEADS, GLA_DK, GLA_DV)),
        "cache_k_g0": nrm(ks[3], cshape(0)),
        "cache_v_g0": nrm(ks[4], cshape(0)),
        "cache_k_g1": nrm(ks[5], cshape(1)),
        "cache_v_g1": nrm(ks[6], cshape(1)),
        "cache_k_g2": nrm(ks[7], cshape(2)),
        "cache_v_g2": nrm(ks[8], cshape(2)),
        "state_ffn_conv": nrm(ks[9], (DEPTH, DEC_BATCH, CONV_WIDTH - 1, 2 * D_FF)),
        "rel_bias": nrm(ks[10], (NUM_BUCKETS, N_GROUPS * DIL_HEADS), 0.2),
        "norm_mix": 1.0 + nrm(ks[11], (DEPTH, D_MODEL), 0.02),
        "norm_ffn": 1.0 + nrm(ks[12], (DEPTH, D_MODEL), 0.02),
        "gla_w_in": nrm(ks[13], (N_GLA_LAYERS, D_MODEL, GLA_IN), D_MODEL ** -0.5),
        "gla_w_gate2": nrm(ks[14], (N_GLA_LAYERS, GLA_GATE_RANK, GLA_KD), GLA_GATE_RANK ** -0.5),
        "gla_b_gate": nrm(ks[15], (N_GLA_LAYERS, GLA_KD), 0.1),
        "gla_norm": 1.0 + nrm(ks[16], (N_GLA_LAYERS, GLA_DV), 0.02),
        "gla_w_out": nrm(ks[17], (N_GLA_LAYERS, GLA_VD, D_MODEL), GLA_VD ** -0.5),
        "dil_w_in": nrm(ks[18], (N_DIL_LAYERS, D_MODEL, DIL_IN), D_MODEL ** -0.5),
        "dil_q_norm": 1.0 + nrm(ks[19], (N_DIL_LAYERS, DIL_HD), 0.02),
        "dil_k_norm": 1.0 + nrm(ks[20], (N_DIL_LAYERS, DIL_HD), 0.02),
        "dil_w_out": nrm(ks[21], (N_DIL_LAYERS, DIL_WIDTH, D_MODEL), DIL_WIDTH ** -0.5),
        "ffn_w_up": nrm(ks[22], (DEPTH, D_MODEL, 2 * D_FF), D_MODEL ** -0.5),
        "ffn_conv_w": nrm(ks[23], (DEPTH, CONV_WIDTH, 2 * D_FF), CONV_WIDTH ** -0.5),
        "ffn_conv_b": nrm(ks[24], (DEPTH, 2 * D_FF), 0.02),
        "ffn_w_down": nrm(ks[25], (DEPTH, D_FF, D_MODEL), D_FF ** -0.5),
    }


def reference(x_prompt, x_sample, state_gla, cache_k_g0, cache_v_g0, cache_k_g1, cache_v_g1,
              cache_k_g2, cache_v_g2, state_ffn_conv, rel_bias, norm_mix, norm_ffn,
              gla_w_in, gla_w_gate2, gla_b_gate, gla_norm, gla_w_out,
              dil_w_in, dil_q_norm, dil_k_norm, dil_w_out,
              ffn_w_up, ffn_conv_w, ffn_conv_b, ffn_w_down):
    k_bufs = (cache_k_g0, cache_k_g1, cache_k_g2)
    v_bufs = (cache_v_g0, cache_v_g1, cache_v_g2)
    B = x_prompt.shape[0]
    xp, xs = x_prompt, x_sample
    gla_p, gla_s = [], []
    kp = [[] for _ in DIL_GROUPS]
    vp = [[] for _ in DIL_GROUPS]
    kq = [[] for _ in DIL_GROUPS]
    vq = [[] for _ in DIL_GROUPS]
    conv_p, conv_s = [], []
    for i in range(DEPTH):
        li = i // N_MIXERS
        hp, hs = _rmsnorm(xp, norm_mix[i]), _rmsnorm(xs, norm_mix[i])
        if i % N_MIXERS == 0:
            s0 = jnp.zeros((B, GLA_HEADS, GLA_DK, GLA_DV), jnp.float32)
            mp, st_p = _gla_mixer(hp, gla_w_in[li], gla_w_gate2[li], gla_b_gate[li],
                                  gla_norm[li], gla_w_out[li], s0)
            ms, st_s = _gla_mixer(hs, gla_w_in[li], gla_w_gate2[li], gla_b_gate[li],
                                  gla_norm[li], gla_w_out[li], state_gla[li])
            gla_p.append(st_p)
            gla_s.append(st_s)
        else:
            mp, krp, vrp = _dil_mixer_prompt(hp, dil_w_in[li], dil_q_norm[li], dil_k_norm[li],
                                             dil_w_out[li], rel_bias)
            ms, krs, vrs = _dil_mixer_sample(hs, dil_w_in[li], dil_q_norm[li], dil_k_norm[li],
                                             dil_w_out[li], rel_bias,
                                             [kb[li] for kb in k_bufs], [vb[li] for vb in v_bufs])
            for g in range(N_GROUPS):
                kp[g].append(krp[g])
                vp[g].append(vrp[g])
                kq[g].append(krs[g])
                vq[g].append(vrs[g])
        xp, xs = xp + mp, xs + ms
        hp, hs = _rmsnorm(xp, norm_ffn[i]), _rmsnorm(xs, norm_ffn[i])
        buf0 = jnp.zeros((B, CONV_WIDTH - 1, 2 * D_FF), xp.dtype)
        fp, cp = _conv_ffn(hp, ffn_w_up[i], ffn_conv_w[i], ffn_conv_b[i], ffn_w_down[i], buf0)
        fs, cs = _conv_ffn(hs, ffn_w_up[i], ffn_conv_w[i], ffn_conv_b[i], ffn_w_down[i],
                           state_ffn_conv[i])
        conv_p.append(cp)
        conv_s.append(cs)
        xp, xs = xp + fp, xs + fs
    state_gla_prompt = jnp.stack(gla_p)
    state_gla_sample = jnp.stack(gla_s)
    cache_k_g0_prompt, cache_k_g0_sample = jnp.stack(kp[0]), jnp.stack(kq[0])
    cache_v_g0_prompt, cache_v_g0_sample = jnp.stack(vp[0]), jnp.stack(vq[0])
    cache_k_g1_prompt, cache_k_g1_sample = jnp.stack(kp[1]), jnp.stack(kq[1])
    cache_v_g1_prompt, cache_v_g1_sample = jnp.stack(vp[1]), jnp.stack(vq[1])
    cache_k_g2_prompt, cache_k_g2_sample = jnp.stack(kp[2]), jnp.stack(kq[2])
    cache_v_g2_prompt, cache_v_g2_sample = jnp.stack(vp[2]), jnp.stack(vq[2])
    state_ffn_conv_prompt = jnp.stack(conv_p)
    state_ffn_conv_sample = jnp.stack(conv_s)
    return (xp, xs, state_gla_prompt, state_gla_sample,
            cache_k_g0_prompt, cache_k_g0_sample, cache_v_g0_prompt, cache_v_g0_sample,
            cache_k_g1_prompt, cache_k_g1_sample, cache_v_g1_prompt, cache_v_g1_sample,
            cache_k_g2_prompt, cache_k_g2_sample, cache_v_g2_prompt, cache_v_g2_sample,
            state_ffn_conv_prompt, state_ffn_conv_sample)
```

```python
import contextlib
import math
import numpy as np
import concourse.bass as bass
import concourse.mybir as mybir
from concourse.bass_utils import run_bass_kernel_spmd

F32 = mybir.dt.float32
BF16 = mybir.dt.bfloat16
AF = mybir.ActivationFunctionType
ALU = mybir.AluOpType
AX = mybir.AxisListType

ENGS = ("pe", "act", "dve", "pool", "sp")

D = 1024
SEQ = 4096
NSEQ_S = 4
TS = 4
DEPTH = 4
GLA_IN = 3088
DFF = 2816
EPS = 1e-6
GROUPS = ((128, 1), (512, 4), (2048, 16))
NB = 32


class T:
    __slots__ = ("name", "h", "writers", "readers", "dsem", "dcount")

    def __init__(self, name, h):
        self.name = name
        self.h = h
        self.writers = []
        self.readers = []
        self.dsem = None
        self.dcount = 0

    def __getitem__(self, k):
        return self.h[k]


class Op:
    __slots__ = ("eng", "fn", "deps", "sig", "signal", "dma_key", "pos")

    def __init__(self, eng, fn):
        self.eng = eng
        self.fn = fn
        self.deps = []
        self.sig = False
        self.signal = None
        self.dma_key = None


class Prog:
    ARENA = 207 * 1024

    def __init__(self, nc):
        self.nc = nc
        self.es = contextlib.ExitStack()
        self.streams = {e: [] for e in ENGS}
        self.out_dmas = []
        self.arena = self.es.enter_context(nc.sbuf_tensor("arena", [128, self.ARENA // 4], F32))
        self.arena_bf = self.arena.bitcast(BF16)
        self.off = 0
        self.pending = {e: [] for e in ENGS}
        self.dma_since = []
        self.peak = 0

    def sb(self, name, shape, dtype):
        isz = 4 if dtype == F32 else 2
        n = 1
        for d in shape[1:]:
            n *= d
        nbytes = (n * isz + 63) // 64 * 64
        off = self.off
        self.off += nbytes
        self.peak = max(self.peak, self.off)
        assert self.off <= self.ARENA, ("SBUF arena overflow", name, self.off)
        base = self.arena if dtype == F32 else self.arena_bf
        a = base[0:shape[0], off // isz:off // isz + n]
        if len(shape) == 3:
            a = a.rearrange("p (a b) -> p a b", b=shape[2])
        return T(name, a)

    def ps(self, name, shape, dtype):
        h = self.es.enter_context(self.nc.psum_tensor(name, list(shape), dtype))
        return T(name, h)

    def dram(self, name, shape, dtype, kind="Internal"):
        h = self.nc.dram_tensor(name, list(shape), dtype, kind=kind)
        return T(name, h)

    def barrier(self):
        deps = []
        for e in ENGS:
            for o in reversed(self.streams[e]):
                if o.dma_key is None:
                    o.sig = True
                    deps.append(o)
                    break
        deps.extend(self.dma_since)
        self.dma_since = []
        for e in ENGS:
            self.pending[e].extend(deps)

    def _track(self, op, reads, writes, partial):
        deps = []
        for t in reads:
            deps.extend(t.writers)
        for t in writes:
            others = [r for r in t.readers if r is not op]
            if others:
                deps.extend(others)
                deps.extend(t.writers)
                t.writers = [op]
                t.readers = []
            elif partial:
                t.writers.append(op)
            else:
                deps.extend(t.writers)
                t.writers = [op]
        for t in reads:
            t.readers.append(op)
        seen = set()
        best = {}
        for d in deps:
            if d is op or id(d) in seen:
                continue
            seen.add(id(d))
            if d.eng == "pe" and op.eng == "pe" and d.dma_key is None and op.dma_key is None:
                continue
            if d.dma_key is None:
                if d.eng not in best or best[d.eng].pos < d.pos:
                    best[d.eng] = d
            else:
                op.deps.append(d)
        for d in best.values():
            op.deps.append(d)
            d.sig = True

    def _pend(self, o):
        if self.pending[o.eng]:
            have = set(id(d) for d in o.deps)
            for d in self.pending[o.eng]:
                if id(d) not in have and d is not o:
                    o.deps.append(d)
            self.pending[o.eng] = []

    def op(self, eng, fn, reads=(), writes=(), partial=False):
        o = Op(eng, fn)
        o.pos = len(self.streams[eng])
        self._track(o, list(reads), list(writes), partial)
        self._pend(o)
        self.streams[eng].append(o)
        return o

    def dma(self, eng, out_ap, in_ap, key, reads=(), writes=(), partial=False, final=False, **kw):
        def fn(e):
            return e.dma_start(out=out_ap, in_=in_ap, **kw)
        o = Op(eng, fn)
        o.pos = len(self.streams[eng])
        o.dma_key = key
        o.sig = True
        self._track(o, list(reads), list(writes), partial)
        self._pend(o)
        self.streams[eng].append(o)
        self.dma_since.append(o)
        if final:
            self.out_dmas.append(o)
        return o

    def emit(self):
        nc = self.nc
        es = self.es
        esem = {e: es.enter_context(nc.semaphore("sem_" + e)) for e in ENGS}
        ecount = {e: 0 for e in ENGS}
        nkeys = 0
        semtab = {}
        for e in ENGS:
            for o in self.streams[e]:
                if o.dma_key is not None:
                    kk = (o.dma_key.name, e)
                    if kk not in semtab:
                        semtab[kk] = [es.enter_context(nc.semaphore("dsem_%s_%s" % kk)), 0]
                        nkeys += 1
                    semtab[kk][1] += 16
                    o.signal = (semtab[kk][0], semtab[kk][1], 16)
                elif o.sig:
                    ecount[e] += 1
                    o.signal = (esem[e], ecount[e], 1)
        self.ecount = ecount
        self.nkeys = nkeys
        streams = self.streams
        finals = {}
        for o in self.out_dmas:
            sem, val, _ = o.signal
            if finals.get(id(sem), (None, 0))[1] < val:
                finals[id(sem)] = (sem, val)

        def run(e, h):
            waited = {}
            for o in streams[e]:
                need = {}
                for d in o.deps:
                    sem, val, _ = d.signal
                    if need.get(id(sem), (None, 0))[1] < val:
                        need[id(sem)] = (sem, val)
                for sem, val in need.values():
                    if waited.get(id(sem), 0) < val:
                        h.wait_ge(sem, val)
                        waited[id(sem)] = val
                ins = o.fn(h)
                if o.signal is not None:
                    ins.then_inc(o.signal[0], o.signal[2])
            if e == "sp":
                for sem, val in finals.values():
                    if waited.get(id(sem), 0) < val:
                        h.wait_ge(sem, val)

        with nc.Block() as block:
            @block.tensor
            def _(h):
                run("pe", h)

            @block.scalar
            def _(h):
                run("act", h)

            @block.vector
            def _(h):
                run("dve", h)

            @block.gpsimd
            def _(h):
                run("pool", h)

            @block.sync
            def _(h):
                run("sp", h)
        es.close()


import os as _os
_STOP = int(_os.environ.get("KDBG_STOP", "0"))


class _Stop(Exception):
    pass


def _chk(k):
    if _STOP == k:
        raise _Stop()


class Ring:
    def __init__(self, items):
        self.items = items
        self.i = 0

    def next(self):
        t = self.items[self.i % len(self.items)]
        self.i += 1
        return t


def _bucket(dist):
    max_exact = NB // 2
    if dist < max_exact:
        return dist
    df = np.float32(max(dist, 1))
    v = np.float32(np.log(df / np.float32(max_exact))) / np.float32(math.log(2048 / max_exact)) * np.float32(NB - max_exact)
    return min(max_exact + int(v), NB - 1)


def host_constants():
    ohp = np.zeros((NB, 3 * 384), np.float32)
    valid = np.zeros((1, 3 * 384), np.float32)
    for g, (W, d) in enumerate(GROUPS):
        for rel in range(129):
            n = rel + 127
            ohp[_bucket(rel * d), g * 384 + n] = 1.0
            valid[0, g * 384 + n] = 1.0
    ohs = np.zeros((NB, 6 * 128), np.float32)
    for s in range(4):
        for m in range(128):
            j = (s - m) if m < s else (128 + s - m)
            ohs[_bucket(j * 1), s * 128 + m] = 1.0
    for v, d in ((4, GROUPS[1][1]), (5, GROUPS[2][1])):
        for m in range(128):
            j = 128 - m
            ohs[_bucket(j * d), v * 128 + m] = 1.0
    return ohp, valid, ohs


def build_program(depth=DEPTH, do_ffn=True):
    NBLK = SEQ // 128
    NT4 = NBLK // 4
    nc = bass.Bass("TRN2", target_bir_lowering=False)
    P = Prog(nc)

    def din(name, shape):
        return P.dram(name, shape, F32, kind="ExternalInput")

    def dout(name, shape):
        return P.dram(name, shape, F32, kind="ExternalOutput")

    xp = din("xp", [SEQ, D])
    xs = din("xs", [16, D])
    sg_in = din("sg", [2 * 4 * 4 * 128, 256])
    ck = [din("ck%d" % g, [2 * 4 * GROUPS[g][0], D]) for g in range(3)]
    cv = [din("cv%d" % g, [2 * 4 * GROUPS[g][0], D]) for g in range(3)]
    sfc = din("sfc", [32, 2 * DFF])
    rel_bias = din("rel_bias", [NB, 48])
    norm_mix = din("norm_mix", [4, D])
    norm_ffn = din("norm_ffn", [4, D])
    gla_w_in = din("gla_w_in", [2 * D, GLA_IN])
    gla_w_g2 = din("gla_w_gate2", [32, 512])
    gla_b_g = din("gla_b_gate", [2, 512])
    gla_norm = din("gla_norm", [2, 256])
    gla_w_out = din("gla_w_out", [2 * D, D])
    dil_w_in = din("dil_w_in", [2 * D, 9216])
    dil_qn = din("dil_q_norm", [2, 64])
    dil_kn = din("dil_k_norm", [2, 64])
    dil_w_out = din("dil_w_out", [2 * D, D])
    ffn_w_up = din("ffn_w_up", [4 * D, 2 * DFF])
    ffn_cw = din("ffn_conv_w", [12, 2 * DFF])
    ffn_cb = din("ffn_conv_b", [4, 2 * DFF])
    ffn_w_down = din("ffn_w_down", [4 * DFF, D])
    c_ohp = din("c_ohp", [NB, 3 * 384])
    c_valid = din("c_valid", [1, 3 * 384])
    c_ohs = din("c_ohs", [NB, 6 * 128])

    yp = dout("yp", [SEQ, D])
    ys = dout("ys", [16, D])
    sgp = dout("sgp", [2 * 4 * 128, 256])
    sgs = dout("sgs", [2 * 4 * 4 * 128, 256])
    keep = [min(GROUPS[g][0], SEQ) for g in range(3)]
    kpo = [dout("kp%d" % g, [2 * keep[g], D]) for g in range(3)]
    vpo = [dout("vp%d" % g, [2 * keep[g], D]) for g in range(3)]
    kso = [dout("ks%d" % g, [2 * 16, D]) for g in range(3)]
    vso = [dout("vs%d" % g, [2 * 16, D]) for g in range(3)]
    fcp = dout("fcp", [8, 2 * DFF])
    fcs = dout("fcs", [32, 2 * DFF])

    ug_scr = [P.dram("ug%d" % g, [SEQ, 1280], F32) for g in (1, 2)]
    wsc = P.dram("wsc", [48, 384], F32)

    yp_blk = [T("ypb%d" % i, yp.h) for i in range(NBLK)]
    ys_blk = T("ysb", ys.h)
    xp_blk = [T("xpb%d" % i, xp.h) for i in range(NBLK)]
    xs_blk = T("xsb", xs.h)

    PSB = [P.ps("psb%d" % i, [128, 1024], BF16) if i < 2 else P.ps("psb%d" % i, [128, 512], F32) for i in range(8)]
    class _RR:
        pass
    RR = _RR()
    RR.tp = Ring([0, 1])
    RR.mmA = Ring([(2, 3), (4, 5)])
    RR.mmB = Ring([6, 7])

    class _Dyn:
        def __init__(self, nm):
            self.nm = nm

        def next(self):
            return getattr(RR, self.nm).next()
    tp_ring = _Dyn("tp")
    mmA_ring = _Dyn("mmA")
    mmB_ring = _Dyn("mmB")

    def psA(pair):
        return PSB[pair[0]], PSB[pair[1]]

    identf = P.sb("identf", [128, 128], F32)
    ident = P.sb("ident", [128, 128], BF16)
    onesF = P.sb("onesF", [128, 128], F32)
    epsT = P.sb("epsT", [128, 1], F32)
    mask_s = P.sb("mask_s", [128, 128], F32)
    Jm = P.sb("Jm", [128, 128], BF16)
    Jf = P.sb("Jf", [128, 128], F32)
    gmix = P.sb("gmix", [128, 4, 8], F32)
    gffn = P.sb("gffn", [128, 4, 8], F32)

    P.op("pool", lambda e: e.memset(identf[:, :], 0.0), writes=[identf])
    P.op("pool", lambda e: e.affine_select(out=identf[:, :], in_=identf[:, :], pattern=[[-1, 128]], compare_op=ALU.not_equal, fill=1.0, base=0, channel_multiplier=1), reads=[identf], writes=[identf])
    P.op("dve", lambda e: e.tensor_copy(out=ident[:, :], in_=identf[:, :]), reads=[identf], writes=[ident])
    P.op("pool", lambda e: e.memset(Jf[:, :], 0.0), writes=[Jf])
    P.op("pool", lambda e: e.affine_select(out=Jf[:, :], in_=Jf[:, :], pattern=[[1, 128]], compare_op=ALU.not_equal, fill=1.0, base=-127, channel_multiplier=1), reads=[Jf], writes=[Jf])
    P.op("dve", lambda e: e.tensor_copy(out=Jm[:, :], in_=Jf[:, :]), reads=[Jf], writes=[Jm])
    P.op("dve", lambda e: e.memset(onesF[:, :], 1.0), writes=[onesF])
    P.op("dve", lambda e: e.memset(epsT[:, :], EPS), writes=[epsT])
    GSC = 128.0 ** -0.5
    P.op("pool", lambda e: e.memset(mask_s[:, :], GSC), writes=[mask_s])
    P.op("pool", lambda e: e.affine_select(out=mask_s[:, :], in_=mask_s[:, :], pattern=[[1, 128]], compare_op=ALU.is_ge, fill=0.0, base=0, channel_multiplier=-1), reads=[mask_s], writes=[mask_s])
    rowtmp = P.sb("rowtmp", [4, 512], F32)

    def load_featmajor(dst_view, dst_unit, src_h, row0, nrows, ncols, bank):
        nch = ncols // 128
        for c0 in range(0, ncols, 512):
            w = min(512, ncols - c0)
            P.dma("sp", rowtmp[0:nrows, 0:w], src_h[row0:row0 + nrows, c0:c0 + w], key=rowtmp, writes=[rowtmp])
            for cc in range(w // 128):
                ch = c0 // 128 + cc
                P.op("pe", lambda e, cc=cc, ch=ch: e.matmul(out=PSB[bank][:, ch * nrows:(ch + 1) * nrows], lhsT=rowtmp[0:nrows, cc * 128:(cc + 1) * 128], rhs=identf[0:nrows, 0:nrows], start=True, stop=True), reads=[rowtmp, identf], writes=[PSB[bank]], partial=True)
        P.op("act", lambda e: e.copy(out=dst_view, in_=PSB[bank][:, 0:nch * nrows].rearrange("p (c r) -> p c r", r=nrows)), reads=[PSB[bank]], writes=[dst_unit])

    load_featmajor(gmix[:, :, :].rearrange("p l c -> p c l"), gmix, norm_mix.h, 0, 4, D, 6)
    load_featmajor(gffn[:, :, :].rearrange("p l c -> p c l"), gffn, norm_ffn.h, 0, 4, D, 7)

    xt_ring = Ring([P.sb("xt%d" % i, [128, D], F32) for i in range(2)])
    xr_ring = Ring([P.sb("xr%d" % i, [128, D], F32) for i in range(2)])
    sq_scr = P.sb("sq_scr", [128, D], F32)
    xn_ring = Ring([P.sb("xn%d" % i, [128, D], BF16) for i in range(2)])
    st_ring = Ring([P.sb("st%d" % i, [128, 8], F32) for i in range(4)])
    HT = {"ring": None}
    on_ring = Ring([P.sb("on%d" % i, [128, D], BF16) for i in range(2)])
    onT_ring = Ring([P.sb("onT%d" % i, [128, 8, 128], BF16) for i in range(2)])
    PERSIST = P.off

    def phase_begin(n_hT, width):
        P.barrier()
        P.off = PERSIST
        HT["ring"] = Ring([P.sb("hT%d" % i, [128, 8, width], BF16) for i in range(n_hT)])

    def rstd_from_ss(st, col_in, col_out, npart, scale):
        w = col_out.stop - col_out.start
        P.op("act", lambda e: e.activation(out=st[0:npart, col_out], in_=st[0:npart, col_in], func=AF.Ln, scale=scale, bias=epsT[0:npart, 0:1]), reads=[st, epsT], writes=[st])
        P.op("act", lambda e: e.activation(out=st[0:npart, col_out], in_=st[0:npart, col_out], func=AF.Exp, scale=-0.5), reads=[st], writes=[st])

    def norm_front(blocks, gain, layer):
        hT = HT["ring"].next()
        col = 0
        for (src_ap, units, n) in blocks:
            xt = xt_ring.next()
            P.dma("sp", xt[0:n, :], src_ap, key=xt, reads=units, writes=[xt])
            st = st_ring.next()
            P.op("act", lambda e, xt=xt, st=st, n=n: e.activation(out=sq_scr[0:n, :], in_=xt[0:n, :], func=AF.Square, accum_out=st[0:n, 0:1]), reads=[xt], writes=[st])
            rstd_from_ss(st, slice(0, 1), slice(1, 2), n, 1.0 / D)
            xn = xn_ring.next()
            P.op("act", lambda e, xt=xt, st=st, xn=xn, n=n: e.activation(out=xn[0:n, :], in_=xt[0:n, :], func=AF.Copy, scale=st[0:n, 1:2]), reads=[xt, st], writes=[xn])
            tb = tp_ring.next()
            tpv = PSB[tb].h
            for c in range(8):
                P.op("pe", lambda e, c=c, xn=xn, n=n, tpv=tpv: e.transpose(out=tpv[:, c * 128:c * 128 + n], in_=xn[0:n, c * 128:(c + 1) * 128], identity=ident[0:n, 0:n]), reads=[xn, ident], writes=[PSB[tb]], partial=True)
            c0 = col
            P.op("dve", lambda e, tpv=tpv, hT=hT, n=n, c0=c0: e.tensor_tensor(
                out=hT[:, :, c0:c0 + n], in0=tpv[:, :].rearrange("p (c t) -> p c t", t=128)[:, :, 0:n],
                in1=gain[:, layer, :].unsqueeze(2).broadcast_to([128, 8, n]), op=ALU.mult),
                reads=[PSB[tb], gain], writes=[hT], partial=True)
            col += n
        return hT, col

    def load_w(dst, cols, src_h, row0, nrows, col0, ncols):
        kc = nrows // 128
        step = 512
        for k0 in range(0, kc, 8):
            k1 = min(kc, k0 + 8)
            for c in range(0, ncols, step):
                w = min(step, ncols - c)
                src = src_h[row0 + k0 * 128:row0 + k1 * 128, col0 + c:col0 + c + w].rearrange("(kc p) n -> p kc n", p=128)
                P.dma("pool", dst[:, k0:k1, cols.start + c:cols.start + c + w], src, key=dst, writes=[dst], partial=True)

    def transpose_to(on, npart, onT):
        tb = tp_ring.next()
        tpv = PSB[tb].h
        for c in range(8):
            P.op("pe", lambda e, c=c, tpv=tpv: e.transpose(out=tpv[:, c * 128:c * 128 + npart], in_=on[0:npart, c * 128:(c + 1) * 128], identity=ident[0:npart, 0:npart]), reads=[on, ident], writes=[PSB[tb]], partial=True)
        P.op("act", lambda e, tpv=tpv: e.copy(out=onT[:, :, 0:npart], in_=tpv[:, :].rearrange("p (c t) -> p c t", t=128)[:, :, 0:npart]), reads=[PSB[tb]], writes=[onT])

    def out_proj_residual(onT, npart, wout, nchunks, src_ap, src_units, dst_ap, dst_units, final, off=0):
        pair = mmA_ring.next()
        for half in range(2):
            b = PSB[pair[half]]
            for c in range(nchunks):
                P.op("pe", lambda e, c=c, b=b, half=half: e.matmul(out=b[0:npart, :], lhsT=onT[:, c, off:off + npart], rhs=wout[:, c, half * 512:(half + 1) * 512], start=(c == 0), stop=(c == nchunks - 1)), reads=[onT, wout], writes=[b], partial=True)
        xr = xr_ring.next()
        P.dma("sp", xr[0:npart, :], src_ap, key=xr, reads=src_units, writes=[xr])
        for half in range(2):
            b = PSB[pair[half]]
            P.op("dve", lambda e, b=b, half=half, xr=xr: e.tensor_tensor(out=xr[0:npart, half * 512:(half + 1) * 512], in0=b[0:npart, :], in1=xr[0:npart, half * 512:(half + 1) * 512], op=ALU.add), reads=[b, xr], writes=[xr])
        P.dma("pool", dst_ap, xr[0:npart, :], key=xr, reads=[xr], writes=dst_units, final=final)

    state = {"first": True}

    def prompt_src(i):
        if state["first"]:
            return xp.h[i * 128:(i + 1) * 128, :], [xp_blk[i]]
        return yp.h[i * 128:(i + 1) * 128, :], [yp_blk[i]]

    def sample_src(b):
        if state["first"]:
            return xs.h[b * 4:(b + 1) * 4, :], [xs_blk]
        return ys.h[b * 4:(b + 1) * 4, :], [ys_blk]

    def sample_src_all():
        if state["first"]:
            return xs.h[0:16, :], [xs_blk]
        return ys.h[0:16, :], [ys_blk]

    gl_w_in = None

    def gla_layer(layer, li, last):
        phase_begin(2, 512)
        w_in = P.sb("gw_in", [128, 8, GLA_IN], BF16)
        w_out = P.sb("gw_out", [128, 8, D], BF16)
        wg2 = P.sb("gwg2", [16, 512], BF16)
        negb = P.sb("gnegb", [128, 4], F32)
        gn = P.sb("ggn", [128, 256], F32)
        load_w(w_in, slice(0, GLA_IN), gla_w_in.h, li * D, D, 0, GLA_IN)
        load_w(w_out, slice(0, D), gla_w_out.h, li * D, D, 0, D)
        P.dma("pool", wg2[:, :], gla_w_g2.h[li * 16:(li + 1) * 16, :], key=wg2, writes=[wg2])
        load_featmajor(negb[:, :].unsqueeze(2), negb, gla_b_g.h, li, 1, 512, mmB_ring.next())
        P.op("dve", lambda e: e.tensor_scalar(out=negb[:, :], in0=negb[:, :], scalar1=-1.0, scalar2=None, op0=ALU.mult), reads=[negb], writes=[negb])
        P.dma("sp", gn[:, :], bass.AP(gla_norm.h, li * 256, [[0, 128], [1, 256]]), key=gn, writes=[gn])

        gzT_ring = Ring([P.sb("g_gz_%d" % i, [16, 512], BF16) for i in range(2)])
        lt = P.sb("g_l", [128, 512], F32)
        bp = P.sb("g_bp", [128, 512], F32)
        Et = Ring([P.sb("g_E_%d" % i, [128, 512], F32) for i in range(2)])
        Ei = Ring([P.sb("g_Ei_%d" % i, [128, 512], F32) for i in range(2)])
        El = Ring([P.sb("g_El_%d" % i, [128, 4, 4], F32) for i in range(2)])
        qt_ring = Ring([P.sb("g_qt_%d" % i, [128, 4, 512], BF16) for i in range(2)])
        kt_ring = Ring([P.sb("g_kt_%d" % i, [128, 4, 512], BF16) for i in range(2)])
        v_ring = Ring([P.sb("g_v_%d" % i, [128, D], BF16) for i in range(3)])
        gsr_ring = Ring([P.sb("g_sr_%d" % i, [128, D], F32) for i in range(3)])
        aT_ring = Ring([P.sb("g_aT_%d" % i, [128, 128], BF16) for i in range(3)])
        ktok_ring = Ring([P.sb("g_ktok_%d" % i, [128, 128], BF16) for i in range(3)])
        osb_ring = Ring([P.sb("g_o_%d" % i, [128, D], F32) for i in range(2)])
        oss_ring = Ring([P.sb("g_oss_%d" % i, [128, 8], F32) for i in range(2)])
        S = [P.sb("g_S_%d" % h, [128, 256], F32) for h in range(4)]
        Dd = [P.sb("g_D_%d" % h, [128, 256], F32) for h in range(4)]
        Dbf = [P.sb("g_Dbf_%d" % h, [128, 256], BF16) for h in range(4)]
        sq2 = P.sb("g_sq2", [128, 256], F32)

        def gla_tile(hT, ntok, L, s0_aps, sT_aps, res_blocks, is_sample_first):
            nch = ntok // L
            bb = mmB_ring.next()
            for kc in range(8):
                P.op("pe", lambda e, kc=kc, bb=bb: e.matmul(out=PSB[bb][0:16, 0:ntok], lhsT=w_in[:, kc, 3072:3088], rhs=hT[:, kc, 0:ntok], start=(kc == 0), stop=(kc == 7)), reads=[hT, w_in], writes=[PSB[bb]], partial=True)
            gz = gzT_ring.next()
            P.op("act", lambda e, bb=bb, gz=gz: e.copy(out=gz[:, 0:ntok], in_=PSB[bb][0:16, 0:ntok]), reads=[PSB[bb]], writes=[gz])
            qt = qt_ring.next()
            kt = kt_ring.next()
            E_l = El.next()
            Es, Eis = [], []
            for hh in range(4):
                bb = mmB_ring.next()
                P.op("pe", lambda e, hh=hh, bb=bb, gz=gz: e.matmul(out=PSB[bb][:, 0:ntok], lhsT=wg2[:, hh * 128:(hh + 1) * 128], rhs=gz[:, 0:ntok], start=True, stop=True), reads=[gz, wg2], writes=[PSB[bb]])
                P.op("act", lambda e, hh=hh, bb=bb: e.activation(out=lt[:, 0:ntok], in_=PSB[bb][:, 0:ntok], func=AF.Exp, scale=-1.0, bias=negb[:, hh:hh + 1]), reads=[PSB[bb], negb], writes=[lt])
                P.op("act", lambda e: e.activation(out=lt[:, 0:ntok], in_=lt[:, 0:ntok], func=AF.Ln, bias=onesF[:, 0:1]), reads=[lt, onesF], writes=[lt])
                for c in range(nch):
                    P.op("dve", lambda e, c=c: e.tensor_tensor_scan(out=bp[:, c * L:(c + 1) * L], data0=onesF[:, 0:L], data1=lt[:, c * L:(c + 1) * L], initial=0.0, op0=ALU.mult, op1=ALU.add), reads=[lt, onesF], writes=[bp], partial=(c > 0))
                E = Et.next()
                Einv = Ei.next()
                P.op("act", lambda e, E=E: e.activation(out=E[:, 0:ntok], in_=bp[:, 0:ntok], func=AF.Exp, scale=-1.0 / 16), reads=[bp], writes=[E])
                P.op("act", lambda e, Einv=Einv: e.activation(out=Einv[:, 0:ntok], in_=bp[:, 0:ntok], func=AF.Exp, scale=1.0 / 16), reads=[bp], writes=[Einv])
                P.op("dve", lambda e, E=E, hh=hh, E_l=E_l: e.tensor_copy(out=E_l[:, hh, 0:nch], in_=E[:, 0:ntok].rearrange("p (c l) -> p c l", l=L)[:, :, L - 1]), reads=[E], writes=[E_l], partial=(hh > 0))
                for (dst, coff, own, oth) in ((qt, 0, E, Einv), (kt, 512, Einv, E)):
                    bb2 = mmB_ring.next()
                    for kc in range(8):
                        P.op("pe", lambda e, kc=kc, bb2=bb2, coff=coff, hh=hh: e.matmul(out=PSB[bb2][:, 0:ntok], lhsT=w_in[:, kc, coff + hh * 128:coff + (hh + 1) * 128], rhs=hT[:, kc, 0:ntok], start=(kc == 0), stop=(kc == 7)), reads=[hT, w_in], writes=[PSB[bb2]], partial=True)
                    for c in range(nch):
                        le = (c + 1) * L - 1
                        P.op("dve", lambda e, c=c, le=le, bb2=bb2, dst=dst, own=own, oth=oth, hh=hh: e.scalar_tensor_tensor(
                            out=dst[:, hh, c * L:(c + 1) * L], in0=PSB[bb2][:, c * L:(c + 1) * L], scalar=oth[:, le:le + 1], in1=own[:, c * L:(c + 1) * L], op0=ALU.mult, op1=ALU.mult),
                            reads=[PSB[bb2], own, oth], writes=[dst], partial=True)
            for c in range(nch):
                cs = slice(c * L, (c + 1) * L)
                pairv = mmA_ring.next()
                for half in range(2):
                    b = PSB[pairv[half]]
                    for kc in range(8):
                        P.op("pe", lambda e, kc=kc, b=b, half=half, cs=cs: e.matmul(out=b[0:L, :], lhsT=hT[:, kc, cs], rhs=w_in[:, kc, 1024 + half * 512:1024 + (half + 1) * 512], start=(kc == 0), stop=(kc == 7)), reads=[hT, w_in], writes=[b], partial=True)
                vsb = v_ring.next()
                for half in range(2):
                    b = PSB[pairv[half]]
                    P.op("act", lambda e, b=b, half=half, vsb=vsb: e.copy(out=vsb[0:L, half * 512:(half + 1) * 512], in_=b[0:L, :]), reads=[b], writes=[vsb], partial=(half > 0))
                pairr = mmA_ring.next()
                for half in range(2):
                    b = PSB[pairr[half]]
                    for kc in range(8):
                        P.op("pe", lambda e, kc=kc, b=b, half=half, cs=cs: e.matmul(out=b[0:L, :], lhsT=hT[:, kc, cs], rhs=w_in[:, kc, 2048 + half * 512:2048 + (half + 1) * 512], start=(kc == 0), stop=(kc == 7)), reads=[hT, w_in], writes=[b], partial=True)
                gsr = gsr_ring.next()
                for half in range(2):
                    b = PSB[pairr[half]]
                    P.op("act", lambda e, b=b, half=half, gsr=gsr: e.activation(out=gsr[0:L, half * 512:(half + 1) * 512], in_=b[0:L, :], func=AF.Silu), reads=[b], writes=[gsr], partial=(half > 0))
                P.op("pool", lambda e, gsr=gsr: e.tensor_tensor(out=gsr[0:L, :].rearrange("p (h e) -> p h e", e=256), in0=gsr[0:L, :].rearrange("p (h e) -> p h e", e=256), in1=gn[0:L, :].unsqueeze(1).broadcast_to([L, 4, 256]), op=ALU.mult), reads=[gsr, gn], writes=[gsr])
                osb = osb_ring.next()
                oss = oss_ring.next()
                for hh in range(4):
                    if s0_aps is not None or (c == 0 and is_sample_first):
                        pass
                    if s0_aps is not None:
                        P.dma("sp", S[hh][:, :], s0_aps[c][hh], key=S[hh], writes=[S[hh]])
                    if s0_aps is None and state["gla_zero"] and c == 0:
                        P.op("pool", lambda e, hh=hh: e.memset(S[hh][:, :], 0.0), writes=[S[hh]])
                    P.op("dve", lambda e, hh=hh, c=c, E_l=E_l: e.tensor_scalar(out=Dd[hh][:, :], in0=S[hh][:, :], scalar1=E_l[:, hh, c:c + 1], scalar2=None, op0=ALU.mult), reads=[S[hh], E_l], writes=[Dd[hh]])
                    P.op("act", lambda e, hh=hh: e.activation(out=Dbf[hh][:, :], in_=Dd[hh][:, :], func=AF.Copy, scale=GSC), reads=[Dd[hh]], writes=[Dbf[hh]])
                    ba = mmB_ring.next()
                    P.op("pe", lambda e, ba=ba, hh=hh, cs=cs: e.matmul(out=PSB[ba][0:L, 0:L], lhsT=kt[:, hh, cs], rhs=qt[:, hh, cs], start=True, stop=True), reads=[kt, qt], writes=[PSB[ba]])
                    aT = aT_ring.next()
                    P.op("dve", lambda e, ba=ba, aT=aT: e.tensor_tensor(out=aT[0:L, 0:L], in0=PSB[ba][0:L, 0:L], in1=mask_s[0:L, 0:L], op=ALU.mult), reads=[PSB[ba], mask_s], writes=[aT])
                    tb = tp_ring.next()
                    tpv = PSB[tb].h
                    P.op("pe", lambda e, tpv=tpv, hh=hh, cs=cs, tb=tb: e.transpose(out=tpv[0:L, 0:128], in_=kt[:, hh, cs], identity=ident[:, :]), reads=[kt, ident], writes=[PSB[tb]])
                    ktok = ktok_ring.next()
                    P.op("act", lambda e, tpv=tpv, ktok=ktok: e.copy(out=ktok[0:L, :], in_=tpv[0:L, 0:128]), reads=[PSB[tb]], writes=[ktok])
                    P.op("pe", lambda e, ba=ba, aT=aT, vsb=vsb, hh=hh: e.matmul(out=PSB[ba][0:L, 128:384], lhsT=aT[0:L, 0:L], rhs=vsb[0:L, hh * 256:(hh + 1) * 256], start=True, stop=False), reads=[aT, vsb], writes=[PSB[ba]])
                    P.op("pe", lambda e, ba=ba, hh=hh, cs=cs: e.matmul(out=PSB[ba][0:L, 128:384], lhsT=qt[:, hh, cs], rhs=Dbf[hh][:, :], start=False, stop=True), reads=[qt, Dbf[hh]], writes=[PSB[ba]], partial=True)
                    P.op("act", lambda e, ba=ba, osb=osb, hh=hh: e.copy(out=osb[0:L, hh * 256:(hh + 1) * 256], in_=PSB[ba][0:L, 128:384]), reads=[PSB[ba]], writes=[osb], partial=(hh > 0))
                    P.op("act", lambda e, ba=ba, oss=oss, hh=hh: e.activation(out=sq2[0:L, :], in_=PSB[ba][0:L, 128:384], func=AF.Square, accum_out=oss[0:L, hh:hh + 1]), reads=[PSB[ba]], writes=[oss], partial=(hh > 0))
                    bk = mmB_ring.next()
                    P.op("pe", lambda e, bk=bk, ktok=ktok, vsb=vsb, hh=hh: e.matmul(out=PSB[bk][:, 0:256], lhsT=ktok[0:L, :], rhs=vsb[0:L, hh * 256:(hh + 1) * 256], start=True, stop=True), reads=[ktok, vsb], writes=[PSB[bk]])
                    P.op("dve", lambda e, bk=bk, hh=hh: e.tensor_tensor(out=S[hh][:, :], in0=PSB[bk][:, 0:256], in1=Dd[hh][:, :], op=ALU.add), reads=[PSB[bk], Dd[hh]], writes=[S[hh]])
                    if sT_aps is not None and sT_aps[c] is not None:
                        P.dma("pool", sT_aps[c][hh][0], S[hh][:, :], key=S[hh], reads=[S[hh]], writes=[sT_aps[c][hh][1]], partial=True, final=True)
                state["gla_zero"] = False
                rstd_from_ss(oss, slice(0, 4), slice(4, 8), L, 1.0 / 256)
                on = on_ring.next()
                for hh in range(4):
                    P.op("dve", lambda e, hh=hh, on=on, osb=osb, oss=oss, gsr=gsr: e.scalar_tensor_tensor(out=on[0:L, hh * 256:(hh + 1) * 256], in0=osb[0:L, hh * 256:(hh + 1) * 256], scalar=oss[0:L, 4 + hh:5 + hh], in1=gsr[0:L, hh * 256:(hh + 1) * 256], op0=ALU.mult, op1=ALU.mult), reads=[osb, oss, gsr], writes=[on], partial=(hh > 0))
                onT = onT_ring.next()
                transpose_to(on, L, onT)
                src_ap, src_units, dst_ap, dst_units, fin = res_blocks[c]
                out_proj_residual(onT, L, w_out, 8, src_ap, src_units, dst_ap, dst_units, fin)

        def blocks_for_tile(t):
            return [(prompt_src(4 * t + j)[0], prompt_src(4 * t + j)[1], 128) for j in range(4)]

        state["gla_zero"] = True
        pre = norm_front(blocks_for_tile(0), gmix, layer)
        for t in range(NT4):
            cur = pre
            if t + 1 < NT4:
                pre = norm_front(blocks_for_tile(t + 1), gmix, layer)
            else:
                pre = norm_front([(sample_src(0)[0], sample_src(0)[1], 4)], gmix, layer)
            res = []
            for j in range(4):
                i = 4 * t + j
                sa, su = prompt_src(i)
                res.append((sa, su, yp.h[i * 128:(i + 1) * 128, :], [yp_blk[i]], last))
            sT = None
            if t == NT4 - 1:
                sT = [None, None, None, [(sgp.h[(li * 4 + hh) * 128:(li * 4 + hh + 1) * 128, :], sgp) for hh in range(4)]]
            gla_tile(cur[0], 512, 128, None, sT, res, False)
        for b in range(4):
            cur = pre
            if b + 1 < 4:
                pre = norm_front([(sample_src(b + 1)[0], sample_src(b + 1)[1], 4)], gmix, layer)
            sa, su = sample_src(b)
            res = [(sa, su, ys.h[b * 4:(b + 1) * 4, :], [ys_blk], last)]
            s0 = [[sg_in.h[((li * 4 + b) * 4 + hh) * 128:((li * 4 + b) * 4 + hh + 1) * 128, :] for hh in range(4)]]
            sT = [[(sgs.h[((li * 4 + b) * 4 + hh) * 128:((li * 4 + b) * 4 + hh + 1) * 128, :], sgs) for hh in range(4)]]
            gla_tile(cur[0], 4, 4, s0, sT, res, True)
        state["first"] = False

    def ffn_layer(layer, last):
        phase_begin(2, 256)
        w_up = P.sb("f_wup", [128, 8, 2 * DFF], BF16)
        w_dn = P.sb("f_wdn", [128, 22, D], BF16)
        cw = P.sb("f_cw", [128, 3, 44], F32)
        cb = P.sb("f_cb", [128, 44], F32)
        load_w(w_up, slice(0, 2 * DFF), ffn_w_up.h, layer * D, D, 0, 2 * DFF)
        load_w(w_dn, slice(0, D), ffn_w_down.h, layer * DFF, DFF, 0, D)
        load_featmajor(cw[:, :, :].rearrange("p i c -> p c i"), cw, ffn_cw.h, layer * 3, 3, 2 * DFF, mmB_ring.next())
        load_featmajor(cb[:, :].unsqueeze(2), cb, ffn_cb.h, layer, 1, 2 * DFF, mmB_ring.next())
        hist = P.sb("f_hist", [128, 44, 2], F32)
        cs_ring = Ring([P.sb("f_c_%d" % i, [128, 256], F32) for i in range(4)])
        sg_ring = Ring([P.sb("f_sg_%d" % i, [128, 256], F32) for i in range(2)])
        act = P.sb("f_act", [128, 22, 256], BF16)
        ul_ring = Ring([P.sb("f_ul_%d" % i, [2, 512], F32) for i in range(2)])
        hrow = P.sb("f_hrow", [2, 512], F32)

        def ffn_tile(hT, N, res_blocks, state_out_ap, state_out_unit):
            for cp in range(22):
                cts = []
                for ch in (cp, 22 + cp):
                    bb = mmB_ring.next()
                    for kc in range(8):
                        P.op("pe", lambda e, kc=kc, bb=bb, ch=ch: e.matmul(out=PSB[bb][:, 0:N], lhsT=w_up[:, kc, ch * 128:(ch + 1) * 128], rhs=hT[:, kc, 0:N], start=(kc == 0), stop=(kc == 7)), reads=[hT, w_up], writes=[PSB[bb]], partial=True)
                    ct = cs_ring.next()
                    P.op("act", lambda e, bb=bb, ct=ct, ch=ch: e.activation(out=ct[:, 0:N], in_=PSB[bb][:, 0:N], func=AF.Identity, scale=cw[:, 2, ch:ch + 1], bias=cb[:, ch:ch + 1]), reads=[PSB[bb], cw, cb], writes=[ct])
                    P.op("dve", lambda e, bb=bb, ct=ct, ch=ch: e.scalar_tensor_tensor(out=ct[:, 1:N], in0=PSB[bb][:, 0:N - 1], scalar=cw[:, 1, ch:ch + 1], in1=ct[:, 1:N], op0=ALU.mult, op1=ALU.add), reads=[PSB[bb], cw, ct], writes=[ct])
                    P.op("dve", lambda e, bb=bb, ct=ct, ch=ch: e.scalar_tensor_tensor(out=ct[:, 2:N], in0=PSB[bb][:, 0:N - 2], scalar=cw[:, 0, ch:ch + 1], in1=ct[:, 2:N], op0=ALU.mult, op1=ALU.add), reads=[PSB[bb], cw, ct], writes=[ct])
                    P.op("dve", lambda e, ct=ct, ch=ch: e.scalar_tensor_tensor(out=ct[:, 0:2], in0=hist[:, ch, 0:2], scalar=cw[:, 0, ch:ch + 1], in1=ct[:, 0:2], op0=ALU.mult, op1=ALU.add), reads=[hist, cw, ct], writes=[ct])
                    P.op("dve", lambda e, ct=ct, ch=ch: e.scalar_tensor_tensor(out=ct[:, 0:1], in0=hist[:, ch, 1:2], scalar=cw[:, 1, ch:ch + 1], in1=ct[:, 0:1], op0=ALU.mult, op1=ALU.add), reads=[hist, cw, ct], writes=[ct])
                    P.op("act", lambda e, bb=bb, ch=ch: e.copy(out=hist[:, ch, 0:2], in_=PSB[bb][:, N - 2:N]), reads=[PSB[bb]], writes=[hist])
                    cts.append(ct)
                sgt = sg_ring.next()
                P.op("act", lambda e, sgt=sgt, c0=cts[0]: e.activation(out=sgt[:, 0:N], in_=c0[:, 0:N], func=AF.Silu), reads=[cts[0]], writes=[sgt])
                P.op("dve", lambda e, sgt=sgt, c1=cts[1], cp=cp: e.tensor_tensor(out=act[:, cp, 0:N], in0=sgt[:, 0:N], in1=c1[:, 0:N], op=ALU.mult), reads=[sgt, cts[1]], writes=[act], partial=(cp > 0))
            if state_out_ap is not None:
                for blk in range(11):
                    bb = mmB_ring.next()
                    for kc in range(8):
                        P.op("pe", lambda e, kc=kc, bb=bb, blk=blk: e.matmul(out=PSB[bb][0:2, :], lhsT=hT[:, kc, N - 2:N], rhs=w_up[:, kc, blk * 512:(blk + 1) * 512], start=(kc == 0), stop=(kc == 7)), reads=[hT, w_up], writes=[PSB[bb]], partial=True)
                    ul = ul_ring.next()
                    P.op("act", lambda e, bb=bb, ul=ul: e.copy(out=ul[0:2, :], in_=PSB[bb][0:2, :]), reads=[PSB[bb]], writes=[ul])
                    P.dma("pool", state_out_ap[:, blk * 512:(blk + 1) * 512], ul[0:2, :], key=ul, reads=[ul], writes=[state_out_unit], partial=True, final=True)
            nb = len(res_blocks)
            for j in range(nb):
                src_ap, src_units, dst_ap, dst_units, fin, n = res_blocks[j]
                out_proj_residual(act, n, w_dn, 22, src_ap, src_units, dst_ap, dst_units, fin, off=j * 128)

        def blocks_for_tile(t):
            return [(prompt_src(2 * t + j)[0], prompt_src(2 * t + j)[1], 128) for j in range(2)]

        P.op("pool", lambda e: e.memset(hist[:, :, :], 0.0), writes=[hist])
        pre = norm_front(blocks_for_tile(0), gffn, layer)
        for t in range(NBLK // 2):
            cur = pre
            if t + 1 < NBLK // 2:
                pre = norm_front(blocks_for_tile(t + 1), gffn, layer)
            else:
                pre = norm_front([(sample_src(0)[0], sample_src(0)[1], 4)], gffn, layer)
            res = []
            for j in range(2):
                i = 2 * t + j
                sa, su = prompt_src(i)
                res.append((sa, su, yp.h[i * 128:(i + 1) * 128, :], [yp_blk[i]], last, 128))
            ffn_tile(cur[0], 256, res, fcp.h[layer * 2:(layer + 1) * 2, :] if t == NBLK // 2 - 1 else None, fcp)
        for b in range(4):
            cur = pre
            if b + 1 < 4:
                pre = norm_front([(sample_src(b + 1)[0], sample_src(b + 1)[1], 4)], gffn, layer)
            r0 = (layer * 4 + b) * 2
            bb = mmB_ring.next()
            for q11 in range(11):
                P.dma("sp", hrow[0:2, :], sfc.h[r0:r0 + 2, q11 * 512:(q11 + 1) * 512], key=hrow, writes=[hrow])
                for cc in range(4):
                    ch = q11 * 4 + cc
                    P.op("pe", lambda e, bb=bb, cc=cc, ch=ch: e.matmul(out=PSB[bb][:, ch * 2:ch * 2 + 2], lhsT=hrow[0:2, cc * 128:(cc + 1) * 128], rhs=identf[0:2, 0:2], start=True, stop=True), reads=[hrow, identf], writes=[PSB[bb]], partial=True)
            P.op("act", lambda e, bb=bb: e.copy(out=hist[:, :, :], in_=PSB[bb][:, 0:88].rearrange("p (c t) -> p c t", t=2)), reads=[PSB[bb]], writes=[hist])
            sa, su = sample_src(b)
            res = [(sa, su, ys.h[b * 4:(b + 1) * 4, :], [ys_blk], last, 4)]
            r1 = (layer * 4 + b) * 2
            ffn_tile(cur[0], 4, res, fcs.h[r1:r1 + 2, :], fcs)
        state["first"] = False


    def dil_setup():
        phase_begin(1, 16)
        relb = P.sb("d_relb", [NB, 48], F32)
        ohp = P.sb("d_ohp", [NB, 3 * 384], F32)
        vld = P.sb("d_vld", [16, 3 * 384], F32)
        P.dma("sp", relb[:, :], rel_bias.h[:, :], key=relb, writes=[relb])
        P.dma("sp", ohp[:, :], c_ohp.h[:, :], key=ohp, writes=[ohp])
        P.dma("sp", vld[:, :], bass.AP(c_valid.h, 0, [[0, 16], [1, 3 * 384]]), key=vld, writes=[vld])
        for g in range(3):
            bb = mmB_ring.next()
            P.op("pe", lambda e, bb=bb, g=g: e.matmul(out=PSB[bb][0:16, 0:384], lhsT=relb[:, g * 16:(g + 1) * 16], rhs=ohp[:, g * 384:(g + 1) * 384], start=True, stop=True), reads=[relb, ohp], writes=[PSB[bb]])
            wv = P.sb("d_wv%d" % g, [16, 384], F32)
            wvb = P.sb("d_wvb%d" % g, [16, 384], F32)
            P.op("act", lambda e, bb=bb, wv=wv: e.activation(out=wv[:, :], in_=PSB[bb][0:16, 0:384], func=AF.Exp), reads=[PSB[bb]], writes=[wv])
            P.op("dve", lambda e, wv=wv, wvb=wvb, g=g: e.tensor_tensor(out=wvb[:, :], in0=wv[:, :], in1=vld[:, g * 384:(g + 1) * 384], op=ALU.mult), reads=[wv, vld], writes=[wvb])
            P.dma("sp", wsc.h[g * 16:(g + 1) * 16, :], wvb[:, :], key=wvb, reads=[wvb], writes=[wsc], partial=True)

    def dil_group(layer, li, g, last):
        _chk(1)
        W, d = GROUPS[g]
        nbk = (SEQ // d) // 128
        RR.tp = Ring([0, 1])
        RR.mmA = Ring([(2, 3)])
        RR.mmB = Ring([7])
        UB = (4, 5, 6)
        phase_begin(1, 512)
        Wg = P.sb("d_Wg", [128, 8, 3072], BF16)
        load_w(Wg, slice(0, 3072), dil_w_in.h, li * D, D, g * 3072, 3072)
        w_out = None
        if g == 0:
            w_out = P.sb("d_wout", [128, 8, D], BF16)
            load_w(w_out, slice(0, D), dil_w_out.h, li * D, D, 0, D)
        qg = P.sb("d_qg", [128, 64], F32)
        kg = P.sb("d_kg", [128, 64], F32)
        P.dma("sp", qg[:, :], bass.AP(dil_qn.h, li * 64, [[0, 128], [1, 64]]), key=qg, writes=[qg])
        P.dma("sp", kg[:, :], bass.AP(dil_kn.h, li * 64, [[0, 128], [1, 64]]), key=kg, writes=[kg])
        relb = P.sb("d_relb", [NB, 48], F32)
        P.dma("sp", relb[:, :], rel_bias.h[:, :], key=relb, writes=[relb])
        M = P.sb("d_M", [128, 16, 256], BF16)
        H_ring = Ring([P.sb("d_H%d" % i, [128, 256], F32) for i in range(2)])
        for h in range(16):
            H = H_ring.next()
            P.dma("sp", H[:, :], bass.AP(wsc.h, (g * 16 + h) * 384, [[1, 128], [1, 256]]), key=H, reads=[wsc], writes=[H])
            bb = mmB_ring.next()
            P.op("pe", lambda e, bb=bb, H=H: e.matmul(out=PSB[bb][:, 0:256], lhsT=Jf[:, :], rhs=H[:, :], start=True, stop=True), reads=[Jf, H], writes=[PSB[bb]])
            P.op("act", lambda e, bb=bb, h=h: e.copy(out=M[:, h, :], in_=PSB[bb][:, 0:256]), reads=[PSB[bb]], writes=[M], partial=(h > 0))
        _chk(2)
        qS = P.sb("d_qS", [16, D], F32)
        kS = P.sb("d_kS", [16, D], F32)
        vS = P.sb("d_vS", [16, D], F32)
        MARK = P.off
        nrm = P.sb("d_nrm", [128, D], F32)
        st16 = Ring([P.sb("d_st%d" % i, [128, 32], F32) for i in range(2)])
        kout_r = Ring([P.sb("d_ko%d" % i, [128, D], F32) for i in range(2)])
        vout_r = Ring([P.sb("d_vo%d" % i, [128, D], F32) for i in range(2)])
        qbf_r = Ring([P.sb("d_qb%d" % i, [128, D], BF16) for i in range(2)])
        qT_r = Ring([P.sb("d_qT%d" % i, [128, 16, 128], BF16) for i in range(2)])
        for t in qT_r.items:
            P.op("pool", lambda e, t=t: e.memset(t[:, :, :], 0.0), writes=[t])
        kT_r = Ring([P.sb("d_kT%d" % i, [128, 8, 128], BF16) for i in range(3)])
        va_r = Ring([P.sb("d_va%d" % i, [128, 16, 80], BF16) for i in range(3)])
        pe_r = Ring([P.sb("d_pe%d" % i, [128, 512], BF16) for i in range(2)])
        pt_r = Ring([P.sb("d_pt%d" % i, [128, 512], BF16) for i in range(2)])
        U_r = Ring([P.sb("d_U%d" % i, [128, 1280], F32) for i in range(2)])
        Ua = P.sb("d_Ua", [128, 1280], F32)
        for t in U_r.items:
            P.op("pool", lambda e, t=t: e.memset(t[:, :], 0.0), writes=[t])
        rden = P.sb("d_rden", [128, 16], F32)
        for t in va_r.items:
            P.op("pool", lambda e, t=t: e.memset(t[:, :, :], 1.0), writes=[t])

        def qkv_block(hT, c0, n, cache_k_ap, cache_v_ap, cache_ku, cache_vu, sample):
            outs = []
            for s_ in range(3):
                pair = mmA_ring.next()
                for half in range(2):
                    b = PSB[pair[half]]
                    for kc in range(8):
                        P.op("pe", lambda e, kc=kc, b=b, half=half, s_=s_: e.matmul(out=b[0:n, :], lhsT=hT[:, kc, c0:c0 + n], rhs=Wg[:, kc, s_ * 1024 + half * 512:s_ * 1024 + (half + 1) * 512], start=(kc == 0), stop=(kc == 7)), reads=[hT, Wg], writes=[b], partial=True)
                if s_ == 2:
                    vo = vS if sample else vout_r.next()
                    for half in range(2):
                        b = PSB[pair[half]]
                        P.op("act", lambda e, b=b, half=half, vo=vo: e.copy(out=vo[0:n, half * 512:(half + 1) * 512], in_=b[0:n, :]), reads=[b], writes=[vo], partial=(half > 0))
                    if cache_v_ap is not None:
                        P.dma("pool", cache_v_ap, vo[0:n, :], key=vo, reads=[vo], writes=[cache_vu], partial=True, final=True)
                    outs.append(vo)
                    continue
                st = st16.next()
                for half in range(2):
                    b = PSB[pair[half]]
                    P.op("act", lambda e, b=b, half=half: e.activation(out=sq_scr[0:n, half * 512:(half + 1) * 512], in_=b[0:n, :], func=AF.Square), reads=[b], writes=[sq_scr], partial=(half > 0))
                P.op("dve", lambda e, st=st: e.tensor_reduce(out=st[0:n, 0:16], in_=sq_scr[0:n, :].rearrange("p (h e) -> p h e", e=64), axis=AX.X, op=ALU.add), reads=[sq_scr], writes=[st])
                rstd_from_ss(st, slice(0, 16), slice(16, 32), n, 1.0 / 64)
                for half in range(2):
                    b = PSB[pair[half]]
                    P.op("dve", lambda e, b=b, half=half, st=st: e.tensor_tensor(out=nrm[0:n, half * 512:(half + 1) * 512].rearrange("p (h e) -> p h e", e=64), in0=b[0:n, :].rearrange("p (h e) -> p h e", e=64), in1=st[0:n, 16 + half * 8:24 + half * 8].unsqueeze(2).broadcast_to([n, 8, 64]), op=ALU.mult), reads=[b, st], writes=[nrm], partial=(half > 0))
                gain = qg if s_ == 0 else kg
                if s_ == 0:
                    dst = qS if sample else qbf_r.next()
                else:
                    dst = kS if sample else kout_r.next()
                P.op("pool", lambda e, dst=dst, gain=gain: e.tensor_tensor(out=dst[0:n, :].rearrange("p (h e) -> p h e", e=64), in0=nrm[0:n, :].rearrange("p (h e) -> p h e", e=64), in1=gain[0:n, :].unsqueeze(1).broadcast_to([n, 16, 64]), op=ALU.mult), reads=[nrm, gain], writes=[dst])
                if s_ == 1 and cache_k_ap is not None:
                    P.dma("pool", cache_k_ap, dst[0:n, :], key=dst, reads=[dst], writes=[cache_ku], partial=True, final=True)
                outs.append(dst)
            return outs

        def to_featmajor(src, dstT, split=False):
            if src.h.dtype != BF16:
                tmp = qbf_r.next()
                P.op("act", lambda e, tmp=tmp, src0=src: e.copy(out=tmp[:, :], in_=src0[:, :]), reads=[src], writes=[tmp])
                src = tmp
            tb = tp_ring.next()
            tpv = PSB[tb].h
            for c in range(8):
                P.op("pe", lambda e, c=c, tpv=tpv, src=src: e.transpose(out=tpv[:, c * 128:(c + 1) * 128], in_=src[:, c * 128:(c + 1) * 128], identity=ident[:, :]), reads=[src, ident], writes=[PSB[tb]], partial=True)
            if split:
                P.op("act", lambda e, tpv=tpv: e.copy(out=dstT[0:64, 0:8, :], in_=tpv[0:64, :].rearrange("p (c t) -> p c t", t=128)), reads=[PSB[tb]], writes=[dstT])
                P.op("act", lambda e, tpv=tpv: e.copy(out=dstT[64:128, 8:16, :], in_=tpv[64:128, :].rearrange("p (c t) -> p c t", t=128)), reads=[PSB[tb]], writes=[dstT], partial=True)
            else:
                P.op("act", lambda e, tpv=tpv: e.copy(out=dstT[:, :, :], in_=tpv[:, :].rearrange("p (c t) -> p c t", t=128)), reads=[PSB[tb]], writes=[dstT])

        hTs, _ = norm_front([(sample_src_all()[0], sample_src_all()[1], 16)], gmix, layer)
        qkv_block(hTs, 0, 16, kso[g].h[li * 16:(li + 1) * 16, :], vso[g].h[li * 16:(li + 1) * 16, :], kso[g], vso[g], True)

        _chk(3)
        blocks = [(r, n) for r in range(d) for n in range(nbk)]

        def blk_src(r, n):
            base = yp.h
            units = [yp_blk[i] for i in range(n * d, (n + 1) * d)]
            return bass.AP(base, (n * 128 * d + r) * D, [[d * D, 128], [1, D]]), units

        def tile_blocks(t):
            return [(blk_src(*blocks[4 * t + j])[0], blk_src(*blocks[4 * t + j])[1], 128) for j in range(4)]

        prev_kT = None
        prev_va = None
        pre = norm_front(tile_blocks(0), gmix, layer)
        for t in range(NT4):
            hT = pre[0]
            for j in range(4):
                r, n = blocks[4 * t + j]
                ck_ap = cv_ap = None
                if n == nbk - 1:
                    ck_ap = bass.AP(kpo[g].h, (li * keep[g] + r) * D, [[d * D, 128], [1, D]])
                    cv_ap = bass.AP(vpo[g].h, (li * keep[g] + r) * D, [[d * D, 128], [1, D]])
                qb, ko, vo = qkv_block(hT, j * 128, 128, ck_ap, cv_ap, kpo[g], vpo[g], False)
                _chk(6)
                qT = qT_r.next()
                kT = kT_r.next()
                to_featmajor(qb, qT, split=True)
                to_featmajor(ko, kT)
                va = va_r.next()
                P.op("pool", lambda e, va=va, vo=vo: e.tensor_copy(out=va[:, :, 0:64], in_=vo[:, :].rearrange("p (h e) -> p h e", e=64)), reads=[vo], writes=[va])
                _chk(7)
                has_prev = n > 0
                for hp in range(8):
                    bb = mmB_ring.next()
                    for hh in range(2):
                        ps_ = slice(hh * 64, (hh + 1) * 64)
                        P.op("pe", lambda e, bb=bb, hh=hh, ps_=ps_, hp=hp, kT=kT, qT=qT: e.matmul(out=PSB[bb][:, hh * 256:hh * 256 + 128], lhsT=kT[:, hp, :], rhs=qT[:, hh * 8 + hp, :], start=True, stop=True), reads=[kT, qT], writes=[PSB[bb]], partial=True)
                        if has_prev:
                            P.op("pe", lambda e, bb=bb, hh=hh, ps_=ps_, hp=hp, pk=prev_kT, qT=qT: e.matmul(out=PSB[bb][:, hh * 256 + 128:hh * 256 + 256], lhsT=pk[:, hp, :], rhs=qT[:, hh * 8 + hp, :], start=True, stop=True), reads=[prev_kT, qT], writes=[PSB[bb]], partial=True)
                    pe_t = pe_r.next()
                    pt = pt_r.next()
                    wdt = 256 if has_prev else 128
                    P.op("act", lambda e, bb=bb, pe_t=pe_t, wdt=wdt: e.activation(out=pe_t[:, :].rearrange("p (h x) -> p h x", x=256)[:, :, 0:wdt], in_=PSB[bb][:, :].rearrange("p (h x) -> p h x", x=256)[:, :, 0:wdt], func=AF.Exp, scale=0.125), reads=[PSB[bb]], writes=[pe_t])
                    P.op("dve", lambda e, pe_t=pe_t, pt=pt, hp=hp, wdt=wdt: e.tensor_tensor(out=pt[:, :].rearrange("p (h x) -> p h x", x=256)[:, :, 0:wdt], in0=pe_t[:, :].rearrange("p (h x) -> p h x", x=256)[:, :, 0:wdt], in1=M[:, 2 * hp:2 * hp + 2, 0:wdt], op=ALU.mult), reads=[pe_t, M], writes=[pt])
                    for hh in range(2):
                        h = 2 * hp + hh
                        ub = PSB[UB[h // 6]]
                        c0 = (h % 6) * 80
                        P.op("pe", lambda e, ub=ub, c0=c0, pt=pt, hh=hh, va=va, h=h, has_prev=has_prev: e.matmul(out=ub[:, c0:c0 + 65], lhsT=pt[:, hh * 256:hh * 256 + 128], rhs=va[:, h, 0:65], start=True, stop=(not has_prev)), reads=[pt, va], writes=[ub], partial=True)
                        if has_prev:
                            P.op("pe", lambda e, ub=ub, c0=c0, pt=pt, hh=hh, pv=prev_va, h=h: e.matmul(out=ub[:, c0:c0 + 65], lhsT=pt[:, hh * 256 + 128:hh * 256 + 256], rhs=pv[:, h, 0:65], start=False, stop=True), reads=[pt, prev_va], writes=[ub], partial=True)
                _chk(8)
                prev_kT, prev_va = kT, va
                U = U_r.next()
                for bi, (a0, a1) in enumerate(((0, 480), (480, 960), (960, 1280))):
                    P.op("act", lambda e, bi=bi, a0=a0, a1=a1, U=U: e.copy(out=U[:, a0:a1].rearrange("p (h e) -> p h e", e=80)[:, :, 0:65], in_=PSB[UB[bi]][:, 0:a1 - a0].rearrange("p (h e) -> p h e", e=80)[:, :, 0:65]), reads=[PSB[UB[bi]]], writes=[U], partial=(bi > 0))
                if g != 0:
                    dst = bass.AP(ug_scr[g - 1].h, (n * 128 * d + r) * 1280, [[d * 1280, 128], [1, 1280]])
                    P.dma("pool", dst, U[:, :], key=U, reads=[U], writes=[ug_scr[g - 1]], partial=True)
                else:
                    i = n
                    for gi in range(2):
                        P.dma("sp", Ua[:, :], ug_scr[gi].h[i * 128:(i + 1) * 128, :], key=Ua, reads=[ug_scr[gi]], writes=[Ua])
                        P.op("pool", lambda e, U=U: e.tensor_tensor(out=U[:, :], in0=U[:, :], in1=Ua[:, :], op=ALU.add), reads=[U, Ua], writes=[U])
                    finish_attn(U, 128, w_out, rden, prompt_src(i)[0], prompt_src(i)[1], yp.h[i * 128:(i + 1) * 128, :], [yp_blk[i]], last)
                _chk(9)
            if t + 1 < NT4:
                pre = norm_front(tile_blocks(t + 1), gmix, layer)
        return MARK, qS, kS, vS, relb, w_out, rden

    def finish_attn(U, n, w_out, rden, src_ap, src_units, dst_ap, dst_units, last):
        Uv = U[0:n, :].rearrange("p (h e) -> p h e", e=80)
        P.op("dve", lambda e: e.reciprocal(out=rden[0:n, :], in_=Uv[:, :, 64]), reads=[U], writes=[rden])
        on = on_ring.next()
        P.op("dve", lambda e, on=on: e.tensor_tensor(out=on[0:n, :].rearrange("p (h e) -> p h e", e=64), in0=Uv[:, :, 0:64], in1=rden[0:n, :].unsqueeze(2).broadcast_to([n, 16, 64]), op=ALU.mult), reads=[U, rden], writes=[on])
        onT = onT_ring.next()
        transpose_to(on, n, onT)
        out_proj_residual(onT, n, w_out, 8, src_ap, src_units, dst_ap, dst_units, last)

    def dil_sample(layer, li, g, ctx, Us_acc, first_group, last):
        MARK, qS, kS, vS, relb, w_out, rden = ctx
        _chk(4)
        W, d = GROUPS[g]
        P.barrier()
        P.off = MARK
        UB = (4, 5, 6)
        ohs = P.sb("s_ohs", [NB, 6 * 128], F32)
        P.dma("sp", ohs[:, :], c_ohs.h[:, :], key=ohs, writes=[ohs])
        sel = P.sb("s_sel", [16, 16, 128], F32)
        selT = P.sb("s_selT", [128, 16, 16], F32)
        P.op("dve", lambda e: e.tensor_copy(out=sel[:, :, :], in_=identf[0:16, 0:16].unsqueeze(2).broadcast_to([16, 16, 128])), reads=[identf], writes=[sel])
        P.op("pool", lambda e: e.memset(selT[:, :, :], 0.0), writes=[selT])
        for t in range(16):
            P.op("pool", lambda e, t=t: e.memset(selT[:, t, t:t + 1], 1.0), writes=[selT])
        nvar = 4 if g == 0 else 1
        BS = P.sb("s_BS", [128, 4, 16], F32)
        for v in range(nvar):
            vv = v if g == 0 else 3 + g
            bb = mmB_ring.next()
            P.op("pe", lambda e, bb=bb, vv=vv: e.matmul(out=PSB[bb][:, 0:16], lhsT=ohs[:, vv * 128:(vv + 1) * 128], rhs=relb[:, g * 16:(g + 1) * 16], start=True, stop=True), reads=[ohs, relb], writes=[PSB[bb]])
            P.op("act", lambda e, bb=bb, v=v: e.activation(out=BS[:, v, :], in_=PSB[bb][:, 0:16], func=AF.Exp), reads=[PSB[bb]], writes=[BS], partial=(v > 0))
        eb0 = P.sb("s_eb0", [16, 16], F32)
        P.dma("sp", eb0[:, :], bass.AP(rel_bias.h, g * 16, [[0, 16], [1, 16]]), key=eb0, writes=[eb0])
        P.op("act", lambda e: e.activation(out=eb0[:, :], in_=eb0[:, :], func=AF.Exp), reads=[eb0], writes=[eb0])
        _chk(5)
        Kt_r = Ring([P.sb("s_Kt%d" % i, [128, D], F32) for i in range(2)])
        Vt_r = Ring([P.sb("s_Vt%d" % i, [128, D], F32) for i in range(2)])
        prod = P.sb("s_prod", [128, D], F32)
        sc_r = Ring([P.sb("s_sc%d" % i, [128, 16], F32) for i in range(2)])
        pw_r = Ring([P.sb("s_pw%d" % i, [128, 16], F32) for i in range(2)])
        Wt_r = Ring([P.sb("s_Wt%d" % i, [128, 1280], F32) for i in range(2)])
        for t in Wt_r.items:
            P.op("pool", lambda e, t=t: e.memset(t[:, :], 0.0), writes=[t])
        Usg = P.sb("s_Usg", [16, 1280], F32)
        p16 = P.sb("s_p16", [16, 32], F32)
        tmp16 = P.sb("s_tmp16", [16, D], F32)
        rden = P.sb("s_rden", [128, 16], F32)
        cnt = 0
        for b in range(4):
            for s_ in range(4):
                tk = 4 * b + s_
                Kt = Kt_r.next()
                Vt = Vt_r.next()
                base = (li * 4 + b) * W
                for (dstt, cache, newo) in ((Kt, ck[g], kso[g]), (Vt, cv[g], vso[g])):
                    if g == 0:
                        P.dma("sp", dstt[s_:128, :], cache.h[base + s_:base + 128, :], key=dstt, writes=[dstt])
                        if s_ > 0:
                            P.dma("sp", dstt[0:s_, :], newo.h[li * 16 + 4 * b:li * 16 + 4 * b + s_, :], key=dstt, reads=[newo], writes=[dstt], partial=True)
                    else:
                        P.dma("sp", dstt[:, :], bass.AP(cache.h, (base + s_) * D, [[d * D, 128], [1, D]]), key=dstt, writes=[dstt])
                pair = mmA_ring.next()
                for half in range(2):
                    bq = PSB[pair[half]]
                    P.op("pe", lambda e, bq=bq, half=half, tk=tk: e.matmul(out=bq[:, :], lhsT=sel[:, tk, :], rhs=qS[0:16, half * 512:(half + 1) * 512], start=True, stop=True), reads=[sel, qS], writes=[bq])
                    P.op("dve", lambda e, bq=bq, half=half, Kt=Kt: e.tensor_tensor(out=prod[:, half * 512:(half + 1) * 512], in0=bq[:, :], in1=Kt[:, half * 512:(half + 1) * 512], op=ALU.mult), reads=[bq, Kt], writes=[prod], partial=(half > 0))
                sc = sc_r.next()
                pw = pw_r.next()
                P.op("dve", lambda e, sc=sc: e.tensor_reduce(out=sc[:, :], in_=prod[:, :].rearrange("p (h e) -> p h e", e=64), axis=AX.X, op=ALU.add), reads=[prod], writes=[sc])
                P.op("act", lambda e, sc=sc: e.activation(out=sc[:, :], in_=sc[:, :], func=AF.Exp, scale=0.125), reads=[sc], writes=[sc])
                var = s_ if g == 0 else 0
                P.op("dve", lambda e, sc=sc, pw=pw, var=var: e.tensor_tensor(out=pw[:, :], in0=sc[:, :], in1=BS[:, var, :], op=ALU.mult), reads=[sc, BS], writes=[pw])
                Wt = Wt_r.next()
                Wv = Wt[:, :].rearrange("p (h e) -> p h e", e=80)
                P.op("dve", lambda e, Wv=Wv, Vt=Vt, pw=pw: e.tensor_tensor(out=Wv[:, :, 0:64], in0=Vt[:, :].rearrange("p (h e) -> p h e", e=64), in1=pw[:, :].unsqueeze(2).broadcast_to([128, 16, 64]), op=ALU.mult), reads=[Vt, pw], writes=[Wt])
                P.op("pool", lambda e, Wv=Wv, pw=pw: e.tensor_copy(out=Wv[:, :, 64], in_=pw[:, :]), reads=[pw], writes=[Wt])
                for bi, (a0, a1) in enumerate(((0, 480), (480, 960), (960, 1280))):
                    P.op("pe", lambda e, bi=bi, a0=a0, a1=a1, Wt=Wt, tk=tk, cnt=cnt: e.matmul(out=PSB[UB[bi]][0:16, 0:a1 - a0], lhsT=selT[:, tk, :], rhs=Wt[:, a0:a1], start=(cnt == 0), stop=(cnt == 15)), reads=[selT, Wt], writes=[PSB[UB[bi]]], partial=True)
                cnt += 1
        for bi, (a0, a1) in enumerate(((0, 480), (480, 960), (960, 1280))):
            P.op("act", lambda e, bi=bi, a0=a0, a1=a1: e.copy(out=Usg[:, a0:a1], in_=PSB[UB[bi]][0:16, 0:a1 - a0]), reads=[PSB[UB[bi]]], writes=[Usg], partial=(bi > 0))
        P.op("dve", lambda e: e.tensor_tensor(out=tmp16[:, :], in0=qS[:, :], in1=kS[:, :], op=ALU.mult), reads=[qS, kS], writes=[tmp16])
        P.op("dve", lambda e: e.tensor_reduce(out=p16[:, 0:16], in_=tmp16[:, :].rearrange("p (h e) -> p h e", e=64), axis=AX.X, op=ALU.add), reads=[tmp16], writes=[p16])
        P.op("act", lambda e: e.activation(out=p16[:, 0:16], in_=p16[:, 0:16], func=AF.Exp, scale=0.125), reads=[p16], writes=[p16])
        P.op("dve", lambda e: e.tensor_tensor(out=p16[:, 16:32], in0=p16[:, 0:16], in1=eb0[:, :], op=ALU.mult), reads=[p16, eb0], writes=[p16])
        Ugv = Usg[:, :].rearrange("p (h e) -> p h e", e=80)
        P.op("dve", lambda e: e.tensor_tensor(out=tmp16[:, :].rearrange("p (h e) -> p h e", e=64), in0=vS[:, :].rearrange("p (h e) -> p h e", e=64), in1=p16[:, 16:32].unsqueeze(2).broadcast_to([16, 16, 64]), op=ALU.mult), reads=[vS, p16], writes=[tmp16])
        P.op("dve", lambda e: e.tensor_tensor(out=Ugv[:, :, 0:64], in0=Ugv[:, :, 0:64], in1=tmp16[:, :].rearrange("p (h e) -> p h e", e=64), op=ALU.add), reads=[Usg, tmp16], writes=[Usg])
        P.op("dve", lambda e: e.tensor_tensor(out=Ugv[:, :, 64], in0=Ugv[:, :, 64], in1=p16[:, 16:32], op=ALU.add), reads=[Usg, p16], writes=[Usg])
        if first_group:
            P.op("dve", lambda e: e.tensor_copy(out=Us_acc[:, :], in_=Usg[:, :]), reads=[Usg], writes=[Us_acc])
        else:
            P.op("dve", lambda e: e.tensor_tensor(out=Us_acc[:, :], in0=Us_acc[:, :], in1=Usg[:, :], op=ALU.add), reads=[Us_acc, Usg], writes=[Us_acc])
        if g == 0:
            sa, su = sample_src_all()
            finish_attn(Us_acc, 16, w_out, rden, sa, su, ys.h[0:16, :], [ys_blk], last)

    def dil_layer(layer, li, last):
        Us_acc = T("Us_acc", Us_acc_t.h)
        for gi, g in enumerate((2, 1, 0)):
            ctx = dil_group(layer, li, g, last)
            dil_sample(layer, li, g, ctx, Us_acc, gi == 0, last)
        RR.tp = Ring([0, 1])
        RR.mmA = Ring([(2, 3), (4, 5)])
        RR.mmB = Ring([6, 7])
        state["first"] = False

    Us_acc_t = P.sb("Us_acc", [16, 1280], F32)
    PERSIST = P.off
    dil_setup()
    try:
        for layer in range(depth):
            li = layer // 2
            if layer % 2 == 0:
                gla_layer(layer, li, (layer == depth - 1) and not do_ffn)
            else:
                dil_layer(layer, li, (layer == depth - 1) and not do_ffn)
            if do_ffn:
                ffn_layer(layer, layer == depth - 1)
    except _Stop:
        pass
    P.emit()
    return nc, P


_CACHE = {}


def kernel(x_prompt, x_sample, state_gla, cache_k_g0, cache_v_g0, cache_k_g1, cache_v_g1,
           cache_k_g2, cache_v_g2, state_ffn_conv, rel_bias, norm_mix, norm_ffn,
           gla_w_in, gla_w_gate2, gla_b_gate, gla_norm, gla_w_out,
           dil_w_in, dil_q_norm, dil_k_norm, dil_w_out,
           ffn_w_up, ffn_conv_w, ffn_conv_b, ffn_w_down):
    f = lambda a: np.ascontiguousarray(np.asarray(a, dtype=np.float32))
    if "nc" not in _CACHE:
        _CACHE["nc"] = build_program()[0]
    nc = _CACHE["nc"]
    ohp, valid, ohs = host_constants()
    cks = [f(cache_k_g0), f(cache_k_g1), f(cache_k_g2)]
    cvs = [f(cache_v_g0), f(cache_v_g1), f(cache_v_g2)]
    x_prompt = f(x_prompt); x_sample = f(x_sample); state_gla = f(state_gla); state_ffn_conv = f(state_ffn_conv)
    shared = {
        "rel_bias": f(rel_bias), "norm_mix": f(norm_mix), "norm_ffn": f(norm_ffn),
        "gla_w_in": f(gla_w_in).reshape(2 * D, GLA_IN), "gla_w_gate2": f(gla_w_gate2).reshape(32, 512),
        "gla_b_gate": f(gla_b_gate), "gla_norm": f(gla_norm), "gla_w_out": f(gla_w_out).reshape(2 * D, D),
        "dil_w_in": f(dil_w_in).reshape(2 * D, 9216), "dil_q_norm": f(dil_q_norm), "dil_k_norm": f(dil_k_norm),
        "dil_w_out": f(dil_w_out).reshape(2 * D, D), "ffn_w_up": f(ffn_w_up).reshape(4 * D, 2 * DFF),
        "ffn_conv_w": f(ffn_conv_w).reshape(12, 2 * DFF), "ffn_conv_b": f(ffn_conv_b),
        "ffn_w_down": f(ffn_w_down).reshape(4 * DFF, D),
        "c_ohp": ohp, "c_valid": valid, "c_ohs": ohs,
    }
    in_maps = []
    for c in range(8):
        m = dict(shared)
        m["xp"] = x_prompt[c % 4]
        m["xs"] = x_sample[4 * c:4 * c + 4].reshape(16, D)
        m["sg"] = np.ascontiguousarray(state_gla[:, 4 * c:4 * c + 4]).reshape(2 * 4 * 4 * 128, 256)
        for g in range(3):
            Wg = GROUPS[g][0]
            m["ck%d" % g] = np.ascontiguousarray(cks[g][:, 4 * c:4 * c + 4]).reshape(2 * 4 * Wg, D)
            m["cv%d" % g] = np.ascontiguousarray(cvs[g][:, 4 * c:4 * c + 4]).reshape(2 * 4 * Wg, D)
        m["sfc"] = np.ascontiguousarray(state_ffn_conv[:, 4 * c:4 * c + 4]).reshape(32, 2 * DFF)
        in_maps.append(m)
    res = run_bass_kernel_spmd(nc, in_maps, core_ids=list(range(8)))
    R = res.results
    B = 4
    y_prompt = np.stack([R[b]["yp"] for b in range(B)]).astype(np.float32)
    y_sample = np.concatenate([R[c]["ys"].reshape(4, 4, D) for c in range(8)], 0).astype(np.float32)
    sgp = np.stack([R[b]["sgp"].reshape(2, 4, 128, 256) for b in range(B)], 1).astype(np.float32)
    sgs = np.concatenate([R[c]["sgs"].reshape(2, 4, 4, 128, 256) for c in range(8)], 1).astype(np.float32)
    outs = [y_prompt, y_sample, sgp, sgs]
    keep = [128, 512, 2048]
    for g in range(3):
        kp = np.stack([R[b]["kp%d" % g].reshape(2, keep[g], 16, 64) for b in range(B)], 1).astype(np.float32)
        ks = np.concatenate([R[c]["ks%d" % g].reshape(2, 4, 4, 16, 64) for c in range(8)], 1).astype(np.float32)
        vp = np.stack([R[b]["vp%d" % g].reshape(2, keep[g], 16, 64) for b in range(B)], 1).astype(np.float32)
        vs = np.concatenate([R[c]["vs%d" % g].reshape(2, 4, 4, 16, 64) for c in range(8)], 1).astype(np.float32)
        outs += [kp, ks, vp, vs]
    fcp = np.stack([R[b]["fcp"].reshape(4, 2, 2 * DFF) for b in range(B)], 1).astype(np.float32)
    fcs = np.concatenate([R[c]["fcs"].reshape(4, 4, 2, 2 * DFF) for c in range(8)], 1).astype(np.float32)
    outs += [fcp, fcs]
    return tuple(outs)
```

```python
import contextlib
import math
import numpy as np
import concourse.bass as bass
import concourse.mybir as mybir
from concourse.bass_utils import run_bass_kernel_spmd

F32 = mybir.dt.float32
BF16 = mybir.dt.bfloat16
AF = mybir.ActivationFunctionType
ALU = mybir.AluOpType
AX = mybir.AxisListType

ENGS = ("pe", "act", "dve", "pool", "sp")

D = 1024
SEQ = 4096
NSEQ_S = 4
TS = 4
DEPTH = 4
GLA_IN = 3088
DFF = 2816
EPS = 1e-6
GROUPS = ((128, 1), (512, 4), (2048, 16))
NB = 32


class T:
    __slots__ = ("name", "h", "writers", "readers", "dsem", "dcount")

    def __init__(self, name, h):
        self.name = name
        self.h = h
        self.writers = []
        self.readers = []
        self.dsem = None
        self.dcount = 0

    def __getitem__(self, k):
        return self.h[k]


class Op:
    __slots__ = ("eng", "fn", "deps", "sig", "signal", "dma_key", "pos")

    def __init__(self, eng, fn):
        self.eng = eng
        self.fn = fn
        self.deps = []
        self.sig = False
        self.signal = None
        self.dma_key = None


class Prog:
    ARENA = 207 * 1024

    def __init__(self, nc):
        self.nc = nc
        self.es = contextlib.ExitStack()
        self.streams = {e: [] for e in ENGS}
        self.out_dmas = []
        self.arena = self.es.enter_context(nc.sbuf_tensor("arena", [128, self.ARENA // 4], F32))
        self.arena_bf = self.arena.bitcast(BF16)
        self.off = 0
        self.pending = {e: [] for e in ENGS}
        self.dma_since = []
        self.peak = 0

    def sb(self, name, shape, dtype):
        isz = 4 if dtype == F32 else 2
        n = 1
        for d in shape[1:]:
            n *= d
        nbytes = (n * isz + 63) // 64 * 64
        off = self.off
        self.off += nbytes
        self.peak = max(self.peak, self.off)
        assert self.off <= self.ARENA, ("SBUF arena overflow", name, self.off)
        base = self.arena if dtype == F32 else self.arena_bf
        a = base[0:shape[0], off // isz:off // isz + n]
        if len(shape) == 3:
            a = a.rearrange("p (a b) -> p a b", b=shape[2])
        return T(name, a)

    def ps(self, name, shape, dtype):
        h = self.es.enter_context(self.nc.psum_tensor(name, list(shape), dtype))
        return T(name, h)

    def dram(self, name, shape, dtype, kind="Internal"):
        h = self.nc.dram_tensor(name, list(shape), dtype, kind=kind)
        return T(name, h)

    def barrier(self):
        deps = []
        for e in ENGS:
            for o in reversed(self.streams[e]):
                if o.dma_key is None:
                    o.sig = True
                    deps.append(o)
                    break
        deps.extend(self.dma_since)
        self.dma_since = []
        for e in ENGS:
            self.pending[e].extend(deps)

    def _track(self, op, reads, writes, partial):
        deps = []
        for t in reads:
            deps.extend(t.writers)
        for t in writes:
            others = [r for r in t.readers if r is not op]
            if others:
                deps.extend(others)
                deps.extend(t.writers)
                t.writers = [op]
                t.readers = []
            elif partial:
                t.writers.append(op)
            else:
                deps.extend(t.writers)
                t.writers = [op]
        for t in reads:
            t.readers.append(op)
        seen = set()
        best = {}
        for d in deps:
            if d is op or id(d) in seen:
                continue
            seen.add(id(d))
            if d.eng == "pe" and op.eng == "pe" and d.dma_key is None and op.dma_key is None:
                continue
            if d.dma_key is None:
                if d.eng not in best or best[d.eng].pos < d.pos:
                    best[d.eng] = d
            else:
                op.deps.append(d)
        for d in best.values():
            op.deps.append(d)
            d.sig = True

    def _pend(self, o):
        if self.pending[o.eng]:
            have = set(id(d) for d in o.deps)
            for d in self.pending[o.eng]:
                if id(d) not in have and d is not o:
                    o.deps.append(d)
            self.pending[o.eng] = []

    def op(self, eng, fn, reads=(), writes=(), partial=False):
        o = Op(eng, fn)
        o.pos = len(self.streams[eng])
        self._track(o, list(reads), list(writes), partial)
        self._pend(o)
        self.streams[eng].append(o)
        return o

    def dma(self, eng, out_ap, in_ap, key, reads=(), writes=(), partial=False, final=False, **kw):
        def fn(e):
            return e.dma_start(out=out_ap, in_=in_ap, **kw)
        o = Op(eng, fn)
        o.pos = len(self.streams[eng])
        o.dma_key = key
        o.sig = True
        self._track(o, list(reads), list(writes), partial)
        self._pend(o)
        self.streams[eng].append(o)
        self.dma_since.append(o)
        if final:
            self.out_dmas.append(o)
        return o

    def emit(self):
        nc = self.nc
        es = self.es
        esem = {e: es.enter_context(nc.semaphore("sem_" + e)) for e in ENGS}
        ecount = {e: 0 for e in ENGS}
        nkeys = 0
        semtab = {}
        for e in ENGS:
            for o in self.streams[e]:
                if o.dma_key is not None:
                    kk = (o.dma_key.name, e)
                    if kk not in semtab:
                        semtab[kk] = [es.enter_context(nc.semaphore("dsem_%s_%s" % kk)), 0]
                        nkeys += 1
                    semtab[kk][1] += 16
                    o.signal = (semtab[kk][0], semtab[kk][1], 16)
                elif o.sig:
                    ecount[e] += 1
                    o.signal = (esem[e], ecount[e], 1)
        self.ecount = ecount
        self.nkeys = nkeys
        streams = self.streams
        finals = {}
        for o in self.out_dmas:
            sem, val, _ = o.signal
            if finals.get(id(sem), (None, 0))[1] < val:
                finals[id(sem)] = (sem, val)

        def run(e, h):
            waited = {}
            for o in streams[e]:
                need = {}
                for d in o.deps:
                    sem, val, _ = d.signal
                    if need.get(id(sem), (None, 0))[1] < val:
                        need[id(sem)] = (sem, val)
                for sem, val in need.values():
                    if waited.get(id(sem), 0) < val:
                        h.wait_ge(sem, val)
                        waited[id(sem)] = val
                ins = o.fn(h)
                if o.signal is not None:
                    ins.then_inc(o.signal[0], o.signal[2])
            if e == "sp":
                for sem, val in finals.values():
                    if waited.get(id(sem), 0) < val:
                        h.wait_ge(sem, val)

        with nc.Block() as block:
            @block.tensor
            def _(h):
                run("pe", h)

            @block.scalar
            def _(h):
                run("act", h)

            @block.vector
            def _(h):
                run("dve", h)

            @block.gpsimd
            def _(h):
                run("pool", h)

            @block.sync
            def _(h):
                run("sp", h)
        es.close()


import os as _os
_STOP = int(_os.environ.get("KDBG_STOP", "0"))


class _Stop(Exception):
    pass


def _chk(k):
    if _STOP == k:
        raise _Stop()


class Ring:
    def __init__(self, items):
        self.items = items
        self.i = 0

    def next(self):
        t = self.items[self.i % len(self.items)]
        self.i += 1
        return t


def _bucket(dist):
    max_exact = NB // 2
    if dist < max_exact:
        return dist
    df = np.float32(max(dist, 1))
    v = np.float32(np.log(df / np.float32(max_exact))) / np.float32(math.log(2048 / max_exact)) * np.float32(NB - max_exact)
    return min(max_exact + int(v), NB - 1)


def host_constants():
    ohp = np.zeros((NB, 3 * 384), np.float32)
    valid = np.zeros((1, 3 * 384), np.float32)
    for g, (W, d) in enumerate(GROUPS):
        for rel in range(129):
            n = rel + 127
            ohp[_bucket(rel * d), g * 384 + n] = 1.0
            valid[0, g * 384 + n] = 1.0
    ohs = np.zeros((NB, 6 * 128), np.float32)
    for s in range(4):
        for m in range(128):
            j = (s - m) if m < s else (128 + s - m)
            ohs[_bucket(j * 1), s * 128 + m] = 1.0
    for v, d in ((4, GROUPS[1][1]), (5, GROUPS[2][1])):
        for m in range(128):
            j = 128 - m
            ohs[_bucket(j * d), v * 128 + m] = 1.0
    return ohp, valid, ohs


def build_program(depth=DEPTH, do_ffn=True):
    NBLK = SEQ // 128
    NT4 = NBLK // 4
    nc = bass.Bass("TRN2", target_bir_lowering=False)
    P = Prog(nc)

    def din(name, shape):
        return P.dram(name, shape, F32, kind="ExternalInput")

    def dout(name, shape):
        return P.dram(name, shape, F32, kind="ExternalOutput")

    xp = din("xp", [SEQ, D])
    xs = din("xs", [16, D])
    sg_in = din("sg", [2 * 4 * 4 * 128, 256])
    ck = [din("ck%d" % g, [2 * 4 * GROUPS[g][0], D]) for g in range(3)]
    cv = [din("cv%d" % g, [2 * 4 * GROUPS[g][0], D]) for g in range(3)]
    sfc = din("sfc", [32, 2 * DFF])
    rel_bias = din("rel_bias", [NB, 48])
    norm_mix = din("norm_mix", [4, D])
    norm_ffn = din("norm_ffn", [4, D])
    gla_w_in = din("gla_w_in", [2 * D, GLA_IN])
    gla_w_g2 = din("gla_w_gate2", [32, 512])
    gla_b_g = din("gla_b_gate", [2, 512])
    gla_norm = din("gla_norm", [2, 256])
    gla_w_out = din("gla_w_out", [2 * D, D])
    dil_w_in = din("dil_w_in", [2 * D, 9216])
    dil_qn = din("dil_q_norm", [2, 64])
    dil_kn = din("dil_k_norm", [2, 64])
    dil_w_out = din("dil_w_out", [2 * D, D])
    ffn_w_up = din("ffn_w_up", [4 * D, 2 * DFF])
    ffn_cw = din("ffn_conv_w", [12, 2 * DFF])
    ffn_cb = din("ffn_conv_b", [4, 2 * DFF])
    ffn_w_down = din("ffn_w_down", [4 * DFF, D])
    c_ohp = din("c_ohp", [NB, 3 * 384])
    c_valid = din("c_valid", [1, 3 * 384])
    c_ohs = din("c_ohs", [NB, 6 * 128])

    yp = dout("yp", [SEQ, D])
    ys = dout("ys", [16, D])
    sgp = dout("sgp", [2 * 4 * 128, 256])
    sgs = dout("sgs", [2 * 4 * 4 * 128, 256])
    keep = [min(GROUPS[g][0], SEQ) for g in range(3)]
    kpo = [dout("kp%d" % g, [2 * keep[g], D]) for g in range(3)]
    vpo = [dout("vp%d" % g, [2 * keep[g], D]) for g in range(3)]
    kso = [dout("ks%d" % g, [2 * 16, D]) for g in range(3)]
    vso = [dout("vs%d" % g, [2 * 16, D]) for g in range(3)]
    fcp = dout("fcp", [8, 2 * DFF])
    fcs = dout("fcs", [32, 2 * DFF])

    ug_scr = [P.dram("ug%d" % g, [SEQ, 1280], F32) for g in (1, 2)]
    wsc = P.dram("wsc", [48, 384], F32)

    yp_blk = [T("ypb%d" % i, yp.h) for i in range(NBLK)]
    ys_blk = T("ysb", ys.h)
    xp_blk = [T("xpb%d" % i, xp.h) for i in range(NBLK)]
    xs_blk = T("xsb", xs.h)

    PSB = [P.ps("psb%d" % i, [128, 1024], BF16) if i < 2 else P.ps("psb%d" % i, [128, 512], F32) for i in range(8)]
    PSF = [PSB[i].h.bitcast(F32) if i < 2 else PSB[i].h for i in range(8)]

    class _RR:
        pass
    RR = _RR()
    RR.tp = Ring([0, 1])
    RR.mmA = Ring([(2, 3), (4, 5)])
    RR.mmB = Ring([6, 7])

    class _Dyn:
        def __init__(self, nm):
            self.nm = nm

        def next(self):
            return getattr(RR, self.nm).next()
    tp_ring = _Dyn("tp")
    mmA_ring = _Dyn("mmA")
    mmB_ring = _Dyn("mmB")

    def psA(pair):
        return PSB[pair[0]], PSB[pair[1]]

    identf = P.sb("identf", [128, 128], F32)
    ident = P.sb("ident", [128, 128], BF16)
    onesF = P.sb("onesF", [128, 128], F32)
    epsT = P.sb("epsT", [128, 1], F32)
    mask_s = P.sb("mask_s", [128, 128], F32)
    Jm = P.sb("Jm", [128, 128], BF16)
    Jf = P.sb("Jf", [128, 128], F32)
    gmix = P.sb("gmix", [128, 4, 8], F32)
    gffn = P.sb("gffn", [128, 4, 8], F32)

    P.op("pool", lambda e: e.memset(identf[:, :], 0.0), writes=[identf])
    P.op("pool", lambda e: e.affine_select(out=identf[:, :], in_=identf[:, :], pattern=[[-1, 128]], compare_op=ALU.not_equal, fill=1.0, base=0, channel_multiplier=1), reads=[identf], writes=[identf])
    P.op("dve", lambda e: e.tensor_copy(out=ident[:, :], in_=identf[:, :]), reads=[identf], writes=[ident])
    P.op("pool", lambda e: e.memset(Jf[:, :], 0.0), writes=[Jf])
    P.op("pool", lambda e: e.affine_select(out=Jf[:, :], in_=Jf[:, :], pattern=[[1, 128]], compare_op=ALU.not_equal, fill=1.0, base=-127, channel_multiplier=1), reads=[Jf], writes=[Jf])
    P.op("dve", lambda e: e.tensor_copy(out=Jm[:, :], in_=Jf[:, :]), reads=[Jf], writes=[Jm])
    P.op("dve", lambda e: e.memset(onesF[:, :], 1.0), writes=[onesF])
    P.op("dve", lambda e: e.memset(epsT[:, :], EPS), writes=[epsT])
    GSC = 128.0 ** -0.5
    P.op("pool", lambda e: e.memset(mask_s[:, :], GSC), writes=[mask_s])
    P.op("pool", lambda e: e.affine_select(out=mask_s[:, :], in_=mask_s[:, :], pattern=[[1, 128]], compare_op=ALU.is_ge, fill=0.0, base=0, channel_multiplier=-1), reads=[mask_s], writes=[mask_s])
    rowtmp = P.sb("rowtmp", [4, 512], F32)

    def load_featmajor(dst_view, dst_unit, src_h, row0, nrows, ncols, bank):
        nch = ncols // 128
        for c0 in range(0, ncols, 512):
            w = min(512, ncols - c0)
            P.dma("sp", rowtmp[0:nrows, 0:w], src_h[row0:row0 + nrows, c0:c0 + w], key=rowtmp, writes=[rowtmp])
            for cc in range(w // 128):
                ch = c0 // 128 + cc
                P.op("pe", lambda e, cc=cc, ch=ch: e.matmul(out=PSB[bank][:, ch * nrows:(ch + 1) * nrows], lhsT=rowtmp[0:nrows, cc * 128:(cc + 1) * 128], rhs=identf[0:nrows, 0:nrows], start=True, stop=True), reads=[rowtmp, identf], writes=[PSB[bank]], partial=True)
        P.op("act", lambda e: e.copy(out=dst_view, in_=PSB[bank][:, 0:nch * nrows].rearrange("p (c r) -> p c r", r=nrows)), reads=[PSB[bank]], writes=[dst_unit])

    load_featmajor(gmix[:, :, :].rearrange("p l c -> p c l"), gmix, norm_mix.h, 0, 4, D, 6)
    load_featmajor(gffn[:, :, :].rearrange("p l c -> p c l"), gffn, norm_ffn.h, 0, 4, D, 7)

    xt_ring = Ring([P.sb("xt%d" % i, [128, D], F32) for i in range(2)])
    xr_ring = Ring([P.sb("xr%d" % i, [128, D], F32) for i in range(2)])
    sq_scr = P.sb("sq_scr", [128, D], F32)
    xn_ring = Ring([P.sb("xn%d" % i, [128, D], BF16) for i in range(2)])
    st_ring = Ring([P.sb("st%d" % i, [128, 8], F32) for i in range(4)])
    HT = {"ring": None}
    on_ring = Ring([P.sb("on%d" % i, [128, D], BF16) for i in range(2)])
    onT_ring = Ring([P.sb("onT%d" % i, [128, 8, 128], BF16) for i in range(2)])
    PERSIST = P.off

    def phase_begin(n_hT, width):
        P.barrier()
        P.off = PERSIST
        HT["ring"] = Ring([P.sb("hT%d" % i, [128, 8, width], BF16) for i in range(n_hT)])

    def rstd_from_ss(st, col_in, col_out, npart, scale):
        w = col_out.stop - col_out.start
        P.op("act", lambda e: e.activation(out=st[0:npart, col_out], in_=st[0:npart, col_in], func=AF.Ln, scale=scale, bias=epsT[0:npart, 0:1]), reads=[st, epsT], writes=[st])
        P.op("act", lambda e: e.activation(out=st[0:npart, col_out], in_=st[0:npart, col_out], func=AF.Exp, scale=-0.5), reads=[st], writes=[st])

    def norm_front(blocks, gain, layer):
        hT = HT["ring"].next()
        col = 0
        for (src_ap, units, n) in blocks:
            xt = xt_ring.next()
            P.dma("sp", xt[0:n, :], src_ap, key=xt, reads=units, writes=[xt])
            st = st_ring.next()
            P.op("act", lambda e, xt=xt, st=st, n=n: e.activation(out=sq_scr[0:n, :], in_=xt[0:n, :], func=AF.Square, accum_out=st[0:n, 0:1]), reads=[xt], writes=[st])
            rstd_from_ss(st, slice(0, 1), slice(1, 2), n, 1.0 / D)
            xn = xn_ring.next()
            P.op("act", lambda e, xt=xt, st=st, xn=xn, n=n: e.activation(out=xn[0:n, :], in_=xt[0:n, :], func=AF.Copy, scale=st[0:n, 1:2]), reads=[xt, st], writes=[xn])
            tb = tp_ring.next()
            tpv = PSB[tb].h
            for c in range(8):
                P.op("pe", lambda e, c=c, xn=xn, n=n, tpv=tpv: e.transpose(out=tpv[:, c * 128:c * 128 + n], in_=xn[0:n, c * 128:(c + 1) * 128], identity=ident[0:n, 0:n]), reads=[xn, ident], writes=[PSB[tb]], partial=True)
            c0 = col
            P.op("dve", lambda e, tpv=tpv, hT=hT, n=n, c0=c0: e.tensor_tensor(
                out=hT[:, :, c0:c0 + n], in0=tpv[:, :].rearrange("p (c t) -> p c t", t=128)[:, :, 0:n],
                in1=gain[:, layer, :].unsqueeze(2).broadcast_to([128, 8, n]), op=ALU.mult),
                reads=[PSB[tb], gain], writes=[hT], partial=True)
            col += n
        return hT, col

    def load_w(dst, cols, src_h, row0, nrows, col0, ncols):
        kc = nrows // 128
        step = 512
        for k0 in range(0, kc, 8):
            k1 = min(kc, k0 + 8)
            for c in range(0, ncols, step):
                w = min(step, ncols - c)
                src = src_h[row0 + k0 * 128:row0 + k1 * 128, col0 + c:col0 + c + w].rearrange("(kc p) n -> p kc n", p=128)
                P.dma("pool", dst[:, k0:k1, cols.start + c:cols.start + c + w], src, key=dst, writes=[dst], partial=True)

    def transpose_to(on, npart, onT):
        tb = tp_ring.next()
        tpv = PSB[tb].h
        for c in range(8):
            P.op("pe", lambda e, c=c, tpv=tpv: e.transpose(out=tpv[:, c * 128:c * 128 + npart], in_=on[0:npart, c * 128:(c + 1) * 128], identity=ident[0:npart, 0:npart]), reads=[on, ident], writes=[PSB[tb]], partial=True)
        P.op("act", lambda e, tpv=tpv: e.copy(out=onT[:, :, 0:npart], in_=tpv[:, :].rearrange("p (c t) -> p c t", t=128)[:, :, 0:npart]), reads=[PSB[tb]], writes=[onT])

    def out_proj_residual(onT, npart, wout, nchunks, src_ap, src_units, dst_ap, dst_units, final, off=0):
        pair = mmA_ring.next()
        for half in range(2):
            b = PSB[pair[half]]
            for c in range(nchunks):
                P.op("pe", lambda e, c=c, b=b, half=half: e.matmul(out=b[0:npart, :], lhsT=onT[:, c, off:off + npart], rhs=wout[:, c, half * 512:(half + 1) * 512], start=(c == 0), stop=(c == nchunks - 1)), reads=[onT, wout], writes=[b], partial=True)
        xr = xr_ring.next()
        P.dma("sp", xr[0:npart, :], src_ap, key=xr, reads=src_units, writes=[xr])
        for half in range(2):
            b = PSB[pair[half]]
            P.op("dve", lambda e, b=b, half=half, xr=xr: e.tensor_tensor(out=xr[0:npart, half * 512:(half + 1) * 512], in0=b[0:npart, :], in1=xr[0:npart, half * 512:(half + 1) * 512], op=ALU.add), reads=[b, xr], writes=[xr])
        P.dma("pool", dst_ap, xr[0:npart, :], key=xr, reads=[xr], writes=dst_units, final=final)

    state = {"first": True}

    def prompt_src(i):
        if state["first"]:
            return xp.h[i * 128:(i + 1) * 128, :], [xp_blk[i]]
        return yp.h[i * 128:(i + 1) * 128, :], [yp_blk[i]]

    def sample_src(b):
        if state["first"]:
            return xs.h[b * 4:(b + 1) * 4, :], [xs_blk]
        return ys.h[b * 4:(b + 1) * 4, :], [ys_blk]

    def sample_src_all():
        if state["first"]:
            return xs.h[0:16, :], [xs_blk]
        return ys.h[0:16, :], [ys_blk]

    gl_w_in = None

    def gla_layer(layer, li, last):
        phase_begin(2, 512)
        w_in = P.sb("gw_in", [128, 8, GLA_IN], BF16)
        w_out = P.sb("gw_out", [128, 8, D], BF16)
        wg2 = P.sb("gwg2", [16, 512], BF16)
        negb = P.sb("gnegb", [128, 4], F32)
        gn = P.sb("ggn", [128, 256], F32)
        load_w(w_in, slice(0, GLA_IN), gla_w_in.h, li * D, D, 0, GLA_IN)
        load_w(w_out, slice(0, D), gla_w_out.h, li * D, D, 0, D)
        P.dma("pool", wg2[:, :], gla_w_g2.h[li * 16:(li + 1) * 16, :], key=wg2, writes=[wg2])
        load_featmajor(negb[:, :].unsqueeze(2), negb, gla_b_g.h, li, 1, 512, mmB_ring.next())
        P.op("dve", lambda e: e.tensor_scalar(out=negb[:, :], in0=negb[:, :], scalar1=-1.0, scalar2=None, op0=ALU.mult), reads=[negb], writes=[negb])
        P.dma("sp", gn[:, :], bass.AP(gla_norm.h, li * 256, [[0, 128], [1, 256]]), key=gn, writes=[gn])

        gzT_ring = Ring([P.sb("g_gz_%d" % i, [16, 512], BF16) for i in range(2)])
        lt = P.sb("g_l", [128, 512], F32)
        bp = P.sb("g_bp", [128, 512], F32)
        Et = Ring([P.sb("g_E_%d" % i, [128, 512], F32) for i in range(2)])
        Ei = Ring([P.sb("g_Ei_%d" % i, [128, 512], F32) for i in range(2)])
        El = Ring([P.sb("g_El_%d" % i, [128, 4, 4], F32) for i in range(2)])
        qt_ring = Ring([P.sb("g_qt_%d" % i, [128, 4, 512], BF16) for i in range(2)])
        kt_ring = Ring([P.sb("g_kt_%d" % i, [128, 4, 512], BF16) for i in range(2)])
        v_ring = Ring([P.sb("g_v_%d" % i, [128, D], BF16) for i in range(3)])
        gsr_ring = Ring([P.sb("g_sr_%d" % i, [128, D], F32) for i in range(3)])
        aT_ring = Ring([P.sb("g_aT_%d" % i, [128, 128], BF16) for i in range(3)])
        ktok_ring = Ring([P.sb("g_ktok_%d" % i, [128, 128], BF16) for i in range(3)])
        osb_ring = Ring([P.sb("g_o_%d" % i, [128, D], F32) for i in range(2)])
        oss_ring = Ring([P.sb("g_oss_%d" % i, [128, 8], F32) for i in range(2)])
        S = [P.sb("g_S_%d" % h, [128, 256], F32) for h in range(4)]
        Dd = [P.sb("g_D_%d" % h, [128, 256], F32) for h in range(4)]
        Dbf = [P.sb("g_Dbf_%d" % h, [128, 256], BF16) for h in range(4)]
        sq2 = P.sb("g_sq2", [128, 256], F32)

        def gla_tile(hT, ntok, L, s0_aps, sT_aps, res_blocks, is_sample_first):
            nch = ntok // L
            bb = mmB_ring.next()
            for kc in range(8):
                P.op("pe", lambda e, kc=kc, bb=bb: e.matmul(out=PSB[bb][0:16, 0:ntok], lhsT=w_in[:, kc, 3072:3088], rhs=hT[:, kc, 0:ntok], start=(kc == 0), stop=(kc == 7)), reads=[hT, w_in], writes=[PSB[bb]], partial=True)
            gz = gzT_ring.next()
            P.op("act", lambda e, bb=bb, gz=gz: e.copy(out=gz[:, 0:ntok], in_=PSB[bb][0:16, 0:ntok]), reads=[PSB[bb]], writes=[gz])
            qt = qt_ring.next()
            kt = kt_ring.next()
            E_l = El.next()
            Es, Eis = [], []
            for hh in range(4):
                bb = mmB_ring.next()
                P.op("pe", lambda e, hh=hh, bb=bb, gz=gz: e.matmul(out=PSB[bb][:, 0:ntok], lhsT=wg2[:, hh * 128:(hh + 1) * 128], rhs=gz[:, 0:ntok], start=True, stop=True), reads=[gz, wg2], writes=[PSB[bb]])
                P.op("act", lambda e, hh=hh, bb=bb: e.activation(out=lt[:, 0:ntok], in_=PSB[bb][:, 0:ntok], func=AF.Exp, scale=-1.0, bias=negb[:, hh:hh + 1]), reads=[PSB[bb], negb], writes=[lt])
                P.op("act", lambda e: e.activation(out=lt[:, 0:ntok], in_=lt[:, 0:ntok], func=AF.Ln, bias=onesF[:, 0:1]), reads=[lt, onesF], writes=[lt])
                for c in range(nch):
                    P.op("dve", lambda e, c=c: e.tensor_tensor_scan(out=bp[:, c * L:(c + 1) * L], data0=onesF[:, 0:L], data1=lt[:, c * L:(c + 1) * L], initial=0.0, op0=ALU.mult, op1=ALU.add), reads=[lt, onesF], writes=[bp], partial=(c > 0))
                E = Et.next()
                Einv = Ei.next()
                P.op("act", lambda e, E=E: e.activation(out=E[:, 0:ntok], in_=bp[:, 0:ntok], func=AF.Exp, scale=-1.0 / 16), reads=[bp], writes=[E])
                P.op("act", lambda e, Einv=Einv: e.activation(out=Einv[:, 0:ntok], in_=bp[:, 0:ntok], func=AF.Exp, scale=1.0 / 16), reads=[bp], writes=[Einv])
                P.op("dve", lambda e, E=E, hh=hh, E_l=E_l: e.tensor_copy(out=E_l[:, hh, 0:nch], in_=E[:, 0:ntok].rearrange("p (c l) -> p c l", l=L)[:, :, L - 1]), reads=[E], writes=[E_l], partial=(hh > 0))
                for (dst, coff, own, oth) in ((qt, 0, E, Einv), (kt, 512, Einv, E)):
                    bb2 = mmB_ring.next()
                    for kc in range(8):
                        P.op("pe", lambda e, kc=kc, bb2=bb2, coff=coff, hh=hh: e.matmul(out=PSB[bb2][:, 0:ntok], lhsT=w_in[:, kc, coff + hh * 128:coff + (hh + 1) * 128], rhs=hT[:, kc, 0:ntok], start=(kc == 0), stop=(kc == 7)), reads=[hT, w_in], writes=[PSB[bb2]], partial=True)
                    for c in range(nch):
                        le = (c + 1) * L - 1
                        P.op("dve", lambda e, c=c, le=le, bb2=bb2, dst=dst, own=own, oth=oth, hh=hh: e.scalar_tensor_tensor(
                            out=dst[:, hh, c * L:(c + 1) * L], in0=PSB[bb2][:, c * L:(c + 1) * L], scalar=oth[:, le:le + 1], in1=own[:, c * L:(c + 1) * L], op0=ALU.mult, op1=ALU.mult),
                            reads=[PSB[bb2], own, oth], writes=[dst], partial=True)
            for c in range(nch):
                cs = slice(c * L, (c + 1) * L)
                pairv = mmA_ring.next()
                for half in range(2):
                    b = PSB[pairv[half]]
                    for kc in range(8):
                        P.op("pe", lambda e, kc=kc, b=b, half=half, cs=cs: e.matmul(out=b[0:L, :], lhsT=hT[:, kc, cs], rhs=w_in[:, kc, 1024 + half * 512:1024 + (half + 1) * 512], start=(kc == 0), stop=(kc == 7)), reads=[hT, w_in], writes=[b], partial=True)
                vsb = v_ring.next()
                for half in range(2):
                    b = PSB[pairv[half]]
                    P.op("act", lambda e, b=b, half=half, vsb=vsb: e.copy(out=vsb[0:L, half * 512:(half + 1) * 512], in_=b[0:L, :]), reads=[b], writes=[vsb], partial=(half > 0))
                pairr = mmA_ring.next()
                for half in range(2):
                    b = PSB[pairr[half]]
                    for kc in range(8):
                        P.op("pe", lambda e, kc=kc, b=b, half=half, cs=cs: e.matmul(out=b[0:L, :], lhsT=hT[:, kc, cs], rhs=w_in[:, kc, 2048 + half * 512:2048 + (half + 1) * 512], start=(kc == 0), stop=(kc == 7)), reads=[hT, w_in], writes=[b], partial=True)
                gsr = gsr_ring.next()
                for half in range(2):
                    b = PSB[pairr[half]]
                    P.op("act", lambda e, b=b, half=half, gsr=gsr: e.activation(out=gsr[0:L, half * 512:(half + 1) * 512], in_=b[0:L, :], func=AF.Silu), reads=[b], writes=[gsr], partial=(half > 0))
                P.op("pool", lambda e, gsr=gsr: e.tensor_tensor(out=gsr[0:L, :].rearrange("p (h e) -> p h e", e=256), in0=gsr[0:L, :].rearrange("p (h e) -> p h e", e=256), in1=gn[0:L, :].unsqueeze(1).broadcast_to([L, 4, 256]), op=ALU.mult), reads=[gsr, gn], writes=[gsr])
                osb = osb_ring.next()
                oss = oss_ring.next()
                for hh in range(4):
                    if s0_aps is not None or (c == 0 and is_sample_first):
                        pass
                    if s0_aps is not None:
                        P.dma("sp", S[hh][:, :], s0_aps[c][hh], key=S[hh], writes=[S[hh]])
                    if s0_aps is None and state["gla_zero"] and c == 0:
                        P.op("pool", lambda e, hh=hh: e.memset(S[hh][:, :], 0.0), writes=[S[hh]])
                    P.op("dve", lambda e, hh=hh, c=c, E_l=E_l: e.tensor_scalar(out=Dd[hh][:, :], in0=S[hh][:, :], scalar1=E_l[:, hh, c:c + 1], scalar2=None, op0=ALU.mult), reads=[S[hh], E_l], writes=[Dd[hh]])
                    P.op("act", lambda e, hh=hh: e.activation(out=Dbf[hh][:, :], in_=Dd[hh][:, :], func=AF.Copy, scale=GSC), reads=[Dd[hh]], writes=[Dbf[hh]])
                    ba = mmB_ring.next()
                    P.op("pe", lambda e, ba=ba, hh=hh, cs=cs: e.matmul(out=PSB[ba][0:L, 0:L], lhsT=kt[:, hh, cs], rhs=qt[:, hh, cs], start=True, stop=True), reads=[kt, qt], writes=[PSB[ba]])
                    aT = aT_ring.next()
                    P.op("dve", lambda e, ba=ba, aT=aT: e.tensor_tensor(out=aT[0:L, 0:L], in0=PSB[ba][0:L, 0:L], in1=mask_s[0:L, 0:L], op=ALU.mult), reads=[PSB[ba], mask_s], writes=[aT])
                    tb = tp_ring.next()
                    tpv = PSB[tb].h
                    P.op("pe", lambda e, tpv=tpv, hh=hh, cs=cs, tb=tb: e.transpose(out=tpv[0:L, 0:128], in_=kt[:, hh, cs], identity=ident[:, :]), reads=[kt, ident], writes=[PSB[tb]])
                    ktok = ktok_ring.next()
                    P.op("act", lambda e, tpv=tpv, ktok=ktok: e.copy(out=ktok[0:L, :], in_=tpv[0:L, 0:128]), reads=[PSB[tb]], writes=[ktok])
                    P.op("pe", lambda e, ba=ba, aT=aT, vsb=vsb, hh=hh: e.matmul(out=PSB[ba][0:L, 128:384], lhsT=aT[0:L, 0:L], rhs=vsb[0:L, hh * 256:(hh + 1) * 256], start=True, stop=False), reads=[aT, vsb], writes=[PSB[ba]])
                    P.op("pe", lambda e, ba=ba, hh=hh, cs=cs: e.matmul(out=PSB[ba][0:L, 128:384], lhsT=qt[:, hh, cs], rhs=Dbf[hh][:, :], start=False, stop=True), reads=[qt, Dbf[hh]], writes=[PSB[ba]], partial=True)
                    P.op("act", lambda e, ba=ba, osb=osb, hh=hh: e.copy(out=osb[0:L, hh * 256:(hh + 1) * 256], in_=PSB[ba][0:L, 128:384]), reads=[PSB[ba]], writes=[osb], partial=(hh > 0))
                    P.op("act", lambda e, ba=ba, oss=oss, hh=hh: e.activation(out=sq2[0:L, :], in_=PSB[ba][0:L, 128:384], func=AF.Square, accum_out=oss[0:L, hh:hh + 1]), reads=[PSB[ba]], writes=[oss], partial=(hh > 0))
                    bk = mmB_ring.next()
                    P.op("pe", lambda e, bk=bk, ktok=ktok, vsb=vsb, hh=hh: e.matmul(out=PSB[bk][:, 0:256], lhsT=ktok[0:L, :], rhs=vsb[0:L, hh * 256:(hh + 1) * 256], start=True, stop=True), reads=[ktok, vsb], writes=[PSB[bk]])
                    P.op("dve", lambda e, bk=bk, hh=hh: e.tensor_tensor(out=S[hh][:, :], in0=PSB[bk][:, 0:256], in1=Dd[hh][:, :], op=ALU.add), reads=[PSB[bk], Dd[hh]], writes=[S[hh]])
                    if sT_aps is not None and sT_aps[c] is not None:
                        P.dma("pool", sT_aps[c][hh][0], S[hh][:, :], key=S[hh], reads=[S[hh]], writes=[sT_aps[c][hh][1]], partial=True, final=True)
                state["gla_zero"] = False
                rstd_from_ss(oss, slice(0, 4), slice(4, 8), L, 1.0 / 256)
                on = on_ring.next()
                for hh in range(4):
                    P.op("dve", lambda e, hh=hh, on=on, osb=osb, oss=oss, gsr=gsr: e.scalar_tensor_tensor(out=on[0:L, hh * 256:(hh + 1) * 256], in0=osb[0:L, hh * 256:(hh + 1) * 256], scalar=oss[0:L, 4 + hh:5 + hh], in1=gsr[0:L, hh * 256:(hh + 1) * 256], op0=ALU.mult, op1=ALU.mult), reads=[osb, oss, gsr], writes=[on], partial=(hh > 0))
                onT = onT_ring.next()
                transpose_to(on, L, onT)
                src_ap, src_units, dst_ap, dst_units, fin = res_blocks[c]
                out_proj_residual(onT, L, w_out, 8, src_ap, src_units, dst_ap, dst_units, fin)

        def blocks_for_tile(t):
            return [(prompt_src(4 * t + j)[0], prompt_src(4 * t + j)[1], 128) for j in range(4)]

        state["gla_zero"] = True
        pre = norm_front(blocks_for_tile(0), gmix, layer)
        for t in range(NT4):
            cur = pre
            if t + 1 < NT4:
                pre = norm_front(blocks_for_tile(t + 1), gmix, layer)
            else:
                pre = norm_front([(sample_src(0)[0], sample_src(0)[1], 4)], gmix, layer)
            res = []
            for j in range(4):
                i = 4 * t + j
                sa, su = prompt_src(i)
                res.append((sa, su, yp.h[i * 128:(i + 1) * 128, :], [yp_blk[i]], last))
            sT = None
            if t == NT4 - 1:
                sT = [None, None, None, [(sgp.h[(li * 4 + hh) * 128:(li * 4 + hh + 1) * 128, :], sgp) for hh in range(4)]]
            gla_tile(cur[0], 512, 128, None, sT, res, False)
        for b in range(4):
            cur = pre
            if b + 1 < 4:
                pre = norm_front([(sample_src(b + 1)[0], sample_src(b + 1)[1], 4)], gmix, layer)
            sa, su = sample_src(b)
            res = [(sa, su, ys.h[b * 4:(b + 1) * 4, :], [ys_blk], last)]
            s0 = [[sg_in.h[((li * 4 + b) * 4 + hh) * 128:((li * 4 + b) * 4 + hh + 1) * 128, :] for hh in range(4)]]
            sT = [[(sgs.h[((li * 4 + b) * 4 + hh) * 128:((li * 4 + b) * 4 + hh + 1) * 128, :], sgs) for hh in range(4)]]
            gla_tile(cur[0], 4, 4, s0, sT, res, True)
        state["first"] = False

    def ffn_layer(layer, last):
        phase_begin(2, 256)
        RR.mmA = Ring([(2, 3)])
        RR.mmB = Ring([4, 5, 6, 7])
        w_up = P.sb("f_wup", [128, 8, 2 * DFF], BF16)
        w_dn = P.sb("f_wdn", [128, 22, D], BF16)
        cw = P.sb("f_cw", [128, 3, 44], F32)
        cb = P.sb("f_cb", [128, 44], F32)
        load_w(w_up, slice(0, 2 * DFF), ffn_w_up.h, layer * D, D, 0, 2 * DFF)
        load_w(w_dn, slice(0, D), ffn_w_down.h, layer * DFF, DFF, 0, D)
        load_featmajor(cw[:, :, :].rearrange("p i c -> p c i"), cw, ffn_cw.h, layer * 3, 3, 2 * DFF, mmB_ring.next())
        load_featmajor(cb[:, :].unsqueeze(2), cb, ffn_cb.h, layer, 1, 2 * DFF, mmB_ring.next())
        hist = P.sb("f_hist", [128, 44, 2], F32)
        cs_ring = Ring([P.sb("f_c_%d" % i, [128, 256], F32) for i in range(4)])
        sg_ring = Ring([P.sb("f_sg_%d" % i, [128, 256], F32) for i in range(2)])
        act = P.sb("f_act", [128, 22, 256], BF16)
        ul_ring = Ring([P.sb("f_ul_%d" % i, [2, 512], F32) for i in range(2)])
        hrow = P.sb("f_hrow", [2, 512], F32)

        def ffn_tile(hT, N, res_blocks, state_out_ap, state_out_unit):
            for cp in range(22):
                cts = []
                for ch in (cp, 22 + cp):
                    bb = mmB_ring.next()
                    for kc in range(8):
                        P.op("pe", lambda e, kc=kc, bb=bb, ch=ch: e.matmul(out=PSB[bb][:, 0:N], lhsT=w_up[:, kc, ch * 128:(ch + 1) * 128], rhs=hT[:, kc, 0:N], start=(kc == 0), stop=(kc == 7)), reads=[hT, w_up], writes=[PSB[bb]], partial=True)
                    ct = cs_ring.next()
                    P.op("act", lambda e, bb=bb, ct=ct, ch=ch: e.activation(out=ct[:, 0:N], in_=PSB[bb][:, 0:N], func=AF.Identity, scale=cw[:, 2, ch:ch + 1], bias=cb[:, ch:ch + 1]), reads=[PSB[bb], cw, cb], writes=[ct])
                    P.op("dve", lambda e, bb=bb, ct=ct, ch=ch: e.scalar_tensor_tensor(out=ct[:, 1:N], in0=PSB[bb][:, 0:N - 1], scalar=cw[:, 1, ch:ch + 1], in1=ct[:, 1:N], op0=ALU.mult, op1=ALU.add), reads=[PSB[bb], cw, ct], writes=[ct])
                    P.op("dve", lambda e, bb=bb, ct=ct, ch=ch: e.scalar_tensor_tensor(out=ct[:, 2:N], in0=PSB[bb][:, 0:N - 2], scalar=cw[:, 0, ch:ch + 1], in1=ct[:, 2:N], op0=ALU.mult, op1=ALU.add), reads=[PSB[bb], cw, ct], writes=[ct])
                    P.op("dve", lambda e, ct=ct, ch=ch: e.scalar_tensor_tensor(out=ct[:, 0:2], in0=hist[:, ch, 0:2], scalar=cw[:, 0, ch:ch + 1], in1=ct[:, 0:2], op0=ALU.mult, op1=ALU.add), reads=[hist, cw, ct], writes=[ct])
                    P.op("dve", lambda e, ct=ct, ch=ch: e.scalar_tensor_tensor(out=ct[:, 0:1], in0=hist[:, ch, 1:2], scalar=cw[:, 1, ch:ch + 1], in1=ct[:, 0:1], op0=ALU.mult, op1=ALU.add), reads=[hist, cw, ct], writes=[ct])
                    P.op("act", lambda e, bb=bb, ch=ch: e.copy(out=hist[:, ch, 0:2], in_=PSB[bb][:, N - 2:N]), reads=[PSB[bb]], writes=[hist])
                    cts.append(ct)
                sgt = sg_ring.next()
                P.op("act", lambda e, sgt=sgt, c0=cts[0]: e.activation(out=sgt[:, 0:N], in_=c0[:, 0:N], func=AF.Silu), reads=[cts[0]], writes=[sgt])
                P.op("dve", lambda e, sgt=sgt, c1=cts[1], cp=cp: e.tensor_tensor(out=act[:, cp, 0:N], in0=sgt[:, 0:N], in1=c1[:, 0:N], op=ALU.mult), reads=[sgt, cts[1]], writes=[act], partial=(cp > 0))
            if state_out_ap is not None:
                for blk in range(11):
                    bb = mmB_ring.next()
                    for kc in range(8):
                        P.op("pe", lambda e, kc=kc, bb=bb, blk=blk: e.matmul(out=PSB[bb][0:2, :], lhsT=hT[:, kc, N - 2:N], rhs=w_up[:, kc, blk * 512:(blk + 1) * 512], start=(kc == 0), stop=(kc == 7)), reads=[hT, w_up], writes=[PSB[bb]], partial=True)
                    ul = ul_ring.next()
                    P.op("act", lambda e, bb=bb, ul=ul: e.copy(out=ul[0:2, :], in_=PSB[bb][0:2, :]), reads=[PSB[bb]], writes=[ul])
                    P.dma("pool", state_out_ap[:, blk * 512:(blk + 1) * 512], ul[0:2, :], key=ul, reads=[ul], writes=[state_out_unit], partial=True, final=True)
            nb = len(res_blocks)
            for j in range(nb):
                src_ap, src_units, dst_ap, dst_units, fin, n = res_blocks[j]
                out_proj_residual(act, n, w_dn, 22, src_ap, src_units, dst_ap, dst_units, fin, off=j * 128)

        def blocks_for_tile(t):
            return [(prompt_src(2 * t + j)[0], prompt_src(2 * t + j)[1], 128) for j in range(2)]

        P.op("pool", lambda e: e.memset(hist[:, :, :], 0.0), writes=[hist])
        pre = norm_front(blocks_for_tile(0), gffn, layer)
        for t in range(NBLK // 2):
            cur = pre
            if t + 1 < NBLK // 2:
                pre = norm_front(blocks_for_tile(t + 1), gffn, layer)
            else:
                pre = norm_front([(sample_src(0)[0], sample_src(0)[1], 4)], gffn, layer)
            res = []
            for j in range(2):
                i = 2 * t + j
                sa, su = prompt_src(i)
                res.append((sa, su, yp.h[i * 128:(i + 1) * 128, :], [yp_blk[i]], last, 128))
            ffn_tile(cur[0], 256, res, fcp.h[layer * 2:(layer + 1) * 2, :] if t == NBLK // 2 - 1 else None, fcp)
        for b in range(4):
            cur = pre
            if b + 1 < 4:
                pre = norm_front([(sample_src(b + 1)[0], sample_src(b + 1)[1], 4)], gffn, layer)
            r0 = (layer * 4 + b) * 2
            bb = mmB_ring.next()
            for q11 in range(11):
                P.dma("sp", hrow[0:2, :], sfc.h[r0:r0 + 2, q11 * 512:(q11 + 1) * 512], key=hrow, writes=[hrow])
                for cc in range(4):
                    ch = q11 * 4 + cc
                    P.op("pe", lambda e, bb=bb, cc=cc, ch=ch: e.matmul(out=PSB[bb][:, ch * 2:ch * 2 + 2], lhsT=hrow[0:2, cc * 128:(cc + 1) * 128], rhs=identf[0:2, 0:2], start=True, stop=True), reads=[hrow, identf], writes=[PSB[bb]], partial=True)
            P.op("act", lambda e, bb=bb: e.copy(out=hist[:, :, :], in_=PSB[bb][:, 0:88].rearrange("p (c t) -> p c t", t=2)), reads=[PSB[bb]], writes=[hist])
            sa, su = sample_src(b)
            res = [(sa, su, ys.h[b * 4:(b + 1) * 4, :], [ys_blk], last, 4)]
            r1 = (layer * 4 + b) * 2
            ffn_tile(cur[0], 4, res, fcs.h[r1:r1 + 2, :], fcs)
        RR.mmA = Ring([(2, 3), (4, 5)])
        RR.mmB = Ring([6, 7])
        state["first"] = False


    def dil_setup():
        phase_begin(1, 16)
        relb = P.sb("d_relb", [NB, 48], F32)
        ohp = P.sb("d_ohp", [NB, 3 * 384], F32)
        vld = P.sb("d_vld", [16, 3 * 384], F32)
        P.dma("sp", relb[:, :], rel_bias.h[:, :], key=relb, writes=[relb])
        P.dma("sp", ohp[:, :], c_ohp.h[:, :], key=ohp, writes=[ohp])
        P.dma("sp", vld[:, :], bass.AP(c_valid.h, 0, [[0, 16], [1, 3 * 384]]), key=vld, writes=[vld])
        for g in range(3):
            bb = mmB_ring.next()
            P.op("pe", lambda e, bb=bb, g=g: e.matmul(out=PSB[bb][0:16, 0:384], lhsT=relb[:, g * 16:(g + 1) * 16], rhs=ohp[:, g * 384:(g + 1) * 384], start=True, stop=True), reads=[relb, ohp], writes=[PSB[bb]])
            wv = P.sb("d_wv%d" % g, [16, 384], F32)
            wvb = P.sb("d_wvb%d" % g, [16, 384], F32)
            P.op("act", lambda e, bb=bb, wv=wv: e.activation(out=wv[:, :], in_=PSB[bb][0:16, 0:384], func=AF.Exp), reads=[PSB[bb]], writes=[wv])
            P.op("dve", lambda e, wv=wv, wvb=wvb, g=g: e.tensor_tensor(out=wvb[:, :], in0=wv[:, :], in1=vld[:, g * 384:(g + 1) * 384], op=ALU.mult), reads=[wv, vld], writes=[wvb])
            P.dma("sp", wsc.h[g * 16:(g + 1) * 16, :], wvb[:, :], key=wvb, reads=[wvb], writes=[wsc], partial=True)

    def dil_group(layer, li, g, last):
        _chk(1)
        W, d = GROUPS[g]
        nbk = (SEQ // d) // 128
        RR.tp = Ring([0])
        RR.mmA = Ring([(2, 3)])
        RR.mmB = Ring([7])
        sc_ring = Ring([1, 7])
        UB = (4, 5, 6)
        phase_begin(1, 512)
        Wg = P.sb("d_Wg", [128, 8, 3072], BF16)
        load_w(Wg, slice(0, 3072), dil_w_in.h, li * D, D, g * 3072, 3072)
        w_out = None
        if g == 0:
            w_out = P.sb("d_wout", [128, 8, D], BF16)
            load_w(w_out, slice(0, D), dil_w_out.h, li * D, D, 0, D)
        qg = P.sb("d_qg", [128, 64], F32)
        kg = P.sb("d_kg", [128, 64], F32)
        P.dma("sp", qg[:, :], bass.AP(dil_qn.h, li * 64, [[0, 128], [1, 64]]), key=qg, writes=[qg])
        P.dma("sp", kg[:, :], bass.AP(dil_kn.h, li * 64, [[0, 128], [1, 64]]), key=kg, writes=[kg])
        relb = P.sb("d_relb", [NB, 48], F32)
        P.dma("sp", relb[:, :], rel_bias.h[:, :], key=relb, writes=[relb])
        M = P.sb("d_M", [128, 16, 256], BF16)
        H_ring = Ring([P.sb("d_H%d" % i, [128, 256], F32) for i in range(2)])
        for h in range(16):
            H = H_ring.next()
            P.dma("sp", H[:, :], bass.AP(wsc.h, (g * 16 + h) * 384, [[1, 128], [1, 256]]), key=H, reads=[wsc], writes=[H])
            bb = mmB_ring.next()
            P.op("pe", lambda e, bb=bb, H=H: e.matmul(out=PSB[bb][:, 0:256], lhsT=Jf[:, :], rhs=H[:, :], start=True, stop=True), reads=[Jf, H], writes=[PSB[bb]])
            P.op("act", lambda e, bb=bb, h=h: e.copy(out=M[:, h, :], in_=PSB[bb][:, 0:256]), reads=[PSB[bb]], writes=[M], partial=(h > 0))
        _chk(2)
        qS = P.sb("d_qS", [16, D], F32)
        kS = P.sb("d_kS", [16, D], F32)
        vS = P.sb("d_vS", [16, D], F32)
        MARK = P.off
        nrm = P.sb("d_nrm", [128, D], F32)
        st16 = Ring([P.sb("d_st%d" % i, [128, 32], F32) for i in range(2)])
        kout_r = Ring([P.sb("d_ko%d" % i, [128, D], F32) for i in range(2)])
        vout_r = Ring([P.sb("d_vo%d" % i, [128, D], F32) for i in range(2)])
        qbf_r = Ring([P.sb("d_qb%d" % i, [128, D], BF16) for i in range(2)])
        qT_r = Ring([P.sb("d_qT%d" % i, [128, 16, 128], BF16) for i in range(2)])
        for t in qT_r.items:
            P.op("pool", lambda e, t=t: e.memset(t[:, :, :], 0.0), writes=[t])
        kT_r = Ring([P.sb("d_kT%d" % i, [128, 8, 128], BF16) for i in range(3)])
        va_r = Ring([P.sb("d_va%d" % i, [128, 16, 80], BF16) for i in range(3)])
        pe_r = Ring([P.sb("d_pe%d" % i, [128, 512], BF16) for i in range(2)])
        pt_r = Ring([P.sb("d_pt%d" % i, [128, 512], BF16) for i in range(2)])
        U_r = Ring([P.sb("d_U%d" % i, [128, 1280], F32) for i in range(2)])
        Ua = P.sb("d_Ua", [128, 1280], F32)
        for t in U_r.items:
            P.op("pool", lambda e, t=t: e.memset(t[:, :], 0.0), writes=[t])
        rden = P.sb("d_rden", [128, 16], F32)
        for t in va_r.items:
            P.op("pool", lambda e, t=t: e.memset(t[:, :, :], 1.0), writes=[t])

        def qkv_block(hT, c0, n, cache_k_ap, cache_v_ap, cache_ku, cache_vu, sample):
            outs = []
            for s_ in range(3):
                pair = mmA_ring.next()
                for half in range(2):
                    b = PSB[pair[half]]
                    for kc in range(8):
                        P.op("pe", lambda e, kc=kc, b=b, half=half, s_=s_: e.matmul(out=b[0:n, :], lhsT=hT[:, kc, c0:c0 + n], rhs=Wg[:, kc, s_ * 1024 + half * 512:s_ * 1024 + (half + 1) * 512], start=(kc == 0), stop=(kc == 7)), reads=[hT, Wg], writes=[b], partial=True)
                if s_ == 2:
                    vo = vS if sample else vout_r.next()
                    for half in range(2):
                        b = PSB[pair[half]]
                        P.op("act", lambda e, b=b, half=half, vo=vo: e.copy(out=vo[0:n, half * 512:(half + 1) * 512], in_=b[0:n, :]), reads=[b], writes=[vo], partial=(half > 0))
                    if cache_v_ap is not None:
                        P.dma("pool", cache_v_ap, vo[0:n, :], key=vo, reads=[vo], writes=[cache_vu], partial=True, final=True)
                    outs.append(vo)
                    continue
                st = st16.next()
                for half in range(2):
                    b = PSB[pair[half]]
                    P.op("act", lambda e, b=b, half=half: e.copy(out=nrm[0:n, half * 512:(half + 1) * 512], in_=b[0:n, :]), reads=[b], writes=[nrm], partial=(half > 0))
                P.op("act", lambda e: e.activation(out=sq_scr[0:n, :], in_=nrm[0:n, :], func=AF.Square), reads=[nrm], writes=[sq_scr])
                P.op("dve", lambda e, st=st: e.tensor_reduce(out=st[0:n, 0:16], in_=sq_scr[0:n, :].rearrange("p (h e) -> p h e", e=64), axis=AX.X, op=ALU.add), reads=[sq_scr], writes=[st])
                rstd_from_ss(st, slice(0, 16), slice(16, 32), n, 1.0 / 64)
                P.op("dve", lambda e, st=st: e.tensor_tensor(out=nrm[0:n, :].rearrange("p (h e) -> p h e", e=64), in0=nrm[0:n, :].rearrange("p (h e) -> p h e", e=64), in1=st[0:n, 16:32].unsqueeze(2).broadcast_to([n, 16, 64]), op=ALU.mult), reads=[nrm, st], writes=[nrm])
                gain = qg if s_ == 0 else kg
                if s_ == 0:
                    dst = qS if sample else qbf_r.next()
                else:
                    dst = kS if sample else kout_r.next()
                P.op("pool", lambda e, dst=dst, gain=gain: e.tensor_tensor(out=dst[0:n, :].rearrange("p (h e) -> p h e", e=64), in0=nrm[0:n, :].rearrange("p (h e) -> p h e", e=64), in1=gain[0:n, :].unsqueeze(1).broadcast_to([n, 16, 64]), op=ALU.mult), reads=[nrm, gain], writes=[dst])
                if s_ == 1 and cache_k_ap is not None:
                    P.dma("pool", cache_k_ap, dst[0:n, :], key=dst, reads=[dst], writes=[cache_ku], partial=True, final=True)
                outs.append(dst)
            return outs

        def to_featmajor(src, dstT, split=False):
            if src.h.dtype != BF16:
                tmp = qbf_r.next()
                P.op("act", lambda e, tmp=tmp, src0=src: e.copy(out=tmp[:, :], in_=src0[:, :]), reads=[src], writes=[tmp])
                src = tmp
            tb = tp_ring.next()
            tpv = PSB[tb].h
            for c in range(8):
                P.op("pe", lambda e, c=c, tpv=tpv, src=src: e.transpose(out=tpv[:, c * 128:(c + 1) * 128], in_=src[:, c * 128:(c + 1) * 128], identity=ident[:, :]), reads=[src, ident], writes=[PSB[tb]], partial=True)
            if split:
                P.op("act", lambda e, tpv=tpv: e.copy(out=dstT[0:64, 0:8, :], in_=tpv[0:64, :].rearrange("p (c t) -> p c t", t=128)), reads=[PSB[tb]], writes=[dstT])
                P.op("act", lambda e, tpv=tpv: e.copy(out=dstT[64:128, 8:16, :], in_=tpv[64:128, :].rearrange("p (c t) -> p c t", t=128)), reads=[PSB[tb]], writes=[dstT], partial=True)
            else:
                P.op("act", lambda e, tpv=tpv: e.copy(out=dstT[:, :, :], in_=tpv[:, :].rearrange("p (c t) -> p c t", t=128)), reads=[PSB[tb]], writes=[dstT])

        hTs, _ = norm_front([(sample_src_all()[0], sample_src_all()[1], 16)], gmix, layer)
        qkv_block(hTs, 0, 16, kso[g].h[li * 16:(li + 1) * 16, :], vso[g].h[li * 16:(li + 1) * 16, :], kso[g], vso[g], True)

        _chk(3)
        blocks = [(r, n) for r in range(d) for n in range(nbk)]

        def blk_src(r, n):
            base = yp.h
            units = [yp_blk[i] for i in range(n * d, (n + 1) * d)]
            return bass.AP(base, (n * 128 * d + r) * D, [[d * D, 128], [1, D]]), units

        def tile_blocks(t):
            return [(blk_src(*blocks[4 * t + j])[0], blk_src(*blocks[4 * t + j])[1], 128) for j in range(4)]

        prev_kT = None
        prev_va = None
        pre = norm_front(tile_blocks(0), gmix, layer)
        for t in range(NT4):
            hT = pre[0]
            for j in range(4):
                r, n = blocks[4 * t + j]
                ck_ap = cv_ap = None
                if n == nbk - 1:
                    ck_ap = bass.AP(kpo[g].h, (li * keep[g] + r) * D, [[d * D, 128], [1, D]])
                    cv_ap = bass.AP(vpo[g].h, (li * keep[g] + r) * D, [[d * D, 128], [1, D]])
                qb, ko, vo = qkv_block(hT, j * 128, 128, ck_ap, cv_ap, kpo[g], vpo[g], False)
                _chk(6)
                qT = qT_r.next()
                kT = kT_r.next()
                to_featmajor(qb, qT, split=True)
                to_featmajor(ko, kT)
                va = va_r.next()
                P.op("pool", lambda e, va=va, vo=vo: e.tensor_copy(out=va[:, :, 0:64], in_=vo[:, :].rearrange("p (h e) -> p h e", e=64)), reads=[vo], writes=[va])
                _chk(7)
                has_prev = n > 0
                for hp in range(8):
                    bb = sc_ring.next()
                    for hh in range(2):
                        ps_ = slice(hh * 64, (hh + 1) * 64)
                        P.op("pe", lambda e, bb=bb, hh=hh, ps_=ps_, hp=hp, kT=kT, qT=qT: e.matmul(out=PSF[bb][:, hh * 256:hh * 256 + 128], lhsT=kT[:, hp, :], rhs=qT[:, hh * 8 + hp, :], start=True, stop=True), reads=[kT, qT], writes=[PSB[bb]], partial=True)
                        if has_prev:
                            P.op("pe", lambda e, bb=bb, hh=hh, ps_=ps_, hp=hp, pk=prev_kT, qT=qT: e.matmul(out=PSF[bb][:, hh * 256 + 128:hh * 256 + 256], lhsT=pk[:, hp, :], rhs=qT[:, hh * 8 + hp, :], start=True, stop=True), reads=[prev_kT, qT], writes=[PSB[bb]], partial=True)
                    pe_t = pe_r.next()
                    pt = pt_r.next()
                    wdt = 256 if has_prev else 128
                    P.op("act", lambda e, bb=bb, pe_t=pe_t, wdt=wdt: e.activation(out=pe_t[:, :].rearrange("p (h x) -> p h x", x=256)[:, :, 0:wdt], in_=PSF[bb][:, :].rearrange("p (h x) -> p h x", x=256)[:, :, 0:wdt], func=AF.Exp, scale=0.125), reads=[PSB[bb]], writes=[pe_t])
                    P.op("dve", lambda e, pe_t=pe_t, pt=pt, hp=hp, wdt=wdt: e.tensor_tensor(out=pt[:, :].rearrange("p (h x) -> p h x", x=256)[:, :, 0:wdt], in0=pe_t[:, :].rearrange("p (h x) -> p h x", x=256)[:, :, 0:wdt], in1=M[:, 2 * hp:2 * hp + 2, 0:wdt], op=ALU.mult), reads=[pe_t, M], writes=[pt])
                    for hh in range(2):
                        h = 2 * hp + hh
                        ub = PSB[UB[h // 6]]
                        c0 = (h % 6) * 80
                        P.op("pe", lambda e, ub=ub, c0=c0, pt=pt, hh=hh, va=va, h=h, has_prev=has_prev: e.matmul(out=ub[:, c0:c0 + 65], lhsT=pt[:, hh * 256:hh * 256 + 128], rhs=va[:, h, 0:65], start=True, stop=(not has_prev)), reads=[pt, va], writes=[ub], partial=True)
                        if has_prev:
                            P.op("pe", lambda e, ub=ub, c0=c0, pt=pt, hh=hh, pv=prev_va, h=h: e.matmul(out=ub[:, c0:c0 + 65], lhsT=pt[:, hh * 256 + 128:hh * 256 + 256], rhs=pv[:, h, 0:65], start=False, stop=True), reads=[pt, prev_va], writes=[ub], partial=True)
                _chk(8)
                prev_kT, prev_va = kT, va
                U = U_r.next()
                for bi, (a0, a1) in enumerate(((0, 480), (480, 960), (960, 1280))):
                    P.op("act", lambda e, bi=bi, a0=a0, a1=a1, U=U: e.copy(out=U[:, a0:a1].rearrange("p (h e) -> p h e", e=80)[:, :, 0:65], in_=PSB[UB[bi]][:, 0:a1 - a0].rearrange("p (h e) -> p h e", e=80)[:, :, 0:65]), reads=[PSB[UB[bi]]], writes=[U], partial=(bi > 0))
                if g != 0:
                    dst = bass.AP(ug_scr[g - 1].h, (n * 128 * d + r) * 1280, [[d * 1280, 128], [1, 1280]])
                    P.dma("pool", dst, U[:, :], key=U, reads=[U], writes=[ug_scr[g - 1]], partial=True)
                else:
                    i = n
                    for gi in range(2):
                        P.dma("sp", Ua[:, :], ug_scr[gi].h[i * 128:(i + 1) * 128, :], key=Ua, reads=[ug_scr[gi]], writes=[Ua])
                        P.op("pool", lambda e, U=U: e.tensor_tensor(out=U[:, :], in0=U[:, :], in1=Ua[:, :], op=ALU.add), reads=[U, Ua], writes=[U])
                    finish_attn(U, 128, w_out, rden, prompt_src(i)[0], prompt_src(i)[1], yp.h[i * 128:(i + 1) * 128, :], [yp_blk[i]], last)
                _chk(9)
            if t + 1 < NT4:
                pre = norm_front(tile_blocks(t + 1), gmix, layer)
        return MARK, qS, kS, vS, relb, w_out, rden

    def finish_attn(U, n, w_out, rden, src_ap, src_units, dst_ap, dst_units, last):
        Uv = U[0:n, :].rearrange("p (h e) -> p h e", e=80)
        P.op("dve", lambda e: e.reciprocal(out=rden[0:n, :], in_=Uv[:, :, 64]), reads=[U], writes=[rden])
        on = on_ring.next()
        P.op("dve", lambda e, on=on: e.tensor_tensor(out=on[0:n, :].rearrange("p (h e) -> p h e", e=64), in0=Uv[:, :, 0:64], in1=rden[0:n, :].unsqueeze(2).broadcast_to([n, 16, 64]), op=ALU.mult), reads=[U, rden], writes=[on])
        onT = onT_ring.next()
        transpose_to(on, n, onT)
        out_proj_residual(onT, n, w_out, 8, src_ap, src_units, dst_ap, dst_units, last)

    def dil_sample(layer, li, g, ctx, Us_acc, first_group, last):
        MARK, qS, kS, vS, relb, w_out, rden = ctx
        _chk(4)
        W, d = GROUPS[g]
        P.barrier()
        P.off = MARK
        UB = (4, 5, 6)
        ohs = P.sb("s_ohs", [NB, 6 * 128], F32)
        P.dma("sp", ohs[:, :], c_ohs.h[:, :], key=ohs, writes=[ohs])
        sel = P.sb("s_sel", [16, 16, 128], F32)
        selT = P.sb("s_selT", [128, 16, 16], F32)
        P.op("dve", lambda e: e.tensor_copy(out=sel[:, :, :], in_=identf[0:16, 0:16].unsqueeze(2).broadcast_to([16, 16, 128])), reads=[identf], writes=[sel])
        P.op("pool", lambda e: e.memset(selT[:, :, :], 0.0), writes=[selT])
        for t in range(16):
            P.op("pool", lambda e, t=t: e.memset(selT[:, t, t:t + 1], 1.0), writes=[selT])
        nvar = 4 if g == 0 else 1
        BS = P.sb("s_BS", [128, 4, 16], F32)
        for v in range(nvar):
            vv = v if g == 0 else 3 + g
            bb = mmB_ring.next()
            P.op("pe", lambda e, bb=bb, vv=vv: e.matmul(out=PSB[bb][:, 0:16], lhsT=ohs[:, vv * 128:(vv + 1) * 128], rhs=relb[:, g * 16:(g + 1) * 16], start=True, stop=True), reads=[ohs, relb], writes=[PSB[bb]])
            P.op("act", lambda e, bb=bb, v=v: e.activation(out=BS[:, v, :], in_=PSB[bb][:, 0:16], func=AF.Exp), reads=[PSB[bb]], writes=[BS], partial=(v > 0))
        eb0 = P.sb("s_eb0", [16, 16], F32)
        P.dma("sp", eb0[:, :], bass.AP(rel_bias.h, g * 16, [[0, 16], [1, 16]]), key=eb0, writes=[eb0])
        P.op("act", lambda e: e.activation(out=eb0[:, :], in_=eb0[:, :], func=AF.Exp), reads=[eb0], writes=[eb0])
        _chk(5)
        Kt_r = Ring([P.sb("s_Kt%d" % i, [128, D], F32) for i in range(2)])
        Vt_r = Ring([P.sb("s_Vt%d" % i, [128, D], F32) for i in range(2)])
        prod = P.sb("s_prod", [128, D], F32)
        sc_r = Ring([P.sb("s_sc%d" % i, [128, 16], F32) for i in range(2)])
        pw_r = Ring([P.sb("s_pw%d" % i, [128, 16], F32) for i in range(2)])
        Wt_r = Ring([P.sb("s_Wt%d" % i, [128, 1280], F32) for i in range(2)])
        for t in Wt_r.items:
            P.op("pool", lambda e, t=t: e.memset(t[:, :], 0.0), writes=[t])
        Usg = P.sb("s_Usg", [16, 1280], F32)
        p16 = P.sb("s_p16", [16, 32], F32)
        tmp16 = P.sb("s_tmp16", [16, D], F32)
        rden = P.sb("s_rden", [128, 16], F32)
        cnt = 0
        for b in range(4):
            for s_ in range(4):
                tk = 4 * b + s_
                Kt = Kt_r.next()
                Vt = Vt_r.next()
                base = (li * 4 + b) * W
                for (dstt, cache, newo) in ((Kt, ck[g], kso[g]), (Vt, cv[g], vso[g])):
                    if g == 0:
                        P.dma("sp", dstt[s_:128, :], cache.h[base + s_:base + 128, :], key=dstt, writes=[dstt])
                        if s_ > 0:
                            P.dma("sp", dstt[0:s_, :], newo.h[li * 16 + 4 * b:li * 16 + 4 * b + s_, :], key=dstt, reads=[newo], writes=[dstt], partial=True)
                    else:
                        P.dma("sp", dstt[:, :], bass.AP(cache.h, (base + s_) * D, [[d * D, 128], [1, D]]), key=dstt, writes=[dstt])
                pair = mmA_ring.next()
                for half in range(2):
                    bq = PSB[pair[half]]
                    P.op("pe", lambda e, bq=bq, half=half, tk=tk: e.matmul(out=bq[:, :], lhsT=sel[:, tk, :], rhs=qS[0:16, half * 512:(half + 1) * 512], start=True, stop=True), reads=[sel, qS], writes=[bq])
                    P.op("dve", lambda e, bq=bq, half=half, Kt=Kt: e.tensor_tensor(out=prod[:, half * 512:(half + 1) * 512], in0=bq[:, :], in1=Kt[:, half * 512:(half + 1) * 512], op=ALU.mult), reads=[bq, Kt], writes=[prod], partial=(half > 0))
                sc = sc_r.next()
                pw = pw_r.next()
                P.op("dve", lambda e, sc=sc: e.tensor_reduce(out=sc[:, :], in_=prod[:, :].rearrange("p (h e) -> p h e", e=64), axis=AX.X, op=ALU.add), reads=[prod], writes=[sc])
                P.op("act", lambda e, sc=sc: e.activation(out=sc[:, :], in_=sc[:, :], func=AF.Exp, scale=0.125), reads=[sc], writes=[sc])
                var = s_ if g == 0 else 0
                P.op("dve", lambda e, sc=sc, pw=pw, var=var: e.tensor_tensor(out=pw[:, :], in0=sc[:, :], in1=BS[:, var, :], op=ALU.mult), reads=[sc, BS], writes=[pw])
                Wt = Wt_r.next()
                Wv = Wt[:, :].rearrange("p (h e) -> p h e", e=80)
                P.op("dve", lambda e, Wv=Wv, Vt=Vt, pw=pw: e.tensor_tensor(out=Wv[:, :, 0:64], in0=Vt[:, :].rearrange("p (h e) -> p h e", e=64), in1=pw[:, :].unsqueeze(2).broadcast_to([128, 16, 64]), op=ALU.mult), reads=[Vt, pw], writes=[Wt])
                P.op("pool", lambda e, Wv=Wv, pw=pw: e.tensor_copy(out=Wv[:, :, 64], in_=pw[:, :]), reads=[pw], writes=[Wt])
                for bi, (a0, a1) in enumerate(((0, 480), (480, 960), (960, 1280))):
                    P.op("pe", lambda e, bi=bi, a0=a0, a1=a1, Wt=Wt, tk=tk, cnt=cnt: e.matmul(out=PSB[UB[bi]][0:16, 0:a1 - a0], lhsT=selT[:, tk, :], rhs=Wt[:, a0:a1], start=(cnt == 0), stop=(cnt == 15)), reads=[selT, Wt], writes=[PSB[UB[bi]]], partial=True)
                cnt += 1
        for bi, (a0, a1) in enumerate(((0, 480), (480, 960), (960, 1280))):
            P.op("act", lambda e, bi=bi, a0=a0, a1=a1: e.copy(out=Usg[:, a0:a1], in_=PSB[UB[bi]][0:16, 0:a1 - a0]), reads=[PSB[UB[bi]]], writes=[Usg], partial=(bi > 0))
        P.op("dve", lambda e: e.tensor_tensor(out=tmp16[:, :], in0=qS[:, :], in1=kS[:, :], op=ALU.mult), reads=[qS, kS], writes=[tmp16])
        P.op("dve", lambda e: e.tensor_reduce(out=p16[:, 0:16], in_=tmp16[:, :].rearrange("p (h e) -> p h e", e=64), axis=AX.X, op=ALU.add), reads=[tmp16], writes=[p16])
        P.op("act", lambda e: e.activation(out=p16[:, 0:16], in_=p16[:, 0:16], func=AF.Exp, scale=0.125), reads=[p16], writes=[p16])
        P.op("dve", lambda e: e.tensor_tensor(out=p16[:, 16:32], in0=p16[:, 0:16], in1=eb0[:, :], op=ALU.mult), reads=[p16, eb0], writes=[p16])
        Ugv = Usg[:, :].rearrange("p (h e) -> p h e", e=80)
        P.op("dve", lambda e: e.tensor_tensor(out=tmp16[:, :].rearrange("p (h e) -> p h e", e=64), in0=vS[:, :].rearrange("p (h e) -> p h e", e=64), in1=p16[:, 16:32].unsqueeze(2).broadcast_to([16, 16, 64]), op=ALU.mult), reads=[vS, p16], writes=[tmp16])
        P.op("dve", lambda e: e.tensor_tensor(out=Ugv[:, :, 0:64], in0=Ugv[:, :, 0:64], in1=tmp16[:, :].rearrange("p (h e) -> p h e", e=64), op=ALU.add), reads=[Usg, tmp16], writes=[Usg])
        P.op("dve", lambda e: e.tensor_tensor(out=Ugv[:, :, 64], in0=Ugv[:, :, 64], in1=p16[:, 16:32], op=ALU.add), reads=[Usg, p16], writes=[Usg])
        if first_group:
            P.op("dve", lambda e: e.tensor_copy(out=Us_acc[:, :], in_=Usg[:, :]), reads=[Usg], writes=[Us_acc])
        else:
            P.op("dve", lambda e: e.tensor_tensor(out=Us_acc[:, :], in0=Us_acc[:, :], in1=Usg[:, :], op=ALU.add), reads=[Us_acc, Usg], writes=[Us_acc])
        if g == 0:
            sa, su = sample_src_all()
            finish_attn(Us_acc, 16, w_out, rden, sa, su, ys.h[0:16, :], [ys_blk], last)

    def dil_layer(layer, li, last):
        Us_acc = T("Us_acc", Us_acc_t.h)
        for gi, g in enumerate((2, 1, 0)):
            ctx = dil_group(layer, li, g, last)
            dil_sample(layer, li, g, ctx, Us_acc, gi == 0, last)
        RR.tp = Ring([0, 1])
        RR.mmA = Ring([(2, 3), (4, 5)])
        RR.mmB = Ring([6, 7])
        state["first"] = False

    Us_acc_t = P.sb("Us_acc", [16, 1280], F32)
    PERSIST = P.off
    dil_setup()
    try:
        for layer in range(depth):
            li = layer // 2
            if layer % 2 == 0:
                gla_layer(layer, li, (layer == depth - 1) and not do_ffn)
            else:
                dil_layer(layer, li, (layer == depth - 1) and not do_ffn)
            if do_ffn:
                ffn_layer(layer, layer == depth - 1)
    except _Stop:
        pass
    P.emit()
    return nc, P


_CACHE = {}


def kernel(x_prompt, x_sample, state_gla, cache_k_g0, cache_v_g0, cache_k_g1, cache_v_g1,
           cache_k_g2, cache_v_g2, state_ffn_conv, rel_bias, norm_mix, norm_ffn,
           gla_w_in, gla_w_gate2, gla_b_gate, gla_norm, gla_w_out,
           dil_w_in, dil_q_norm, dil_k_norm, dil_w_out,
           ffn_w_up, ffn_conv_w, ffn_conv_b, ffn_w_down):
    f = lambda a: np.ascontiguousarray(np.asarray(a, dtype=np.float32))
    if "nc" not in _CACHE:
        _CACHE["nc"] = build_program()[0]
    nc = _CACHE["nc"]
    ohp, valid, ohs = host_constants()
    cks = [f(cache_k_g0), f(cache_k_g1), f(cache_k_g2)]
    cvs = [f(cache_v_g0), f(cache_v_g1), f(cache_v_g2)]
    x_prompt = f(x_prompt); x_sample = f(x_sample); state_gla = f(state_gla); state_ffn_conv = f(state_ffn_conv)
    shared = {
        "rel_bias": f(rel_bias), "norm_mix": f(norm_mix), "norm_ffn": f(norm_ffn),
        "gla_w_in": f(gla_w_in).reshape(2 * D, GLA_IN), "gla_w_gate2": f(gla_w_gate2).reshape(32, 512),
        "gla_b_gate": f(gla_b_gate), "gla_norm": f(gla_norm), "gla_w_out": f(gla_w_out).reshape(2 * D, D),
        "dil_w_in": f(dil_w_in).reshape(2 * D, 9216), "dil_q_norm": f(dil_q_norm), "dil_k_norm": f(dil_k_norm),
        "dil_w_out": f(dil_w_out).reshape(2 * D, D), "ffn_w_up": f(ffn_w_up).reshape(4 * D, 2 * DFF),
        "ffn_conv_w": f(ffn_conv_w).reshape(12, 2 * DFF), "ffn_conv_b": f(ffn_conv_b),
        "ffn_w_down": f(ffn_w_down).reshape(4 * DFF, D),
        "c_ohp": ohp, "c_valid": valid, "c_ohs": ohs,
    }
    in_maps = []
    for c in range(8):
        m = dict(shared)
        m["xp"] = x_prompt[c % 4]
        m["xs"] = x_sample[4 * c:4 * c + 4].reshape(16, D)
        m["sg"] = np.ascontiguousarray(state_gla[:, 4 * c:4 * c + 4]).reshape(2 * 4 * 4 * 128, 256)
        for g in range(3):
            Wg = GROUPS[g][0]
            m["ck%d" % g] = np.ascontiguousarray(cks[g][:, 4 * c:4 * c + 4]).reshape(2 * 4 * Wg, D)
            m["cv%d" % g] = np.ascontiguousarray(cvs[g][:, 4 * c:4 * c + 4]).reshape(2 * 4 * Wg, D)
        m["sfc"] = np.ascontiguousarray(state_ffn_conv[:, 4 * c:4 * c + 4]).reshape(32, 2 * DFF)
        in_maps.append(m)
    res = run_bass_kernel_spmd(nc, in_maps, core_ids=list(range(8)))
    R = res.results
    B = 4
    y_prompt = np.stack([R[b]["yp"] for b in range(B)]).astype(np.float32)
    y_sample = np.concatenate([R[c]["ys"].reshape(4, 4, D) for c in range(8)], 0).astype(np.float32)
    sgp = np.stack([R[b]["sgp"].reshape(2, 4, 128, 256) for b in range(B)], 1).astype(np.float32)
    sgs = np.concatenate([R[c]["sgs"].reshape(2, 4, 4, 128, 256) for c in range(8)], 1).astype(np.float32)
    outs = [y_prompt, y_sample, sgp, sgs]
    keep = [128, 512, 2048]
    for g in range(3):
        kp = np.stack([R[b]["kp%d" % g].reshape(2, keep[g], 16, 64) for b in range(B)], 1).astype(np.float32)
        ks = np.concatenate([R[c]["ks%d" % g].reshape(2, 4, 4, 16, 64) for c in range(8)], 1).astype(np.float32)
        vp = np.stack([R[b]["vp%d" % g].reshape(2, keep[g], 16, 64) for b in range(B)], 1).astype(np.float32)
        vs = np.concatenate([R[c]["vs%d" % g].reshape(2, 4, 4, 16, 64) for c in range(8)], 1).astype(np.float32)
        outs += [kp, ks, vp, vs]
    fcp = np.stack([R[b]["fcp"].reshape(4, 2, 2 * DFF) for b in range(B)], 1).astype(np.float32)
    fcs = np.concatenate([R[c]["fcs"].reshape(4, 4, 2, 2 * DFF) for c in range(8)], 1).astype(np.float32)
    outs += [fcp, fcs]
    return tuple(outs)
```

```python
import contextlib
import math
import numpy as np
import concourse.bass as bass
import concourse.mybir as mybir
from concourse.bass_utils import run_bass_kernel_spmd

F32 = mybir.dt.float32
BF16 = mybir.dt.bfloat16
AF = mybir.ActivationFunctionType
ALU = mybir.AluOpType
AX = mybir.AxisListType

ENGS = ("pe", "act", "dve", "pool", "sp")

D = 1024
SEQ = 4096
NSEQ_S = 4
TS = 4
DEPTH = 4
GLA_IN = 3088
DFF = 2816
EPS = 1e-6
GROUPS = ((128, 1), (512, 4), (2048, 16))
NB = 32


class T:
    __slots__ = ("name", "h", "writers", "readers", "dsem", "dcount")

    def __init__(self, name, h):
        self.name = name
        self.h = h
        self.writers = []
        self.readers = []
        self.dsem = None
        self.dcount = 0

    def __getitem__(self, k):
        return self.h[k]


class Op:
    __slots__ = ("eng", "fn", "deps", "sig", "signal", "dma_key", "pos")

    def __init__(self, eng, fn):
        self.eng = eng
        self.fn = fn
        self.deps = []
        self.sig = False
        self.signal = None
        self.dma_key = None


class Prog:
    ARENA = 207 * 1024

    def __init__(self, nc):
        self.nc = nc
        self.es = contextlib.ExitStack()
        self.streams = {e: [] for e in ENGS}
        self.out_dmas = []
        self.arena = self.es.enter_context(nc.sbuf_tensor("arena", [128, self.ARENA // 4], F32))
        self.arena_bf = self.arena.bitcast(BF16)
        self.off = 0
        self.pending = {e: [] for e in ENGS}
        self.dma_since = []
        self.peak = 0

    def sb(self, name, shape, dtype):
        isz = 4 if dtype == F32 else 2
        n = 1
        for d in shape[1:]:
            n *= d
        nbytes = (n * isz + 63) // 64 * 64
        off = self.off
        self.off += nbytes
        self.peak = max(self.peak, self.off)
        assert self.off <= self.ARENA, ("SBUF arena overflow", name, self.off)
        base = self.arena if dtype == F32 else self.arena_bf
        a = base[0:shape[0], off // isz:off // isz + n]
        if len(shape) == 3:
            a = a.rearrange("p (a b) -> p a b", b=shape[2])
        return T(name, a)

    def ps(self, name, shape, dtype):
        h = self.es.enter_context(self.nc.psum_tensor(name, list(shape), dtype))
        return T(name, h)

    def dram(self, name, shape, dtype, kind="Internal"):
        h = self.nc.dram_tensor(name, list(shape), dtype, kind=kind)
        return T(name, h)

    def barrier(self):
        deps = []
        for e in ENGS:
            for o in reversed(self.streams[e]):
                if o.dma_key is None:
                    o.sig = True
                    deps.append(o)
                    break
        deps.extend(self.dma_since)
        self.dma_since = []
        for e in ENGS:
            self.pending[e].extend(deps)

    def _track(self, op, reads, writes, partial):
        deps = []
        for t in reads:
            deps.extend(t.writers)
        for t in writes:
            others = [r for r in t.readers if r is not op]
            if others:
                deps.extend(others)
                deps.extend(t.writers)
                t.writers = [op]
                t.readers = []
            elif partial:
                t.writers.append(op)
            else:
                deps.extend(t.writers)
                t.writers = [op]
        for t in reads:
            t.readers.append(op)
        seen = set()
        best = {}
        for d in deps:
            if d is op or id(d) in seen:
                continue
            seen.add(id(d))
            if d.eng == "pe" and op.eng == "pe" and d.dma_key is None and op.dma_key is None:
                continue
            if d.dma_key is None:
                if d.eng not in best or best[d.eng].pos < d.pos:
                    best[d.eng] = d
            else:
                op.deps.append(d)
        for d in best.values():
            op.deps.append(d)
            d.sig = True

    def _pend(self, o):
        if self.pending[o.eng]:
            have = set(id(d) for d in o.deps)
            for d in self.pending[o.eng]:
                if id(d) not in have and d is not o:
                    o.deps.append(d)
            self.pending[o.eng] = []

    def op(self, eng, fn, reads=(), writes=(), partial=False):
        o = Op(eng, fn)
        o.pos = len(self.streams[eng])
        self._track(o, list(reads), list(writes), partial)
        self._pend(o)
        self.streams[eng].append(o)
        return o

    def dma(self, eng, out_ap, in_ap, key, reads=(), writes=(), partial=False, final=False, **kw):
        def fn(e):
            return e.dma_start(out=out_ap, in_=in_ap, **kw)
        o = Op(eng, fn)
        o.pos = len(self.streams[eng])
        o.dma_key = key
        o.sig = True
        self._track(o, list(reads), list(writes), partial)
        self._pend(o)
        self.streams[eng].append(o)
        self.dma_since.append(o)
        if final:
            self.out_dmas.append(o)
        return o

    def emit(self):
        nc = self.nc
        es = self.es
        esem = {e: es.enter_context(nc.semaphore("sem_" + e)) for e in ENGS}
        ecount = {e: 0 for e in ENGS}
        nkeys = 0
        semtab = {}
        for e in ENGS:
            for o in self.streams[e]:
                if o.dma_key is not None:
                    kk = (o.dma_key.name, e)
                    if kk not in semtab:
                        semtab[kk] = [es.enter_context(nc.semaphore("dsem_%s_%s" % kk)), 0]
                        nkeys += 1
                    semtab[kk][1] += 16
                    o.signal = (semtab[kk][0], semtab[kk][1], 16)
                elif o.sig:
                    ecount[e] += 1
                    o.signal = (esem[e], ecount[e], 1)
        self.ecount = ecount
        self.nkeys = nkeys
        streams = self.streams
        finals = {}
        for o in self.out_dmas:
            sem, val, _ = o.signal
            if finals.get(id(sem), (None, 0))[1] < val:
                finals[id(sem)] = (sem, val)

        def run(e, h):
            waited = {}
            for o in streams[e]:
                need = {}
                for d in o.deps:
                    sem, val, _ = d.signal
                    if need.get(id(sem), (None, 0))[1] < val:
                        need[id(sem)] = (sem, val)
                for sem, val in need.values():
                    if waited.get(id(sem), 0) < val:
                        h.wait_ge(sem, val)
                        waited[id(sem)] = val
                ins = o.fn(h)
                if o.signal is not None:
                    ins.then_inc(o.signal[0], o.signal[2])
            if e == "sp":
                for sem, val in finals.values():
                    if waited.get(id(sem), 0) < val:
                        h.wait_ge(sem, val)

        with nc.Block() as block:
            @block.tensor
            def _(h):
                run("pe", h)

            @block.scalar
            def _(h):
                run("act", h)

            @block.vector
            def _(h):
                run("dve", h)

            @block.gpsimd
            def _(h):
                run("pool", h)

            @block.sync
            def _(h):
                run("sp", h)
        es.close()


import os as _os
_STOP = int(_os.environ.get("KDBG_STOP", "0"))


class _Stop(Exception):
    pass


def _chk(k):
    if _STOP == k:
        raise _Stop()


class Ring:
    def __init__(self, items):
        self.items = items
        self.i = 0

    def next(self):
        t = self.items[self.i % len(self.items)]
        self.i += 1
        return t


def _bucket(dist):
    max_exact = NB // 2
    if dist < max_exact:
        return dist
    df = np.float32(max(dist, 1))
    v = np.float32(np.log(df / np.float32(max_exact))) / np.float32(math.log(2048 / max_exact)) * np.float32(NB - max_exact)
    return min(max_exact + int(v), NB - 1)


def host_constants():
    ohp = np.zeros((NB, 3 * 384), np.float32)
    valid = np.zeros((1, 3 * 384), np.float32)
    for g, (W, d) in enumerate(GROUPS):
        for rel in range(129):
            n = rel + 127
            ohp[_bucket(rel * d), g * 384 + n] = 1.0
            valid[0, g * 384 + n] = 1.0
    ohs = np.zeros((NB, 6 * 128), np.float32)
    for s in range(4):
        for m in range(128):
            j = (s - m) if m < s else (128 + s - m)
            ohs[_bucket(j * 1), s * 128 + m] = 1.0
    for v, d in ((4, GROUPS[1][1]), (5, GROUPS[2][1])):
        for m in range(128):
            j = 128 - m
            ohs[_bucket(j * d), v * 128 + m] = 1.0
    return ohp, valid, ohs


def build_program(depth=DEPTH, do_ffn=True):
    NBLK = SEQ // 128
    NT4 = NBLK // 4
    nc = bass.Bass("TRN2", target_bir_lowering=False)
    P = Prog(nc)

    def din(name, shape):
        return P.dram(name, shape, F32, kind="ExternalInput")

    def dout(name, shape):
        return P.dram(name, shape, F32, kind="ExternalOutput")

    xp = din("xp", [SEQ, D])
    xs = din("xs", [16, D])
    sg_in = din("sg", [2 * 4 * 4 * 128, 256])
    ck = [din("ck%d" % g, [2 * 4 * GROUPS[g][0], D]) for g in range(3)]
    cv = [din("cv%d" % g, [2 * 4 * GROUPS[g][0], D]) for g in range(3)]
    sfc = din("sfc", [32, 2 * DFF])
    rel_bias = din("rel_bias", [NB, 48])
    norm_mix = din("norm_mix", [4, D])
    norm_ffn = din("norm_ffn", [4, D])
    gla_w_in = din("gla_w_in", [2 * D, GLA_IN])
    gla_w_g2 = din("gla_w_gate2", [32, 512])
    gla_b_g = din("gla_b_gate", [2, 512])
    gla_norm = din("gla_norm", [2, 256])
    gla_w_out = din("gla_w_out", [2 * D, D])
    dil_w_in = din("dil_w_in", [2 * D, 9216])
    dil_qn = din("dil_q_norm", [2, 64])
    dil_kn = din("dil_k_norm", [2, 64])
    dil_w_out = din("dil_w_out", [2 * D, D])
    ffn_w_up = din("ffn_w_up", [4 * D, 2 * DFF])
    ffn_cw = din("ffn_conv_w", [12, 2 * DFF])
    ffn_cb = din("ffn_conv_b", [4, 2 * DFF])
    ffn_w_down = din("ffn_w_down", [4 * DFF, D])
    c_ohp = din("c_ohp", [NB, 3 * 384])
    c_valid = din("c_valid", [1, 3 * 384])
    c_ohs = din("c_ohs", [NB, 6 * 128])

    yp = dout("yp", [SEQ, D])
    ys = dout("ys", [16, D])
    sgp = dout("sgp", [2 * 4 * 128, 256])
    sgs = dout("sgs", [2 * 4 * 4 * 128, 256])
    keep = [min(GROUPS[g][0], SEQ) for g in range(3)]
    kpo = [dout("kp%d" % g, [2 * keep[g], D]) for g in range(3)]
    vpo = [dout("vp%d" % g, [2 * keep[g], D]) for g in range(3)]
    kso = [dout("ks%d" % g, [2 * 16, D]) for g in range(3)]
    vso = [dout("vs%d" % g, [2 * 16, D]) for g in range(3)]
    fcp = dout("fcp", [8, 2 * DFF])
    fcs = dout("fcs", [32, 2 * DFF])

    ug_scr = [P.dram("ug%d" % g, [SEQ, 1280], F32) for g in (1, 2)]
    wsc = P.dram("wsc", [48, 384], F32)

    yp_blk = [T("ypb%d" % i, yp.h) for i in range(NBLK)]
    ys_blk = T("ysb", ys.h)
    xp_blk = [T("xpb%d" % i, xp.h) for i in range(NBLK)]
    xs_blk = T("xsb", xs.h)

    PSB = [P.ps("psb%d" % i, [128, 1024], BF16) if i < 2 else P.ps("psb%d" % i, [128, 512], F32) for i in range(8)]
    PSF = [PSB[i].h.bitcast(F32) if i < 2 else PSB[i].h for i in range(8)]

    class _RR:
        pass
    RR = _RR()
    RR.tp = Ring([0, 1])
    RR.mmA = Ring([(2, 3), (4, 5)])
    RR.mmB = Ring([6, 7])

    class _Dyn:
        def __init__(self, nm):
            self.nm = nm

        def next(self):
            return getattr(RR, self.nm).next()
    tp_ring = _Dyn("tp")
    mmA_ring = _Dyn("mmA")
    mmB_ring = _Dyn("mmB")

    def psA(pair):
        return PSB[pair[0]], PSB[pair[1]]

    identf = P.sb("identf", [128, 128], F32)
    ident = P.sb("ident", [128, 128], BF16)
    onesF = P.sb("onesF", [128, 128], F32)
    epsT = P.sb("epsT", [128, 1], F32)
    mask_s = P.sb("mask_s", [128, 128], F32)
    Jm = P.sb("Jm", [128, 128], BF16)
    Jf = P.sb("Jf", [128, 128], F32)
    gmix = P.sb("gmix", [128, 4, 8], F32)
    gffn = P.sb("gffn", [128, 4, 8], F32)

    P.op("pool", lambda e: e.memset(identf[:, :], 0.0), writes=[identf])
    P.op("pool", lambda e: e.affine_select(out=identf[:, :], in_=identf[:, :], pattern=[[-1, 128]], compare_op=ALU.not_equal, fill=1.0, base=0, channel_multiplier=1), reads=[identf], writes=[identf])
    P.op("dve", lambda e: e.tensor_copy(out=ident[:, :], in_=identf[:, :]), reads=[identf], writes=[ident])
    P.op("pool", lambda e: e.memset(Jf[:, :], 0.0), writes=[Jf])
    P.op("pool", lambda e: e.affine_select(out=Jf[:, :], in_=Jf[:, :], pattern=[[1, 128]], compare_op=ALU.not_equal, fill=1.0, base=-127, channel_multiplier=1), reads=[Jf], writes=[Jf])
    P.op("dve", lambda e: e.tensor_copy(out=Jm[:, :], in_=Jf[:, :]), reads=[Jf], writes=[Jm])
    P.op("dve", lambda e: e.memset(onesF[:, :], 1.0), writes=[onesF])
    P.op("dve", lambda e: e.memset(epsT[:, :], EPS), writes=[epsT])
    GSC = 128.0 ** -0.5
    P.op("pool", lambda e: e.memset(mask_s[:, :], GSC), writes=[mask_s])
    P.op("pool", lambda e: e.affine_select(out=mask_s[:, :], in_=mask_s[:, :], pattern=[[1, 128]], compare_op=ALU.is_ge, fill=0.0, base=0, channel_multiplier=-1), reads=[mask_s], writes=[mask_s])
    rowtmp = P.sb("rowtmp", [4, 512], F32)

    def load_featmajor(dst_view, dst_unit, src_h, row0, nrows, ncols, bank):
        nch = ncols // 128
        for c0 in range(0, ncols, 512):
            w = min(512, ncols - c0)
            P.dma("sp", rowtmp[0:nrows, 0:w], src_h[row0:row0 + nrows, c0:c0 + w], key=rowtmp, writes=[rowtmp])
            for cc in range(w // 128):
                ch = c0 // 128 + cc
                P.op("pe", lambda e, cc=cc, ch=ch: e.matmul(out=PSB[bank][:, ch * nrows:(ch + 1) * nrows], lhsT=rowtmp[0:nrows, cc * 128:(cc + 1) * 128], rhs=identf[0:nrows, 0:nrows], start=True, stop=True), reads=[rowtmp, identf], writes=[PSB[bank]], partial=True)
        P.op("act", lambda e: e.copy(out=dst_view, in_=PSB[bank][:, 0:nch * nrows].rearrange("p (c r) -> p c r", r=nrows)), reads=[PSB[bank]], writes=[dst_unit])

    load_featmajor(gmix[:, :, :].rearrange("p l c -> p c l"), gmix, norm_mix.h, 0, 4, D, 6)
    load_featmajor(gffn[:, :, :].rearrange("p l c -> p c l"), gffn, norm_ffn.h, 0, 4, D, 7)

    xt_ring = Ring([P.sb("xt%d" % i, [128, D], F32) for i in range(2)])
    xr_ring = Ring([P.sb("xr%d" % i, [128, D], F32) for i in range(2)])
    sq_scr = P.sb("sq_scr", [128, D], F32)
    xn_ring = Ring([P.sb("xn%d" % i, [128, D], BF16) for i in range(2)])
    st_ring = Ring([P.sb("st%d" % i, [128, 8], F32) for i in range(4)])
    HT = {"ring": None}
    on_ring = Ring([P.sb("on%d" % i, [128, D], BF16) for i in range(2)])
    onT_ring = Ring([P.sb("onT%d" % i, [128, 8, 128], BF16) for i in range(2)])
    PERSIST = P.off

    def phase_begin(n_hT, width):
        P.barrier()
        P.off = PERSIST
        HT["ring"] = Ring([P.sb("hT%d" % i, [128, 8, width], BF16) for i in range(n_hT)])

    def rstd_from_ss(st, col_in, col_out, npart, scale):
        w = col_out.stop - col_out.start
        P.op("act", lambda e: e.activation(out=st[0:npart, col_out], in_=st[0:npart, col_in], func=AF.Ln, scale=scale, bias=epsT[0:npart, 0:1]), reads=[st, epsT], writes=[st])
        P.op("act", lambda e: e.activation(out=st[0:npart, col_out], in_=st[0:npart, col_out], func=AF.Exp, scale=-0.5), reads=[st], writes=[st])

    def norm_front(blocks, gain, layer):
        hT = HT["ring"].next()
        col = 0
        for (src_ap, units, n) in blocks:
            xt = xt_ring.next()
            P.dma("sp", xt[0:n, :], src_ap, key=xt, reads=units, writes=[xt])
            st = st_ring.next()
            P.op("act", lambda e, xt=xt, st=st, n=n: e.activation(out=sq_scr[0:n, :], in_=xt[0:n, :], func=AF.Square, accum_out=st[0:n, 0:1]), reads=[xt], writes=[st])
            rstd_from_ss(st, slice(0, 1), slice(1, 2), n, 1.0 / D)
            xn = xn_ring.next()
            P.op("act", lambda e, xt=xt, st=st, xn=xn, n=n: e.activation(out=xn[0:n, :], in_=xt[0:n, :], func=AF.Copy, scale=st[0:n, 1:2]), reads=[xt, st], writes=[xn])
            tb = tp_ring.next()
            tpv = PSB[tb].h
            for c in range(8):
                P.op("pe", lambda e, c=c, xn=xn, n=n, tpv=tpv: e.transpose(out=tpv[:, c * 128:c * 128 + n], in_=xn[0:n, c * 128:(c + 1) * 128], identity=ident[0:n, 0:n]), reads=[xn, ident], writes=[PSB[tb]], partial=True)
            c0 = col
            P.op("dve", lambda e, tpv=tpv, hT=hT, n=n, c0=c0: e.tensor_tensor(
                out=hT[:, :, c0:c0 + n], in0=tpv[:, :].rearrange("p (c t) -> p c t", t=128)[:, :, 0:n],
                in1=gain[:, layer, :].unsqueeze(2).broadcast_to([128, 8, n]), op=ALU.mult),
                reads=[PSB[tb], gain], writes=[hT], partial=True)
            col += n
        return hT, col

    def load_w(dst, cols, src_h, row0, nrows, col0, ncols):
        kc = nrows // 128
        step = 512
        for k0 in range(0, kc, 8):
            k1 = min(kc, k0 + 8)
            for c in range(0, ncols, step):
                w = min(step, ncols - c)
                src = src_h[row0 + k0 * 128:row0 + k1 * 128, col0 + c:col0 + c + w].rearrange("(kc p) n -> p kc n", p=128)
                P.dma("pool", dst[:, k0:k1, cols.start + c:cols.start + c + w], src, key=dst, writes=[dst], partial=True)

    def transpose_to(on, npart, onT):
        tb = tp_ring.next()
        tpv = PSB[tb].h
        for c in range(8):
            P.op("pe", lambda e, c=c, tpv=tpv: e.transpose(out=tpv[:, c * 128:c * 128 + npart], in_=on[0:npart, c * 128:(c + 1) * 128], identity=ident[0:npart, 0:npart]), reads=[on, ident], writes=[PSB[tb]], partial=True)
        P.op("act", lambda e, tpv=tpv: e.copy(out=onT[:, :, 0:npart], in_=tpv[:, :].rearrange("p (c t) -> p c t", t=128)[:, :, 0:npart]), reads=[PSB[tb]], writes=[onT])

    def out_proj_residual(onT, npart, wout, nchunks, src_ap, src_units, dst_ap, dst_units, final, off=0):
        pair = mmA_ring.next()
        for half in range(2):
            b = PSB[pair[half]]
            for c in range(nchunks):
                P.op("pe", lambda e, c=c, b=b, half=half: e.matmul(out=b[0:npart, :], lhsT=onT[:, c, off:off + npart], rhs=wout[:, c, half * 512:(half + 1) * 512], start=(c == 0), stop=(c == nchunks - 1)), reads=[onT, wout], writes=[b], partial=True)
        xr = xr_ring.next()
        P.dma("sp", xr[0:npart, :], src_ap, key=xr, reads=src_units, writes=[xr])
        for half in range(2):
            b = PSB[pair[half]]
            P.op("dve", lambda e, b=b, half=half, xr=xr: e.tensor_tensor(out=xr[0:npart, half * 512:(half + 1) * 512], in0=b[0:npart, :], in1=xr[0:npart, half * 512:(half + 1) * 512], op=ALU.add), reads=[b, xr], writes=[xr])
        P.dma("pool", dst_ap, xr[0:npart, :], key=xr, reads=[xr], writes=dst_units, final=final)

    state = {"first": True}

    def prompt_src(i):
        if state["first"]:
            return xp.h[i * 128:(i + 1) * 128, :], [xp_blk[i]]
        return yp.h[i * 128:(i + 1) * 128, :], [yp_blk[i]]

    def sample_src(b):
        if state["first"]:
            return xs.h[b * 4:(b + 1) * 4, :], [xs_blk]
        return ys.h[b * 4:(b + 1) * 4, :], [ys_blk]

    def sample_src_all():
        if state["first"]:
            return xs.h[0:16, :], [xs_blk]
        return ys.h[0:16, :], [ys_blk]

    gl_w_in = None

    def gla_layer(layer, li, last):
        phase_begin(2, 512)
        w_in = P.sb("gw_in", [128, 8, GLA_IN], BF16)
        w_out = P.sb("gw_out", [128, 8, D], BF16)
        wg2 = P.sb("gwg2", [16, 512], BF16)
        negb = P.sb("gnegb", [128, 4], F32)
        gn = P.sb("ggn", [128, 256], F32)
        load_w(w_in, slice(0, GLA_IN), gla_w_in.h, li * D, D, 0, GLA_IN)
        load_w(w_out, slice(0, D), gla_w_out.h, li * D, D, 0, D)
        P.dma("pool", wg2[:, :], gla_w_g2.h[li * 16:(li + 1) * 16, :], key=wg2, writes=[wg2])
        load_featmajor(negb[:, :].unsqueeze(2), negb, gla_b_g.h, li, 1, 512, mmB_ring.next())
        P.op("dve", lambda e: e.tensor_scalar(out=negb[:, :], in0=negb[:, :], scalar1=-1.0, scalar2=None, op0=ALU.mult), reads=[negb], writes=[negb])
        P.dma("sp", gn[:, :], bass.AP(gla_norm.h, li * 256, [[0, 128], [1, 256]]), key=gn, writes=[gn])

        gzT_ring = Ring([P.sb("g_gz_%d" % i, [16, 512], BF16) for i in range(2)])
        lt = P.sb("g_l", [128, 512], F32)
        bp = P.sb("g_bp", [128, 512], F32)
        Et = Ring([P.sb("g_E_%d" % i, [128, 512], F32) for i in range(2)])
        Ei = Ring([P.sb("g_Ei_%d" % i, [128, 512], F32) for i in range(2)])
        El = Ring([P.sb("g_El_%d" % i, [128, 4, 4], F32) for i in range(2)])
        qt_ring = Ring([P.sb("g_qt_%d" % i, [128, 4, 512], BF16) for i in range(2)])
        kt_ring = Ring([P.sb("g_kt_%d" % i, [128, 4, 512], BF16) for i in range(2)])
        v_ring = Ring([P.sb("g_v_%d" % i, [128, D], BF16) for i in range(3)])
        gsr_ring = Ring([P.sb("g_sr_%d" % i, [128, D], F32) for i in range(3)])
        aT_ring = Ring([P.sb("g_aT_%d" % i, [128, 128], BF16) for i in range(3)])
        ktok_ring = Ring([P.sb("g_ktok_%d" % i, [128, 128], BF16) for i in range(3)])
        osb_ring = Ring([P.sb("g_o_%d" % i, [128, D], F32) for i in range(2)])
        oss_ring = Ring([P.sb("g_oss_%d" % i, [128, 8], F32) for i in range(2)])
        S = [P.sb("g_S_%d" % h, [128, 256], F32) for h in range(4)]
        Dd = [P.sb("g_D_%d" % h, [128, 256], F32) for h in range(4)]
        Dbf = [P.sb("g_Dbf_%d" % h, [128, 256], BF16) for h in range(4)]
        sq2 = P.sb("g_sq2", [128, 256], F32)

        def gla_tile(hT, ntok, L, s0_aps, sT_aps, res_blocks, is_sample_first):
            nch = ntok // L
            bb = mmB_ring.next()
            for kc in range(8):
                P.op("pe", lambda e, kc=kc, bb=bb: e.matmul(out=PSB[bb][0:16, 0:ntok], lhsT=w_in[:, kc, 3072:3088], rhs=hT[:, kc, 0:ntok], start=(kc == 0), stop=(kc == 7)), reads=[hT, w_in], writes=[PSB[bb]], partial=True)
            gz = gzT_ring.next()
            P.op("act", lambda e, bb=bb, gz=gz: e.copy(out=gz[:, 0:ntok], in_=PSB[bb][0:16, 0:ntok]), reads=[PSB[bb]], writes=[gz])
            qt = qt_ring.next()
            kt = kt_ring.next()
            E_l = El.next()
            Es, Eis = [], []
            for hh in range(4):
                bb = mmB_ring.next()
                P.op("pe", lambda e, hh=hh, bb=bb, gz=gz: e.matmul(out=PSB[bb][:, 0:ntok], lhsT=wg2[:, hh * 128:(hh + 1) * 128], rhs=gz[:, 0:ntok], start=True, stop=True), reads=[gz, wg2], writes=[PSB[bb]])
                P.op("act", lambda e, hh=hh, bb=bb: e.activation(out=lt[:, 0:ntok], in_=PSB[bb][:, 0:ntok], func=AF.Exp, scale=-1.0, bias=negb[:, hh:hh + 1]), reads=[PSB[bb], negb], writes=[lt])
                P.op("act", lambda e: e.activation(out=lt[:, 0:ntok], in_=lt[:, 0:ntok], func=AF.Ln, bias=onesF[:, 0:1]), reads=[lt, onesF], writes=[lt])
                for c in range(nch):
                    P.op("dve", lambda e, c=c: e.tensor_tensor_scan(out=bp[:, c * L:(c + 1) * L], data0=onesF[:, 0:L], data1=lt[:, c * L:(c + 1) * L], initial=0.0, op0=ALU.mult, op1=ALU.add), reads=[lt, onesF], writes=[bp], partial=(c > 0))
                E = Et.next()
                Einv = Ei.next()
                P.op("act", lambda e, E=E: e.activation(out=E[:, 0:ntok], in_=bp[:, 0:ntok], func=AF.Exp, scale=-1.0 / 16), reads=[bp], writes=[E])
                P.op("act", lambda e, Einv=Einv: e.activation(out=Einv[:, 0:ntok], in_=bp[:, 0:ntok], func=AF.Exp, scale=1.0 / 16), reads=[bp], writes=[Einv])
                P.op("dve", lambda e, E=E, hh=hh, E_l=E_l: e.tensor_copy(out=E_l[:, hh, 0:nch], in_=E[:, 0:ntok].rearrange("p (c l) -> p c l", l=L)[:, :, L - 1]), reads=[E], writes=[E_l], partial=(hh > 0))
                for (dst, coff, own, oth) in ((qt, 0, E, Einv), (kt, 512, Einv, E)):
                    bb2 = mmB_ring.next()
                    for kc in range(8):
                        P.op("pe", lambda e, kc=kc, bb2=bb2, coff=coff, hh=hh: e.matmul(out=PSB[bb2][:, 0:ntok], lhsT=w_in[:, kc, coff + hh * 128:coff + (hh + 1) * 128], rhs=hT[:, kc, 0:ntok], start=(kc == 0), stop=(kc == 7)), reads=[hT, w_in], writes=[PSB[bb2]], partial=True)
                    for c in range(nch):
                        le = (c + 1) * L - 1
                        P.op("dve", lambda e, c=c, le=le, bb2=bb2, dst=dst, own=own, oth=oth, hh=hh: e.scalar_tensor_tensor(
                            out=dst[:, hh, c * L:(c + 1) * L], in0=PSB[bb2][:, c * L:(c + 1) * L], scalar=oth[:, le:le + 1], in1=own[:, c * L:(c + 1) * L], op0=ALU.mult, op1=ALU.mult),
                            reads=[PSB[bb2], own, oth], writes=[dst], partial=True)
            for c in range(nch):
                cs = slice(c * L, (c + 1) * L)
                pairv = mmA_ring.next()
                for half in range(2):
                    b = PSB[pairv[half]]
                    for kc in range(8):
                        P.op("pe", lambda e, kc=kc, b=b, half=half, cs=cs: e.matmul(out=b[0:L, :], lhsT=hT[:, kc, cs], rhs=w_in[:, kc, 1024 + half * 512:1024 + (half + 1) * 512], start=(kc == 0), stop=(kc == 7)), reads=[hT, w_in], writes=[b], partial=True)
                vsb = v_ring.next()
                for half in range(2):
                    b = PSB[pairv[half]]
                    P.op("act", lambda e, b=b, half=half, vsb=vsb: e.copy(out=vsb[0:L, half * 512:(half + 1) * 512], in_=b[0:L, :]), reads=[b], writes=[vsb], partial=(half > 0))
                pairr = mmA_ring.next()
                for half in range(2):
                    b = PSB[pairr[half]]
                    for kc in range(8):
                        P.op("pe", lambda e, kc=kc, b=b, half=half, cs=cs: e.matmul(out=b[0:L, :], lhsT=hT[:, kc, cs], rhs=w_in[:, kc, 2048 + half * 512:2048 + (half + 1) * 512], start=(kc == 0), stop=(kc == 7)), reads=[hT, w_in], writes=[b], partial=True)
                gsr = gsr_ring.next()
                for half in range(2):
                    b = PSB[pairr[half]]
                    P.op("act", lambda e, b=b, half=half, gsr=gsr: e.activation(out=gsr[0:L, half * 512:(half + 1) * 512], in_=b[0:L, :], func=AF.Silu), reads=[b], writes=[gsr], partial=(half > 0))
                P.op("pool", lambda e, gsr=gsr: e.tensor_tensor(out=gsr[0:L, :].rearrange("p (h e) -> p h e", e=256), in0=gsr[0:L, :].rearrange("p (h e) -> p h e", e=256), in1=gn[0:L, :].unsqueeze(1).broadcast_to([L, 4, 256]), op=ALU.mult), reads=[gsr, gn], writes=[gsr])
                osb = osb_ring.next()
                oss = oss_ring.next()
                for hh in range(4):
                    if s0_aps is not None or (c == 0 and is_sample_first):
                        pass
                    if s0_aps is not None:
                        P.dma("sp", S[hh][:, :], s0_aps[c][hh], key=S[hh], writes=[S[hh]])
                    if s0_aps is None and state["gla_zero"] and c == 0:
                        P.op("pool", lambda e, hh=hh: e.memset(S[hh][:, :], 0.0), writes=[S[hh]])
                    P.op("dve", lambda e, hh=hh, c=c, E_l=E_l: e.tensor_scalar(out=Dd[hh][:, :], in0=S[hh][:, :], scalar1=E_l[:, hh, c:c + 1], scalar2=None, op0=ALU.mult), reads=[S[hh], E_l], writes=[Dd[hh]])
                    P.op("act", lambda e, hh=hh: e.activation(out=Dbf[hh][:, :], in_=Dd[hh][:, :], func=AF.Copy, scale=GSC), reads=[Dd[hh]], writes=[Dbf[hh]])
                    ba = mmB_ring.next()
                    P.op("pe", lambda e, ba=ba, hh=hh, cs=cs: e.matmul(out=PSB[ba][0:L, 0:L], lhsT=kt[:, hh, cs], rhs=qt[:, hh, cs], start=True, stop=True), reads=[kt, qt], writes=[PSB[ba]])
                    aT = aT_ring.next()
                    P.op("dve", lambda e, ba=ba, aT=aT: e.tensor_tensor(out=aT[0:L, 0:L], in0=PSB[ba][0:L, 0:L], in1=mask_s[0:L, 0:L], op=ALU.mult), reads=[PSB[ba], mask_s], writes=[aT])
                    tb = tp_ring.next()
                    tpv = PSB[tb].h
                    P.op("pe", lambda e, tpv=tpv, hh=hh, cs=cs, tb=tb: e.transpose(out=tpv[0:L, 0:128], in_=kt[:, hh, cs], identity=ident[:, :]), reads=[kt, ident], writes=[PSB[tb]])
                    ktok = ktok_ring.next()
                    P.op("act", lambda e, tpv=tpv, ktok=ktok: e.copy(out=ktok[0:L, :], in_=tpv[0:L, 0:128]), reads=[PSB[tb]], writes=[ktok])
                    P.op("pe", lambda e, ba=ba, aT=aT, vsb=vsb, hh=hh: e.matmul(out=PSB[ba][0:L, 128:384], lhsT=aT[0:L, 0:L], rhs=vsb[0:L, hh * 256:(hh + 1) * 256], start=True, stop=False), reads=[aT, vsb], writes=[PSB[ba]])
                    P.op("pe", lambda e, ba=ba, hh=hh, cs=cs: e.matmul(out=PSB[ba][0:L, 128:384], lhsT=qt[:, hh, cs], rhs=Dbf[hh][:, :], start=False, stop=True), reads=[qt, Dbf[hh]], writes=[PSB[ba]], partial=True)
                    P.op("act", lambda e, ba=ba, osb=osb, hh=hh: e.copy(out=osb[0:L, hh * 256:(hh + 1) * 256], in_=PSB[ba][0:L, 128:384]), reads=[PSB[ba]], writes=[osb], partial=(hh > 0))
                    P.op("act", lambda e, ba=ba, oss=oss, hh=hh: e.activation(out=sq2[0:L, :], in_=PSB[ba][0:L, 128:384], func=AF.Square, accum_out=oss[0:L, hh:hh + 1]), reads=[PSB[ba]], writes=[oss], partial=(hh > 0))
                    bk = mmB_ring.next()
                    P.op("pe", lambda e, bk=bk, ktok=ktok, vsb=vsb, hh=hh: e.matmul(out=PSB[bk][:, 0:256], lhsT=ktok[0:L, :], rhs=vsb[0:L, hh * 256:(hh + 1) * 256], start=True, stop=True), reads=[ktok, vsb], writes=[PSB[bk]])
                    P.op("dve", lambda e, bk=bk, hh=hh: e.tensor_tensor(out=S[hh][:, :], in0=PSB[bk][:, 0:256], in1=Dd[hh][:, :], op=ALU.add), reads=[PSB[bk], Dd[hh]], writes=[S[hh]])
                    if sT_aps is not None and sT_aps[c] is not None:
                        P.dma("pool", sT_aps[c][hh][0], S[hh][:, :], key=S[hh], reads=[S[hh]], writes=[sT_aps[c][hh][1]], partial=True, final=True)
                state["gla_zero"] = False
                rstd_from_ss(oss, slice(0, 4), slice(4, 8), L, 1.0 / 256)
                on = on_ring.next()
                for hh in range(4):
                    P.op("dve", lambda e, hh=hh, on=on, osb=osb, oss=oss, gsr=gsr: e.scalar_tensor_tensor(out=on[0:L, hh * 256:(hh + 1) * 256], in0=osb[0:L, hh * 256:(hh + 1) * 256], scalar=oss[0:L, 4 + hh:5 + hh], in1=gsr[0:L, hh * 256:(hh + 1) * 256], op0=ALU.mult, op1=ALU.mult), reads=[osb, oss, gsr], writes=[on], partial=(hh > 0))
                onT = onT_ring.next()
                transpose_to(on, L, onT)
                src_ap, src_units, dst_ap, dst_units, fin = res_blocks[c]
                out_proj_residual(onT, L, w_out, 8, src_ap, src_units, dst_ap, dst_units, fin)

        def blocks_for_tile(t):
            return [(prompt_src(4 * t + j)[0], prompt_src(4 * t + j)[1], 128) for j in range(4)]

        state["gla_zero"] = True
        pre = norm_front(blocks_for_tile(0), gmix, layer)
        for t in range(NT4):
            cur = pre
            if t + 1 < NT4:
                pre = norm_front(blocks_for_tile(t + 1), gmix, layer)
            else:
                pre = norm_front([(sample_src(0)[0], sample_src(0)[1], 4)], gmix, layer)
            res = []
            for j in range(4):
                i = 4 * t + j
                sa, su = prompt_src(i)
                res.append((sa, su, yp.h[i * 128:(i + 1) * 128, :], [yp_blk[i]], last))
            sT = None
            if t == NT4 - 1:
                sT = [None, None, None, [(sgp.h[(li * 4 + hh) * 128:(li * 4 + hh + 1) * 128, :], sgp) for hh in range(4)]]
            gla_tile(cur[0], 512, 128, None, sT, res, False)
        for b in range(4):
            cur = pre
            if b + 1 < 4:
                pre = norm_front([(sample_src(b + 1)[0], sample_src(b + 1)[1], 4)], gmix, layer)
            sa, su = sample_src(b)
            res = [(sa, su, ys.h[b * 4:(b + 1) * 4, :], [ys_blk], last)]
            s0 = [[sg_in.h[((li * 4 + b) * 4 + hh) * 128:((li * 4 + b) * 4 + hh + 1) * 128, :] for hh in range(4)]]
            sT = [[(sgs.h[((li * 4 + b) * 4 + hh) * 128:((li * 4 + b) * 4 + hh + 1) * 128, :], sgs) for hh in range(4)]]
            gla_tile(cur[0], 4, 4, s0, sT, res, True)
        state["first"] = False

    def ffn_layer(layer, last):
        phase_begin(2, 256)
        RR.mmA = Ring([(2, 3)])
        RR.mmB = Ring([4, 5, 6, 7])
        w_up = P.sb("f_wup", [128, 8, 2 * DFF], BF16)
        w_dn = P.sb("f_wdn", [128, 22, D], BF16)
        cw = P.sb("f_cw", [128, 3, 44], F32)
        cb = P.sb("f_cb", [128, 44], F32)
        load_w(w_up, slice(0, 2 * DFF), ffn_w_up.h, layer * D, D, 0, 2 * DFF)
        load_w(w_dn, slice(0, D), ffn_w_down.h, layer * DFF, DFF, 0, D)
        load_featmajor(cw[:, :, :].rearrange("p i c -> p c i"), cw, ffn_cw.h, layer * 3, 3, 2 * DFF, mmB_ring.next())
        load_featmajor(cb[:, :].unsqueeze(2), cb, ffn_cb.h, layer, 1, 2 * DFF, mmB_ring.next())
        hist = P.sb("f_hist", [128, 44, 2], F32)
        cs_ring = Ring([P.sb("f_c_%d" % i, [128, 256], F32) for i in range(4)])
        sg_ring = Ring([P.sb("f_sg_%d" % i, [128, 256], F32) for i in range(2)])
        act = P.sb("f_act", [128, 22, 256], BF16)
        ul_ring = Ring([P.sb("f_ul_%d" % i, [2, 512], F32) for i in range(2)])
        hrow = P.sb("f_hrow", [2, 512], F32)

        def ffn_tile(hT, N, res_blocks, state_out_ap, state_out_unit):
            for cp in range(22):
                cts = []
                for ch in (cp, 22 + cp):
                    bb = mmB_ring.next()
                    for kc in range(8):
                        P.op("pe", lambda e, kc=kc, bb=bb, ch=ch: e.matmul(out=PSB[bb][:, 0:N], lhsT=w_up[:, kc, ch * 128:(ch + 1) * 128], rhs=hT[:, kc, 0:N], start=(kc == 0), stop=(kc == 7)), reads=[hT, w_up], writes=[PSB[bb]], partial=True)
                    ct = cs_ring.next()
                    P.op("act", lambda e, bb=bb, ct=ct, ch=ch: e.activation(out=ct[:, 0:N], in_=PSB[bb][:, 0:N], func=AF.Identity, scale=cw[:, 2, ch:ch + 1], bias=cb[:, ch:ch + 1]), reads=[PSB[bb], cw, cb], writes=[ct])
                    P.op("dve", lambda e, bb=bb, ct=ct, ch=ch: e.scalar_tensor_tensor(out=ct[:, 1:N], in0=PSB[bb][:, 0:N - 1], scalar=cw[:, 1, ch:ch + 1], in1=ct[:, 1:N], op0=ALU.mult, op1=ALU.add), reads=[PSB[bb], cw, ct], writes=[ct])
                    P.op("dve", lambda e, bb=bb, ct=ct, ch=ch: e.scalar_tensor_tensor(out=ct[:, 2:N], in0=PSB[bb][:, 0:N - 2], scalar=cw[:, 0, ch:ch + 1], in1=ct[:, 2:N], op0=ALU.mult, op1=ALU.add), reads=[PSB[bb], cw, ct], writes=[ct])
                    P.op("dve", lambda e, ct=ct, ch=ch: e.scalar_tensor_tensor(out=ct[:, 0:2], in0=hist[:, ch, 0:2], scalar=cw[:, 0, ch:ch + 1], in1=ct[:, 0:2], op0=ALU.mult, op1=ALU.add), reads=[hist, cw, ct], writes=[ct])
                    P.op("dve", lambda e, ct=ct, ch=ch: e.scalar_tensor_tensor(out=ct[:, 0:1], in0=hist[:, ch, 1:2], scalar=cw[:, 1, ch:ch + 1], in1=ct[:, 0:1], op0=ALU.mult, op1=ALU.add), reads=[hist, cw, ct], writes=[ct])
                    P.op("act", lambda e, bb=bb, ch=ch: e.copy(out=hist[:, ch, 0:2], in_=PSB[bb][:, N - 2:N]), reads=[PSB[bb]], writes=[hist])
                    cts.append(ct)
                sgt = sg_ring.next()
                P.op("act", lambda e, sgt=sgt, c0=cts[0]: e.activation(out=sgt[:, 0:N], in_=c0[:, 0:N], func=AF.Silu), reads=[cts[0]], writes=[sgt])
                P.op("dve", lambda e, sgt=sgt, c1=cts[1], cp=cp: e.tensor_tensor(out=act[:, cp, 0:N], in0=sgt[:, 0:N], in1=c1[:, 0:N], op=ALU.mult), reads=[sgt, cts[1]], writes=[act], partial=(cp > 0))
            if state_out_ap is not None:
                for blk in range(11):
                    bb = mmB_ring.next()
                    for kc in range(8):
                        P.op("pe", lambda e, kc=kc, bb=bb, blk=blk: e.matmul(out=PSB[bb][0:2, :], lhsT=hT[:, kc, N - 2:N], rhs=w_up[:, kc, blk * 512:(blk + 1) * 512], start=(kc == 0), stop=(kc == 7)), reads=[hT, w_up], writes=[PSB[bb]], partial=True)
                    ul = ul_ring.next()
                    P.op("act", lambda e, bb=bb, ul=ul: e.copy(out=ul[0:2, :], in_=PSB[bb][0:2, :]), reads=[PSB[bb]], writes=[ul])
                    P.dma("pool", state_out_ap[:, blk * 512:(blk + 1) * 512], ul[0:2, :], key=ul, reads=[ul], writes=[state_out_unit], partial=True, final=True)
            nb = len(res_blocks)
            for j in range(nb):
                src_ap, src_units, dst_ap, dst_units, fin, n = res_blocks[j]
                out_proj_residual(act, n, w_dn, 22, src_ap, src_units, dst_ap, dst_units, fin, off=j * 128)

        def blocks_for_tile(t):
            return [(prompt_src(2 * t + j)[0], prompt_src(2 * t + j)[1], 128) for j in range(2)]

        P.op("pool", lambda e: e.memset(hist[:, :, :], 0.0), writes=[hist])
        pre = norm_front(blocks_for_tile(0), gffn, layer)
        for t in range(NBLK // 2):
            cur = pre
            if t + 1 < NBLK // 2:
                pre = norm_front(blocks_for_tile(t + 1), gffn, layer)
            else:
                pre = norm_front([(sample_src(0)[0], sample_src(0)[1], 4)], gffn, layer)
            res = []
            for j in range(2):
                i = 2 * t + j
                sa, su = prompt_src(i)
                res.append((sa, su, yp.h[i * 128:(i + 1) * 128, :], [yp_blk[i]], last, 128))
            ffn_tile(cur[0], 256, res, fcp.h[layer * 2:(layer + 1) * 2, :] if t == NBLK // 2 - 1 else None, fcp)
        for b in range(4):
            cur = pre
            if b + 1 < 4:
                pre = norm_front([(sample_src(b + 1)[0], sample_src(b + 1)[1], 4)], gffn, layer)
            r0 = (layer * 4 + b) * 2
            bb = mmB_ring.next()
            for q11 in range(11):
                P.dma("sp", hrow[0:2, :], sfc.h[r0:r0 + 2, q11 * 512:(q11 + 1) * 512], key=hrow, writes=[hrow])
                for cc in range(4):
                    ch = q11 * 4 + cc
                    P.op("pe", lambda e, bb=bb, cc=cc, ch=ch: e.matmul(out=PSB[bb][:, ch * 2:ch * 2 + 2], lhsT=hrow[0:2, cc * 128:(cc + 1) * 128], rhs=identf[0:2, 0:2], start=True, stop=True), reads=[hrow, identf], writes=[PSB[bb]], partial=True)
            P.op("act", lambda e, bb=bb: e.copy(out=hist[:, :, :], in_=PSB[bb][:, 0:88].rearrange("p (c t) -> p c t", t=2)), reads=[PSB[bb]], writes=[hist])
            sa, su = sample_src(b)
            res = [(sa, su, ys.h[b * 4:(b + 1) * 4, :], [ys_blk], last, 4)]
            r1 = (layer * 4 + b) * 2
            ffn_tile(cur[0], 4, res, fcs.h[r1:r1 + 2, :], fcs)
        RR.mmA = Ring([(2, 3), (4, 5)])
        RR.mmB = Ring([6, 7])
        state["first"] = False


    def dil_setup():
        phase_begin(1, 16)
        relb = P.sb("d_relb", [NB, 48], F32)
        ohp = P.sb("d_ohp", [NB, 3 * 384], F32)
        vld = P.sb("d_vld", [16, 3 * 384], F32)
        P.dma("sp", relb[:, :], rel_bias.h[:, :], key=relb, writes=[relb])
        P.dma("sp", ohp[:, :], c_ohp.h[:, :], key=ohp, writes=[ohp])
        P.dma("sp", vld[:, :], bass.AP(c_valid.h, 0, [[0, 16], [1, 3 * 384]]), key=vld, writes=[vld])
        for g in range(3):
            bb = mmB_ring.next()
            P.op("pe", lambda e, bb=bb, g=g: e.matmul(out=PSB[bb][0:16, 0:384], lhsT=relb[:, g * 16:(g + 1) * 16], rhs=ohp[:, g * 384:(g + 1) * 384], start=True, stop=True), reads=[relb, ohp], writes=[PSB[bb]])
            wv = P.sb("d_wv%d" % g, [16, 384], F32)
            wvb = P.sb("d_wvb%d" % g, [16, 384], F32)
            P.op("act", lambda e, bb=bb, wv=wv: e.activation(out=wv[:, :], in_=PSB[bb][0:16, 0:384], func=AF.Exp), reads=[PSB[bb]], writes=[wv])
            P.op("dve", lambda e, wv=wv, wvb=wvb, g=g: e.tensor_tensor(out=wvb[:, :], in0=wv[:, :], in1=vld[:, g * 384:(g + 1) * 384], op=ALU.mult), reads=[wv, vld], writes=[wvb])
            P.dma("sp", wsc.h[g * 16:(g + 1) * 16, :], wvb[:, :], key=wvb, reads=[wvb], writes=[wsc], partial=True)

    def dil_group(layer, li, g, last):
        _chk(1)
        W, d = GROUPS[g]
        nbk = (SEQ // d) // 128
        RR.tp = Ring([0])
        RR.mmA = Ring([(2, 3)])
        RR.mmB = Ring([7])
        sc_ring = Ring([1, 7])
        UB = (4, 5, 6)
        phase_begin(1, 512)
        Wg = P.sb("d_Wg", [128, 8, 3072], BF16)
        load_w(Wg, slice(0, 3072), dil_w_in.h, li * D, D, g * 3072, 3072)
        w_out = None
        if g == 0:
            w_out = P.sb("d_wout", [128, 8, D], BF16)
            load_w(w_out, slice(0, D), dil_w_out.h, li * D, D, 0, D)
        qg = P.sb("d_qg", [128, 64], F32)
        kg = P.sb("d_kg", [128, 64], F32)
        P.dma("sp", qg[:, :], bass.AP(dil_qn.h, li * 64, [[0, 128], [1, 64]]), key=qg, writes=[qg])
        P.dma("sp", kg[:, :], bass.AP(dil_kn.h, li * 64, [[0, 128], [1, 64]]), key=kg, writes=[kg])
        gcol = P.sb("d_gcol", [128, 2], F32)
        for ci, srch in ((0, dil_qn), (1, dil_kn)):
            P.dma("sp", rowtmp[0:1, 0:64], srch.h[li:li + 1, :], key=rowtmp, writes=[rowtmp])
            P.dma("sp", rowtmp[0:1, 64:128], srch.h[li:li + 1, :], key=rowtmp, writes=[rowtmp], partial=True)
            bb = mmB_ring.next()
            P.op("pe", lambda e, bb=bb: e.matmul(out=PSB[bb][:, 0:1], lhsT=rowtmp[0:1, 0:128], rhs=identf[0:1, 0:1], start=True, stop=True), reads=[rowtmp, identf], writes=[PSB[bb]])
            P.op("act", lambda e, bb=bb, ci=ci: e.copy(out=gcol[:, ci:ci + 1], in_=PSB[bb][:, 0:1]), reads=[PSB[bb]], writes=[gcol], partial=(ci > 0))
        relb = P.sb("d_relb", [NB, 48], F32)
        P.dma("sp", relb[:, :], rel_bias.h[:, :], key=relb, writes=[relb])
        M = P.sb("d_M", [128, 16, 256], BF16)
        H_ring = Ring([P.sb("d_H%d" % i, [128, 256], F32) for i in range(2)])
        for h in range(16):
            H = H_ring.next()
            P.dma("sp", H[:, :], bass.AP(wsc.h, (g * 16 + h) * 384, [[1, 128], [1, 256]]), key=H, reads=[wsc], writes=[H])
            bb = mmB_ring.next()
            P.op("pe", lambda e, bb=bb, H=H: e.matmul(out=PSB[bb][:, 0:256], lhsT=Jf[:, :], rhs=H[:, :], start=True, stop=True), reads=[Jf, H], writes=[PSB[bb]])
            P.op("act", lambda e, bb=bb, h=h: e.copy(out=M[:, h, :], in_=PSB[bb][:, 0:256]), reads=[PSB[bb]], writes=[M], partial=(h > 0))
        _chk(2)
        qS = P.sb("d_qS", [16, D], F32)
        kS = P.sb("d_kS", [16, D], F32)
        vS = P.sb("d_vS", [16, D], F32)
        MARK = P.off
        nrm_r = Ring([P.sb("d_nrm%d" % i, [128, D], F32) for i in range(2)])
        st16 = Ring([P.sb("d_st%d" % i, [128, 32], F32) for i in range(2)])
        kout = P.sb("d_ko", [128, D], F32)
        vout = P.sb("d_vo", [128, D], F32)
        qbf_r = Ring([P.sb("d_qb%d" % i, [128, D], BF16) for i in range(2)])
        qT_r = Ring([P.sb("d_qT%d" % i, [128, 16, 128], BF16) for i in range(2)])
        for t in qT_r.items:
            P.op("pool", lambda e, t=t: e.memset(t[:, :, :], 0.0), writes=[t])
        kT_r = Ring([P.sb("d_kT%d" % i, [128, 8, 128], BF16) for i in range(3)])
        va_r = Ring([P.sb("d_va%d" % i, [128, 16, 80], BF16) for i in range(3)])
        pe_r = Ring([P.sb("d_pe%d" % i, [128, 512], BF16) for i in range(2)])
        pt_r = Ring([P.sb("d_pt%d" % i, [128, 512], BF16) for i in range(2)])
        U_r = Ring([P.sb("d_U%d" % i, [128, 1280], F32) for i in range(2)])
        Ua = P.sb("d_Ua", [128, 1280], F32)
        for t in U_r.items:
            P.op("pool", lambda e, t=t: e.memset(t[:, :], 0.0), writes=[t])
        rden = P.sb("d_rden", [128, 16], F32)
        for t in va_r.items:
            P.op("pool", lambda e, t=t: e.memset(t[:, :, :], 1.0), writes=[t])

        def qkv_block(hT, c0, n, cache_k_ap, cache_v_ap, cache_ku, cache_vu, sample):
            outs = []
            for s_ in range(3):
                pair = mmA_ring.next()
                for half in range(2):
                    b = PSB[pair[half]]
                    for kc in range(8):
                        P.op("pe", lambda e, kc=kc, b=b, half=half, s_=s_: e.matmul(out=b[0:n, :], lhsT=hT[:, kc, c0:c0 + n], rhs=Wg[:, kc, s_ * 1024 + half * 512:s_ * 1024 + (half + 1) * 512], start=(kc == 0), stop=(kc == 7)), reads=[hT, Wg], writes=[b], partial=True)
                if s_ == 2:
                    if sample:
                        for half in range(2):
                            b = PSB[pair[half]]
                            P.op("act", lambda e, b=b, half=half: e.copy(out=vS[0:n, half * 512:(half + 1) * 512], in_=b[0:n, :]), reads=[b], writes=[vS], partial=(half > 0))
                        P.dma("pool", cache_v_ap, vS[0:n, :], key=vS, reads=[vS], writes=[cache_vu], partial=True, final=True)
                        outs.append(vS)
                        continue
                    va = va_r.next()
                    for half in range(2):
                        b = PSB[pair[half]]
                        P.op("act", lambda e, b=b, half=half, va=va: e.copy(out=va[0:n, half * 8:(half + 1) * 8, 0:64], in_=b[0:n, :].rearrange("p (h e) -> p h e", e=64)), reads=[b], writes=[va], partial=(half > 0))
                    if cache_v_ap is not None:
                        for half in range(2):
                            b = PSB[pair[half]]
                            P.op("act", lambda e, b=b, half=half: e.copy(out=vout[0:n, half * 512:(half + 1) * 512], in_=b[0:n, :]), reads=[b], writes=[vout], partial=(half > 0))
                        P.dma("pool", cache_v_ap, vout[0:n, :], key=vout, reads=[vout], writes=[cache_vu], partial=True, final=True)
                    outs.append(va)
                    continue
                st = st16.next()
                nrm = nrm_r.next()
                for half in range(2):
                    b = PSB[pair[half]]
                    P.op("act", lambda e, b=b, half=half, nrm=nrm: e.copy(out=nrm[0:n, half * 512:(half + 1) * 512], in_=b[0:n, :]), reads=[b], writes=[nrm], partial=(half > 0))
                P.op("act", lambda e, nrm=nrm: e.activation(out=sq_scr[0:n, :], in_=nrm[0:n, :], func=AF.Square), reads=[nrm], writes=[sq_scr])
                P.op("dve", lambda e, st=st: e.tensor_reduce(out=st[0:n, 0:16], in_=sq_scr[0:n, :].rearrange("p (h e) -> p h e", e=64), axis=AX.X, op=ALU.add), reads=[sq_scr], writes=[st])
                rstd_from_ss(st, slice(0, 16), slice(16, 32), n, 1.0 / 64)
                gain = qg if s_ == 0 else kg
                need_f32 = sample or (s_ == 1 and cache_k_ap is not None)
                if not need_f32:
                    dst = qbf_r.next()
                    P.op("dve", lambda e, st=st, nrm=nrm, dst=dst: e.tensor_tensor(out=dst[0:n, :].rearrange("p (h e) -> p h e", e=64), in0=nrm[0:n, :].rearrange("p (h e) -> p h e", e=64), in1=st[0:n, 16:32].unsqueeze(2).broadcast_to([n, 16, 64]), op=ALU.mult), reads=[nrm, st], writes=[dst])
                    outs.append(dst)
                    continue
                P.op("dve", lambda e, st=st, nrm=nrm: e.tensor_tensor(out=nrm[0:n, :].rearrange("p (h e) -> p h e", e=64), in0=nrm[0:n, :].rearrange("p (h e) -> p h e", e=64), in1=st[0:n, 16:32].unsqueeze(2).broadcast_to([n, 16, 64]), op=ALU.mult), reads=[nrm, st], writes=[nrm])
                if sample:
                    full = qS if s_ == 0 else kS
                else:
                    full = kout
                    dst = qbf_r.next()
                    P.op("act", lambda e, nrm=nrm, dst=dst: e.copy(out=dst[0:n, :], in_=nrm[0:n, :]), reads=[nrm], writes=[dst])
                    outs.append(dst)
                P.op("pool", lambda e, full=full, gain=gain, nrm=nrm: e.tensor_tensor(out=full[0:n, :].rearrange("p (h e) -> p h e", e=64), in0=nrm[0:n, :].rearrange("p (h e) -> p h e", e=64), in1=gain[0:n, :].unsqueeze(1).broadcast_to([n, 16, 64]), op=ALU.mult), reads=[nrm, gain], writes=[full])
                if s_ == 1:
                    P.dma("pool", cache_k_ap, full[0:n, :], key=full, reads=[full], writes=[cache_ku], partial=True, final=True)
                if sample:
                    outs.append(full)
            return outs

        def to_featmajor(src, dstT, ci, split=False):
            tb = tp_ring.next()
            tpv = PSB[tb].h
            for c in range(8):
                P.op("pe", lambda e, c=c, tpv=tpv, src=src: e.transpose(out=tpv[:, c * 128:(c + 1) * 128], in_=src[:, c * 128:(c + 1) * 128], identity=ident[:, :]), reads=[src, ident], writes=[PSB[tb]], partial=True)
            if split:
                P.op("act", lambda e, tpv=tpv: e.activation(out=dstT[0:64, 0:8, :], in_=tpv[0:64, :].rearrange("p (c t) -> p c t", t=128), func=AF.Copy, scale=gcol[0:64, ci:ci + 1]), reads=[PSB[tb], gcol], writes=[dstT])
                P.op("act", lambda e, tpv=tpv: e.activation(out=dstT[64:128, 8:16, :], in_=tpv[64:128, :].rearrange("p (c t) -> p c t", t=128), func=AF.Copy, scale=gcol[64:128, ci:ci + 1]), reads=[PSB[tb], gcol], writes=[dstT], partial=True)
            else:
                P.op("act", lambda e, tpv=tpv: e.activation(out=dstT[:, :, :], in_=tpv[:, :].rearrange("p (c t) -> p c t", t=128), func=AF.Copy, scale=gcol[:, ci:ci + 1]), reads=[PSB[tb], gcol], writes=[dstT])

        hTs, _ = norm_front([(sample_src_all()[0], sample_src_all()[1], 16)], gmix, layer)
        qkv_block(hTs, 0, 16, kso[g].h[li * 16:(li + 1) * 16, :], vso[g].h[li * 16:(li + 1) * 16, :], kso[g], vso[g], True)

        _chk(3)
        blocks = [(r, n) for r in range(d) for n in range(nbk)]

        def blk_src(r, n):
            base = yp.h
            units = [yp_blk[i] for i in range(n * d, (n + 1) * d)]
            return bass.AP(base, (n * 128 * d + r) * D, [[d * D, 128], [1, D]]), units

        def tile_blocks(t):
            return [(blk_src(*blocks[4 * t + j])[0], blk_src(*blocks[4 * t + j])[1], 128) for j in range(4)]

        prev_kT = None
        prev_va = None
        hTcur = {"t": 0, "hT": norm_front(tile_blocks(0), gmix, layer)[0]}

        def front(bi):
            t_, j_ = bi // 4, bi % 4
            if t_ != hTcur["t"]:
                hTcur["t"] = t_
                hTcur["hT"] = norm_front(tile_blocks(t_), gmix, layer)[0]
            r_, n_ = blocks[bi]
            ck_ap = cv_ap = None
            if n_ == nbk - 1:
                ck_ap = bass.AP(kpo[g].h, (li * keep[g] + r_) * D, [[d * D, 128], [1, D]])
                cv_ap = bass.AP(vpo[g].h, (li * keep[g] + r_) * D, [[d * D, 128], [1, D]])
            qb, kb, va_ = qkv_block(hTcur["hT"], j_ * 128, 128, ck_ap, cv_ap, kpo[g], vpo[g], False)
            qT_ = qT_r.next()
            kT_ = kT_r.next()
            to_featmajor(qb, qT_, 0, split=True)
            to_featmajor(kb, kT_, 1)
            return qT_, kT_, va_

        nxt = front(0)
        for t in range(NT4):
            for j in range(4):
                bi = 4 * t + j
                r, n = blocks[bi]
                qT, kT, va = nxt
                if bi + 1 < len(blocks):
                    nxt = front(bi + 1)
                _chk(7)
                has_prev = n > 0
                for hp in range(8):
                    bb = sc_ring.next()
                    for hh in range(2):
                        ps_ = slice(hh * 64, (hh + 1) * 64)
                        P.op("pe", lambda e, bb=bb, hh=hh, ps_=ps_, hp=hp, kT=kT, qT=qT: e.matmul(out=PSF[bb][:, hh * 256:hh * 256 + 128], lhsT=kT[:, hp, :], rhs=qT[:, hh * 8 + hp, :], start=True, stop=True), reads=[kT, qT], writes=[PSB[bb]], partial=True)
                        if has_prev:
                            P.op("pe", lambda e, bb=bb, hh=hh, ps_=ps_, hp=hp, pk=prev_kT, qT=qT: e.matmul(out=PSF[bb][:, hh * 256 + 128:hh * 256 + 256], lhsT=pk[:, hp, :], rhs=qT[:, hh * 8 + hp, :], start=True, stop=True), reads=[prev_kT, qT], writes=[PSB[bb]], partial=True)
                    pe_t = pe_r.next()
                    pt = pt_r.next()
                    wdt = 256 if has_prev else 128
                    P.op("act", lambda e, bb=bb, pe_t=pe_t, wdt=wdt: e.activation(out=pe_t[:, :].rearrange("p (h x) -> p h x", x=256)[:, :, 0:wdt], in_=PSF[bb][:, :].rearrange("p (h x) -> p h x", x=256)[:, :, 0:wdt], func=AF.Exp, scale=0.125), reads=[PSB[bb]], writes=[pe_t])
                    P.op("dve", lambda e, pe_t=pe_t, pt=pt, hp=hp, wdt=wdt: e.tensor_tensor(out=pt[:, :].rearrange("p (h x) -> p h x", x=256)[:, :, 0:wdt], in0=pe_t[:, :].rearrange("p (h x) -> p h x", x=256)[:, :, 0:wdt], in1=M[:, 2 * hp:2 * hp + 2, 0:wdt], op=ALU.mult), reads=[pe_t, M], writes=[pt])
                    for hh in range(2):
                        h = 2 * hp + hh
                        ub = PSB[UB[h // 6]]
                        c0 = (h % 6) * 80
                        P.op("pe", lambda e, ub=ub, c0=c0, pt=pt, hh=hh, va=va, h=h, has_prev=has_prev: e.matmul(out=ub[:, c0:c0 + 65], lhsT=pt[:, hh * 256:hh * 256 + 128], rhs=va[:, h, 0:65], start=True, stop=(not has_prev)), reads=[pt, va], writes=[ub], partial=True)
                        if has_prev:
                            P.op("pe", lambda e, ub=ub, c0=c0, pt=pt, hh=hh, pv=prev_va, h=h: e.matmul(out=ub[:, c0:c0 + 65], lhsT=pt[:, hh * 256 + 128:hh * 256 + 256], rhs=pv[:, h, 0:65], start=False, stop=True), reads=[pt, prev_va], writes=[ub], partial=True)
                _chk(8)
                prev_kT, prev_va = kT, va
                U = U_r.next()
                for bi, (a0, a1) in enumerate(((0, 480), (480, 960), (960, 1280))):
                    P.op("act", lambda e, bi=bi, a0=a0, a1=a1, U=U: e.copy(out=U[:, a0:a1].rearrange("p (h e) -> p h e", e=80)[:, :, 0:65], in_=PSB[UB[bi]][:, 0:a1 - a0].rearrange("p (h e) -> p h e", e=80)[:, :, 0:65]), reads=[PSB[UB[bi]]], writes=[U], partial=(bi > 0))
                if g != 0:
                    dst = bass.AP(ug_scr[g - 1].h, (n * 128 * d + r) * 1280, [[d * 1280, 128], [1, 1280]])
                    P.dma("pool", dst, U[:, :], key=U, reads=[U], writes=[ug_scr[g - 1]], partial=True)
                else:
                    i = n
                    for gi in range(2):
                        P.dma("sp", Ua[:, :], ug_scr[gi].h[i * 128:(i + 1) * 128, :], key=Ua, reads=[ug_scr[gi]], writes=[Ua])
                        P.op("pool", lambda e, U=U: e.tensor_tensor(out=U[:, :], in0=U[:, :], in1=Ua[:, :], op=ALU.add), reads=[U, Ua], writes=[U])
                    finish_attn(U, 128, w_out, rden, prompt_src(i)[0], prompt_src(i)[1], yp.h[i * 128:(i + 1) * 128, :], [yp_blk[i]], last)
                _chk(9)
        return MARK, qS, kS, vS, relb, w_out, rden

    def finish_attn(U, n, w_out, rden, src_ap, src_units, dst_ap, dst_units, last):
        Uv = U[0:n, :].rearrange("p (h e) -> p h e", e=80)
        P.op("dve", lambda e: e.reciprocal(out=rden[0:n, :], in_=Uv[:, :, 64]), reads=[U], writes=[rden])
        on = on_ring.next()
        P.op("dve", lambda e, on=on: e.tensor_tensor(out=on[0:n, :].rearrange("p (h e) -> p h e", e=64), in0=Uv[:, :, 0:64], in1=rden[0:n, :].unsqueeze(2).broadcast_to([n, 16, 64]), op=ALU.mult), reads=[U, rden], writes=[on])
        onT = onT_ring.next()
        transpose_to(on, n, onT)
        out_proj_residual(onT, n, w_out, 8, src_ap, src_units, dst_ap, dst_units, last)

    def dil_sample(layer, li, g, ctx, Us_acc, first_group, last):
        MARK, qS, kS, vS, relb, w_out, rden = ctx
        _chk(4)
        W, d = GROUPS[g]
        P.barrier()
        P.off = MARK
        UB = (4, 5, 6)
        ohs = P.sb("s_ohs", [NB, 6 * 128], F32)
        P.dma("sp", ohs[:, :], c_ohs.h[:, :], key=ohs, writes=[ohs])
        sel = P.sb("s_sel", [16, 16, 128], F32)
        selT = P.sb("s_selT", [128, 16, 16], F32)
        P.op("dve", lambda e: e.tensor_copy(out=sel[:, :, :], in_=identf[0:16, 0:16].unsqueeze(2).broadcast_to([16, 16, 128])), reads=[identf], writes=[sel])
        P.op("pool", lambda e: e.memset(selT[:, :, :], 0.0), writes=[selT])
        for t in range(16):
            P.op("pool", lambda e, t=t: e.memset(selT[:, t, t:t + 1], 1.0), writes=[selT])
        nvar = 4 if g == 0 else 1
        BS = P.sb("s_BS", [128, 4, 16], F32)
        for v in range(nvar):
            vv = v if g == 0 else 3 + g
            bb = mmB_ring.next()
            P.op("pe", lambda e, bb=bb, vv=vv: e.matmul(out=PSB[bb][:, 0:16], lhsT=ohs[:, vv * 128:(vv + 1) * 128], rhs=relb[:, g * 16:(g + 1) * 16], start=True, stop=True), reads=[ohs, relb], writes=[PSB[bb]])
            P.op("act", lambda e, bb=bb, v=v: e.activation(out=BS[:, v, :], in_=PSB[bb][:, 0:16], func=AF.Exp), reads=[PSB[bb]], writes=[BS], partial=(v > 0))
        eb0 = P.sb("s_eb0", [16, 16], F32)
        P.dma("sp", eb0[:, :], bass.AP(rel_bias.h, g * 16, [[0, 16], [1, 16]]), key=eb0, writes=[eb0])
        P.op("act", lambda e: e.activation(out=eb0[:, :], in_=eb0[:, :], func=AF.Exp), reads=[eb0], writes=[eb0])
        _chk(5)
        Kt_r = Ring([P.sb("s_Kt%d" % i, [128, D], F32) for i in range(2)])
        Vt_r = Ring([P.sb("s_Vt%d" % i, [128, D], F32) for i in range(2)])
        prod = P.sb("s_prod", [128, D], F32)
        sc_r = Ring([P.sb("s_sc%d" % i, [128, 16], F32) for i in range(2)])
        pw_r = Ring([P.sb("s_pw%d" % i, [128, 16], F32) for i in range(2)])
        Wt_r = Ring([P.sb("s_Wt%d" % i, [128, 1280], F32) for i in range(2)])
        for t in Wt_r.items:
            P.op("pool", lambda e, t=t: e.memset(t[:, :], 0.0), writes=[t])
        Usg = P.sb("s_Usg", [16, 1280], F32)
        p16 = P.sb("s_p16", [16, 32], F32)
        tmp16 = P.sb("s_tmp16", [16, D], F32)
        rden = P.sb("s_rden", [128, 16], F32)
        cnt = 0
        for b in range(4):
            for s_ in range(4):
                tk = 4 * b + s_
                Kt = Kt_r.next()
                Vt = Vt_r.next()
                base = (li * 4 + b) * W
                for (dstt, cache, newo) in ((Kt, ck[g], kso[g]), (Vt, cv[g], vso[g])):
                    if g == 0:
                        P.dma("sp", dstt[s_:128, :], cache.h[base + s_:base + 128, :], key=dstt, writes=[dstt])
                        if s_ > 0:
                            P.dma("sp", dstt[0:s_, :], newo.h[li * 16 + 4 * b:li * 16 + 4 * b + s_, :], key=dstt, reads=[newo], writes=[dstt], partial=True)
                    else:
                        P.dma("sp", dstt[:, :], bass.AP(cache.h, (base + s_) * D, [[d * D, 128], [1, D]]), key=dstt, writes=[dstt])
                pair = mmA_ring.next()
                for half in range(2):
                    bq = PSB[pair[half]]
                    P.op("pe", lambda e, bq=bq, half=half, tk=tk: e.matmul(out=bq[:, :], lhsT=sel[:, tk, :], rhs=qS[0:16, half * 512:(half + 1) * 512], start=True, stop=True), reads=[sel, qS], writes=[bq])
                    P.op("dve", lambda e, bq=bq, half=half, Kt=Kt: e.tensor_tensor(out=prod[:, half * 512:(half + 1) * 512], in0=bq[:, :], in1=Kt[:, half * 512:(half + 1) * 512], op=ALU.mult), reads=[bq, Kt], writes=[prod], partial=(half > 0))
                sc = sc_r.next()
                pw = pw_r.next()
                P.op("dve", lambda e, sc=sc: e.tensor_reduce(out=sc[:, :], in_=prod[:, :].rearrange("p (h e) -> p h e", e=64), axis=AX.X, op=ALU.add), reads=[prod], writes=[sc])
                P.op("act", lambda e, sc=sc: e.activation(out=sc[:, :], in_=sc[:, :], func=AF.Exp, scale=0.125), reads=[sc], writes=[sc])
                var = s_ if g == 0 else 0
                P.op("dve", lambda e, sc=sc, pw=pw, var=var: e.tensor_tensor(out=pw[:, :], in0=sc[:, :], in1=BS[:, var, :], op=ALU.mult), reads=[sc, BS], writes=[pw])
                Wt = Wt_r.next()
                Wv = Wt[:, :].rearrange("p (h e) -> p h e", e=80)
                P.op("dve", lambda e, Wv=Wv, Vt=Vt, pw=pw: e.tensor_tensor(out=Wv[:, :, 0:64], in0=Vt[:, :].rearrange("p (h e) -> p h e", e=64), in1=pw[:, :].unsqueeze(2).broadcast_to([128, 16, 64]), op=ALU.mult), reads=[Vt, pw], writes=[Wt])
                P.op("pool", lambda e, Wv=Wv, pw=pw: e.tensor_copy(out=Wv[:, :, 64], in_=pw[:, :]), reads=[pw], writes=[Wt])
                for bi, (a0, a1) in enumerate(((0, 480), (480, 960), (960, 1280))):
                    P.op("pe", lambda e, bi=bi, a0=a0, a1=a1, Wt=Wt, tk=tk, cnt=cnt: e.matmul(out=PSB[UB[bi]][0:16, 0:a1 - a0], lhsT=selT[:, tk, :], rhs=Wt[:, a0:a1], start=(cnt == 0), stop=(cnt == 15)), reads=[selT, Wt], writes=[PSB[UB[bi]]], partial=True)
                cnt += 1
        for bi, (a0, a1) in enumerate(((0, 480), (480, 960), (960, 1280))):
            P.op("act", lambda e, bi=bi, a0=a0, a1=a1: e.copy(out=Usg[:, a0:a1], in_=PSB[UB[bi]][0:16, 0:a1 - a0]), reads=[PSB[UB[bi]]], writes=[Usg], partial=(bi > 0))
        P.op("dve", lambda e: e.tensor_tensor(out=tmp16[:, :], in0=qS[:, :], in1=kS[:, :], op=ALU.mult), reads=[qS, kS], writes=[tmp16])
        P.op("dve", lambda e: e.tensor_reduce(out=p16[:, 0:16], in_=tmp16[:, :].rearrange("p (h e) -> p h e", e=64), axis=AX.X, op=ALU.add), reads=[tmp16], writes=[p16])
        P.op("act", lambda e: e.activation(out=p16[:, 0:16], in_=p16[:, 0:16], func=AF.Exp, scale=0.125), reads=[p16], writes=[p16])
        P.op("dve", lambda e: e.tensor_tensor(out=p16[:, 16:32], in0=p16[:, 0:16], in1=eb0[:, :], op=ALU.mult), reads=[p16, eb0], writes=[p16])
        Ugv = Usg[:, :].rearrange("p (h e) -> p h e", e=80)
        P.op("dve", lambda e: e.tensor_tensor(out=tmp16[:, :].rearrange("p (h e) -> p h e", e=64), in0=vS[:, :].rearrange("p (h e) -> p h e", e=64), in1=p16[:, 16:32].unsqueeze(2).broadcast_to([16, 16, 64]), op=ALU.mult), reads=[vS, p16], writes=[tmp16])
        P.op("dve", lambda e: e.tensor_tensor(out=Ugv[:, :, 0:64], in0=Ugv[:, :, 0:64], in1=tmp16[:, :].rearrange("p (h e) -> p h e", e=64), op=ALU.add), reads=[Usg, tmp16], writes=[Usg])
        P.op("dve", lambda e: e.tensor_tensor(out=Ugv[:, :, 64], in0=Ugv[:, :, 64], in1=p16[:, 16:32], op=ALU.add), reads=[Usg, p16], writes=[Usg])
        if first_group:
            P.op("dve", lambda e: e.tensor_copy(out=Us_acc[:, :], in_=Usg[:, :]), reads=[Usg], writes=[Us_acc])
        else:
            P.op("dve", lambda e: e.tensor_tensor(out=Us_acc[:, :], in0=Us_acc[:, :], in1=Usg[:, :], op=ALU.add), reads=[Us_acc, Usg], writes=[Us_acc])
        if g == 0:
            sa, su = sample_src_all()
            finish_attn(Us_acc, 16, w_out, rden, sa, su, ys.h[0:16, :], [ys_blk], last)

    def dil_layer(layer, li, last):
        Us_acc = T("Us_acc", Us_acc_t.h)
        for gi, g in enumerate((2, 1, 0)):
            ctx = dil_group(layer, li, g, last)
            dil_sample(layer, li, g, ctx, Us_acc, gi == 0, last)
        RR.tp = Ring([0, 1])
        RR.mmA = Ring([(2, 3), (4, 5)])
        RR.mmB = Ring([6, 7])
        state["first"] = False

    Us_acc_t = P.sb("Us_acc", [16, 1280], F32)
    PERSIST = P.off
    dil_setup()
    try:
        for layer in range(depth):
            li = layer // 2
            if layer % 2 == 0:
                gla_layer(layer, li, (layer == depth - 1) and not do_ffn)
            else:
                dil_layer(layer, li, (layer == depth - 1) and not do_ffn)
            if do_ffn:
                ffn_layer(layer, layer == depth - 1)
    except _Stop:
        pass
    P.emit()
    return nc, P


_CACHE = {}


def kernel(x_prompt, x_sample, state_gla, cache_k_g0, cache_v_g0, cache_k_g1, cache_v_g1,
           cache_k_g2, cache_v_g2, state_ffn_conv, rel_bias, norm_mix, norm_ffn,
           gla_w_in, gla_w_gate2, gla_b_gate, gla_norm, gla_w_out,
           dil_w_in, dil_q_norm, dil_k_norm, dil_w_out,
           ffn_w_up, ffn_conv_w, ffn_conv_b, ffn_w_down):
    f = lambda a: np.ascontiguousarray(np.asarray(a, dtype=np.float32))
    if "nc" not in _CACHE:
        _CACHE["nc"] = build_program()[0]
    nc = _CACHE["nc"]
    ohp, valid, ohs = host_constants()
    cks = [f(cache_k_g0), f(cache_k_g1), f(cache_k_g2)]
    cvs = [f(cache_v_g0), f(cache_v_g1), f(cache_v_g2)]
    x_prompt = f(x_prompt); x_sample = f(x_sample); state_gla = f(state_gla); state_ffn_conv = f(state_ffn_conv)
    shared = {
        "rel_bias": f(rel_bias), "norm_mix": f(norm_mix), "norm_ffn": f(norm_ffn),
        "gla_w_in": f(gla_w_in).reshape(2 * D, GLA_IN), "gla_w_gate2": f(gla_w_gate2).reshape(32, 512),
        "gla_b_gate": f(gla_b_gate), "gla_norm": f(gla_norm), "gla_w_out": f(gla_w_out).reshape(2 * D, D),
        "dil_w_in": f(dil_w_in).reshape(2 * D, 9216), "dil_q_norm": f(dil_q_norm), "dil_k_norm": f(dil_k_norm),
        "dil_w_out": f(dil_w_out).reshape(2 * D, D), "ffn_w_up": f(ffn_w_up).reshape(4 * D, 2 * DFF),
        "ffn_conv_w": f(ffn_conv_w).reshape(12, 2 * DFF), "ffn_conv_b": f(ffn_conv_b),
        "ffn_w_down": f(ffn_w_down).reshape(4 * DFF, D),
        "c_ohp": ohp, "c_valid": valid, "c_ohs": ohs,
    }
    in_maps = []
    for c in range(8):
        m = dict(shared)
        m["xp"] = x_prompt[c % 4]
        m["xs"] = x_sample[4 * c:4 * c + 4].reshape(16, D)
        m["sg"] = np.ascontiguousarray(state_gla[:, 4 * c:4 * c + 4]).reshape(2 * 4 * 4 * 128, 256)
        for g in range(3):
            Wg = GROUPS[g][0]
            m["ck%d" % g] = np.ascontiguousarray(cks[g][:, 4 * c:4 * c + 4]).reshape(2 * 4 * Wg, D)
            m["cv%d" % g] = np.ascontiguousarray(cvs[g][:, 4 * c:4 * c + 4]).reshape(2 * 4 * Wg, D)
        m["sfc"] = np.ascontiguousarray(state_ffn_conv[:, 4 * c:4 * c + 4]).reshape(32, 2 * DFF)
        in_maps.append(m)
    res = run_bass_kernel_spmd(nc, in_maps, core_ids=list(range(8)))
    R = res.results
    B = 4
    y_prompt = np.stack([R[b]["yp"] for b in range(B)]).astype(np.float32)
    y_sample = np.concatenate([R[c]["ys"].reshape(4, 4, D) for c in range(8)], 0).astype(np.float32)
    sgp = np.stack([R[b]["sgp"].reshape(2, 4, 128, 256) for b in range(B)], 1).astype(np.float32)
    sgs = np.concatenate([R[c]["sgs"].reshape(2, 4, 4, 128, 256) for c in range(8)], 1).astype(np.float32)
    outs = [y_prompt, y_sample, sgp, sgs]
    keep = [128, 512, 2048]
    for g in range(3):
        kp = np.stack([R[b]["kp%d" % g].reshape(2, keep[g], 16, 64) for b in range(B)], 1).astype(np.float32)
        ks = np.concatenate([R[c]["ks%d" % g].reshape(2, 4, 4, 16, 64) for c in range(8)], 1).astype(np.float32)
        vp = np.stack([R[b]["vp%d" % g].reshape(2, keep[g], 16, 64) for b in range(B)], 1).astype(np.float32)
        vs = np.concatenate([R[c]["vs%d" % g].reshape(2, 4, 4, 16, 64) for c in range(8)], 1).astype(np.float32)
        outs += [kp, ks, vp, vs]
    fcp = np.stack([R[b]["fcp"].reshape(4, 2, 2 * DFF) for b in range(B)], 1).astype(np.float32)
    fcs = np.concatenate([R[c]["fcs"].reshape(4, 4, 2, 2 * DFF) for c in range(8)], 1).astype(np.float32)
    outs += [fcp, fcs]
    return tuple(outs)
```

```python
import contextlib
import math
import numpy as np
import concourse.bass as bass
import concourse.mybir as mybir
from concourse.bass_utils import run_bass_kernel_spmd

F32 = mybir.dt.float32
BF16 = mybir.dt.bfloat16
AF = mybir.ActivationFunctionType
ALU = mybir.AluOpType
AX = mybir.AxisListType

ENGS = ("pe", "act", "dve", "pool", "sp")

D = 1024
SEQ = 4096
NSEQ_S = 4
TS = 4
DEPTH = 4
GLA_IN = 3088
DFF = 2816
EPS = 1e-6
GROUPS = ((128, 1), (512, 4), (2048, 16))
NB = 32


class T:
    __slots__ = ("name", "h", "writers", "readers", "dsem", "dcount")

    def __init__(self, name, h):
        self.name = name
        self.h = h
        self.writers = []
        self.readers = []
        self.dsem = None
        self.dcount = 0

    def __getitem__(self, k):
        return self.h[k]


class Op:
    __slots__ = ("eng", "fn", "deps", "sig", "signal", "dma_key", "pos")

    def __init__(self, eng, fn):
        self.eng = eng
        self.fn = fn
        self.deps = []
        self.sig = False
        self.signal = None
        self.dma_key = None


class Prog:
    ARENA = 207 * 1024

    def __init__(self, nc):
        self.nc = nc
        self.es = contextlib.ExitStack()
        self.streams = {e: [] for e in ENGS}
        self.out_dmas = []
        self.arena = self.es.enter_context(nc.sbuf_tensor("arena", [128, self.ARENA // 4], F32))
        self.arena_bf = self.arena.bitcast(BF16)
        self.off = 0
        self.pending = {e: [] for e in ENGS}
        self.dma_since = []
        self.peak = 0

    def sb(self, name, shape, dtype):
        isz = 4 if dtype == F32 else 2
        n = 1
        for d in shape[1:]:
            n *= d
        nbytes = (n * isz + 63) // 64 * 64
        off = self.off
        self.off += nbytes
        self.peak = max(self.peak, self.off)
        assert self.off <= self.ARENA, ("SBUF arena overflow", name, self.off)
        base = self.arena if dtype == F32 else self.arena_bf
        a = base[0:shape[0], off // isz:off // isz + n]
        if len(shape) == 3:
            a = a.rearrange("p (a b) -> p a b", b=shape[2])
        return T(name, a)

    def ps(self, name, shape, dtype):
        h = self.es.enter_context(self.nc.psum_tensor(name, list(shape), dtype))
        return T(name, h)

    def dram(self, name, shape, dtype, kind="Internal"):
        h = self.nc.dram_tensor(name, list(shape), dtype, kind=kind)
        return T(name, h)

    def barrier(self):
        deps = []
        for e in ENGS:
            for o in reversed(self.streams[e]):
                if o.dma_key is None:
                    o.sig = True
                    deps.append(o)
                    break
        deps.extend(self.dma_since)
        self.dma_since = []
        for e in ENGS:
            self.pending[e].extend(deps)

    def _track(self, op, reads, writes, partial):
        deps = []
        for t in reads:
            deps.extend(t.writers)
        for t in writes:
            others = [r for r in t.readers if r is not op]
            if others:
                deps.extend(others)
                deps.extend(t.writers)
                t.writers = [op]
                t.readers = []
            elif partial:
                t.writers.append(op)
            else:
                deps.extend(t.writers)
                t.writers = [op]
        for t in reads:
            t.readers.append(op)
        seen = set()
        best = {}
        for d in deps:
            if d is op or id(d) in seen:
                continue
            seen.add(id(d))
            if d.eng == "pe" and op.eng == "pe" and d.dma_key is None and op.dma_key is None:
                continue
            if d.dma_key is None:
                if d.eng not in best or best[d.eng].pos < d.pos:
                    best[d.eng] = d
            else:
                op.deps.append(d)
        for d in best.values():
            op.deps.append(d)
            d.sig = True

    def _pend(self, o):
        if self.pending[o.eng]:
            have = set(id(d) for d in o.deps)
            for d in self.pending[o.eng]:
                if id(d) not in have and d is not o:
                    o.deps.append(d)
            self.pending[o.eng] = []

    def op(self, eng, fn, reads=(), writes=(), partial=False):
        o = Op(eng, fn)
        o.pos = len(self.streams[eng])
        self._track(o, list(reads), list(writes), partial)
        self._pend(o)
        self.streams[eng].append(o)
        return o

    def dma(self, eng, out_ap, in_ap, key, reads=(), writes=(), partial=False, final=False, **kw):
        def fn(e):
            return e.dma_start(out=out_ap, in_=in_ap, **kw)
        o = Op(eng, fn)
        o.pos = len(self.streams[eng])
        o.dma_key = key
        o.sig = True
        self._track(o, list(reads), list(writes), partial)
        self._pend(o)
        self.streams[eng].append(o)
        self.dma_since.append(o)
        if final:
            self.out_dmas.append(o)
        return o

    def emit(self):
        nc = self.nc
        es = self.es
        esem = {e: es.enter_context(nc.semaphore("sem_" + e)) for e in ENGS}
        ecount = {e: 0 for e in ENGS}
        nkeys = 0
        semtab = {}
        for e in ENGS:
            for o in self.streams[e]:
                if o.dma_key is not None:
                    kk = (o.dma_key.name, e)
                    if kk not in semtab:
                        semtab[kk] = [es.enter_context(nc.semaphore("dsem_%s_%s" % kk)), 0]
                        nkeys += 1
                    semtab[kk][1] += 16
                    o.signal = (semtab[kk][0], semtab[kk][1], 16)
                elif o.sig:
                    ecount[e] += 1
                    o.signal = (esem[e], ecount[e], 1)
        self.ecount = ecount
        self.nkeys = nkeys
        streams = self.streams
        finals = {}
        for o in self.out_dmas:
            sem, val, _ = o.signal
            if finals.get(id(sem), (None, 0))[1] < val:
                finals[id(sem)] = (sem, val)

        def run(e, h):
            waited = {}
            for o in streams[e]:
                need = {}
                for d in o.deps:
                    sem, val, _ = d.signal
                    if need.get(id(sem), (None, 0))[1] < val:
                        need[id(sem)] = (sem, val)
                for sem, val in need.values():
                    if waited.get(id(sem), 0) < val:
                        h.wait_ge(sem, val)
                        waited[id(sem)] = val
                ins = o.fn(h)
                if o.signal is not None:
                    ins.then_inc(o.signal[0], o.signal[2])
            if e == "sp":
                for sem, val in finals.values():
                    if waited.get(id(sem), 0) < val:
                        h.wait_ge(sem, val)

        with nc.Block() as block:
            @block.tensor
            def _(h):
                run("pe", h)

            @block.scalar
            def _(h):
                run("act", h)

            @block.vector
            def _(h):
                run("dve", h)

            @block.gpsimd
            def _(h):
                run("pool", h)

            @block.sync
            def _(h):
                run("sp", h)
        es.close()


import os as _os
_STOP = int(_os.environ.get("KDBG_STOP", "0"))


class _Stop(Exception):
    pass


def _chk(k):
    if _STOP == k:
        raise _Stop()


class Ring:
    def __init__(self, items):
        self.items = items
        self.i = 0

    def next(self):
        t = self.items[self.i % len(self.items)]
        self.i += 1
        return t


def _bucket(dist):
    max_exact = NB // 2
    if dist < max_exact:
        return dist
    df = np.float32(max(dist, 1))
    v = np.float32(np.log(df / np.float32(max_exact))) / np.float32(math.log(2048 / max_exact)) * np.float32(NB - max_exact)
    return min(max_exact + int(v), NB - 1)


def host_constants():
    ohp = np.zeros((NB, 3 * 384), np.float32)
    valid = np.zeros((1, 3 * 384), np.float32)
    for g, (W, d) in enumerate(GROUPS):
        for rel in range(129):
            n = rel + 127
            ohp[_bucket(rel * d), g * 384 + n] = 1.0
            valid[0, g * 384 + n] = 1.0
    ohs = np.zeros((NB, 6 * 128), np.float32)
    for s in range(4):
        for m in range(128):
            j = (s - m) if m < s else (128 + s - m)
            ohs[_bucket(j * 1), s * 128 + m] = 1.0
    for v, d in ((4, GROUPS[1][1]), (5, GROUPS[2][1])):
        for m in range(128):
            j = 128 - m
            ohs[_bucket(j * d), v * 128 + m] = 1.0
    return ohp, valid, ohs


def build_program(depth=DEPTH, do_ffn=True):
    NBLK = SEQ // 128
    NT4 = NBLK // 4
    nc = bass.Bass("TRN2", target_bir_lowering=False)
    P = Prog(nc)

    def din(name, shape):
        return P.dram(name, shape, F32, kind="ExternalInput")

    def dout(name, shape):
        return P.dram(name, shape, F32, kind="ExternalOutput")

    xp = din("xp", [SEQ, D])
    xs = din("xs", [16, D])
    sg_in = din("sg", [2 * 4 * 4 * 128, 256])
    ck = [din("ck%d" % g, [2 * 4 * GROUPS[g][0], D]) for g in range(3)]
    cv = [din("cv%d" % g, [2 * 4 * GROUPS[g][0], D]) for g in range(3)]
    sfc = din("sfc", [32, 2 * DFF])
    rel_bias = din("rel_bias", [NB, 48])
    norm_mix = din("norm_mix", [4, D])
    norm_ffn = din("norm_ffn", [4, D])
    gla_w_in = din("gla_w_in", [2 * D, GLA_IN])
    gla_w_g2 = din("gla_w_gate2", [32, 512])
    gla_b_g = din("gla_b_gate", [2, 512])
    gla_norm = din("gla_norm", [2, 256])
    gla_w_out = din("gla_w_out", [2 * D, D])
    dil_w_in = din("dil_w_in", [2 * D, 9216])
    dil_qn = din("dil_q_norm", [2, 64])
    dil_kn = din("dil_k_norm", [2, 64])
    dil_w_out = din("dil_w_out", [2 * D, D])
    ffn_w_up = din("ffn_w_up", [4 * D, 2 * DFF])
    ffn_cw = din("ffn_conv_w", [12, 2 * DFF])
    ffn_cb = din("ffn_conv_b", [4, 2 * DFF])
    ffn_w_down = din("ffn_w_down", [4 * DFF, D])
    c_ohp = din("c_ohp", [NB, 3 * 384])
    c_valid = din("c_valid", [1, 3 * 384])
    c_ohs = din("c_ohs", [NB, 6 * 128])

    yp = dout("yp", [SEQ, D])
    ys = dout("ys", [16, D])
    sgp = dout("sgp", [2 * 4 * 128, 256])
    sgs = dout("sgs", [2 * 4 * 4 * 128, 256])
    keep = [min(GROUPS[g][0], SEQ) for g in range(3)]
    kpo = [dout("kp%d" % g, [2 * keep[g], D]) for g in range(3)]
    vpo = [dout("vp%d" % g, [2 * keep[g], D]) for g in range(3)]
    kso = [dout("ks%d" % g, [2 * 16, D]) for g in range(3)]
    vso = [dout("vs%d" % g, [2 * 16, D]) for g in range(3)]
    fcp = dout("fcp", [8, 2 * DFF])
    fcs = dout("fcs", [32, 2 * DFF])

    ug_scr = [P.dram("ug%d" % g, [SEQ, 1280], F32) for g in (1, 2)]
    wsc = P.dram("wsc", [48, 384], F32)

    yp_blk = [T("ypb%d" % i, yp.h) for i in range(NBLK)]
    ys_blk = T("ysb", ys.h)
    xp_blk = [T("xpb%d" % i, xp.h) for i in range(NBLK)]
    xs_blk = T("xsb", xs.h)

    PSB = [P.ps("psb%d" % i, [128, 1024], BF16) if i < 2 else P.ps("psb%d" % i, [128, 512], F32) for i in range(8)]
    PSF = [PSB[i].h.bitcast(F32) if i < 2 else PSB[i].h for i in range(8)]

    class _RR:
        pass
    RR = _RR()
    RR.tp = Ring([0, 1])
    RR.mmA = Ring([(2, 3), (4, 5)])
    RR.mmB = Ring([6, 7])

    class _Dyn:
        def __init__(self, nm):
            self.nm = nm

        def next(self):
            return getattr(RR, self.nm).next()
    tp_ring = _Dyn("tp")
    mmA_ring = _Dyn("mmA")
    mmB_ring = _Dyn("mmB")

    def psA(pair):
        return PSB[pair[0]], PSB[pair[1]]

    identf = P.sb("identf", [128, 128], F32)
    ident = P.sb("ident", [128, 128], BF16)
    onesF = P.sb("onesF", [128, 128], F32)
    epsT = P.sb("epsT", [128, 1], F32)
    mask_s = P.sb("mask_s", [128, 128], F32)
    Jm = P.sb("Jm", [128, 128], BF16)
    Jf = P.sb("Jf", [128, 128], F32)
    gmix = P.sb("gmix", [128, 4, 8], F32)
    gffn = P.sb("gffn", [128, 4, 8], F32)

    P.op("pool", lambda e: e.memset(identf[:, :], 0.0), writes=[identf])
    P.op("pool", lambda e: e.affine_select(out=identf[:, :], in_=identf[:, :], pattern=[[-1, 128]], compare_op=ALU.not_equal, fill=1.0, base=0, channel_multiplier=1), reads=[identf], writes=[identf])
    P.op("dve", lambda e: e.tensor_copy(out=ident[:, :], in_=identf[:, :]), reads=[identf], writes=[ident])
    P.op("pool", lambda e: e.memset(Jf[:, :], 0.0), writes=[Jf])
    P.op("pool", lambda e: e.affine_select(out=Jf[:, :], in_=Jf[:, :], pattern=[[1, 128]], compare_op=ALU.not_equal, fill=1.0, base=-127, channel_multiplier=1), reads=[Jf], writes=[Jf])
    P.op("dve", lambda e: e.tensor_copy(out=Jm[:, :], in_=Jf[:, :]), reads=[Jf], writes=[Jm])
    P.op("dve", lambda e: e.memset(onesF[:, :], 1.0), writes=[onesF])
    P.op("dve", lambda e: e.memset(epsT[:, :], EPS), writes=[epsT])
    GSC = 128.0 ** -0.5
    P.op("pool", lambda e: e.memset(mask_s[:, :], GSC), writes=[mask_s])
    P.op("pool", lambda e: e.affine_select(out=mask_s[:, :], in_=mask_s[:, :], pattern=[[1, 128]], compare_op=ALU.is_ge, fill=0.0, base=0, channel_multiplier=-1), reads=[mask_s], writes=[mask_s])
    rowtmp = P.sb("rowtmp", [4, 512], F32)

    def load_featmajor(dst_view, dst_unit, src_h, row0, nrows, ncols, bank):
        nch = ncols // 128
        for c0 in range(0, ncols, 512):
            w = min(512, ncols - c0)
            P.dma("sp", rowtmp[0:nrows, 0:w], src_h[row0:row0 + nrows, c0:c0 + w], key=rowtmp, writes=[rowtmp])
            for cc in range(w // 128):
                ch = c0 // 128 + cc
                P.op("pe", lambda e, cc=cc, ch=ch: e.matmul(out=PSB[bank][:, ch * nrows:(ch + 1) * nrows], lhsT=rowtmp[0:nrows, cc * 128:(cc + 1) * 128], rhs=identf[0:nrows, 0:nrows], start=True, stop=True), reads=[rowtmp, identf], writes=[PSB[bank]], partial=True)
        P.op("act", lambda e: e.copy(out=dst_view, in_=PSB[bank][:, 0:nch * nrows].rearrange("p (c r) -> p c r", r=nrows)), reads=[PSB[bank]], writes=[dst_unit])

    load_featmajor(gmix[:, :, :].rearrange("p l c -> p c l"), gmix, norm_mix.h, 0, 4, D, 6)
    load_featmajor(gffn[:, :, :].rearrange("p l c -> p c l"), gffn, norm_ffn.h, 0, 4, D, 7)

    xt_ring = Ring([P.sb("xt%d" % i, [128, D], F32) for i in range(2)])
    xr_ring = Ring([P.sb("xr%d" % i, [128, D], F32) for i in range(2)])
    sq_scr = P.sb("sq_scr", [128, D], F32)
    xn_ring = Ring([P.sb("xn%d" % i, [128, D], BF16) for i in range(2)])
    st_ring = Ring([P.sb("st%d" % i, [128, 8], F32) for i in range(4)])
    HT = {"ring": None}
    on_ring = Ring([P.sb("on%d" % i, [128, D], BF16) for i in range(2)])
    onT_ring = Ring([P.sb("onT%d" % i, [128, 8, 128], BF16) for i in range(2)])
    PERSIST = P.off

    def phase_begin(n_hT, width):
        P.barrier()
        P.off = PERSIST
        HT["ring"] = Ring([P.sb("hT%d" % i, [128, 8, width], BF16) for i in range(n_hT)])

    def rstd_from_ss(st, col_in, col_out, npart, scale):
        w = col_out.stop - col_out.start
        P.op("act", lambda e: e.activation(out=st[0:npart, col_out], in_=st[0:npart, col_in], func=AF.Ln, scale=scale, bias=epsT[0:npart, 0:1]), reads=[st, epsT], writes=[st])
        P.op("act", lambda e: e.activation(out=st[0:npart, col_out], in_=st[0:npart, col_out], func=AF.Exp, scale=-0.5), reads=[st], writes=[st])

    def norm_front(blocks, gain, layer):
        hT = HT["ring"].next()
        col = 0
        for (src_ap, units, n) in blocks:
            xt = xt_ring.next()
            P.dma("sp", xt[0:n, :], src_ap, key=xt, reads=units, writes=[xt])
            st = st_ring.next()
            P.op("act", lambda e, xt=xt, st=st, n=n: e.activation(out=sq_scr[0:n, :], in_=xt[0:n, :], func=AF.Square, accum_out=st[0:n, 0:1]), reads=[xt], writes=[st])
            rstd_from_ss(st, slice(0, 1), slice(1, 2), n, 1.0 / D)
            xn = xn_ring.next()
            P.op("act", lambda e, xt=xt, st=st, xn=xn, n=n: e.activation(out=xn[0:n, :], in_=xt[0:n, :], func=AF.Copy, scale=st[0:n, 1:2]), reads=[xt, st], writes=[xn])
            tb = tp_ring.next()
            tpv = PSB[tb].h
            for c in range(8):
                P.op("pe", lambda e, c=c, xn=xn, n=n, tpv=tpv: e.transpose(out=tpv[:, c * 128:c * 128 + n], in_=xn[0:n, c * 128:(c + 1) * 128], identity=ident[0:n, 0:n]), reads=[xn, ident], writes=[PSB[tb]], partial=True)
            c0 = col
            P.op("dve", lambda e, tpv=tpv, hT=hT, n=n, c0=c0: e.tensor_tensor(
                out=hT[:, :, c0:c0 + n], in0=tpv[:, :].rearrange("p (c t) -> p c t", t=128)[:, :, 0:n],
                in1=gain[:, layer, :].unsqueeze(2).broadcast_to([128, 8, n]), op=ALU.mult),
                reads=[PSB[tb], gain], writes=[hT], partial=True)
            col += n
        return hT, col

    def load_w(dst, cols, src_h, row0, nrows, col0, ncols):
        kc = nrows // 128
        step = 512
        for k0 in range(0, kc, 8):
            k1 = min(kc, k0 + 8)
            for c in range(0, ncols, step):
                w = min(step, ncols - c)
                src = src_h[row0 + k0 * 128:row0 + k1 * 128, col0 + c:col0 + c + w].rearrange("(kc p) n -> p kc n", p=128)
                P.dma("pool", dst[:, k0:k1, cols.start + c:cols.start + c + w], src, key=dst, writes=[dst], partial=True)

    def transpose_to(on, npart, onT):
        tb = tp_ring.next()
        tpv = PSB[tb].h
        for c in range(8):
            P.op("pe", lambda e, c=c, tpv=tpv: e.transpose(out=tpv[:, c * 128:c * 128 + npart], in_=on[0:npart, c * 128:(c + 1) * 128], identity=ident[0:npart, 0:npart]), reads=[on, ident], writes=[PSB[tb]], partial=True)
        P.op("act", lambda e, tpv=tpv: e.copy(out=onT[:, :, 0:npart], in_=tpv[:, :].rearrange("p (c t) -> p c t", t=128)[:, :, 0:npart]), reads=[PSB[tb]], writes=[onT])

    def out_proj_residual(onT, npart, wout, nchunks, src_ap, src_units, dst_ap, dst_units, final, off=0):
        pair = mmA_ring.next()
        for half in range(2):
            b = PSB[pair[half]]
            for c in range(nchunks):
                P.op("pe", lambda e, c=c, b=b, half=half: e.matmul(out=b[0:npart, :], lhsT=onT[:, c, off:off + npart], rhs=wout[:, c, half * 512:(half + 1) * 512], start=(c == 0), stop=(c == nchunks - 1)), reads=[onT, wout], writes=[b], partial=True)
        xr = xr_ring.next()
        P.dma("sp", xr[0:npart, :], src_ap, key=xr, reads=src_units, writes=[xr])
        for half in range(2):
            b = PSB[pair[half]]
            P.op("dve", lambda e, b=b, half=half, xr=xr: e.tensor_tensor(out=xr[0:npart, half * 512:(half + 1) * 512], in0=b[0:npart, :], in1=xr[0:npart, half * 512:(half + 1) * 512], op=ALU.add), reads=[b, xr], writes=[xr])
        P.dma("pool", dst_ap, xr[0:npart, :], key=xr, reads=[xr], writes=dst_units, final=final)

    state = {"first": True}

    def prompt_src(i):
        if state["first"]:
            return xp.h[i * 128:(i + 1) * 128, :], [xp_blk[i]]
        return yp.h[i * 128:(i + 1) * 128, :], [yp_blk[i]]

    def sample_src(b):
        if state["first"]:
            return xs.h[b * 4:(b + 1) * 4, :], [xs_blk]
        return ys.h[b * 4:(b + 1) * 4, :], [ys_blk]

    def sample_src_all():
        if state["first"]:
            return xs.h[0:16, :], [xs_blk]
        return ys.h[0:16, :], [ys_blk]

    gl_w_in = None

    def gla_layer(layer, li, last):
        phase_begin(2, 512)
        w_in = P.sb("gw_in", [128, 8, GLA_IN], BF16)
        w_out = P.sb("gw_out", [128, 8, D], BF16)
        wg2 = P.sb("gwg2", [16, 512], BF16)
        negb = P.sb("gnegb", [128, 4], F32)
        gn = P.sb("ggn", [128, 256], F32)
        load_w(w_in, slice(0, GLA_IN), gla_w_in.h, li * D, D, 0, GLA_IN)
        load_w(w_out, slice(0, D), gla_w_out.h, li * D, D, 0, D)
        P.dma("pool", wg2[:, :], gla_w_g2.h[li * 16:(li + 1) * 16, :], key=wg2, writes=[wg2])
        load_featmajor(negb[:, :].unsqueeze(2), negb, gla_b_g.h, li, 1, 512, mmB_ring.next())
        P.op("dve", lambda e: e.tensor_scalar(out=negb[:, :], in0=negb[:, :], scalar1=-1.0, scalar2=None, op0=ALU.mult), reads=[negb], writes=[negb])
        P.dma("sp", gn[:, :], bass.AP(gla_norm.h, li * 256, [[0, 128], [1, 256]]), key=gn, writes=[gn])

        gzT_ring = Ring([P.sb("g_gz_%d" % i, [16, 512], BF16) for i in range(2)])
        lt = P.sb("g_l", [128, 512], F32)
        bp = P.sb("g_bp", [128, 512], F32)
        Et = Ring([P.sb("g_E_%d" % i, [128, 512], F32) for i in range(2)])
        Ei = Ring([P.sb("g_Ei_%d" % i, [128, 512], F32) for i in range(2)])
        El = Ring([P.sb("g_El_%d" % i, [128, 4, 4], F32) for i in range(2)])
        qt_ring = Ring([P.sb("g_qt_%d" % i, [128, 4, 512], BF16) for i in range(2)])
        kt_ring = Ring([P.sb("g_kt_%d" % i, [128, 4, 512], BF16) for i in range(2)])
        v_ring = Ring([P.sb("g_v_%d" % i, [128, D], BF16) for i in range(3)])
        gsr_ring = Ring([P.sb("g_sr_%d" % i, [128, D], F32) for i in range(3)])
        aT_ring = Ring([P.sb("g_aT_%d" % i, [128, 128], BF16) for i in range(3)])
        ktok_ring = Ring([P.sb("g_ktok_%d" % i, [128, 128], BF16) for i in range(3)])
        osb_ring = Ring([P.sb("g_o_%d" % i, [128, D], F32) for i in range(2)])
        oss_ring = Ring([P.sb("g_oss_%d" % i, [128, 8], F32) for i in range(2)])
        S = [P.sb("g_S_%d" % h, [128, 256], F32) for h in range(4)]
        Dd = [P.sb("g_D_%d" % h, [128, 256], F32) for h in range(4)]
        Dbf = [P.sb("g_Dbf_%d" % h, [128, 256], BF16) for h in range(4)]
        sq2 = P.sb("g_sq2", [128, 256], F32)

        def gla_tile(hT, ntok, L, s0_aps, sT_aps, res_blocks, is_sample_first):
            nch = ntok // L
            bb = mmB_ring.next()
            for kc in range(8):
                P.op("pe", lambda e, kc=kc, bb=bb: e.matmul(out=PSB[bb][0:16, 0:ntok], lhsT=w_in[:, kc, 3072:3088], rhs=hT[:, kc, 0:ntok], start=(kc == 0), stop=(kc == 7)), reads=[hT, w_in], writes=[PSB[bb]], partial=True)
            gz = gzT_ring.next()
            P.op("act", lambda e, bb=bb, gz=gz: e.copy(out=gz[:, 0:ntok], in_=PSB[bb][0:16, 0:ntok]), reads=[PSB[bb]], writes=[gz])
            qt = qt_ring.next()
            kt = kt_ring.next()
            E_l = El.next()
            Es, Eis = [], []
            for hh in range(4):
                bb = mmB_ring.next()
                P.op("pe", lambda e, hh=hh, bb=bb, gz=gz: e.matmul(out=PSB[bb][:, 0:ntok], lhsT=wg2[:, hh * 128:(hh + 1) * 128], rhs=gz[:, 0:ntok], start=True, stop=True), reads=[gz, wg2], writes=[PSB[bb]])
                P.op("act", lambda e, hh=hh, bb=bb: e.activation(out=lt[:, 0:ntok], in_=PSB[bb][:, 0:ntok], func=AF.Exp, scale=-1.0, bias=negb[:, hh:hh + 1]), reads=[PSB[bb], negb], writes=[lt])
                P.op("act", lambda e: e.activation(out=lt[:, 0:ntok], in_=lt[:, 0:ntok], func=AF.Ln, bias=onesF[:, 0:1]), reads=[lt, onesF], writes=[lt])
                for c in range(nch):
                    P.op("dve", lambda e, c=c: e.tensor_tensor_scan(out=bp[:, c * L:(c + 1) * L], data0=onesF[:, 0:L], data1=lt[:, c * L:(c + 1) * L], initial=0.0, op0=ALU.mult, op1=ALU.add), reads=[lt, onesF], writes=[bp], partial=(c > 0))
                E = Et.next()
                Einv = Ei.next()
                P.op("act", lambda e, E=E: e.activation(out=E[:, 0:ntok], in_=bp[:, 0:ntok], func=AF.Exp, scale=-1.0 / 16), reads=[bp], writes=[E])
                P.op("act", lambda e, Einv=Einv: e.activation(out=Einv[:, 0:ntok], in_=bp[:, 0:ntok], func=AF.Exp, scale=1.0 / 16), reads=[bp], writes=[Einv])
                P.op("dve", lambda e, E=E, hh=hh, E_l=E_l: e.tensor_copy(out=E_l[:, hh, 0:nch], in_=E[:, 0:ntok].rearrange("p (c l) -> p c l", l=L)[:, :, L - 1]), reads=[E], writes=[E_l], partial=(hh > 0))
                for (dst, coff, own, oth) in ((qt, 0, E, Einv), (kt, 512, Einv, E)):
                    bb2 = mmB_ring.next()
                    for kc in range(8):
                        P.op("pe", lambda e, kc=kc, bb2=bb2, coff=coff, hh=hh: e.matmul(out=PSB[bb2][:, 0:ntok], lhsT=w_in[:, kc, coff + hh * 128:coff + (hh + 1) * 128], rhs=hT[:, kc, 0:ntok], start=(kc == 0), stop=(kc == 7)), reads=[hT, w_in], writes=[PSB[bb2]], partial=True)
                    for c in range(nch):
                        le = (c + 1) * L - 1
                        P.op("dve", lambda e, c=c, le=le, bb2=bb2, dst=dst, own=own, oth=oth, hh=hh: e.scalar_tensor_tensor(
                            out=dst[:, hh, c * L:(c + 1) * L], in0=PSB[bb2][:, c * L:(c + 1) * L], scalar=oth[:, le:le + 1], in1=own[:, c * L:(c + 1) * L], op0=ALU.mult, op1=ALU.mult),
                            reads=[PSB[bb2], own, oth], writes=[dst], partial=True)
            for c in range(nch):
                cs = slice(c * L, (c + 1) * L)
                pairv = mmA_ring.next()
                for half in range(2):
                    b = PSB[pairv[half]]
                    for kc in range(8):
                        P.op("pe", lambda e, kc=kc, b=b, half=half, cs=cs: e.matmul(out=b[0:L, :], lhsT=hT[:, kc, cs], rhs=w_in[:, kc, 1024 + half * 512:1024 + (half + 1) * 512], start=(kc == 0), stop=(kc == 7)), reads=[hT, w_in], writes=[b], partial=True)
                vsb = v_ring.next()
                for half in range(2):
                    b = PSB[pairv[half]]
                    P.op("act", lambda e, b=b, half=half, vsb=vsb: e.copy(out=vsb[0:L, half * 512:(half + 1) * 512], in_=b[0:L, :]), reads=[b], writes=[vsb], partial=(half > 0))
                pairr = mmA_ring.next()
                for half in range(2):
                    b = PSB[pairr[half]]
                    for kc in range(8):
                        P.op("pe", lambda e, kc=kc, b=b, half=half, cs=cs: e.matmul(out=b[0:L, :], lhsT=hT[:, kc, cs], rhs=w_in[:, kc, 2048 + half * 512:2048 + (half + 1) * 512], start=(kc == 0), stop=(kc == 7)), reads=[hT, w_in], writes=[b], partial=True)
                gsr = gsr_ring.next()
                for half in range(2):
                    b = PSB[pairr[half]]
                    P.op("act", lambda e, b=b, half=half, gsr=gsr: e.activation(out=gsr[0:L, half * 512:(half + 1) * 512], in_=b[0:L, :], func=AF.Silu), reads=[b], writes=[gsr], partial=(half > 0))
                P.op("pool", lambda e, gsr=gsr: e.tensor_tensor(out=gsr[0:L, :].rearrange("p (h e) -> p h e", e=256), in0=gsr[0:L, :].rearrange("p (h e) -> p h e", e=256), in1=gn[0:L, :].unsqueeze(1).broadcast_to([L, 4, 256]), op=ALU.mult), reads=[gsr, gn], writes=[gsr])
                osb = osb_ring.next()
                oss = oss_ring.next()
                for hh in range(4):
                    if s0_aps is not None or (c == 0 and is_sample_first):
                        pass
                    if s0_aps is not None:
                        P.dma("sp", S[hh][:, :], s0_aps[c][hh], key=S[hh], writes=[S[hh]])
                    if s0_aps is None and state["gla_zero"] and c == 0:
                        P.op("pool", lambda e, hh=hh: e.memset(S[hh][:, :], 0.0), writes=[S[hh]])
                    P.op("dve", lambda e, hh=hh, c=c, E_l=E_l: e.tensor_scalar(out=Dd[hh][:, :], in0=S[hh][:, :], scalar1=E_l[:, hh, c:c + 1], scalar2=None, op0=ALU.mult), reads=[S[hh], E_l], writes=[Dd[hh]])
                    P.op("act", lambda e, hh=hh: e.activation(out=Dbf[hh][:, :], in_=Dd[hh][:, :], func=AF.Copy, scale=GSC), reads=[Dd[hh]], writes=[Dbf[hh]])
                    ba = mmB_ring.next()
                    P.op("pe", lambda e, ba=ba, hh=hh, cs=cs: e.matmul(out=PSB[ba][0:L, 0:L], lhsT=kt[:, hh, cs], rhs=qt[:, hh, cs], start=True, stop=True), reads=[kt, qt], writes=[PSB[ba]])
                    aT = aT_ring.next()
                    P.op("dve", lambda e, ba=ba, aT=aT: e.tensor_tensor(out=aT[0:L, 0:L], in0=PSB[ba][0:L, 0:L], in1=mask_s[0:L, 0:L], op=ALU.mult), reads=[PSB[ba], mask_s], writes=[aT])
                    tb = tp_ring.next()
                    tpv = PSB[tb].h
                    P.op("pe", lambda e, tpv=tpv, hh=hh, cs=cs, tb=tb: e.transpose(out=tpv[0:L, 0:128], in_=kt[:, hh, cs], identity=ident[:, :]), reads=[kt, ident], writes=[PSB[tb]])
                    ktok = ktok_ring.next()
                    P.op("act", lambda e, tpv=tpv, ktok=ktok: e.copy(out=ktok[0:L, :], in_=tpv[0:L, 0:128]), reads=[PSB[tb]], writes=[ktok])
                    P.op("pe", lambda e, ba=ba, aT=aT, vsb=vsb, hh=hh: e.matmul(out=PSB[ba][0:L, 128:384], lhsT=aT[0:L, 0:L], rhs=vsb[0:L, hh * 256:(hh + 1) * 256], start=True, stop=False), reads=[aT, vsb], writes=[PSB[ba]])
                    P.op("pe", lambda e, ba=ba, hh=hh, cs=cs: e.matmul(out=PSB[ba][0:L, 128:384], lhsT=qt[:, hh, cs], rhs=Dbf[hh][:, :], start=False, stop=True), reads=[qt, Dbf[hh]], writes=[PSB[ba]], partial=True)
                    P.op("act", lambda e, ba=ba, osb=osb, hh=hh: e.copy(out=osb[0:L, hh * 256:(hh + 1) * 256], in_=PSB[ba][0:L, 128:384]), reads=[PSB[ba]], writes=[osb], partial=(hh > 0))
                    P.op("act", lambda e, ba=ba, oss=oss, hh=hh: e.activation(out=sq2[0:L, :], in_=PSB[ba][0:L, 128:384], func=AF.Square, accum_out=oss[0:L, hh:hh + 1]), reads=[PSB[ba]], writes=[oss], partial=(hh > 0))
                    bk = mmB_ring.next()
                    P.op("pe", lambda e, bk=bk, ktok=ktok, vsb=vsb, hh=hh: e.matmul(out=PSB[bk][:, 0:256], lhsT=ktok[0:L, :], rhs=vsb[0:L, hh * 256:(hh + 1) * 256], start=True, stop=True), reads=[ktok, vsb], writes=[PSB[bk]])
                    P.op("dve", lambda e, bk=bk, hh=hh: e.tensor_tensor(out=S[hh][:, :], in0=PSB[bk][:, 0:256], in1=Dd[hh][:, :], op=ALU.add), reads=[PSB[bk], Dd[hh]], writes=[S[hh]])
                    if sT_aps is not None and sT_aps[c] is not None:
                        P.dma("pool", sT_aps[c][hh][0], S[hh][:, :], key=S[hh], reads=[S[hh]], writes=[sT_aps[c][hh][1]], partial=True, final=True)
                state["gla_zero"] = False
                rstd_from_ss(oss, slice(0, 4), slice(4, 8), L, 1.0 / 256)
                on = on_ring.next()
                for hh in range(4):
                    P.op("dve", lambda e, hh=hh, on=on, osb=osb, oss=oss, gsr=gsr: e.scalar_tensor_tensor(out=on[0:L, hh * 256:(hh + 1) * 256], in0=osb[0:L, hh * 256:(hh + 1) * 256], scalar=oss[0:L, 4 + hh:5 + hh], in1=gsr[0:L, hh * 256:(hh + 1) * 256], op0=ALU.mult, op1=ALU.mult), reads=[osb, oss, gsr], writes=[on], partial=(hh > 0))
                onT = onT_ring.next()
                transpose_to(on, L, onT)
                src_ap, src_units, dst_ap, dst_units, fin = res_blocks[c]
                out_proj_residual(onT, L, w_out, 8, src_ap, src_units, dst_ap, dst_units, fin)

        def blocks_for_tile(t):
            return [(prompt_src(4 * t + j)[0], prompt_src(4 * t + j)[1], 128) for j in range(4)]

        state["gla_zero"] = True
        pre = norm_front(blocks_for_tile(0), gmix, layer)
        for t in range(NT4):
            cur = pre
            if t + 1 < NT4:
                pre = norm_front(blocks_for_tile(t + 1), gmix, layer)
            else:
                pre = norm_front([(sample_src(0)[0], sample_src(0)[1], 4)], gmix, layer)
            res = []
            for j in range(4):
                i = 4 * t + j
                sa, su = prompt_src(i)
                res.append((sa, su, yp.h[i * 128:(i + 1) * 128, :], [yp_blk[i]], last))
            sT = None
            if t == NT4 - 1:
                sT = [None, None, None, [(sgp.h[(li * 4 + hh) * 128:(li * 4 + hh + 1) * 128, :], sgp) for hh in range(4)]]
            gla_tile(cur[0], 512, 128, None, sT, res, False)
        for b in range(4):
            cur = pre
            if b + 1 < 4:
                pre = norm_front([(sample_src(b + 1)[0], sample_src(b + 1)[1], 4)], gmix, layer)
            sa, su = sample_src(b)
            res = [(sa, su, ys.h[b * 4:(b + 1) * 4, :], [ys_blk], last)]
            s0 = [[sg_in.h[((li * 4 + b) * 4 + hh) * 128:((li * 4 + b) * 4 + hh + 1) * 128, :] for hh in range(4)]]
            sT = [[(sgs.h[((li * 4 + b) * 4 + hh) * 128:((li * 4 + b) * 4 + hh + 1) * 128, :], sgs) for hh in range(4)]]
            gla_tile(cur[0], 4, 4, s0, sT, res, True)
        state["first"] = False

    def ffn_layer(layer, last):
        phase_begin(2, 256)
        RR.mmA = Ring([(2, 3)])
        RR.mmB = Ring([4, 5, 6, 7])
        w_up = P.sb("f_wup", [128, 8, 2 * DFF], BF16)
        w_dn = P.sb("f_wdn", [128, 22, D], BF16)
        cw = P.sb("f_cw", [128, 3, 44], F32)
        cb = P.sb("f_cb", [128, 44], F32)
        load_w(w_up, slice(0, 2 * DFF), ffn_w_up.h, layer * D, D, 0, 2 * DFF)
        load_w(w_dn, slice(0, D), ffn_w_down.h, layer * DFF, DFF, 0, D)
        load_featmajor(cw[:, :, :].rearrange("p i c -> p c i"), cw, ffn_cw.h, layer * 3, 3, 2 * DFF, mmB_ring.next())
        load_featmajor(cb[:, :].unsqueeze(2), cb, ffn_cb.h, layer, 1, 2 * DFF, mmB_ring.next())
        hist = P.sb("f_hist", [128, 44, 2], F32)
        cs_ring = Ring([P.sb("f_c_%d" % i, [128, 256], F32) for i in range(4)])
        sg_ring = Ring([P.sb("f_sg_%d" % i, [128, 256], F32) for i in range(2)])
        act = P.sb("f_act", [128, 22, 256], BF16)
        ul_ring = Ring([P.sb("f_ul_%d" % i, [2, 512], F32) for i in range(2)])
        hrow = P.sb("f_hrow", [2, 512], F32)

        def ffn_tile(hT, N, res_blocks, state_out_ap, state_out_unit):
            for cp in range(22):
                cts = []
                for ch in (cp, 22 + cp):
                    bb = mmB_ring.next()
                    for kc in range(8):
                        P.op("pe", lambda e, kc=kc, bb=bb, ch=ch: e.matmul(out=PSB[bb][:, 0:N], lhsT=w_up[:, kc, ch * 128:(ch + 1) * 128], rhs=hT[:, kc, 0:N], start=(kc == 0), stop=(kc == 7)), reads=[hT, w_up], writes=[PSB[bb]], partial=True)
                    ct = cs_ring.next()
                    P.op("act", lambda e, bb=bb, ct=ct, ch=ch: e.activation(out=ct[:, 0:N], in_=PSB[bb][:, 0:N], func=AF.Identity, scale=cw[:, 2, ch:ch + 1], bias=cb[:, ch:ch + 1]), reads=[PSB[bb], cw, cb], writes=[ct])
                    P.op("dve", lambda e, bb=bb, ct=ct, ch=ch: e.scalar_tensor_tensor(out=ct[:, 1:N], in0=PSB[bb][:, 0:N - 1], scalar=cw[:, 1, ch:ch + 1], in1=ct[:, 1:N], op0=ALU.mult, op1=ALU.add), reads=[PSB[bb], cw, ct], writes=[ct])
                    P.op("dve", lambda e, bb=bb, ct=ct, ch=ch: e.scalar_tensor_tensor(out=ct[:, 2:N], in0=PSB[bb][:, 0:N - 2], scalar=cw[:, 0, ch:ch + 1], in1=ct[:, 2:N], op0=ALU.mult, op1=ALU.add), reads=[PSB[bb], cw, ct], writes=[ct])
                    P.op("dve", lambda e, ct=ct, ch=ch: e.scalar_tensor_tensor(out=ct[:, 0:2], in0=hist[:, ch, 0:2], scalar=cw[:, 0, ch:ch + 1], in1=ct[:, 0:2], op0=ALU.mult, op1=ALU.add), reads=[hist, cw, ct], writes=[ct])
                    P.op("dve", lambda e, ct=ct, ch=ch: e.scalar_tensor_tensor(out=ct[:, 0:1], in0=hist[:, ch, 1:2], scalar=cw[:, 1, ch:ch + 1], in1=ct[:, 0:1], op0=ALU.mult, op1=ALU.add), reads=[hist, cw, ct], writes=[ct])
                    P.op("act", lambda e, bb=bb, ch=ch: e.copy(out=hist[:, ch, 0:2], in_=PSB[bb][:, N - 2:N]), reads=[PSB[bb]], writes=[hist])
                    cts.append(ct)
                sgt = sg_ring.next()
                P.op("act", lambda e, sgt=sgt, c0=cts[0]: e.activation(out=sgt[:, 0:N], in_=c0[:, 0:N], func=AF.Silu), reads=[cts[0]], writes=[sgt])
                P.op("dve", lambda e, sgt=sgt, c1=cts[1], cp=cp: e.tensor_tensor(out=act[:, cp, 0:N], in0=sgt[:, 0:N], in1=c1[:, 0:N], op=ALU.mult), reads=[sgt, cts[1]], writes=[act], partial=(cp > 0))
            if state_out_ap is not None:
                for blk in range(11):
                    bb = mmB_ring.next()
                    for kc in range(8):
                        P.op("pe", lambda e, kc=kc, bb=bb, blk=blk: e.matmul(out=PSB[bb][0:2, :], lhsT=hT[:, kc, N - 2:N], rhs=w_up[:, kc, blk * 512:(blk + 1) * 512], start=(kc == 0), stop=(kc == 7)), reads=[hT, w_up], writes=[PSB[bb]], partial=True)
                    ul = ul_ring.next()
                    P.op("act", lambda e, bb=bb, ul=ul: e.copy(out=ul[0:2, :], in_=PSB[bb][0:2, :]), reads=[PSB[bb]], writes=[ul])
                    P.dma("pool", state_out_ap[:, blk * 512:(blk + 1) * 512], ul[0:2, :], key=ul, reads=[ul], writes=[state_out_unit], partial=True, final=True)
            nb = len(res_blocks)
            for j in range(nb):
                src_ap, src_units, dst_ap, dst_units, fin, n = res_blocks[j]
                out_proj_residual(act, n, w_dn, 22, src_ap, src_units, dst_ap, dst_units, fin, off=j * 128)

        def blocks_for_tile(t):
            return [(prompt_src(2 * t + j)[0], prompt_src(2 * t + j)[1], 128) for j in range(2)]

        P.op("pool", lambda e: e.memset(hist[:, :, :], 0.0), writes=[hist])
        pre = norm_front(blocks_for_tile(0), gffn, layer)
        for t in range(NBLK // 2):
            cur = pre
            if t + 1 < NBLK // 2:
                pre = norm_front(blocks_for_tile(t + 1), gffn, layer)
            else:
                pre = norm_front([(sample_src(0)[0], sample_src(0)[1], 4)], gffn, layer)
            res = []
            for j in range(2):
                i = 2 * t + j
                sa, su = prompt_src(i)
                res.append((sa, su, yp.h[i * 128:(i + 1) * 128, :], [yp_blk[i]], last, 128))
            ffn_tile(cur[0], 256, res, fcp.h[layer * 2:(layer + 1) * 2, :] if t == NBLK // 2 - 1 else None, fcp)
        for b in range(4):
            cur = pre
            if b + 1 < 4:
                pre = norm_front([(sample_src(b + 1)[0], sample_src(b + 1)[1], 4)], gffn, layer)
            r0 = (layer * 4 + b) * 2
            bb = mmB_ring.next()
            for q11 in range(11):
                P.dma("sp", hrow[0:2, :], sfc.h[r0:r0 + 2, q11 * 512:(q11 + 1) * 512], key=hrow, writes=[hrow])
                for cc in range(4):
                    ch = q11 * 4 + cc
                    P.op("pe", lambda e, bb=bb, cc=cc, ch=ch: e.matmul(out=PSB[bb][:, ch * 2:ch * 2 + 2], lhsT=hrow[0:2, cc * 128:(cc + 1) * 128], rhs=identf[0:2, 0:2], start=True, stop=True), reads=[hrow, identf], writes=[PSB[bb]], partial=True)
            P.op("act", lambda e, bb=bb: e.copy(out=hist[:, :, :], in_=PSB[bb][:, 0:88].rearrange("p (c t) -> p c t", t=2)), reads=[PSB[bb]], writes=[hist])
            sa, su = sample_src(b)
            res = [(sa, su, ys.h[b * 4:(b + 1) * 4, :], [ys_blk], last, 4)]
            r1 = (layer * 4 + b) * 2
            ffn_tile(cur[0], 4, res, fcs.h[r1:r1 + 2, :], fcs)
        RR.mmA = Ring([(2, 3), (4, 5)])
        RR.mmB = Ring([6, 7])
        state["first"] = False


    def dil_setup():
        phase_begin(1, 16)
        relb = P.sb("d_relb", [NB, 48], F32)
        ohp = P.sb("d_ohp", [NB, 3 * 384], F32)
        vld = P.sb("d_vld", [16, 3 * 384], F32)
        P.dma("sp", relb[:, :], rel_bias.h[:, :], key=relb, writes=[relb])
        P.dma("sp", ohp[:, :], c_ohp.h[:, :], key=ohp, writes=[ohp])
        P.dma("sp", vld[:, :], bass.AP(c_valid.h, 0, [[0, 16], [1, 3 * 384]]), key=vld, writes=[vld])
        for g in range(3):
            bb = mmB_ring.next()
            P.op("pe", lambda e, bb=bb, g=g: e.matmul(out=PSB[bb][0:16, 0:384], lhsT=relb[:, g * 16:(g + 1) * 16], rhs=ohp[:, g * 384:(g + 1) * 384], start=True, stop=True), reads=[relb, ohp], writes=[PSB[bb]])
            wv = P.sb("d_wv%d" % g, [16, 384], F32)
            wvb = P.sb("d_wvb%d" % g, [16, 384], F32)
            P.op("act", lambda e, bb=bb, wv=wv: e.activation(out=wv[:, :], in_=PSB[bb][0:16, 0:384], func=AF.Exp), reads=[PSB[bb]], writes=[wv])
            P.op("dve", lambda e, wv=wv, wvb=wvb, g=g: e.tensor_tensor(out=wvb[:, :], in0=wv[:, :], in1=vld[:, g * 384:(g + 1) * 384], op=ALU.mult), reads=[wv, vld], writes=[wvb])
            P.dma("sp", wsc.h[g * 16:(g + 1) * 16, :], wvb[:, :], key=wvb, reads=[wvb], writes=[wsc], partial=True)

    def dil_group(layer, li, g, last):
        _chk(1)
        W, d = GROUPS[g]
        nbk = (SEQ // d) // 128
        RR.tp = Ring([0])
        RR.mmA = Ring([(2, 3)])
        RR.mmB = Ring([7])
        sc_ring = Ring([1, 7])
        UB = (4, 5, 6)
        phase_begin(1, 512)
        Wg = P.sb("d_Wg", [128, 8, 3072], BF16)
        load_w(Wg, slice(0, 3072), dil_w_in.h, li * D, D, g * 3072, 3072)
        w_out = None
        if g == 0:
            w_out = P.sb("d_wout", [128, 8, D], BF16)
            load_w(w_out, slice(0, D), dil_w_out.h, li * D, D, 0, D)
        qg = P.sb("d_qg", [128, 64], F32)
        kg = P.sb("d_kg", [128, 64], F32)
        P.dma("sp", qg[:, :], bass.AP(dil_qn.h, li * 64, [[0, 128], [1, 64]]), key=qg, writes=[qg])
        P.dma("sp", kg[:, :], bass.AP(dil_kn.h, li * 64, [[0, 128], [1, 64]]), key=kg, writes=[kg])
        gcol = P.sb("d_gcol", [128, 2], F32)
        for ci, srch in ((0, dil_qn), (1, dil_kn)):
            P.dma("sp", rowtmp[0:1, 0:64], srch.h[li:li + 1, :], key=rowtmp, writes=[rowtmp])
            P.dma("sp", rowtmp[0:1, 64:128], srch.h[li:li + 1, :], key=rowtmp, writes=[rowtmp], partial=True)
            bb = mmB_ring.next()
            P.op("pe", lambda e, bb=bb: e.matmul(out=PSB[bb][:, 0:1], lhsT=rowtmp[0:1, 0:128], rhs=identf[0:1, 0:1], start=True, stop=True), reads=[rowtmp, identf], writes=[PSB[bb]])
            P.op("act", lambda e, bb=bb, ci=ci: e.copy(out=gcol[:, ci:ci + 1], in_=PSB[bb][:, 0:1]), reads=[PSB[bb]], writes=[gcol], partial=(ci > 0))
        relb = P.sb("d_relb", [NB, 48], F32)
        P.dma("sp", relb[:, :], rel_bias.h[:, :], key=relb, writes=[relb])
        M = P.sb("d_M", [128, 16, 256], BF16)
        H_ring = Ring([P.sb("d_H%d" % i, [128, 256], F32) for i in range(2)])
        for h in range(16):
            H = H_ring.next()
            P.dma("sp", H[:, :], bass.AP(wsc.h, (g * 16 + h) * 384, [[1, 128], [1, 256]]), key=H, reads=[wsc], writes=[H])
            bb = mmB_ring.next()
            P.op("pe", lambda e, bb=bb, H=H: e.matmul(out=PSB[bb][:, 0:256], lhsT=Jf[:, :], rhs=H[:, :], start=True, stop=True), reads=[Jf, H], writes=[PSB[bb]])
            P.op("act", lambda e, bb=bb, h=h: e.copy(out=M[:, h, :], in_=PSB[bb][:, 0:256]), reads=[PSB[bb]], writes=[M], partial=(h > 0))
        _chk(2)
        qS = P.sb("d_qS", [16, D], F32)
        kS = P.sb("d_kS", [16, D], F32)
        vS = P.sb("d_vS", [16, D], F32)
        MARK = P.off
        nrm_r = Ring([P.sb("d_nrm%d" % i, [128, D], F32) for i in range(2)])
        st16 = Ring([P.sb("d_st%d" % i, [128, 32], F32) for i in range(2)])
        kout = P.sb("d_ko", [128, D], F32)
        vout = P.sb("d_vo", [128, D], F32)
        qbf_r = Ring([P.sb("d_qb%d" % i, [128, D], BF16) for i in range(2)])
        qT_r = Ring([P.sb("d_qT%d" % i, [128, 16, 128], BF16) for i in range(2)])
        for t in qT_r.items:
            P.op("pool", lambda e, t=t: e.memset(t[:, :, :], 0.0), writes=[t])
        kT_r = Ring([P.sb("d_kT%d" % i, [128, 8, 128], BF16) for i in range(3)])
        va_r = Ring([P.sb("d_va%d" % i, [128, 16, 80], BF16) for i in range(3)])
        pe_r = Ring([P.sb("d_pe%d" % i, [128, 512], BF16) for i in range(2)])
        pt_r = Ring([P.sb("d_pt%d" % i, [128, 512], BF16) for i in range(2)])
        U_r = Ring([P.sb("d_U%d" % i, [128, 1280], F32) for i in range(2)])
        Ua = P.sb("d_Ua", [128, 1280], F32)
        for t in U_r.items:
            P.op("pool", lambda e, t=t: e.memset(t[:, :], 0.0), writes=[t])
        rden = P.sb("d_rden", [128, 16], F32)
        for t in va_r.items:
            P.op("pool", lambda e, t=t: e.memset(t[:, :, :], 1.0), writes=[t])

        def qkv_block(hT, c0, n, cache_k_ap, cache_v_ap, cache_ku, cache_vu, sample):
            outs = []
            for s_ in range(3):
                pair = mmA_ring.next()
                for half in range(2):
                    b = PSB[pair[half]]
                    for kc in range(8):
                        P.op("pe", lambda e, kc=kc, b=b, half=half, s_=s_: e.matmul(out=b[0:n, :], lhsT=hT[:, kc, c0:c0 + n], rhs=Wg[:, kc, s_ * 1024 + half * 512:s_ * 1024 + (half + 1) * 512], start=(kc == 0), stop=(kc == 7)), reads=[hT, Wg], writes=[b], partial=True)
                if s_ == 2:
                    if sample:
                        for half in range(2):
                            b = PSB[pair[half]]
                            P.op("act", lambda e, b=b, half=half: e.copy(out=vS[0:n, half * 512:(half + 1) * 512], in_=b[0:n, :]), reads=[b], writes=[vS], partial=(half > 0))
                        P.dma("pool", cache_v_ap, vS[0:n, :], key=vS, reads=[vS], writes=[cache_vu], partial=True, final=True)
                        outs.append(vS)
                        continue
                    va = va_r.next()
                    for half in range(2):
                        b = PSB[pair[half]]
                        P.op("act", lambda e, b=b, half=half, va=va: e.copy(out=va[0:n, half * 8:(half + 1) * 8, 0:64], in_=b[0:n, :].rearrange("p (h e) -> p h e", e=64)), reads=[b], writes=[va], partial=(half > 0))
                    if cache_v_ap is not None:
                        for half in range(2):
                            b = PSB[pair[half]]
                            P.op("act", lambda e, b=b, half=half: e.copy(out=vout[0:n, half * 512:(half + 1) * 512], in_=b[0:n, :]), reads=[b], writes=[vout], partial=(half > 0))
                        P.dma("pool", cache_v_ap, vout[0:n, :], key=vout, reads=[vout], writes=[cache_vu], partial=True, final=True)
                    outs.append(va)
                    continue
                st = st16.next()
                nrm = nrm_r.next()
                for half in range(2):
                    b = PSB[pair[half]]
                    P.op("act", lambda e, b=b, half=half, nrm=nrm: e.copy(out=nrm[0:n, half * 512:(half + 1) * 512], in_=b[0:n, :]), reads=[b], writes=[nrm], partial=(half > 0))
                P.op("act", lambda e, nrm=nrm: e.activation(out=sq_scr[0:n, :], in_=nrm[0:n, :], func=AF.Square), reads=[nrm], writes=[sq_scr])
                P.op("dve", lambda e, st=st: e.tensor_reduce(out=st[0:n, 0:16], in_=sq_scr[0:n, :].rearrange("p (h e) -> p h e", e=64), axis=AX.X, op=ALU.add), reads=[sq_scr], writes=[st])
                rstd_from_ss(st, slice(0, 16), slice(16, 32), n, 1.0 / 64)
                gain = qg if s_ == 0 else kg
                need_f32 = sample or (s_ == 1 and cache_k_ap is not None)
                if not need_f32:
                    dst = qbf_r.next()
                    P.op("dve", lambda e, st=st, nrm=nrm, dst=dst: e.tensor_tensor(out=dst[0:n, :].rearrange("p (h e) -> p h e", e=64), in0=nrm[0:n, :].rearrange("p (h e) -> p h e", e=64), in1=st[0:n, 16:32].unsqueeze(2).broadcast_to([n, 16, 64]), op=ALU.mult), reads=[nrm, st], writes=[dst])
                    outs.append(dst)
                    continue
                P.op("dve", lambda e, st=st, nrm=nrm: e.tensor_tensor(out=nrm[0:n, :].rearrange("p (h e) -> p h e", e=64), in0=nrm[0:n, :].rearrange("p (h e) -> p h e", e=64), in1=st[0:n, 16:32].unsqueeze(2).broadcast_to([n, 16, 64]), op=ALU.mult), reads=[nrm, st], writes=[nrm])
                if sample:
                    full = qS if s_ == 0 else kS
                else:
                    full = kout
                    dst = qbf_r.next()
                    P.op("act", lambda e, nrm=nrm, dst=dst: e.copy(out=dst[0:n, :], in_=nrm[0:n, :]), reads=[nrm], writes=[dst])
                    outs.append(dst)
                P.op("pool", lambda e, full=full, gain=gain, nrm=nrm: e.tensor_tensor(out=full[0:n, :].rearrange("p (h e) -> p h e", e=64), in0=nrm[0:n, :].rearrange("p (h e) -> p h e", e=64), in1=gain[0:n, :].unsqueeze(1).broadcast_to([n, 16, 64]), op=ALU.mult), reads=[nrm, gain], writes=[full])
                if s_ == 1:
                    P.dma("pool", cache_k_ap, full[0:n, :], key=full, reads=[full], writes=[cache_ku], partial=True, final=True)
                if sample:
                    outs.append(full)
            return outs

        def to_featmajor(src, dstT, ci, split=False):
            tb = tp_ring.next()
            tpv = PSB[tb].h
            for c in range(8):
                P.op("pe", lambda e, c=c, tpv=tpv, src=src: e.transpose(out=tpv[:, c * 128:(c + 1) * 128], in_=src[:, c * 128:(c + 1) * 128], identity=ident[:, :]), reads=[src, ident], writes=[PSB[tb]], partial=True)
            if split:
                P.op("act", lambda e, tpv=tpv: e.activation(out=dstT[0:64, 0:8, :], in_=tpv[0:64, :].rearrange("p (c t) -> p c t", t=128), func=AF.Copy, scale=gcol[0:64, ci:ci + 1]), reads=[PSB[tb], gcol], writes=[dstT])
                P.op("act", lambda e, tpv=tpv: e.activation(out=dstT[64:128, 8:16, :], in_=tpv[64:128, :].rearrange("p (c t) -> p c t", t=128), func=AF.Copy, scale=gcol[64:128, ci:ci + 1]), reads=[PSB[tb], gcol], writes=[dstT], partial=True)
            else:
                P.op("act", lambda e, tpv=tpv: e.activation(out=dstT[:, :, :], in_=tpv[:, :].rearrange("p (c t) -> p c t", t=128), func=AF.Copy, scale=gcol[:, ci:ci + 1]), reads=[PSB[tb], gcol], writes=[dstT])

        hTs, _ = norm_front([(sample_src_all()[0], sample_src_all()[1], 16)], gmix, layer)
        qkv_block(hTs, 0, 16, kso[g].h[li * 16:(li + 1) * 16, :], vso[g].h[li * 16:(li + 1) * 16, :], kso[g], vso[g], True)

        _chk(3)
        blocks = [(r, n) for r in range(d) for n in range(nbk)]

        def blk_src(r, n):
            base = yp.h
            units = [yp_blk[i] for i in range(n * d, (n + 1) * d)]
            return bass.AP(base, (n * 128 * d + r) * D, [[d * D, 128], [1, D]]), units

        def tile_blocks(t):
            return [(blk_src(*blocks[4 * t + j])[0], blk_src(*blocks[4 * t + j])[1], 128) for j in range(4)]

        prev_kT = None
        prev_va = None
        hTcur = {"t": 0, "hT": norm_front(tile_blocks(0), gmix, layer)[0]}

        def front(bi):
            t_, j_ = bi // 4, bi % 4
            if t_ != hTcur["t"]:
                hTcur["t"] = t_
                hTcur["hT"] = norm_front(tile_blocks(t_), gmix, layer)[0]
            r_, n_ = blocks[bi]
            ck_ap = cv_ap = None
            if n_ == nbk - 1:
                ck_ap = bass.AP(kpo[g].h, (li * keep[g] + r_) * D, [[d * D, 128], [1, D]])
                cv_ap = bass.AP(vpo[g].h, (li * keep[g] + r_) * D, [[d * D, 128], [1, D]])
            qb, kb, va_ = qkv_block(hTcur["hT"], j_ * 128, 128, ck_ap, cv_ap, kpo[g], vpo[g], False)
            qT_ = qT_r.next()
            kT_ = kT_r.next()
            to_featmajor(qb, qT_, 0, split=True)
            to_featmajor(kb, kT_, 1)
            return qT_, kT_, va_

        nxt = front(0)
        for t in range(NT4):
            for j in range(4):
                bi = 4 * t + j
                r, n = blocks[bi]
                qT, kT, va = nxt
                if bi + 1 < len(blocks):
                    nxt = front(bi + 1)
                _chk(7)
                has_prev = n > 0
                def qk_scores(hp, kT=kT, qT=qT, has_prev=has_prev, pk=prev_kT):
                    bb = sc_ring.next()
                    for hh in range(2):
                        P.op("pe", lambda e, bb=bb, hh=hh, hp=hp: e.matmul(out=PSF[bb][:, hh * 256:hh * 256 + 128], lhsT=kT[:, hp, :], rhs=qT[:, hh * 8 + hp, :], start=True, stop=True), reads=[kT, qT], writes=[PSB[bb]], partial=True)
                        if has_prev:
                            P.op("pe", lambda e, bb=bb, hh=hh, hp=hp: e.matmul(out=PSF[bb][:, hh * 256 + 128:hh * 256 + 256], lhsT=pk[:, hp, :], rhs=qT[:, hh * 8 + hp, :], start=True, stop=True), reads=[pk, qT], writes=[PSB[bb]], partial=True)
                    return bb

                bb_next = qk_scores(0)
                for hp in range(8):
                    bb = bb_next
                    if hp + 1 < 8:
                        bb_next = qk_scores(hp + 1)
                    pe_t = pe_r.next()
                    pt = pt_r.next()
                    wdt = 256 if has_prev else 128
                    P.op("act", lambda e, bb=bb, pe_t=pe_t, wdt=wdt: e.activation(out=pe_t[:, :].rearrange("p (h x) -> p h x", x=256)[:, :, 0:wdt], in_=PSF[bb][:, :].rearrange("p (h x) -> p h x", x=256)[:, :, 0:wdt], func=AF.Exp, scale=0.125), reads=[PSB[bb]], writes=[pe_t])
                    P.op("dve", lambda e, pe_t=pe_t, pt=pt, hp=hp, wdt=wdt: e.tensor_tensor(out=pt[:, :].rearrange("p (h x) -> p h x", x=256)[:, :, 0:wdt], in0=pe_t[:, :].rearrange("p (h x) -> p h x", x=256)[:, :, 0:wdt], in1=M[:, 2 * hp:2 * hp + 2, 0:wdt], op=ALU.mult), reads=[pe_t, M], writes=[pt])
                    for hh in range(2):
                        h = 2 * hp + hh
                        ub = PSB[UB[h // 6]]
                        c0 = (h % 6) * 80
                        P.op("pe", lambda e, ub=ub, c0=c0, pt=pt, hh=hh, va=va, h=h, has_prev=has_prev: e.matmul(out=ub[:, c0:c0 + 65], lhsT=pt[:, hh * 256:hh * 256 + 128], rhs=va[:, h, 0:65], start=True, stop=(not has_prev)), reads=[pt, va], writes=[ub], partial=True)
                        if has_prev:
                            P.op("pe", lambda e, ub=ub, c0=c0, pt=pt, hh=hh, pv=prev_va, h=h: e.matmul(out=ub[:, c0:c0 + 65], lhsT=pt[:, hh * 256 + 128:hh * 256 + 256], rhs=pv[:, h, 0:65], start=False, stop=True), reads=[pt, prev_va], writes=[ub], partial=True)
                _chk(8)
                prev_kT, prev_va = kT, va
                U = U_r.next()
                for bi, (a0, a1) in enumerate(((0, 480), (480, 960), (960, 1280))):
                    P.op("act", lambda e, bi=bi, a0=a0, a1=a1, U=U: e.copy(out=U[:, a0:a1].rearrange("p (h e) -> p h e", e=80)[:, :, 0:65], in_=PSB[UB[bi]][:, 0:a1 - a0].rearrange("p (h e) -> p h e", e=80)[:, :, 0:65]), reads=[PSB[UB[bi]]], writes=[U], partial=(bi > 0))
                if g != 0:
                    dst = bass.AP(ug_scr[g - 1].h, (n * 128 * d + r) * 1280, [[d * 1280, 128], [1, 1280]])
                    P.dma("pool", dst, U[:, :], key=U, reads=[U], writes=[ug_scr[g - 1]], partial=True)
                else:
                    i = n
                    for gi in range(2):
                        P.dma("sp", Ua[:, :], ug_scr[gi].h[i * 128:(i + 1) * 128, :], key=Ua, reads=[ug_scr[gi]], writes=[Ua])
                        P.op("pool", lambda e, U=U: e.tensor_tensor(out=U[:, :], in0=U[:, :], in1=Ua[:, :], op=ALU.add), reads=[U, Ua], writes=[U])
                    finish_attn(U, 128, w_out, rden, prompt_src(i)[0], prompt_src(i)[1], yp.h[i * 128:(i + 1) * 128, :], [yp_blk[i]], last)
                _chk(9)
        return MARK, qS, kS, vS, relb, w_out, rden

    def finish_attn(U, n, w_out, rden, src_ap, src_units, dst_ap, dst_units, last):
        Uv = U[0:n, :].rearrange("p (h e) -> p h e", e=80)
        P.op("dve", lambda e: e.reciprocal(out=rden[0:n, :], in_=Uv[:, :, 64]), reads=[U], writes=[rden])
        on = on_ring.next()
        P.op("dve", lambda e, on=on: e.tensor_tensor(out=on[0:n, :].rearrange("p (h e) -> p h e", e=64), in0=Uv[:, :, 0:64], in1=rden[0:n, :].unsqueeze(2).broadcast_to([n, 16, 64]), op=ALU.mult), reads=[U, rden], writes=[on])
        onT = onT_ring.next()
        transpose_to(on, n, onT)
        out_proj_residual(onT, n, w_out, 8, src_ap, src_units, dst_ap, dst_units, last)

    def dil_sample(layer, li, g, ctx, Us_acc, first_group, last):
        MARK, qS, kS, vS, relb, w_out, rden = ctx
        _chk(4)
        W, d = GROUPS[g]
        P.barrier()
        P.off = MARK
        UB = (4, 5, 6)
        ohs = P.sb("s_ohs", [NB, 6 * 128], F32)
        P.dma("sp", ohs[:, :], c_ohs.h[:, :], key=ohs, writes=[ohs])
        sel = P.sb("s_sel", [16, 16, 128], F32)
        selT = P.sb("s_selT", [128, 16, 16], F32)
        P.op("dve", lambda e: e.tensor_copy(out=sel[:, :, :], in_=identf[0:16, 0:16].unsqueeze(2).broadcast_to([16, 16, 128])), reads=[identf], writes=[sel])
        P.op("pool", lambda e: e.memset(selT[:, :, :], 0.0), writes=[selT])
        for t in range(16):
            P.op("pool", lambda e, t=t: e.memset(selT[:, t, t:t + 1], 1.0), writes=[selT])
        nvar = 4 if g == 0 else 1
        BS = P.sb("s_BS", [128, 4, 16], F32)
        for v in range(nvar):
            vv = v if g == 0 else 3 + g
            bb = mmB_ring.next()
            P.op("pe", lambda e, bb=bb, vv=vv: e.matmul(out=PSB[bb][:, 0:16], lhsT=ohs[:, vv * 128:(vv + 1) * 128], rhs=relb[:, g * 16:(g + 1) * 16], start=True, stop=True), reads=[ohs, relb], writes=[PSB[bb]])
            P.op("act", lambda e, bb=bb, v=v: e.activation(out=BS[:, v, :], in_=PSB[bb][:, 0:16], func=AF.Exp), reads=[PSB[bb]], writes=[BS], partial=(v > 0))
        eb0 = P.sb("s_eb0", [16, 16], F32)
        P.dma("sp", eb0[:, :], bass.AP(rel_bias.h, g * 16, [[0, 16], [1, 16]]), key=eb0, writes=[eb0])
        P.op("act", lambda e: e.activation(out=eb0[:, :], in_=eb0[:, :], func=AF.Exp), reads=[eb0], writes=[eb0])
        _chk(5)
        Kt_r = Ring([P.sb("s_Kt%d" % i, [128, D], F32) for i in range(2)])
        Vt_r = Ring([P.sb("s_Vt%d" % i, [128, D], F32) for i in range(2)])
        prod = P.sb("s_prod", [128, D], F32)
        sc_r = Ring([P.sb("s_sc%d" % i, [128, 16], F32) for i in range(2)])
        pw_r = Ring([P.sb("s_pw%d" % i, [128, 16], F32) for i in range(2)])
        Wt_r = Ring([P.sb("s_Wt%d" % i, [128, 1280], F32) for i in range(2)])
        for t in Wt_r.items:
            P.op("pool", lambda e, t=t: e.memset(t[:, :], 0.0), writes=[t])
        Usg = P.sb("s_Usg", [16, 1280], F32)
        p16 = P.sb("s_p16", [16, 32], F32)
        tmp16 = P.sb("s_tmp16", [16, D], F32)
        rden = P.sb("s_rden", [128, 16], F32)
        cnt = 0
        for b in range(4):
            for s_ in range(4):
                tk = 4 * b + s_
                Kt = Kt_r.next()
                Vt = Vt_r.next()
                base = (li * 4 + b) * W
                for (dstt, cache, newo) in ((Kt, ck[g], kso[g]), (Vt, cv[g], vso[g])):
                    if g == 0:
                        P.dma("sp", dstt[s_:128, :], cache.h[base + s_:base + 128, :], key=dstt, writes=[dstt])
                        if s_ > 0:
                            P.dma("sp", dstt[0:s_, :], newo.h[li * 16 + 4 * b:li * 16 + 4 * b + s_, :], key=dstt, reads=[newo], writes=[dstt], partial=True)
                    else:
                        P.dma("sp", dstt[:, :], bass.AP(cache.h, (base + s_) * D, [[d * D, 128], [1, D]]), key=dstt, writes=[dstt])
                pair = mmA_ring.next()
                for half in range(2):
                    bq = PSB[pair[half]]
                    P.op("pe", lambda e, bq=bq, half=half, tk=tk: e.matmul(out=bq[:, :], lhsT=sel[:, tk, :], rhs=qS[0:16, half * 512:(half + 1) * 512], start=True, stop=True), reads=[sel, qS], writes=[bq])
                    P.op("dve", lambda e, bq=bq, half=half, Kt=Kt: e.tensor_tensor(out=prod[:, half * 512:(half + 1) * 512], in0=bq[:, :], in1=Kt[:, half * 512:(half + 1) * 512], op=ALU.mult), reads=[bq, Kt], writes=[prod], partial=(half > 0))
                sc = sc_r.next()
                pw = pw_r.next()
                P.op("dve", lambda e, sc=sc: e.tensor_reduce(out=sc[:, :], in_=prod[:, :].rearrange("p (h e) -> p h e", e=64), axis=AX.X, op=ALU.add), reads=[prod], writes=[sc])
                P.op("act", lambda e, sc=sc: e.activation(out=sc[:, :], in_=sc[:, :], func=AF.Exp, scale=0.125), reads=[sc], writes=[sc])
                var = s_ if g == 0 else 0
                P.op("dve", lambda e, sc=sc, pw=pw, var=var: e.tensor_tensor(out=pw[:, :], in0=sc[:, :], in1=BS[:, var, :], op=ALU.mult), reads=[sc, BS], writes=[pw])
                Wt = Wt_r.next()
                Wv = Wt[:, :].rearrange("p (h e) -> p h e", e=80)
                P.op("dve", lambda e, Wv=Wv, Vt=Vt, pw=pw: e.tensor_tensor(out=Wv[:, :, 0:64], in0=Vt[:, :].rearrange("p (h e) -> p h e", e=64), in1=pw[:, :].unsqueeze(2).broadcast_to([128, 16, 64]), op=ALU.mult), reads=[Vt, pw], writes=[Wt])
                P.op("pool", lambda e, Wv=Wv, pw=pw: e.tensor_copy(out=Wv[:, :, 64], in_=pw[:, :]), reads=[pw], writes=[Wt])
                for bi, (a0, a1) in enumerate(((0, 480), (480, 960), (960, 1280))):
                    P.op("pe", lambda e, bi=bi, a0=a0, a1=a1, Wt=Wt, tk=tk, cnt=cnt: e.matmul(out=PSB[UB[bi]][0:16, 0:a1 - a0], lhsT=selT[:, tk, :], rhs=Wt[:, a0:a1], start=(cnt == 0), stop=(cnt == 15)), reads=[selT, Wt], writes=[PSB[UB[bi]]], partial=True)
                cnt += 1
        for bi, (a0, a1) in enumerate(((0, 480), (480, 960), (960, 1280))):
            P.op("act", lambda e, bi=bi, a0=a0, a1=a1: e.copy(out=Usg[:, a0:a1], in_=PSB[UB[bi]][0:16, 0:a1 - a0]), reads=[PSB[UB[bi]]], writes=[Usg], partial=(bi > 0))
        P.op("dve", lambda e: e.tensor_tensor(out=tmp16[:, :], in0=qS[:, :], in1=kS[:, :], op=ALU.mult), reads=[qS, kS], writes=[tmp16])
        P.op("dve", lambda e: e.tensor_reduce(out=p16[:, 0:16], in_=tmp16[:, :].rearrange("p (h e) -> p h e", e=64), axis=AX.X, op=ALU.add), reads=[tmp16], writes=[p16])
        P.op("act", lambda e: e.activation(out=p16[:, 0:16], in_=p16[:, 0:16], func=AF.Exp, scale=0.125), reads=[p16], writes=[p16])
        P.op("dve", lambda e: e.tensor_tensor(out=p16[:, 16:32], in0=p16[:, 0:16], in1=eb0[:, :], op=ALU.mult), reads=[p16, eb0], writes=[p16])
        Ugv = Usg[:, :].rearrange("p (h e) -> p h e", e=80)
        P.op("dve", lambda e: e.tensor_tensor(out=tmp16[:, :].rearrange("p (h e) -> p h e", e=64), in0=vS[:, :].rearrange("p (h e) -> p h e", e=64), in1=p16[:, 16:32].unsqueeze(2).broadcast_to([16, 16, 64]), op=ALU.mult), reads=[vS, p16], writes=[tmp16])
        P.op("dve", lambda e: e.tensor_tensor(out=Ugv[:, :, 0:64], in0=Ugv[:, :, 0:64], in1=tmp16[:, :].rearrange("p (h e) -> p h e", e=64), op=ALU.add), reads=[Usg, tmp16], writes=[Usg])
        P.op("dve", lambda e: e.tensor_tensor(out=Ugv[:, :, 64], in0=Ugv[:, :, 64], in1=p16[:, 16:32], op=ALU.add), reads=[Usg, p16], writes=[Usg])
        if first_group:
            P.op("dve", lambda e: e.tensor_copy(out=Us_acc[:, :], in_=Usg[:, :]), reads=[Usg], writes=[Us_acc])
        else:
            P.op("dve", lambda e: e.tensor_tensor(out=Us_acc[:, :], in0=Us_acc[:, :], in1=Usg[:, :], op=ALU.add), reads=[Us_acc, Usg], writes=[Us_acc])
        if g == 0:
            sa, su = sample_src_all()
            finish_attn(Us_acc, 16, w_out, rden, sa, su, ys.h[0:16, :], [ys_blk], last)

    def dil_layer(layer, li, last):
        Us_acc = T("Us_acc", Us_acc_t.h)
        for gi, g in enumerate((2, 1, 0)):
            ctx = dil_group(layer, li, g, last)
            dil_sample(layer, li, g, ctx, Us_acc, gi == 0, last)
        RR.tp = Ring([0, 1])
        RR.mmA = Ring([(2, 3), (4, 5)])
        RR.mmB = Ring([6, 7])
        state["first"] = False

    Us_acc_t = P.sb("Us_acc", [16, 1280], F32)
    PERSIST = P.off
    dil_setup()
    try:
        for layer in range(depth):
            li = layer // 2
            if layer % 2 == 0:
                gla_layer(layer, li, (layer == depth - 1) and not do_ffn)
            else:
                dil_layer(layer, li, (layer == depth - 1) and not do_ffn)
            if do_ffn:
                ffn_layer(layer, layer == depth - 1)
    except _Stop:
        pass
    P.emit()
    return nc, P


_CACHE = {}


def kernel(x_prompt, x_sample, state_gla, cache_k_g0, cache_v_g0, cache_k_g1, cache_v_g1,
           cache_k_g2, cache_v_g2, state_ffn_conv, rel_bias, norm_mix, norm_ffn,
           gla_w_in, gla_w_gate2, gla_b_gate, gla_norm, gla_w_out,
           dil_w_in, dil_q_norm, dil_k_norm, dil_w_out,
           ffn_w_up, ffn_conv_w, ffn_conv_b, ffn_w_down):
    f = lambda a: np.ascontiguousarray(np.asarray(a, dtype=np.float32))
    if "nc" not in _CACHE:
        _CACHE["nc"] = build_program()[0]
    nc = _CACHE["nc"]
    ohp, valid, ohs = host_constants()
    cks = [f(cache_k_g0), f(cache_k_g1), f(cache_k_g2)]
    cvs = [f(cache_v_g0), f(cache_v_g1), f(cache_v_g2)]
    x_prompt = f(x_prompt); x_sample = f(x_sample); state_gla = f(state_gla); state_ffn_conv = f(state_ffn_conv)
    shared = {
        "rel_bias": f(rel_bias), "norm_mix": f(norm_mix), "norm_ffn": f(norm_ffn),
        "gla_w_in": f(gla_w_in).reshape(2 * D, GLA_IN), "gla_w_gate2": f(gla_w_gate2).reshape(32, 512),
        "gla_b_gate": f(gla_b_gate), "gla_norm": f(gla_norm), "gla_w_out": f(gla_w_out).reshape(2 * D, D),
        "dil_w_in": f(dil_w_in).reshape(2 * D, 9216), "dil_q_norm": f(dil_q_norm), "dil_k_norm": f(dil_k_norm),
        "dil_w_out": f(dil_w_out).reshape(2 * D, D), "ffn_w_up": f(ffn_w_up).reshape(4 * D, 2 * DFF),
        "ffn_conv_w": f(ffn_conv_w).reshape(12, 2 * DFF), "ffn_conv_b": f(ffn_conv_b),
        "ffn_w_down": f(ffn_w_down).reshape(4 * DFF, D),
        "c_ohp": ohp, "c_valid": valid, "c_ohs": ohs,
    }
    in_maps = []
    for c in range(8):
        m = dict(shared)
        m["xp"] = x_prompt[c % 4]
        m["xs"] = x_sample[4 * c:4 * c + 4].reshape(16, D)
        m["sg"] = np.ascontiguousarray(state_gla[:, 4 * c:4 * c + 4]).reshape(2 * 4 * 4 * 128, 256)
        for g in range(3):
            Wg = GROUPS[g][0]
            m["ck%d" % g] = np.ascontiguousarray(cks[g][:, 4 * c:4 * c + 4]).reshape(2 * 4 * Wg, D)
            m["cv%d" % g] = np.ascontiguousarray(cvs[g][:, 4 * c:4 * c + 4]).reshape(2 * 4 * Wg, D)
        m["sfc"] = np.ascontiguousarray(state_ffn_conv[:, 4 * c:4 * c + 4]).reshape(32, 2 * DFF)
        in_maps.append(m)
    res = run_bass_kernel_spmd(nc, in_maps, core_ids=list(range(8)))
    R = res.results
    B = 4
    y_prompt = np.stack([R[b]["yp"] for b in range(B)]).astype(np.float32)
    y_sample = np.concatenate([R[c]["ys"].reshape(4, 4, D) for c in range(8)], 0).astype(np.float32)
    sgp = np.stack([R[b]["sgp"].reshape(2, 4, 128, 256) for b in range(B)], 1).astype(np.float32)
    sgs = np.concatenate([R[c]["sgs"].reshape(2, 4, 4, 128, 256) for c in range(8)], 1).astype(np.float32)
    outs = [y_prompt, y_sample, sgp, sgs]
    keep = [128, 512, 2048]
    for g in range(3):
        kp = np.stack([R[b]["kp%d" % g].reshape(2, keep[g], 16, 64) for b in range(B)], 1).astype(np.float32)
        ks = np.concatenate([R[c]["ks%d" % g].reshape(2, 4, 4, 16, 64) for c in range(8)], 1).astype(np.float32)
        vp = np.stack([R[b]["vp%d" % g].reshape(2, keep[g], 16, 64) for b in range(B)], 1).astype(np.float32)
        vs = np.concatenate([R[c]["vs%d" % g].reshape(2, 4, 4, 16, 64) for c in range(8)], 1).astype(np.float32)
        outs += [kp, ks, vp, vs]
    fcp = np.stack([R[b]["fcp"].reshape(4, 2, 2 * DFF) for b in range(B)], 1).astype(np.float32)
    fcs = np.concatenate([R[c]["fcs"].reshape(4, 4, 2, 2 * DFF) for c in range(8)], 1).astype(np.float32)
    outs += [fcp, fcs]
    return tuple(outs)
```

```python
import contextlib
import math
import numpy as np
import concourse.bass as bass
import concourse.mybir as mybir
from concourse.bass_utils import run_bass_kernel_spmd

F32 = mybir.dt.float32
BF16 = mybir.dt.bfloat16
AF = mybir.ActivationFunctionType
ALU = mybir.AluOpType
AX = mybir.AxisListType

ENGS = ("pe", "act", "dve", "pool", "sp")

D = 1024
SEQ = 4096
NSEQ_S = 4
TS = 4
DEPTH = 4
GLA_IN = 3088
DFF = 2816
EPS = 1e-6
GROUPS = ((128, 1), (512, 4), (2048, 16))
NB = 32


class T:
    __slots__ = ("name", "h", "writers", "readers", "dsem", "dcount")

    def __init__(self, name, h):
        self.name = name
        self.h = h
        self.writers = []
        self.readers = []
        self.dsem = None
        self.dcount = 0

    def __getitem__(self, k):
        return self.h[k]


class Op:
    __slots__ = ("eng", "fn", "deps", "sig", "signal", "dma_key", "pos")

    def __init__(self, eng, fn):
        self.eng = eng
        self.fn = fn
        self.deps = []
        self.sig = False
        self.signal = None
        self.dma_key = None


class Prog:
    ARENA = 207 * 1024

    def __init__(self, nc):
        self.nc = nc
        self.es = contextlib.ExitStack()
        self.streams = {e: [] for e in ENGS}
        self.out_dmas = []
        self.arena = self.es.enter_context(nc.sbuf_tensor("arena", [128, self.ARENA // 4], F32))
        self.arena_bf = self.arena.bitcast(BF16)
        self.off = 0
        self.pending = {e: [] for e in ENGS}
        self.dma_since = []
        self.peak = 0

    def sb(self, name, shape, dtype):
        isz = 4 if dtype == F32 else 2
        n = 1
        for d in shape[1:]:
            n *= d
        nbytes = (n * isz + 63) // 64 * 64
        off = self.off
        self.off += nbytes
        self.peak = max(self.peak, self.off)
        assert self.off <= self.ARENA, ("SBUF arena overflow", name, self.off)
        base = self.arena if dtype == F32 else self.arena_bf
        a = base[0:shape[0], off // isz:off // isz + n]
        if len(shape) == 3:
            a = a.rearrange("p (a b) -> p a b", b=shape[2])
        return T(name, a)

    def ps(self, name, shape, dtype):
        h = self.es.enter_context(self.nc.psum_tensor(name, list(shape), dtype))
        return T(name, h)

    def dram(self, name, shape, dtype, kind="Internal"):
        h = self.nc.dram_tensor(name, list(shape), dtype, kind=kind)
        return T(name, h)

    def barrier(self):
        deps = []
        for e in ENGS:
            for o in reversed(self.streams[e]):
                if o.dma_key is None:
                    o.sig = True
                    deps.append(o)
                    break
        deps.extend(self.dma_since)
        self.dma_since = []
        for e in ENGS:
            self.pending[e].extend(deps)

    def _track(self, op, reads, writes, partial):
        deps = []
        for t in reads:
            deps.extend(t.writers)
        for t in writes:
            others = [r for r in t.readers if r is not op]
            if others:
                deps.extend(others)
                deps.extend(t.writers)
                t.writers = [op]
                t.readers = []
            elif partial:
                t.writers.append(op)
            else:
                deps.extend(t.writers)
                t.writers = [op]
        for t in reads:
            t.readers.append(op)
        seen = set()
        best = {}
        for d in deps:
            if d is op or id(d) in seen:
                continue
            seen.add(id(d))
            if d.eng == "pe" and op.eng == "pe" and d.dma_key is None and op.dma_key is None:
                continue
            if d.dma_key is None:
                if d.eng not in best or best[d.eng].pos < d.pos:
                    best[d.eng] = d
            else:
                op.deps.append(d)
        for d in best.values():
            op.deps.append(d)
            d.sig = True

    def _pend(self, o):
        if self.pending[o.eng]:
            have = set(id(d) for d in o.deps)
            for d in self.pending[o.eng]:
                if id(d) not in have and d is not o:
                    o.deps.append(d)
            self.pending[o.eng] = []

    def op(self, eng, fn, reads=(), writes=(), partial=False):
        o = Op(eng, fn)
        o.pos = len(self.streams[eng])
        self._track(o, list(reads), list(writes), partial)
        self._pend(o)
        self.streams[eng].append(o)
        return o

    def dma(self, eng, out_ap, in_ap, key, reads=(), writes=(), partial=False, final=False, **kw):
        def fn(e):
            return e.dma_start(out=out_ap, in_=in_ap, **kw)
        o = Op(eng, fn)
        o.pos = len(self.streams[eng])
        o.dma_key = key
        o.sig = True
        self._track(o, list(reads), list(writes), partial)
        self._pend(o)
        self.streams[eng].append(o)
        self.dma_since.append(o)
        if final:
            self.out_dmas.append(o)
        return o

    def emit(self):
        nc = self.nc
        es = self.es
        esem = {e: es.enter_context(nc.semaphore("sem_" + e)) for e in ENGS}
        ecount = {e: 0 for e in ENGS}
        nkeys = 0
        semtab = {}
        for e in ENGS:
            for o in self.streams[e]:
                if o.dma_key is not None:
                    kk = (o.dma_key.name, e)
                    if kk not in semtab:
                        semtab[kk] = [es.enter_context(nc.semaphore("dsem_%s_%s" % kk)), 0]
                        nkeys += 1
                    semtab[kk][1] += 16
                    o.signal = (semtab[kk][0], semtab[kk][1], 16)
                elif o.sig:
                    ecount[e] += 1
                    o.signal = (esem[e], ecount[e], 1)
        self.ecount = ecount
        self.nkeys = nkeys
        streams = self.streams
        finals = {}
        for o in self.out_dmas:
            sem, val, _ = o.signal
            if finals.get(id(sem), (None, 0))[1] < val:
                finals[id(sem)] = (sem, val)

        def run(e, h):
            waited = {}
            for o in streams[e]:
                need = {}
                for d in o.deps:
                    sem, val, _ = d.signal
                    if need.get(id(sem), (None, 0))[1] < val:
                        need[id(sem)] = (sem, val)
                for sem, val in need.values():
                    if waited.get(id(sem), 0) < val:
                        h.wait_ge(sem, val)
                        waited[id(sem)] = val
                ins = o.fn(h)
                if o.signal is not None:
                    ins.then_inc(o.signal[0], o.signal[2])
            if e == "sp":
                for sem, val in finals.values():
                    if waited.get(id(sem), 0) < val:
                        h.wait_ge(sem, val)

        with nc.Block() as block:
            @block.tensor
            def _(h):
                run("pe", h)

            @block.scalar
            def _(h):
                run("act", h)

            @block.vector
            def _(h):
                run("dve", h)

            @block.gpsimd
            def _(h):
                run("pool", h)

            @block.sync
            def _(h):
                run("sp", h)
        es.close()


import os as _os
_STOP = int(_os.environ.get("KDBG_STOP", "0"))


class _Stop(Exception):
    pass


def _chk(k):
    if _STOP == k:
        raise _Stop()


class Ring:
    def __init__(self, items):
        self.items = items
        self.i = 0

    def next(self):
        t = self.items[self.i % len(self.items)]
        self.i += 1
        return t


def _bucket(dist):
    max_exact = NB // 2
    if dist < max_exact:
        return dist
    df = np.float32(max(dist, 1))
    v = np.float32(np.log(df / np.float32(max_exact))) / np.float32(math.log(2048 / max_exact)) * np.float32(NB - max_exact)
    return min(max_exact + int(v), NB - 1)


def host_constants():
    ohp = np.zeros((NB, 3 * 384), np.float32)
    valid = np.zeros((1, 3 * 384), np.float32)
    for g, (W, d) in enumerate(GROUPS):
        for rel in range(129):
            n = rel + 127
            ohp[_bucket(rel * d), g * 384 + n] = 1.0
            valid[0, g * 384 + n] = 1.0
    ohs = np.zeros((NB, 6 * 128), np.float32)
    for s in range(4):
        for m in range(128):
            j = (s - m) if m < s else (128 + s - m)
            ohs[_bucket(j * 1), s * 128 + m] = 1.0
    for v, d in ((4, GROUPS[1][1]), (5, GROUPS[2][1])):
        for m in range(128):
            j = 128 - m
            ohs[_bucket(j * d), v * 128 + m] = 1.0
    return ohp, valid, ohs


def build_program(depth=DEPTH, do_ffn=True):
    NBLK = SEQ // 128
    NT4 = NBLK // 4
    nc = bass.Bass("TRN2", target_bir_lowering=False)
    P = Prog(nc)

    def din(name, shape):
        return P.dram(name, shape, F32, kind="ExternalInput")

    def dout(name, shape):
        return P.dram(name, shape, F32, kind="ExternalOutput")

    xp = din("xp", [SEQ, D])
    xs = din("xs", [16, D])
    sg_in = din("sg", [2 * 4 * 4 * 128, 256])
    ck = [din("ck%d" % g, [2 * 4 * GROUPS[g][0], D]) for g in range(3)]
    cv = [din("cv%d" % g, [2 * 4 * GROUPS[g][0], D]) for g in range(3)]
    sfc = din("sfc", [32, 2 * DFF])
    rel_bias = din("rel_bias", [NB, 48])
    norm_mix = din("norm_mix", [4, D])
    norm_ffn = din("norm_ffn", [4, D])
    gla_w_in = din("gla_w_in", [2 * D, GLA_IN])
    gla_w_g2 = din("gla_w_gate2", [32, 512])
    gla_b_g = din("gla_b_gate", [2, 512])
    gla_norm = din("gla_norm", [2, 256])
    gla_w_out = din("gla_w_out", [2 * D, D])
    dil_w_in = din("dil_w_in", [2 * D, 9216])
    dil_qn = din("dil_q_norm", [2, 64])
    dil_kn = din("dil_k_norm", [2, 64])
    dil_w_out = din("dil_w_out", [2 * D, D])
    ffn_w_up = din("ffn_w_up", [4 * D, 2 * DFF])
    ffn_cw = din("ffn_conv_w", [12, 2 * DFF])
    ffn_cb = din("ffn_conv_b", [4, 2 * DFF])
    ffn_w_down = din("ffn_w_down", [4 * DFF, D])
    c_ohp = din("c_ohp", [NB, 3 * 384])
    c_valid = din("c_valid", [1, 3 * 384])
    c_ohs = din("c_ohs", [NB, 6 * 128])

    yp = dout("yp", [SEQ, D])
    ys = dout("ys", [16, D])
    sgp = dout("sgp", [2 * 4 * 128, 256])
    sgs = dout("sgs", [2 * 4 * 4 * 128, 256])
    keep = [min(GROUPS[g][0], SEQ) for g in range(3)]
    kpo = [dout("kp%d" % g, [2 * keep[g], D]) for g in range(3)]
    vpo = [dout("vp%d" % g, [2 * keep[g], D]) for g in range(3)]
    kso = [dout("ks%d" % g, [2 * 16, D]) for g in range(3)]
    vso = [dout("vs%d" % g, [2 * 16, D]) for g in range(3)]
    fcp = dout("fcp", [8, 2 * DFF])
    fcs = dout("fcs", [32, 2 * DFF])

    ug_scr = [P.dram("ug%d" % g, [SEQ, 1280], F32) for g in (1, 2)]
    wsc = P.dram("wsc", [48, 384], F32)

    yp_blk = [T("ypb%d" % i, yp.h) for i in range(NBLK)]
    ys_blk = T("ysb", ys.h)
    xp_blk = [T("xpb%d" % i, xp.h) for i in range(NBLK)]
    xs_blk = T("xsb", xs.h)

    PSB = [P.ps("psb%d" % i, [128, 1024], BF16) if i < 2 else P.ps("psb%d" % i, [128, 512], F32) for i in range(8)]
    PSF = [PSB[i].h.bitcast(F32) if i < 2 else PSB[i].h for i in range(8)]

    class _RR:
        pass
    RR = _RR()
    RR.tp = Ring([0, 1])
    RR.mmA = Ring([(2, 3), (4, 5)])
    RR.mmB = Ring([6, 7])

    class _Dyn:
        def __init__(self, nm):
            self.nm = nm

        def next(self):
            return getattr(RR, self.nm).next()
    tp_ring = _Dyn("tp")
    mmA_ring = _Dyn("mmA")
    mmB_ring = _Dyn("mmB")

    def psA(pair):
        return PSB[pair[0]], PSB[pair[1]]

    identf = P.sb("identf", [128, 128], F32)
    ident = P.sb("ident", [128, 128], BF16)
    onesF = P.sb("onesF", [128, 128], F32)
    epsT = P.sb("epsT", [128, 1], F32)
    mask_s = P.sb("mask_s", [128, 128], F32)
    Jm = P.sb("Jm", [128, 128], BF16)
    Jf = P.sb("Jf", [128, 128], F32)
    gmix = P.sb("gmix", [128, 4, 8], F32)
    gffn = P.sb("gffn", [128, 4, 8], F32)

    P.op("pool", lambda e: e.memset(identf[:, :], 0.0), writes=[identf])
    P.op("pool", lambda e: e.affine_select(out=identf[:, :], in_=identf[:, :], pattern=[[-1, 128]], compare_op=ALU.not_equal, fill=1.0, base=0, channel_multiplier=1), reads=[identf], writes=[identf])
    P.op("dve", lambda e: e.tensor_copy(out=ident[:, :], in_=identf[:, :]), reads=[identf], writes=[ident])
    P.op("pool", lambda e: e.memset(Jf[:, :], 0.0), writes=[Jf])
    P.op("pool", lambda e: e.affine_select(out=Jf[:, :], in_=Jf[:, :], pattern=[[1, 128]], compare_op=ALU.not_equal, fill=1.0, base=-127, channel_multiplier=1), reads=[Jf], writes=[Jf])
    P.op("dve", lambda e: e.tensor_copy(out=Jm[:, :], in_=Jf[:, :]), reads=[Jf], writes=[Jm])
    P.op("dve", lambda e: e.memset(onesF[:, :], 1.0), writes=[onesF])
    P.op("dve", lambda e: e.memset(epsT[:, :], EPS), writes=[epsT])
    GSC = 128.0 ** -0.5
    P.op("pool", lambda e: e.memset(mask_s[:, :], GSC), writes=[mask_s])
    P.op("pool", lambda e: e.affine_select(out=mask_s[:, :], in_=mask_s[:, :], pattern=[[1, 128]], compare_op=ALU.is_ge, fill=0.0, base=0, channel_multiplier=-1), reads=[mask_s], writes=[mask_s])
    rowtmp = P.sb("rowtmp", [4, 512], F32)

    def load_featmajor(dst_view, dst_unit, src_h, row0, nrows, ncols, bank):
        nch = ncols // 128
        for c0 in range(0, ncols, 512):
            w = min(512, ncols - c0)
            P.dma("sp", rowtmp[0:nrows, 0:w], src_h[row0:row0 + nrows, c0:c0 + w], key=rowtmp, writes=[rowtmp])
            for cc in range(w // 128):
                ch = c0 // 128 + cc
                P.op("pe", lambda e, cc=cc, ch=ch: e.matmul(out=PSB[bank][:, ch * nrows:(ch + 1) * nrows], lhsT=rowtmp[0:nrows, cc * 128:(cc + 1) * 128], rhs=identf[0:nrows, 0:nrows], start=True, stop=True), reads=[rowtmp, identf], writes=[PSB[bank]], partial=True)
        P.op("act", lambda e: e.copy(out=dst_view, in_=PSB[bank][:, 0:nch * nrows].rearrange("p (c r) -> p c r", r=nrows)), reads=[PSB[bank]], writes=[dst_unit])

    load_featmajor(gmix[:, :, :].rearrange("p l c -> p c l"), gmix, norm_mix.h, 0, 4, D, 6)
    load_featmajor(gffn[:, :, :].rearrange("p l c -> p c l"), gffn, norm_ffn.h, 0, 4, D, 7)

    xt_ring = Ring([P.sb("xt%d" % i, [128, D], F32) for i in range(2)])
    xr_ring = Ring([P.sb("xr%d" % i, [128, D], F32) for i in range(2)])
    sq_scr = P.sb("sq_scr", [128, D], F32)
    xn_ring = Ring([P.sb("xn%d" % i, [128, D], BF16) for i in range(2)])
    st_ring = Ring([P.sb("st%d" % i, [128, 8], F32) for i in range(4)])
    HT = {"ring": None}
    on_ring = Ring([P.sb("on%d" % i, [128, D], BF16) for i in range(2)])
    onT_ring = Ring([P.sb("onT%d" % i, [128, 8, 128], BF16) for i in range(2)])
    PERSIST = P.off

    def phase_begin(n_hT, width):
        P.barrier()
        P.off = PERSIST
        HT["ring"] = Ring([P.sb("hT%d" % i, [128, 8, width], BF16) for i in range(n_hT)])

    def rstd_from_ss(st, col_in, col_out, npart, scale):
        w = col_out.stop - col_out.start
        P.op("act", lambda e: e.activation(out=st[0:npart, col_out], in_=st[0:npart, col_in], func=AF.Ln, scale=scale, bias=epsT[0:npart, 0:1]), reads=[st, epsT], writes=[st])
        P.op("act", lambda e: e.activation(out=st[0:npart, col_out], in_=st[0:npart, col_out], func=AF.Exp, scale=-0.5), reads=[st], writes=[st])

    def norm_front(blocks, gain, layer):
        hT = HT["ring"].next()
        col = 0
        for (src_ap, units, n) in blocks:
            xt = xt_ring.next()
            P.dma("sp", xt[0:n, :], src_ap, key=xt, reads=units, writes=[xt])
            st = st_ring.next()
            P.op("act", lambda e, xt=xt, st=st, n=n: e.activation(out=sq_scr[0:n, :], in_=xt[0:n, :], func=AF.Square, accum_out=st[0:n, 0:1]), reads=[xt], writes=[st])
            rstd_from_ss(st, slice(0, 1), slice(1, 2), n, 1.0 / D)
            xn = xn_ring.next()
            P.op("act", lambda e, xt=xt, st=st, xn=xn, n=n: e.activation(out=xn[0:n, :], in_=xt[0:n, :], func=AF.Copy, scale=st[0:n, 1:2]), reads=[xt, st], writes=[xn])
            tb = tp_ring.next()
            tpv = PSB[tb].h
            for c in range(8):
                P.op("pe", lambda e, c=c, xn=xn, n=n, tpv=tpv: e.transpose(out=tpv[:, c * 128:c * 128 + n], in_=xn[0:n, c * 128:(c + 1) * 128], identity=ident[0:n, 0:n]), reads=[xn, ident], writes=[PSB[tb]], partial=True)
            c0 = col
            P.op("dve", lambda e, tpv=tpv, hT=hT, n=n, c0=c0: e.tensor_tensor(
                out=hT[:, :, c0:c0 + n], in0=tpv[:, :].rearrange("p (c t) -> p c t", t=128)[:, :, 0:n],
                in1=gain[:, layer, :].unsqueeze(2).broadcast_to([128, 8, n]), op=ALU.mult),
                reads=[PSB[tb], gain], writes=[hT], partial=True)
            col += n
        return hT, col

    def load_w(dst, cols, src_h, row0, nrows, col0, ncols):
        kc = nrows // 128
        step = 512
        for k0 in range(0, kc, 8):
            k1 = min(kc, k0 + 8)
            for c in range(0, ncols, step):
                w = min(step, ncols - c)
                src = src_h[row0 + k0 * 128:row0 + k1 * 128, col0 + c:col0 + c + w].rearrange("(kc p) n -> p kc n", p=128)
                P.dma("pool", dst[:, k0:k1, cols.start + c:cols.start + c + w], src, key=dst, writes=[dst], partial=True)

    def transpose_to(on, npart, onT):
        tb = tp_ring.next()
        tpv = PSB[tb].h
        for c in range(8):
            P.op("pe", lambda e, c=c, tpv=tpv: e.transpose(out=tpv[:, c * 128:c * 128 + npart], in_=on[0:npart, c * 128:(c + 1) * 128], identity=ident[0:npart, 0:npart]), reads=[on, ident], writes=[PSB[tb]], partial=True)
        P.op("act", lambda e, tpv=tpv: e.copy(out=onT[:, :, 0:npart], in_=tpv[:, :].rearrange("p (c t) -> p c t", t=128)[:, :, 0:npart]), reads=[PSB[tb]], writes=[onT])

    def out_proj_residual(onT, npart, wout, nchunks, src_ap, src_units, dst_ap, dst_units, final, off=0):
        pair = mmA_ring.next()
        for half in range(2):
            b = PSB[pair[half]]
            for c in range(nchunks):
                P.op("pe", lambda e, c=c, b=b, half=half: e.matmul(out=b[0:npart, :], lhsT=onT[:, c, off:off + npart], rhs=wout[:, c, half * 512:(half + 1) * 512], start=(c == 0), stop=(c == nchunks - 1)), reads=[onT, wout], writes=[b], partial=True)
        xr = xr_ring.next()
        P.dma("sp", xr[0:npart, :], src_ap, key=xr, reads=src_units, writes=[xr])
        for half in range(2):
            b = PSB[pair[half]]
            P.op("dve", lambda e, b=b, half=half, xr=xr: e.tensor_tensor(out=xr[0:npart, half * 512:(half + 1) * 512], in0=b[0:npart, :], in1=xr[0:npart, half * 512:(half + 1) * 512], op=ALU.add), reads=[b, xr], writes=[xr])
        P.dma("pool", dst_ap, xr[0:npart, :], key=xr, reads=[xr], writes=dst_units, final=final)

    state = {"first": True}

    def prompt_src(i):
        if state["first"]:
            return xp.h[i * 128:(i + 1) * 128, :], [xp_blk[i]]
        return yp.h[i * 128:(i + 1) * 128, :], [yp_blk[i]]

    def sample_src(b):
        if state["first"]:
            return xs.h[b * 4:(b + 1) * 4, :], [xs_blk]
        return ys.h[b * 4:(b + 1) * 4, :], [ys_blk]

    def sample_src_all():
        if state["first"]:
            return xs.h[0:16, :], [xs_blk]
        return ys.h[0:16, :], [ys_blk]

    gl_w_in = None

    def gla_layer(layer, li, last):
        phase_begin(2, 512)
        w_in = P.sb("gw_in", [128, 8, GLA_IN], BF16)
        w_out = P.sb("gw_out", [128, 8, D], BF16)
        wg2 = P.sb("gwg2", [16, 512], BF16)
        negb = P.sb("gnegb", [128, 4], F32)
        gn = P.sb("ggn", [128, 256], F32)
        load_w(w_in, slice(0, GLA_IN), gla_w_in.h, li * D, D, 0, GLA_IN)
        load_w(w_out, slice(0, D), gla_w_out.h, li * D, D, 0, D)
        P.dma("pool", wg2[:, :], gla_w_g2.h[li * 16:(li + 1) * 16, :], key=wg2, writes=[wg2])
        load_featmajor(negb[:, :].unsqueeze(2), negb, gla_b_g.h, li, 1, 512, mmB_ring.next())
        P.op("dve", lambda e: e.tensor_scalar(out=negb[:, :], in0=negb[:, :], scalar1=-1.0, scalar2=None, op0=ALU.mult), reads=[negb], writes=[negb])
        P.dma("sp", gn[:, :], bass.AP(gla_norm.h, li * 256, [[0, 128], [1, 256]]), key=gn, writes=[gn])

        gzT_ring = Ring([P.sb("g_gz_%d" % i, [16, 512], BF16) for i in range(2)])
        lt = P.sb("g_l", [128, 512], F32)
        bp = P.sb("g_bp", [128, 512], F32)
        Et = Ring([P.sb("g_E_%d" % i, [128, 512], F32) for i in range(2)])
        Ei = Ring([P.sb("g_Ei_%d" % i, [128, 512], F32) for i in range(2)])
        El = Ring([P.sb("g_El_%d" % i, [128, 4, 4], F32) for i in range(2)])
        qt_ring = Ring([P.sb("g_qt_%d" % i, [128, 4, 512], BF16) for i in range(2)])
        kt_ring = Ring([P.sb("g_kt_%d" % i, [128, 4, 512], BF16) for i in range(2)])
        v_ring = Ring([P.sb("g_v_%d" % i, [128, D], BF16) for i in range(3)])
        gsr_ring = Ring([P.sb("g_sr_%d" % i, [128, D], F32) for i in range(3)])
        aT_ring = Ring([P.sb("g_aT_%d" % i, [128, 128], BF16) for i in range(3)])
        ktok_ring = Ring([P.sb("g_ktok_%d" % i, [128, 128], BF16) for i in range(3)])
        osb_ring = Ring([P.sb("g_o_%d" % i, [128, D], F32) for i in range(2)])
        oss_ring = Ring([P.sb("g_oss_%d" % i, [128, 8], F32) for i in range(2)])
        S = [P.sb("g_S_%d" % h, [128, 256], F32) for h in range(4)]
        Dd = [P.sb("g_D_%d" % h, [128, 256], F32) for h in range(4)]
        Dbf = [P.sb("g_Dbf_%d" % h, [128, 256], BF16) for h in range(4)]
        sq2 = P.sb("g_sq2", [128, 256], F32)

        def gla_tile(hT, ntok, L, s0_aps, sT_aps, res_blocks, is_sample_first):
            nch = ntok // L
            bb = mmB_ring.next()
            for kc in range(8):
                P.op("pe", lambda e, kc=kc, bb=bb: e.matmul(out=PSB[bb][0:16, 0:ntok], lhsT=w_in[:, kc, 3072:3088], rhs=hT[:, kc, 0:ntok], start=(kc == 0), stop=(kc == 7)), reads=[hT, w_in], writes=[PSB[bb]], partial=True)
            gz = gzT_ring.next()
            P.op("act", lambda e, bb=bb, gz=gz: e.copy(out=gz[:, 0:ntok], in_=PSB[bb][0:16, 0:ntok]), reads=[PSB[bb]], writes=[gz])
            qt = qt_ring.next()
            kt = kt_ring.next()
            E_l = El.next()
            Es, Eis = [], []
            for hh in range(4):
                bb = mmB_ring.next()
                P.op("pe", lambda e, hh=hh, bb=bb, gz=gz: e.matmul(out=PSB[bb][:, 0:ntok], lhsT=wg2[:, hh * 128:(hh + 1) * 128], rhs=gz[:, 0:ntok], start=True, stop=True), reads=[gz, wg2], writes=[PSB[bb]])
                P.op("act", lambda e, hh=hh, bb=bb: e.activation(out=lt[:, 0:ntok], in_=PSB[bb][:, 0:ntok], func=AF.Exp, scale=-1.0, bias=negb[:, hh:hh + 1]), reads=[PSB[bb], negb], writes=[lt])
                P.op("act", lambda e: e.activation(out=lt[:, 0:ntok], in_=lt[:, 0:ntok], func=AF.Ln, bias=onesF[:, 0:1]), reads=[lt, onesF], writes=[lt])
                for c in range(nch):
                    P.op("dve", lambda e, c=c: e.tensor_tensor_scan(out=bp[:, c * L:(c + 1) * L], data0=onesF[:, 0:L], data1=lt[:, c * L:(c + 1) * L], initial=0.0, op0=ALU.mult, op1=ALU.add), reads=[lt, onesF], writes=[bp], partial=(c > 0))
                E = Et.next()
                Einv = Ei.next()
                P.op("act", lambda e, E=E: e.activation(out=E[:, 0:ntok], in_=bp[:, 0:ntok], func=AF.Exp, scale=-1.0 / 16), reads=[bp], writes=[E])
                P.op("act", lambda e, Einv=Einv: e.activation(out=Einv[:, 0:ntok], in_=bp[:, 0:ntok], func=AF.Exp, scale=1.0 / 16), reads=[bp], writes=[Einv])
                P.op("dve", lambda e, E=E, hh=hh, E_l=E_l: e.tensor_copy(out=E_l[:, hh, 0:nch], in_=E[:, 0:ntok].rearrange("p (c l) -> p c l", l=L)[:, :, L - 1]), reads=[E], writes=[E_l], partial=(hh > 0))
                for (dst, coff, own, oth) in ((qt, 0, E, Einv), (kt, 512, Einv, E)):
                    bb2 = mmB_ring.next()
                    for kc in range(8):
                        P.op("pe", lambda e, kc=kc, bb2=bb2, coff=coff, hh=hh: e.matmul(out=PSB[bb2][:, 0:ntok], lhsT=w_in[:, kc, coff + hh * 128:coff + (hh + 1) * 128], rhs=hT[:, kc, 0:ntok], start=(kc == 0), stop=(kc == 7)), reads=[hT, w_in], writes=[PSB[bb2]], partial=True)
                    for c in range(nch):
                        le = (c + 1) * L - 1
                        P.op("dve", lambda e, c=c, le=le, bb2=bb2, dst=dst, own=own, oth=oth, hh=hh: e.scalar_tensor_tensor(
                            out=dst[:, hh, c * L:(c + 1) * L], in0=PSB[bb2][:, c * L:(c + 1) * L], scalar=oth[:, le:le + 1], in1=own[:, c * L:(c + 1) * L], op0=ALU.mult, op1=ALU.mult),
                            reads=[PSB[bb2], own, oth], writes=[dst], partial=True)
            for c in range(nch):
                cs = slice(c * L, (c + 1) * L)
                pairv = mmA_ring.next()
                for half in range(2):
                    b = PSB[pairv[half]]
                    for kc in range(8):
                        P.op("pe", lambda e, kc=kc, b=b, half=half, cs=cs: e.matmul(out=b[0:L, :], lhsT=hT[:, kc, cs], rhs=w_in[:, kc, 1024 + half * 512:1024 + (half + 1) * 512], start=(kc == 0), stop=(kc == 7)), reads=[hT, w_in], writes=[b], partial=True)
                vsb = v_ring.next()
                for half in range(2):
                    b = PSB[pairv[half]]
                    P.op("act", lambda e, b=b, half=half, vsb=vsb: e.copy(out=vsb[0:L, half * 512:(half + 1) * 512], in_=b[0:L, :]), reads=[b], writes=[vsb], partial=(half > 0))
                pairr = mmA_ring.next()
                for half in range(2):
                    b = PSB[pairr[half]]
                    for kc in range(8):
                        P.op("pe", lambda e, kc=kc, b=b, half=half, cs=cs: e.matmul(out=b[0:L, :], lhsT=hT[:, kc, cs], rhs=w_in[:, kc, 2048 + half * 512:2048 + (half + 1) * 512], start=(kc == 0), stop=(kc == 7)), reads=[hT, w_in], writes=[b], partial=True)
                gsr = gsr_ring.next()
                for half in range(2):
                    b = PSB[pairr[half]]
                    P.op("act", lambda e, b=b, half=half, gsr=gsr: e.activation(out=gsr[0:L, half * 512:(half + 1) * 512], in_=b[0:L, :], func=AF.Silu), reads=[b], writes=[gsr], partial=(half > 0))
                P.op("pool", lambda e, gsr=gsr: e.tensor_tensor(out=gsr[0:L, :].rearrange("p (h e) -> p h e", e=256), in0=gsr[0:L, :].rearrange("p (h e) -> p h e", e=256), in1=gn[0:L, :].unsqueeze(1).broadcast_to([L, 4, 256]), op=ALU.mult), reads=[gsr, gn], writes=[gsr])
                osb = osb_ring.next()
                oss = oss_ring.next()
                for hh in range(4):
                    if s0_aps is not None or (c == 0 and is_sample_first):
                        pass
                    if s0_aps is not None:
                        P.dma("sp", S[hh][:, :], s0_aps[c][hh], key=S[hh], writes=[S[hh]])
                    if s0_aps is None and state["gla_zero"] and c == 0:
                        P.op("pool", lambda e, hh=hh: e.memset(S[hh][:, :], 0.0), writes=[S[hh]])
                    P.op("dve", lambda e, hh=hh, c=c, E_l=E_l: e.tensor_scalar(out=Dd[hh][:, :], in0=S[hh][:, :], scalar1=E_l[:, hh, c:c + 1], scalar2=None, op0=ALU.mult), reads=[S[hh], E_l], writes=[Dd[hh]])
                    P.op("act", lambda e, hh=hh: e.activation(out=Dbf[hh][:, :], in_=Dd[hh][:, :], func=AF.Copy, scale=GSC), reads=[Dd[hh]], writes=[Dbf[hh]])
                    ba = mmB_ring.next()
                    P.op("pe", lambda e, ba=ba, hh=hh, cs=cs: e.matmul(out=PSB[ba][0:L, 0:L], lhsT=kt[:, hh, cs], rhs=qt[:, hh, cs], start=True, stop=True), reads=[kt, qt], writes=[PSB[ba]])
                    aT = aT_ring.next()
                    P.op("dve", lambda e, ba=ba, aT=aT: e.tensor_tensor(out=aT[0:L, 0:L], in0=PSB[ba][0:L, 0:L], in1=mask_s[0:L, 0:L], op=ALU.mult), reads=[PSB[ba], mask_s], writes=[aT])
                    tb = tp_ring.next()
                    tpv = PSB[tb].h
                    P.op("pe", lambda e, tpv=tpv, hh=hh, cs=cs, tb=tb: e.transpose(out=tpv[0:L, 0:128], in_=kt[:, hh, cs], identity=ident[:, :]), reads=[kt, ident], writes=[PSB[tb]])
                    ktok = ktok_ring.next()
                    P.op("act", lambda e, tpv=tpv, ktok=ktok: e.copy(out=ktok[0:L, :], in_=tpv[0:L, 0:128]), reads=[PSB[tb]], writes=[ktok])
                    P.op("pe", lambda e, ba=ba, aT=aT, vsb=vsb, hh=hh: e.matmul(out=PSB[ba][0:L, 128:384], lhsT=aT[0:L, 0:L], rhs=vsb[0:L, hh * 256:(hh + 1) * 256], start=True, stop=False), reads=[aT, vsb], writes=[PSB[ba]])
                    P.op("pe", lambda e, ba=ba, hh=hh, cs=cs: e.matmul(out=PSB[ba][0:L, 128:384], lhsT=qt[:, hh, cs], rhs=Dbf[hh][:, :], start=False, stop=True), reads=[qt, Dbf[hh]], writes=[PSB[ba]], partial=True)
                    P.op("act", lambda e, ba=ba, osb=osb, hh=hh: e.copy(out=osb[0:L, hh * 256:(hh + 1) * 256], in_=PSB[ba][0:L, 128:384]), reads=[PSB[ba]], writes=[osb], partial=(hh > 0))
                    P.op("act", lambda e, ba=ba, oss=oss, hh=hh: e.activation(out=sq2[0:L, :], in_=PSB[ba][0:L, 128:384], func=AF.Square, accum_out=oss[0:L, hh:hh + 1]), reads=[PSB[ba]], writes=[oss], partial=(hh > 0))
                    bk = mmB_ring.next()
                    P.op("pe", lambda e, bk=bk, ktok=ktok, vsb=vsb, hh=hh: e.matmul(out=PSB[bk][:, 0:256], lhsT=ktok[0:L, :], rhs=vsb[0:L, hh * 256:(hh + 1) * 256], start=True, stop=True), reads=[ktok, vsb], writes=[PSB[bk]])
                    P.op("dve", lambda e, bk=bk, hh=hh: e.tensor_tensor(out=S[hh][:, :], in0=PSB[bk][:, 0:256], in1=Dd[hh][:, :], op=ALU.add), reads=[PSB[bk], Dd[hh]], writes=[S[hh]])
                    if sT_aps is not None and sT_aps[c] is not None:
                        P.dma("pool", sT_aps[c][hh][0], S[hh][:, :], key=S[hh], reads=[S[hh]], writes=[sT_aps[c][hh][1]], partial=True, final=True)
                state["gla_zero"] = False
                rstd_from_ss(oss, slice(0, 4), slice(4, 8), L, 1.0 / 256)
                on = on_ring.next()
                for hh in range(4):
                    P.op("dve", lambda e, hh=hh, on=on, osb=osb, oss=oss, gsr=gsr: e.scalar_tensor_tensor(out=on[0:L, hh * 256:(hh + 1) * 256], in0=osb[0:L, hh * 256:(hh + 1) * 256], scalar=oss[0:L, 4 + hh:5 + hh], in1=gsr[0:L, hh * 256:(hh + 1) * 256], op0=ALU.mult, op1=ALU.mult), reads=[osb, oss, gsr], writes=[on], partial=(hh > 0))
                onT = onT_ring.next()
                transpose_to(on, L, onT)
                src_ap, src_units, dst_ap, dst_units, fin = res_blocks[c]
                out_proj_residual(onT, L, w_out, 8, src_ap, src_units, dst_ap, dst_units, fin)

        def blocks_for_tile(t):
            return [(prompt_src(4 * t + j)[0], prompt_src(4 * t + j)[1], 128) for j in range(4)]

        state["gla_zero"] = True
        pre = norm_front(blocks_for_tile(0), gmix, layer)
        for t in range(NT4):
            cur = pre
            if t + 1 < NT4:
                pre = norm_front(blocks_for_tile(t + 1), gmix, layer)
            else:
                pre = norm_front([(sample_src(0)[0], sample_src(0)[1], 4)], gmix, layer)
            res = []
            for j in range(4):
                i = 4 * t + j
                sa, su = prompt_src(i)
                res.append((sa, su, yp.h[i * 128:(i + 1) * 128, :], [yp_blk[i]], last))
            sT = None
            if t == NT4 - 1:
                sT = [None, None, None, [(sgp.h[(li * 4 + hh) * 128:(li * 4 + hh + 1) * 128, :], sgp) for hh in range(4)]]
            gla_tile(cur[0], 512, 128, None, sT, res, False)
        for b in range(4):
            cur = pre
            if b + 1 < 4:
                pre = norm_front([(sample_src(b + 1)[0], sample_src(b + 1)[1], 4)], gmix, layer)
            sa, su = sample_src(b)
            res = [(sa, su, ys.h[b * 4:(b + 1) * 4, :], [ys_blk], last)]
            s0 = [[sg_in.h[((li * 4 + b) * 4 + hh) * 128:((li * 4 + b) * 4 + hh + 1) * 128, :] for hh in range(4)]]
            sT = [[(sgs.h[((li * 4 + b) * 4 + hh) * 128:((li * 4 + b) * 4 + hh + 1) * 128, :], sgs) for hh in range(4)]]
            gla_tile(cur[0], 4, 4, s0, sT, res, True)
        state["first"] = False

    def ffn_layer(layer, last):
        phase_begin(2, 256)
        RR.mmA = Ring([(2, 3)])
        RR.mmB = Ring([4, 5, 6, 7])
        w_up = P.sb("f_wup", [128, 8, 2 * DFF], BF16)
        w_dn = P.sb("f_wdn", [128, 22, D], BF16)
        cw = P.sb("f_cw", [128, 3, 44], F32)
        cb = P.sb("f_cb", [128, 44], F32)
        load_w(w_up, slice(0, 2 * DFF), ffn_w_up.h, layer * D, D, 0, 2 * DFF)
        load_w(w_dn, slice(0, D), ffn_w_down.h, layer * DFF, DFF, 0, D)
        load_featmajor(cw[:, :, :].rearrange("p i c -> p c i"), cw, ffn_cw.h, layer * 3, 3, 2 * DFF, mmB_ring.next())
        load_featmajor(cb[:, :].unsqueeze(2), cb, ffn_cb.h, layer, 1, 2 * DFF, mmB_ring.next())
        hist = P.sb("f_hist", [128, 44, 2], F32)
        cs_ring = Ring([P.sb("f_c_%d" % i, [128, 256], F32) for i in range(4)])
        sg_ring = Ring([P.sb("f_sg_%d" % i, [128, 256], F32) for i in range(2)])
        act = P.sb("f_act", [128, 22, 256], BF16)
        ul_ring = Ring([P.sb("f_ul_%d" % i, [2, 512], F32) for i in range(2)])
        hrow = P.sb("f_hrow", [2, 512], F32)

        def ffn_tile(hT, N, res_blocks, state_out_ap, state_out_unit):
            for cp in range(22):
                chs = (cp, 22 + cp)
                bbs, cts = [], []
                for ch in chs:
                    bb = mmB_ring.next()
                    for kc in range(8):
                        P.op("pe", lambda e, kc=kc, bb=bb, ch=ch: e.matmul(out=PSB[bb][:, 0:N], lhsT=w_up[:, kc, ch * 128:(ch + 1) * 128], rhs=hT[:, kc, 0:N], start=(kc == 0), stop=(kc == 7)), reads=[hT, w_up], writes=[PSB[bb]], partial=True)
                    bbs.append(bb)
                    cts.append(cs_ring.next())
                for bb, ct, ch in zip(bbs, cts, chs):
                    P.op("act", lambda e, bb=bb, ct=ct, ch=ch: e.activation(out=ct[:, 0:N], in_=PSB[bb][:, 0:N], func=AF.Identity, scale=cw[:, 2, ch:ch + 1], bias=cb[:, ch:ch + 1]), reads=[PSB[bb], cw, cb], writes=[ct])
                for bb, ct, ch in zip(bbs, cts, chs):
                    P.op("dve", lambda e, bb=bb, ct=ct, ch=ch: e.scalar_tensor_tensor(out=ct[:, 1:N], in0=PSB[bb][:, 0:N - 1], scalar=cw[:, 1, ch:ch + 1], in1=ct[:, 1:N], op0=ALU.mult, op1=ALU.add), reads=[PSB[bb], cw, ct], writes=[ct])
                for bb, ct, ch in zip(bbs, cts, chs):
                    P.op("dve", lambda e, bb=bb, ct=ct, ch=ch: e.scalar_tensor_tensor(out=ct[:, 2:N], in0=PSB[bb][:, 0:N - 2], scalar=cw[:, 0, ch:ch + 1], in1=ct[:, 2:N], op0=ALU.mult, op1=ALU.add), reads=[PSB[bb], cw, ct], writes=[ct])
                for bb, ct, ch in zip(bbs, cts, chs):
                    P.op("dve", lambda e, ct=ct, ch=ch: e.scalar_tensor_tensor(out=ct[:, 0:2], in0=hist[:, ch, 0:2], scalar=cw[:, 0, ch:ch + 1], in1=ct[:, 0:2], op0=ALU.mult, op1=ALU.add), reads=[hist, cw, ct], writes=[ct])
                for bb, ct, ch in zip(bbs, cts, chs):
                    P.op("dve", lambda e, ct=ct, ch=ch: e.scalar_tensor_tensor(out=ct[:, 0:1], in0=hist[:, ch, 1:2], scalar=cw[:, 1, ch:ch + 1], in1=ct[:, 0:1], op0=ALU.mult, op1=ALU.add), reads=[hist, cw, ct], writes=[ct])
                for bb, ct, ch in zip(bbs, cts, chs):
                    P.op("act", lambda e, bb=bb, ch=ch: e.copy(out=hist[:, ch, 0:2], in_=PSB[bb][:, N - 2:N]), reads=[PSB[bb]], writes=[hist])
                sgt = sg_ring.next()
                P.op("act", lambda e, sgt=sgt, c0=cts[0]: e.activation(out=sgt[:, 0:N], in_=c0[:, 0:N], func=AF.Silu), reads=[cts[0]], writes=[sgt])
                P.op("pool", lambda e, sgt=sgt, c1=cts[1], cp=cp: e.tensor_tensor(out=act[:, cp, 0:N], in0=sgt[:, 0:N], in1=c1[:, 0:N], op=ALU.mult), reads=[sgt, cts[1]], writes=[act], partial=(cp > 0))
            if state_out_ap is not None:
                for blk in range(11):
                    bb = mmB_ring.next()
                    for kc in range(8):
                        P.op("pe", lambda e, kc=kc, bb=bb, blk=blk: e.matmul(out=PSB[bb][0:2, :], lhsT=hT[:, kc, N - 2:N], rhs=w_up[:, kc, blk * 512:(blk + 1) * 512], start=(kc == 0), stop=(kc == 7)), reads=[hT, w_up], writes=[PSB[bb]], partial=True)
                    ul = ul_ring.next()
                    P.op("act", lambda e, bb=bb, ul=ul: e.copy(out=ul[0:2, :], in_=PSB[bb][0:2, :]), reads=[PSB[bb]], writes=[ul])
                    P.dma("pool", state_out_ap[:, blk * 512:(blk + 1) * 512], ul[0:2, :], key=ul, reads=[ul], writes=[state_out_unit], partial=True, final=True)
            nb = len(res_blocks)
            for j in range(nb):
                src_ap, src_units, dst_ap, dst_units, fin, n = res_blocks[j]
                out_proj_residual(act, n, w_dn, 22, src_ap, src_units, dst_ap, dst_units, fin, off=j * 128)

        def blocks_for_tile(t):
            return [(prompt_src(2 * t + j)[0], prompt_src(2 * t + j)[1], 128) for j in range(2)]

        P.op("pool", lambda e: e.memset(hist[:, :, :], 0.0), writes=[hist])
        pre = norm_front(blocks_for_tile(0), gffn, layer)
        for t in range(NBLK // 2):
            cur = pre
            if t + 1 < NBLK // 2:
                pre = norm_front(blocks_for_tile(t + 1), gffn, layer)
            else:
                pre = norm_front([(sample_src(0)[0], sample_src(0)[1], 4)], gffn, layer)
            res = []
            for j in range(2):
                i = 2 * t + j
                sa, su = prompt_src(i)
                res.append((sa, su, yp.h[i * 128:(i + 1) * 128, :], [yp_blk[i]], last, 128))
            ffn_tile(cur[0], 256, res, fcp.h[layer * 2:(layer + 1) * 2, :] if t == NBLK // 2 - 1 else None, fcp)
        for b in range(4):
            cur = pre
            if b + 1 < 4:
                pre = norm_front([(sample_src(b + 1)[0], sample_src(b + 1)[1], 4)], gffn, layer)
            r0 = (layer * 4 + b) * 2
            bb = mmB_ring.next()
            for q11 in range(11):
                P.dma("sp", hrow[0:2, :], sfc.h[r0:r0 + 2, q11 * 512:(q11 + 1) * 512], key=hrow, writes=[hrow])
                for cc in range(4):
                    ch = q11 * 4 + cc
                    P.op("pe", lambda e, bb=bb, cc=cc, ch=ch: e.matmul(out=PSB[bb][:, ch * 2:ch * 2 + 2], lhsT=hrow[0:2, cc * 128:(cc + 1) * 128], rhs=identf[0:2, 0:2], start=True, stop=True), reads=[hrow, identf], writes=[PSB[bb]], partial=True)
            P.op("act", lambda e, bb=bb: e.copy(out=hist[:, :, :], in_=PSB[bb][:, 0:88].rearrange("p (c t) -> p c t", t=2)), reads=[PSB[bb]], writes=[hist])
            sa, su = sample_src(b)
            res = [(sa, su, ys.h[b * 4:(b + 1) * 4, :], [ys_blk], last, 4)]
            r1 = (layer * 4 + b) * 2
            ffn_tile(cur[0], 4, res, fcs.h[r1:r1 + 2, :], fcs)
        RR.mmA = Ring([(2, 3), (4, 5)])
        RR.mmB = Ring([6, 7])
        state["first"] = False


    def dil_setup():
        phase_begin(1, 16)
        relb = P.sb("d_relb", [NB, 48], F32)
        ohp = P.sb("d_ohp", [NB, 3 * 384], F32)
        vld = P.sb("d_vld", [16, 3 * 384], F32)
        P.dma("sp", relb[:, :], rel_bias.h[:, :], key=relb, writes=[relb])
        P.dma("sp", ohp[:, :], c_ohp.h[:, :], key=ohp, writes=[ohp])
        P.dma("sp", vld[:, :], bass.AP(c_valid.h, 0, [[0, 16], [1, 3 * 384]]), key=vld, writes=[vld])
        for g in range(3):
            bb = mmB_ring.next()
            P.op("pe", lambda e, bb=bb, g=g: e.matmul(out=PSB[bb][0:16, 0:384], lhsT=relb[:, g * 16:(g + 1) * 16], rhs=ohp[:, g * 384:(g + 1) * 384], start=True, stop=True), reads=[relb, ohp], writes=[PSB[bb]])
            wv = P.sb("d_wv%d" % g, [16, 384], F32)
            wvb = P.sb("d_wvb%d" % g, [16, 384], F32)
            P.op("act", lambda e, bb=bb, wv=wv: e.activation(out=wv[:, :], in_=PSB[bb][0:16, 0:384], func=AF.Exp), reads=[PSB[bb]], writes=[wv])
            P.op("dve", lambda e, wv=wv, wvb=wvb, g=g: e.tensor_tensor(out=wvb[:, :], in0=wv[:, :], in1=vld[:, g * 384:(g + 1) * 384], op=ALU.mult), reads=[wv, vld], writes=[wvb])
            P.dma("sp", wsc.h[g * 16:(g + 1) * 16, :], wvb[:, :], key=wvb, reads=[wvb], writes=[wsc], partial=True)

    def dil_group(layer, li, g, last):
        _chk(1)
        W, d = GROUPS[g]
        nbk = (SEQ // d) // 128
        RR.tp = Ring([0])
        RR.mmA = Ring([(2, 3)])
        RR.mmB = Ring([7])
        sc_ring = Ring([1, 7])
        UB = (4, 5, 6)
        phase_begin(1, 512)
        Wg = P.sb("d_Wg", [128, 8, 3072], BF16)
        load_w(Wg, slice(0, 3072), dil_w_in.h, li * D, D, g * 3072, 3072)
        w_out = None
        if g == 0:
            w_out = P.sb("d_wout", [128, 8, D], BF16)
            load_w(w_out, slice(0, D), dil_w_out.h, li * D, D, 0, D)
        qg = P.sb("d_qg", [128, 64], F32)
        kg = P.sb("d_kg", [128, 64], F32)
        P.dma("sp", qg[:, :], bass.AP(dil_qn.h, li * 64, [[0, 128], [1, 64]]), key=qg, writes=[qg])
        P.dma("sp", kg[:, :], bass.AP(dil_kn.h, li * 64, [[0, 128], [1, 64]]), key=kg, writes=[kg])
        gcol = P.sb("d_gcol", [128, 2], F32)
        for ci, srch in ((0, dil_qn), (1, dil_kn)):
            P.dma("sp", rowtmp[0:1, 0:64], srch.h[li:li + 1, :], key=rowtmp, writes=[rowtmp])
            P.dma("sp", rowtmp[0:1, 64:128], srch.h[li:li + 1, :], key=rowtmp, writes=[rowtmp], partial=True)
            bb = mmB_ring.next()
            P.op("pe", lambda e, bb=bb: e.matmul(out=PSB[bb][:, 0:1], lhsT=rowtmp[0:1, 0:128], rhs=identf[0:1, 0:1], start=True, stop=True), reads=[rowtmp, identf], writes=[PSB[bb]])
            P.op("act", lambda e, bb=bb, ci=ci: e.copy(out=gcol[:, ci:ci + 1], in_=PSB[bb][:, 0:1]), reads=[PSB[bb]], writes=[gcol], partial=(ci > 0))
        relb = P.sb("d_relb", [NB, 48], F32)
        P.dma("sp", relb[:, :], rel_bias.h[:, :], key=relb, writes=[relb])
        M = P.sb("d_M", [128, 16, 256], BF16)
        H_ring = Ring([P.sb("d_H%d" % i, [128, 256], F32) for i in range(2)])
        for h in range(16):
            H = H_ring.next()
            P.dma("sp", H[:, :], bass.AP(wsc.h, (g * 16 + h) * 384, [[1, 128], [1, 256]]), key=H, reads=[wsc], writes=[H])
            bb = mmB_ring.next()
            P.op("pe", lambda e, bb=bb, H=H: e.matmul(out=PSB[bb][:, 0:256], lhsT=Jf[:, :], rhs=H[:, :], start=True, stop=True), reads=[Jf, H], writes=[PSB[bb]])
            P.op("act", lambda e, bb=bb, h=h: e.copy(out=M[:, h, :], in_=PSB[bb][:, 0:256]), reads=[PSB[bb]], writes=[M], partial=(h > 0))
        _chk(2)
        qS = P.sb("d_qS", [16, D], F32)
        kS = P.sb("d_kS", [16, D], F32)
        vS = P.sb("d_vS", [16, D], F32)
        MARK = P.off
        nrm_r = Ring([P.sb("d_nrm%d" % i, [128, D], F32) for i in range(2)])
        st16 = Ring([P.sb("d_st%d" % i, [128, 32], F32) for i in range(2)])
        kout = P.sb("d_ko", [128, D], F32)
        vout = P.sb("d_vo", [128, D], F32)
        qbf_r = Ring([P.sb("d_qb%d" % i, [128, D], BF16) for i in range(2)])
        qT_r = Ring([P.sb("d_qT%d" % i, [128, 16, 128], BF16) for i in range(2)])
        for t in qT_r.items:
            P.op("pool", lambda e, t=t: e.memset(t[:, :, :], 0.0), writes=[t])
        kT_r = Ring([P.sb("d_kT%d" % i, [128, 8, 128], BF16) for i in range(3)])
        va_r = Ring([P.sb("d_va%d" % i, [128, 16, 80], BF16) for i in range(3)])
        pe_r = Ring([P.sb("d_pe%d" % i, [128, 512], BF16) for i in range(2)])
        pt_r = Ring([P.sb("d_pt%d" % i, [128, 512], BF16) for i in range(2)])
        U_r = Ring([P.sb("d_U%d" % i, [128, 1280], F32) for i in range(2)])
        Ua = P.sb("d_Ua", [128, 1280], F32)
        for t in U_r.items:
            P.op("pool", lambda e, t=t: e.memset(t[:, :], 0.0), writes=[t])
        rden = P.sb("d_rden", [128, 16], F32)
        for t in va_r.items:
            P.op("pool", lambda e, t=t: e.memset(t[:, :, :], 1.0), writes=[t])

        def qkv_block(hT, c0, n, cache_k_ap, cache_v_ap, cache_ku, cache_vu, sample):
            outs = []
            for s_ in range(3):
                pair = mmA_ring.next()
                for half in range(2):
                    b = PSB[pair[half]]
                    for kc in range(8):
                        P.op("pe", lambda e, kc=kc, b=b, half=half, s_=s_: e.matmul(out=b[0:n, :], lhsT=hT[:, kc, c0:c0 + n], rhs=Wg[:, kc, s_ * 1024 + half * 512:s_ * 1024 + (half + 1) * 512], start=(kc == 0), stop=(kc == 7)), reads=[hT, Wg], writes=[b], partial=True)
                if s_ == 2:
                    if sample:
                        for half in range(2):
                            b = PSB[pair[half]]
                            P.op("act", lambda e, b=b, half=half: e.copy(out=vS[0:n, half * 512:(half + 1) * 512], in_=b[0:n, :]), reads=[b], writes=[vS], partial=(half > 0))
                        P.dma("pool", cache_v_ap, vS[0:n, :], key=vS, reads=[vS], writes=[cache_vu], partial=True, final=True)
                        outs.append(vS)
                        continue
                    va = va_r.next()
                    for half in range(2):
                        b = PSB[pair[half]]
                        P.op("act", lambda e, b=b, half=half, va=va: e.copy(out=va[0:n, half * 8:(half + 1) * 8, 0:64], in_=b[0:n, :].rearrange("p (h e) -> p h e", e=64)), reads=[b], writes=[va], partial=(half > 0))
                    if cache_v_ap is not None:
                        for half in range(2):
                            b = PSB[pair[half]]
                            P.op("act", lambda e, b=b, half=half: e.copy(out=vout[0:n, half * 512:(half + 1) * 512], in_=b[0:n, :]), reads=[b], writes=[vout], partial=(half > 0))
                        P.dma("pool", cache_v_ap, vout[0:n, :], key=vout, reads=[vout], writes=[cache_vu], partial=True, final=True)
                    outs.append(va)
                    continue
                st = st16.next()
                nrm = nrm_r.next()
                for half in range(2):
                    b = PSB[pair[half]]
                    P.op("act", lambda e, b=b, half=half, nrm=nrm: e.copy(out=nrm[0:n, half * 512:(half + 1) * 512], in_=b[0:n, :]), reads=[b], writes=[nrm], partial=(half > 0))
                P.op("act", lambda e, nrm=nrm: e.activation(out=sq_scr[0:n, :], in_=nrm[0:n, :], func=AF.Square), reads=[nrm], writes=[sq_scr])
                P.op("dve", lambda e, st=st: e.tensor_reduce(out=st[0:n, 0:16], in_=sq_scr[0:n, :].rearrange("p (h e) -> p h e", e=64), axis=AX.X, op=ALU.add), reads=[sq_scr], writes=[st])
                rstd_from_ss(st, slice(0, 16), slice(16, 32), n, 1.0 / 64)
                gain = qg if s_ == 0 else kg
                need_f32 = sample or (s_ == 1 and cache_k_ap is not None)
                if not need_f32:
                    dst = qbf_r.next()
                    P.op("dve", lambda e, st=st, nrm=nrm, dst=dst: e.tensor_tensor(out=dst[0:n, :].rearrange("p (h e) -> p h e", e=64), in0=nrm[0:n, :].rearrange("p (h e) -> p h e", e=64), in1=st[0:n, 16:32].unsqueeze(2).broadcast_to([n, 16, 64]), op=ALU.mult), reads=[nrm, st], writes=[dst])
                    outs.append(dst)
                    continue
                P.op("dve", lambda e, st=st, nrm=nrm: e.tensor_tensor(out=nrm[0:n, :].rearrange("p (h e) -> p h e", e=64), in0=nrm[0:n, :].rearrange("p (h e) -> p h e", e=64), in1=st[0:n, 16:32].unsqueeze(2).broadcast_to([n, 16, 64]), op=ALU.mult), reads=[nrm, st], writes=[nrm])
                if sample:
                    full = qS if s_ == 0 else kS
                else:
                    full = kout
                    dst = qbf_r.next()
                    P.op("act", lambda e, nrm=nrm, dst=dst: e.copy(out=dst[0:n, :], in_=nrm[0:n, :]), reads=[nrm], writes=[dst])
                    outs.append(dst)
                P.op("pool", lambda e, full=full, gain=gain, nrm=nrm: e.tensor_tensor(out=full[0:n, :].rearrange("p (h e) -> p h e", e=64), in0=nrm[0:n, :].rearrange("p (h e) -> p h e", e=64), in1=gain[0:n, :].unsqueeze(1).broadcast_to([n, 16, 64]), op=ALU.mult), reads=[nrm, gain], writes=[full])
                if s_ == 1:
                    P.dma("pool", cache_k_ap, full[0:n, :], key=full, reads=[full], writes=[cache_ku], partial=True, final=True)
                if sample:
                    outs.append(full)
            return outs

        def to_featmajor(src, dstT, ci, split=False):
            tb = tp_ring.next()
            tpv = PSB[tb].h
            for c in range(8):
                P.op("pe", lambda e, c=c, tpv=tpv, src=src: e.transpose(out=tpv[:, c * 128:(c + 1) * 128], in_=src[:, c * 128:(c + 1) * 128], identity=ident[:, :]), reads=[src, ident], writes=[PSB[tb]], partial=True)
            if split:
                P.op("act", lambda e, tpv=tpv: e.activation(out=dstT[0:64, 0:8, :], in_=tpv[0:64, :].rearrange("p (c t) -> p c t", t=128), func=AF.Copy, scale=gcol[0:64, ci:ci + 1]), reads=[PSB[tb], gcol], writes=[dstT])
                P.op("act", lambda e, tpv=tpv: e.activation(out=dstT[64:128, 8:16, :], in_=tpv[64:128, :].rearrange("p (c t) -> p c t", t=128), func=AF.Copy, scale=gcol[64:128, ci:ci + 1]), reads=[PSB[tb], gcol], writes=[dstT], partial=True)
            else:
                P.op("act", lambda e, tpv=tpv: e.activation(out=dstT[:, :, :], in_=tpv[:, :].rearrange("p (c t) -> p c t", t=128), func=AF.Copy, scale=gcol[:, ci:ci + 1]), reads=[PSB[tb], gcol], writes=[dstT])

        hTs, _ = norm_front([(sample_src_all()[0], sample_src_all()[1], 16)], gmix, layer)
        qkv_block(hTs, 0, 16, kso[g].h[li * 16:(li + 1) * 16, :], vso[g].h[li * 16:(li + 1) * 16, :], kso[g], vso[g], True)

        _chk(3)
        blocks = [(r, n) for r in range(d) for n in range(nbk)]

        def blk_src(r, n):
            base = yp.h
            units = [yp_blk[i] for i in range(n * d, (n + 1) * d)]
            return bass.AP(base, (n * 128 * d + r) * D, [[d * D, 128], [1, D]]), units

        def tile_blocks(t):
            return [(blk_src(*blocks[4 * t + j])[0], blk_src(*blocks[4 * t + j])[1], 128) for j in range(4)]

        prev_kT = None
        prev_va = None
        hTcur = {"t": 0, "hT": norm_front(tile_blocks(0), gmix, layer)[0]}

        def front(bi):
            t_, j_ = bi // 4, bi % 4
            if t_ != hTcur["t"]:
                hTcur["t"] = t_
                hTcur["hT"] = norm_front(tile_blocks(t_), gmix, layer)[0]
            r_, n_ = blocks[bi]
            ck_ap = cv_ap = None
            if n_ == nbk - 1:
                ck_ap = bass.AP(kpo[g].h, (li * keep[g] + r_) * D, [[d * D, 128], [1, D]])
                cv_ap = bass.AP(vpo[g].h, (li * keep[g] + r_) * D, [[d * D, 128], [1, D]])
            qb, kb, va_ = qkv_block(hTcur["hT"], j_ * 128, 128, ck_ap, cv_ap, kpo[g], vpo[g], False)
            qT_ = qT_r.next()
            kT_ = kT_r.next()
            to_featmajor(qb, qT_, 0, split=True)
            to_featmajor(kb, kT_, 1)
            return qT_, kT_, va_

        nxt = front(0)
        for t in range(NT4):
            for j in range(4):
                bi = 4 * t + j
                r, n = blocks[bi]
                qT, kT, va = nxt
                if bi + 1 < len(blocks):
                    nxt = front(bi + 1)
                _chk(7)
                has_prev = n > 0
                def qk_scores(hp, kT=kT, qT=qT, has_prev=has_prev, pk=prev_kT):
                    bb = sc_ring.next()
                    for hh in range(2):
                        P.op("pe", lambda e, bb=bb, hh=hh, hp=hp: e.matmul(out=PSF[bb][:, hh * 256:hh * 256 + 128], lhsT=kT[:, hp, :], rhs=qT[:, hh * 8 + hp, :], start=True, stop=True), reads=[kT, qT], writes=[PSB[bb]], partial=True)
                        if has_prev:
                            P.op("pe", lambda e, bb=bb, hh=hh, hp=hp: e.matmul(out=PSF[bb][:, hh * 256 + 128:hh * 256 + 256], lhsT=pk[:, hp, :], rhs=qT[:, hh * 8 + hp, :], start=True, stop=True), reads=[pk, qT], writes=[PSB[bb]], partial=True)
                    return bb

                bb_next = qk_scores(0)
                for hp in range(8):
                    bb = bb_next
                    if hp + 1 < 8:
                        bb_next = qk_scores(hp + 1)
                    pe_t = pe_r.next()
                    pt = pt_r.next()
                    wdt = 256 if has_prev else 128
                    P.op("act", lambda e, bb=bb, pe_t=pe_t, wdt=wdt: e.activation(out=pe_t[:, :].rearrange("p (h x) -> p h x", x=256)[:, :, 0:wdt], in_=PSF[bb][:, :].rearrange("p (h x) -> p h x", x=256)[:, :, 0:wdt], func=AF.Exp, scale=0.125), reads=[PSB[bb]], writes=[pe_t])
                    P.op("dve", lambda e, pe_t=pe_t, pt=pt, hp=hp, wdt=wdt: e.tensor_tensor(out=pt[:, :].rearrange("p (h x) -> p h x", x=256)[:, :, 0:wdt], in0=pe_t[:, :].rearrange("p (h x) -> p h x", x=256)[:, :, 0:wdt], in1=M[:, 2 * hp:2 * hp + 2, 0:wdt], op=ALU.mult), reads=[pe_t, M], writes=[pt])
                    for hh in range(2):
                        h = 2 * hp + hh
                        ub = PSB[UB[h // 6]]
                        c0 = (h % 6) * 80
                        P.op("pe", lambda e, ub=ub, c0=c0, pt=pt, hh=hh, va=va, h=h, has_prev=has_prev: e.matmul(out=ub[:, c0:c0 + 65], lhsT=pt[:, hh * 256:hh * 256 + 128], rhs=va[:, h, 0:65], start=True, stop=(not has_prev)), reads=[pt, va], writes=[ub], partial=True)
                        if has_prev:
                            P.op("pe", lambda e, ub=ub, c0=c0, pt=pt, hh=hh, pv=prev_va, h=h: e.matmul(out=ub[:, c0:c0 + 65], lhsT=pt[:, hh * 256 + 128:hh * 256 + 256], rhs=pv[:, h, 0:65], start=False, stop=True), reads=[pt, prev_va], writes=[ub], partial=True)
                _chk(8)
                prev_kT, prev_va = kT, va
                U = U_r.next()
                for bi, (a0, a1) in enumerate(((0, 480), (480, 960), (960, 1280))):
                    P.op("act", lambda e, bi=bi, a0=a0, a1=a1, U=U: e.copy(out=U[:, a0:a1].rearrange("p (h e) -> p h e", e=80)[:, :, 0:65], in_=PSB[UB[bi]][:, 0:a1 - a0].rearrange("p (h e) -> p h e", e=80)[:, :, 0:65]), reads=[PSB[UB[bi]]], writes=[U], partial=(bi > 0))
                if g != 0:
                    dst = bass.AP(ug_scr[g - 1].h, (n * 128 * d + r) * 1280, [[d * 1280, 128], [1, 1280]])
                    P.dma("pool", dst, U[:, :], key=U, reads=[U], writes=[ug_scr[g - 1]], partial=True)
                else:
                    i = n
                    for gi in range(2):
                        P.dma("sp", Ua[:, :], ug_scr[gi].h[i * 128:(i + 1) * 128, :], key=Ua, reads=[ug_scr[gi]], writes=[Ua])
                        P.op("pool", lambda e, U=U: e.tensor_tensor(out=U[:, :], in0=U[:, :], in1=Ua[:, :], op=ALU.add), reads=[U, Ua], writes=[U])
                    finish_attn(U, 128, w_out, rden, prompt_src(i)[0], prompt_src(i)[1], yp.h[i * 128:(i + 1) * 128, :], [yp_blk[i]], last)
                _chk(9)
        return MARK, qS, kS, vS, relb, w_out, rden

    def finish_attn(U, n, w_out, rden, src_ap, src_units, dst_ap, dst_units, last):
        Uv = U[0:n, :].rearrange("p (h e) -> p h e", e=80)
        P.op("dve", lambda e: e.reciprocal(out=rden[0:n, :], in_=Uv[:, :, 64]), reads=[U], writes=[rden])
        on = on_ring.next()
        P.op("dve", lambda e, on=on: e.tensor_tensor(out=on[0:n, :].rearrange("p (h e) -> p h e", e=64), in0=Uv[:, :, 0:64], in1=rden[0:n, :].unsqueeze(2).broadcast_to([n, 16, 64]), op=ALU.mult), reads=[U, rden], writes=[on])
        onT = onT_ring.next()
        transpose_to(on, n, onT)
        out_proj_residual(onT, n, w_out, 8, src_ap, src_units, dst_ap, dst_units, last)

    def dil_sample(layer, li, g, ctx, Us_acc, first_group, last):
        MARK, qS, kS, vS, relb, w_out, rden = ctx
        _chk(4)
        W, d = GROUPS[g]
        P.barrier()
        P.off = MARK
        UB = (4, 5, 6)
        ohs = P.sb("s_ohs", [NB, 6 * 128], F32)
        P.dma("sp", ohs[:, :], c_ohs.h[:, :], key=ohs, writes=[ohs])
        sel = P.sb("s_sel", [16, 16, 128], F32)
        selT = P.sb("s_selT", [128, 16, 16], F32)
        P.op("dve", lambda e: e.tensor_copy(out=sel[:, :, :], in_=identf[0:16, 0:16].unsqueeze(2).broadcast_to([16, 16, 128])), reads=[identf], writes=[sel])
        P.op("pool", lambda e: e.memset(selT[:, :, :], 0.0), writes=[selT])
        for t in range(16):
            P.op("pool", lambda e, t=t: e.memset(selT[:, t, t:t + 1], 1.0), writes=[selT])
        nvar = 4 if g == 0 else 1
        BS = P.sb("s_BS", [128, 4, 16], F32)
        for v in range(nvar):
            vv = v if g == 0 else 3 + g
            bb = mmB_ring.next()
            P.op("pe", lambda e, bb=bb, vv=vv: e.matmul(out=PSB[bb][:, 0:16], lhsT=ohs[:, vv * 128:(vv + 1) * 128], rhs=relb[:, g * 16:(g + 1) * 16], start=True, stop=True), reads=[ohs, relb], writes=[PSB[bb]])
            P.op("act", lambda e, bb=bb, v=v: e.activation(out=BS[:, v, :], in_=PSB[bb][:, 0:16], func=AF.Exp), reads=[PSB[bb]], writes=[BS], partial=(v > 0))
        eb0 = P.sb("s_eb0", [16, 16], F32)
        P.dma("sp", eb0[:, :], bass.AP(rel_bias.h, g * 16, [[0, 16], [1, 16]]), key=eb0, writes=[eb0])
        P.op("act", lambda e: e.activation(out=eb0[:, :], in_=eb0[:, :], func=AF.Exp), reads=[eb0], writes=[eb0])
        _chk(5)
        Kt_r = Ring([P.sb("s_Kt%d" % i, [128, D], F32) for i in range(2)])
        Vt_r = Ring([P.sb("s_Vt%d" % i, [128, D], F32) for i in range(2)])
        prod = P.sb("s_prod", [128, D], F32)
        sc_r = Ring([P.sb("s_sc%d" % i, [128, 16], F32) for i in range(2)])
        pw_r = Ring([P.sb("s_pw%d" % i, [128, 16], F32) for i in range(2)])
        Wt_r = Ring([P.sb("s_Wt%d" % i, [128, 1280], F32) for i in range(2)])
        for t in Wt_r.items:
            P.op("pool", lambda e, t=t: e.memset(t[:, :], 0.0), writes=[t])
        Usg = P.sb("s_Usg", [16, 1280], F32)
        p16 = P.sb("s_p16", [16, 32], F32)
        tmp16 = P.sb("s_tmp16", [16, D], F32)
        rden = P.sb("s_rden", [128, 16], F32)
        cnt = 0
        for b in range(4):
            for s_ in range(4):
                tk = 4 * b + s_
                Kt = Kt_r.next()
                Vt = Vt_r.next()
                base = (li * 4 + b) * W
                for (dstt, cache, newo) in ((Kt, ck[g], kso[g]), (Vt, cv[g], vso[g])):
                    if g == 0:
                        P.dma("sp", dstt[s_:128, :], cache.h[base + s_:base + 128, :], key=dstt, writes=[dstt])
                        if s_ > 0:
                            P.dma("sp", dstt[0:s_, :], newo.h[li * 16 + 4 * b:li * 16 + 4 * b + s_, :], key=dstt, reads=[newo], writes=[dstt], partial=True)
                    else:
                        P.dma("sp", dstt[:, :], bass.AP(cache.h, (base + s_) * D, [[d * D, 128], [1, D]]), key=dstt, writes=[dstt])
                pair = mmA_ring.next()
                for half in range(2):
                    bq = PSB[pair[half]]
                    P.op("pe", lambda e, bq=bq, half=half, tk=tk: e.matmul(out=bq[:, :], lhsT=sel[:, tk, :], rhs=qS[0:16, half * 512:(half + 1) * 512], start=True, stop=True), reads=[sel, qS], writes=[bq])
                    P.op("dve", lambda e, bq=bq, half=half, Kt=Kt: e.tensor_tensor(out=prod[:, half * 512:(half + 1) * 512], in0=bq[:, :], in1=Kt[:, half * 512:(half + 1) * 512], op=ALU.mult), reads=[bq, Kt], writes=[prod], partial=(half > 0))
                sc = sc_r.next()
                pw = pw_r.next()
                P.op("dve", lambda e, sc=sc: e.tensor_reduce(out=sc[:, :], in_=prod[:, :].rearrange("p (h e) -> p h e", e=64), axis=AX.X, op=ALU.add), reads=[prod], writes=[sc])
                P.op("act", lambda e, sc=sc: e.activation(out=sc[:, :], in_=sc[:, :], func=AF.Exp, scale=0.125), reads=[sc], writes=[sc])
                var = s_ if g == 0 else 0
                P.op("dve", lambda e, sc=sc, pw=pw, var=var: e.tensor_tensor(out=pw[:, :], in0=sc[:, :], in1=BS[:, var, :], op=ALU.mult), reads=[sc, BS], writes=[pw])
                Wt = Wt_r.next()
                Wv = Wt[:, :].rearrange("p (h e) -> p h e", e=80)
                P.op("dve", lambda e, Wv=Wv, Vt=Vt, pw=pw: e.tensor_tensor(out=Wv[:, :, 0:64], in0=Vt[:, :].rearrange("p (h e) -> p h e", e=64), in1=pw[:, :].unsqueeze(2).broadcast_to([128, 16, 64]), op=ALU.mult), reads=[Vt, pw], writes=[Wt])
                P.op("pool", lambda e, Wv=Wv, pw=pw: e.tensor_copy(out=Wv[:, :, 64], in_=pw[:, :]), reads=[pw], writes=[Wt])
                for bi, (a0, a1) in enumerate(((0, 480), (480, 960), (960, 1280))):
                    P.op("pe", lambda e, bi=bi, a0=a0, a1=a1, Wt=Wt, tk=tk, cnt=cnt: e.matmul(out=PSB[UB[bi]][0:16, 0:a1 - a0], lhsT=selT[:, tk, :], rhs=Wt[:, a0:a1], start=(cnt == 0), stop=(cnt == 15)), reads=[selT, Wt], writes=[PSB[UB[bi]]], partial=True)
                cnt += 1
        for bi, (a0, a1) in enumerate(((0, 480), (480, 960), (960, 1280))):
            P.op("act", lambda e, bi=bi, a0=a0, a1=a1: e.copy(out=Usg[:, a0:a1], in_=PSB[UB[bi]][0:16, 0:a1 - a0]), reads=[PSB[UB[bi]]], writes=[Usg], partial=(bi > 0))
        P.op("dve", lambda e: e.tensor_tensor(out=tmp16[:, :], in0=qS[:, :], in1=kS[:, :], op=ALU.mult), reads=[qS, kS], writes=[tmp16])
        P.op("dve", lambda e: e.tensor_reduce(out=p16[:, 0:16], in_=tmp16[:, :].rearrange("p (h e) -> p h e", e=64), axis=AX.X, op=ALU.add), reads=[tmp16], writes=[p16])
        P.op("act", lambda e: e.activation(out=p16[:, 0:16], in_=p16[:, 0:16], func=AF.Exp, scale=0.125), reads=[p16], writes=[p16])
        P.op("dve", lambda e: e.tensor_tensor(out=p16[:, 16:32], in0=p16[:, 0:16], in1=eb0[:, :], op=ALU.mult), reads=[p16, eb0], writes=[p16])
        Ugv = Usg[:, :].rearrange("p (h e) -> p h e", e=80)
        P.op("dve", lambda e: e.tensor_tensor(out=tmp16[:, :].rearrange("p (h e) -> p h e", e=64), in0=vS[:, :].rearrange("p (h e) -> p h e", e=64), in1=p16[:, 16:32].unsqueeze(2).broadcast_to([16, 16, 64]), op=ALU.mult), reads=[vS, p16], writes=[tmp16])
        P.op("dve", lambda e: e.tensor_tensor(out=Ugv[:, :, 0:64], in0=Ugv[:, :, 0:64], in1=tmp16[:, :].rearrange("p (h e) -> p h e", e=64), op=ALU.add), reads=[Usg, tmp16], writes=[Usg])
        P.op("dve", lambda e: e.tensor_tensor(out=Ugv[:, :, 64], in0=Ugv[:, :, 64], in1=p16[:, 16:32], op=ALU.add), reads=[Usg, p16], writes=[Usg])
        if first_group:
            P.op("dve", lambda e: e.tensor_copy(out=Us_acc[:, :], in_=Usg[:, :]), reads=[Usg], writes=[Us_acc])
        else:
            P.op("dve", lambda e: e.tensor_tensor(out=Us_acc[:, :], in0=Us_acc[:, :], in1=Usg[:, :], op=ALU.add), reads=[Us_acc, Usg], writes=[Us_acc])
        if g == 0:
            sa, su = sample_src_all()
            finish_attn(Us_acc, 16, w_out, rden, sa, su, ys.h[0:16, :], [ys_blk], last)

    def dil_layer(layer, li, last):
        Us_acc = T("Us_acc", Us_acc_t.h)
        for gi, g in enumerate((2, 1, 0)):
            ctx = dil_group(layer, li, g, last)
            dil_sample(layer, li, g, ctx, Us_acc, gi == 0, last)
        RR.tp = Ring([0, 1])
        RR.mmA = Ring([(2, 3), (4, 5)])
        RR.mmB = Ring([6, 7])
        state["first"] = False

    Us_acc_t = P.sb("Us_acc", [16, 1280], F32)
    PERSIST = P.off
    dil_setup()
    try:
        for layer in range(depth):
            li = layer // 2
            if layer % 2 == 0:
                gla_layer(layer, li, (layer == depth - 1) and not do_ffn)
            else:
                dil_layer(layer, li, (layer == depth - 1) and not do_ffn)
            if do_ffn:
                ffn_layer(layer, layer == depth - 1)
    except _Stop:
        pass
    P.emit()
    return nc, P


_CACHE = {}


def kernel(x_prompt, x_sample, state_gla, cache_k_g0, cache_v_g0, cache_k_g1, cache_v_g1,
           cache_k_g2, cache_v_g2, state_ffn_conv, rel_bias, norm_mix, norm_ffn,
           gla_w_in, gla_w_gate2, gla_b_gate, gla_norm, gla_w_out,
           dil_w_in, dil_q_norm, dil_k_norm, dil_w_out,
           ffn_w_up, ffn_conv_w, ffn_conv_b, ffn_w_down):
    f = lambda a: np.ascontiguousarray(np.asarray(a, dtype=np.float32))
    if "nc" not in _CACHE:
        _CACHE["nc"] = build_program()[0]
    nc = _CACHE["nc"]
    ohp, valid, ohs = host_constants()
    cks = [f(cache_k_g0), f(cache_k_g1), f(cache_k_g2)]
    cvs = [f(cache_v_g0), f(cache_v_g1), f(cache_v_g2)]
    x_prompt = f(x_prompt); x_sample = f(x_sample); state_gla = f(state_gla); state_ffn_conv = f(state_ffn_conv)
    shared = {
        "rel_bias": f(rel_bias), "norm_mix": f(norm_mix), "norm_ffn": f(norm_ffn),
        "gla_w_in": f(gla_w_in).reshape(2 * D, GLA_IN), "gla_w_gate2": f(gla_w_gate2).reshape(32, 512),
        "gla_b_gate": f(gla_b_gate), "gla_norm": f(gla_norm), "gla_w_out": f(gla_w_out).reshape(2 * D, D),
        "dil_w_in": f(dil_w_in).reshape(2 * D, 9216), "dil_q_norm": f(dil_q_norm), "dil_k_norm": f(dil_k_norm),
        "dil_w_out": f(dil_w_out).reshape(2 * D, D), "ffn_w_up": f(ffn_w_up).reshape(4 * D, 2 * DFF),
        "ffn_conv_w": f(ffn_conv_w).reshape(12, 2 * DFF), "ffn_conv_b": f(ffn_conv_b),
        "ffn_w_down": f(ffn_w_down).reshape(4 * DFF, D),
        "c_ohp": ohp, "c_valid": valid, "c_ohs": ohs,
    }
    in_maps = []
    for c in range(8):
        m = dict(shared)
        m["xp"] = x_prompt[c % 4]
        m["xs"] = x_sample[4 * c:4 * c + 4].reshape(16, D)
        m["sg"] = np.ascontiguousarray(state_gla[:, 4 * c:4 * c + 4]).reshape(2 * 4 * 4 * 128, 256)
        for g in range(3):
            Wg = GROUPS[g][0]
            m["ck%d" % g] = np.ascontiguousarray(cks[g][:, 4 * c:4 * c + 4]).reshape(2 * 4 * Wg, D)
            m["cv%d" % g] = np.ascontiguousarray(cvs[g][:, 4 * c:4 * c + 4]).reshape(2 * 4 * Wg, D)
        m["sfc"] = np.ascontiguousarray(state_ffn_conv[:, 4 * c:4 * c + 4]).reshape(32, 2 * DFF)
        in_maps.append(m)
    res = run_bass_kernel_spmd(nc, in_maps, core_ids=list(range(8)))
    R = res.results
    B = 4
    y_prompt = np.stack([R[b]["yp"] for b in range(B)]).astype(np.float32)
    y_sample = np.concatenate([R[c]["ys"].reshape(4, 4, D) for c in range(8)], 0).astype(np.float32)
    sgp = np.stack([R[b]["sgp"].reshape(2, 4, 128, 256) for b in range(B)], 1).astype(np.float32)
    sgs = np.concatenate([R[c]["sgs"].reshape(2, 4, 4, 128, 256) for c in range(8)], 1).astype(np.float32)
    outs = [y_prompt, y_sample, sgp, sgs]
    keep = [128, 512, 2048]
    for g in range(3):
        kp = np.stack([R[b]["kp%d" % g].reshape(2, keep[g], 16, 64) for b in range(B)], 1).astype(np.float32)
        ks = np.concatenate([R[c]["ks%d" % g].reshape(2, 4, 4, 16, 64) for c in range(8)], 1).astype(np.float32)
        vp = np.stack([R[b]["vp%d" % g].reshape(2, keep[g], 16, 64) for b in range(B)], 1).astype(np.float32)
        vs = np.concatenate([R[c]["vs%d" % g].reshape(2, 4, 4, 16, 64) for c in range(8)], 1).astype(np.float32)
        outs += [kp, ks, vp, vs]
    fcp = np.stack([R[b]["fcp"].reshape(4, 2, 2 * DFF) for b in range(B)], 1).astype(np.float32)
    fcs = np.concatenate([R[c]["fcs"].reshape(4, 4, 2, 2 * DFF) for c in range(8)], 1).astype(np.float32)
    outs += [fcp, fcs]
    return tuple(outs)
```
